# Optimizing a Trainium2 kernel written in Bass

```python
import math
import jax, jax.numpy as jnp
from jax import lax
import numpy as np

D_MODEL = 1024
BATCH = 2
SEQ = 8192
DEPTH = 2

BLOCK = 128
A_WIDTH = D_MODEL // 2
A_HEAD_DIM = 64
A_HEADS = A_WIDTH // A_HEAD_DIM
A_KV_HEADS = A_HEADS // 4
WINDOW = 128
B_WIDTH = D_MODEL - A_WIDTH
SSM_GROUP = 16
SSM_GROUPS = B_WIDTH // SSM_GROUP
SSM_STATE = 64
DT_MIN = 0.001
DT_MAX = 0.1
C_WIDTH = D_MODEL
C_HEADS = A_HEADS
C_HEAD_DIM = C_WIDTH // C_HEADS
C_KV_HEADS = C_HEADS // 4
IDX_HEADS = 8
IDX_DIM = 64
TOPK_MAX = 256
NUM_BUCKETS = 32
REL_MAX_DIST = 1024
EPS = 1e-6
NEG_INF = -1e30

EVEN_SPLITS = (A_WIDTH, A_KV_HEADS * A_HEAD_DIM, A_KV_HEADS * A_HEAD_DIM, A_WIDTH, B_WIDTH, B_WIDTH)
ODD_SPLITS = (C_WIDTH, C_KV_HEADS * C_HEAD_DIM, C_KV_HEADS * C_HEAD_DIM, C_WIDTH,
              IDX_HEADS * IDX_DIM, IDX_DIM, IDX_HEADS)
N_EVEN = (DEPTH + 1) // 2
N_ODD = DEPTH // 2

kernel_name = 'hybrid_swa_s5_dsa_block'


def rms_norm(x, g):
    xf = x.astype(jnp.float32)
    y = xf * lax.rsqrt(jnp.mean(xf * xf, axis=-1, keepdims=True) + EPS)
    return (y * g.astype(jnp.float32)).astype(x.dtype)


def split_cols(z, sizes):
    idx = np.cumsum(sizes)[:-1].tolist()
    return jnp.split(z, idx, axis=-1)


def t5_bucket(dist):
    n = jnp.maximum(dist, 0)
    max_exact = NUM_BUCKETS // 2
    nf = jnp.maximum(n, 1).astype(jnp.float32)
    large = max_exact + (jnp.log(nf / max_exact) / math.log(REL_MAX_DIST / max_exact)
                         * (NUM_BUCKETS - max_exact)).astype(jnp.int32)
    large = jnp.minimum(large, NUM_BUCKETS - 1)
    return jnp.where(n < max_exact, n, large)


def swa_sink_attention(q, k, v, sinks, rel_bias):
    bsz, L, H, Dh = q.shape
    hkv = k.shape[2]
    G = H // hkv
    nb = L // BLOCK
    qb = q.reshape(bsz, nb, BLOCK, hkv, G, Dh)
    kb = k.reshape(bsz, nb, BLOCK, hkv, Dh)
    vb = v.reshape(bsz, nb, BLOCK, hkv, Dh)
    k_band = jnp.concatenate([jnp.concatenate([jnp.zeros_like(kb[:, :1]), kb[:, :-1]], axis=1), kb], axis=2)
    v_band = jnp.concatenate([jnp.concatenate([jnp.zeros_like(vb[:, :1]), vb[:, :-1]], axis=1), vb], axis=2)
    logits = jnp.einsum('bnqhgd,bnkhd->bnhgqk', qb, k_band).astype(jnp.float32) * (Dh ** -0.5)
    i = jnp.arange(BLOCK, dtype=jnp.int32)[:, None]
    j = jnp.arange(2 * BLOCK, dtype=jnp.int32)[None, :]
    d = i + BLOCK - j
    in_window = (d >= 0) & (d < WINDOW)
    first = (jnp.arange(nb) == 0)[:, None, None] & (j < BLOCK)[None]
    mask = in_window[None] & ~first
    bias = rel_bias[t5_bucket(d)].astype(jnp.float32)
    bias = bias.transpose(2, 0, 1).reshape(hkv, G, BLOCK, 2 * BLOCK)
    logits = jnp.where(mask[None, :, None, None], logits + bias, NEG_INF)
    sink = jnp.broadcast_to(sinks.astype(jnp.float32).reshape(1, 1, hkv, G, 1, 1), logits.shape[:-1] + (1,))
    p = jax.nn.softmax(jnp.concatenate([logits, sink], axis=-1), axis=-1)[..., :-1]
    out = jnp.einsum('bnhgqk,bnkhd->bnqhgd', p.astype(v.dtype), v_band)
    return out.reshape(bsz, L, H * Dh)


def ssm_combine(e1, e2):
    a1r, a1i, b1r, b1i = e1
    a2r, a2i, b2r, b2i = e2
    ar = a1r * a2r - a1i * a2i
    ai = a1r * a2i + a1i * a2r
    br = a2r * b1r - a2i * b1i + b2r
    bi = a2r * b1i + a2i * b1r + b2i
    return (ar, ai, br, bi)


def s5_mixer(u, log_dt, a_re, a_im, b_re, b_im, c_re, c_im, d_skip, glu_w, glu_b):
    bsz, L, W = u.shape
    G, P = a_re.shape
    ug = u.reshape(bsz, L, G, W // G)
    dt = jnp.exp(log_dt)[:, None]
    mag = jnp.exp(a_re * dt)
    ang = a_im * dt
    ab_re = mag * jnp.cos(ang)
    ab_im = mag * jnp.sin(ang)
    den = a_re * a_re + a_im * a_im
    n_re = ab_re - 1.0
    n_im = ab_im
    f_re = (n_re * a_re + n_im * a_im) / den
    f_im = (n_im * a_re - n_re * a_im) / den
    bb_re = f_re[..., None] * b_re - f_im[..., None] * b_im
    bb_im = f_re[..., None] * b_im + f_im[..., None] * b_re
    bu_re = jnp.einsum('blgh,gph->blgp', ug, bb_re)
    bu_im = jnp.einsum('blgh,gph->blgp', ug, bb_im)
    at_re = jnp.broadcast_to(ab_re[None, None], (1, L, G, P))
    at_im = jnp.broadcast_to(ab_im[None, None], (1, L, G, P))
    _, _, x_re, x_im = lax.associative_scan(ssm_combine, (at_re, at_im, bu_re, bu_im), axis=1)
    y = (jnp.einsum('blgp,ghp->blgh', x_re, c_re) - jnp.einsum('blgp,ghp->blgh', x_im, c_im)
         + d_skip * ug)
    y = jax.nn.gelu(y.reshape(bsz, L, W))
    h = y @ glu_w + glu_b
    return h[..., :W] * jax.nn.sigmoid(h[..., W:])


def dsa_attention(q, k, v, qi, ki, wi, rel_bias):
    bsz, L, H, Dh = q.shape
    hkv = k.shape[2]
    G = H // hkv
    nb = L // BLOCK
    topk = min(TOPK_MAX, L // 4)
    qb = q.reshape(bsz, nb, BLOCK, hkv, G, Dh).transpose(1, 0, 2, 3, 4, 5)
    qib = qi.reshape(bsz, nb, BLOCK, IDX_HEADS, IDX_DIM).transpose(1, 0, 2, 3, 4)
    wib = wi.reshape(bsz, nb, BLOCK, IDX_HEADS).transpose(1, 0, 2, 3)
    starts = jnp.arange(nb, dtype=jnp.int32) * BLOCK
    key_pos = jnp.arange(L, dtype=jnp.int32)

    def one_block(args):
        q_blk, qi_blk, w_blk, start = args
        t = start + jnp.arange(BLOCK, dtype=jnp.int32)
        rel = jax.nn.relu(jnp.einsum('bqhd,bsd->bqhs', qi_blk, ki).astype(jnp.float32) * (IDX_DIM ** -0.5))
        score = jnp.einsum('bqhs,bqh->bqs', rel, w_blk.astype(jnp.float32))
        score = jnp.where(key_pos[None, None, :] <= t[None, :, None], score, NEG_INF)
        _, idx = lax.top_k(score, topk)
        valid = idx <= t[None, :, None]
        k_sel = jax.vmap(lambda kk, ii: kk[ii])(k, idx)
        v_sel = jax.vmap(lambda vv, ii: vv[ii])(v, idx)
        logits = jnp.einsum('bqhgd,bqkhd->bqhgk', q_blk, k_sel).astype(jnp.float32) * (Dh ** -0.5)
        bias = rel_bias[t5_bucket(t[None, :, None] - idx)].astype(jnp.float32)
        bias = bias.reshape(bsz, BLOCK, topk, hkv, G).transpose(0, 1, 3, 4, 2)
        logits = jnp.where(valid[:, :, None, None, :], logits + bias, NEG_INF)
        p = jax.nn.softmax(logits, axis=-1).astype(v.dtype)
        return jnp.einsum('bqhgk,bqkhd->bqhgd', p, v_sel)

    out = lax.map(one_block, (qb, qib, wib, starts))
    return out.transpose(1, 0, 2, 3, 4, 5).reshape(bsz, L, H * Dh)


def even_layer(hn, rel_bias, w_in, w_out, q_g, k_g, sinks, log_dt, a_re, a_im, b_re, b_im,
               c_re, c_im, d_skip, glu_w, glu_b):
    bsz, L, _ = hn.shape
    q, k, v, gate_a, u, gate_b = split_cols(hn @ w_in, EVEN_SPLITS)
    q = rms_norm(q.reshape(bsz, L, A_HEADS, A_HEAD_DIM), q_g)
    k = rms_norm(k.reshape(bsz, L, A_KV_HEADS, A_HEAD_DIM), k_g)
    v = v.reshape(bsz, L, A_KV_HEADS, A_HEAD_DIM)
    att = swa_sink_attention(q, k, v, sinks, rel_bias) * jax.nn.silu(gate_a)
    ssm = s5_mixer(u, log_dt, a_re, a_im, b_re, b_im, c_re, c_im, d_skip, glu_w, glu_b) * jax.nn.silu(gate_b)
    return jnp.concatenate([att, ssm], axis=-1) @ w_out


def odd_layer(hn, rel_bias, w_in, w_out, q_g, k_g):
    bsz, L, _ = hn.shape
    q, k, v, gate, qi, ki, wi = split_cols(hn @ w_in, ODD_SPLITS)
    q = rms_norm(q.reshape(bsz, L, C_HEADS, C_HEAD_DIM), q_g)
    k = rms_norm(k.reshape(bsz, L, C_KV_HEADS, C_HEAD_DIM), k_g)
    v = v.reshape(bsz, L, C_KV_HEADS, C_HEAD_DIM)
    qi = qi.reshape(bsz, L, IDX_HEADS, IDX_DIM)
    wi = wi * (IDX_HEADS ** -0.5)
    att = dsa_attention(q, k, v, qi, ki, wi, rel_bias)
    return (att * jax.nn.silu(gate)) @ w_out


def setup_inputs(seed: int = 0) -> dict:
    key = jax.random.key(seed)
    ks = jax.random.split(key, 24)
    f32 = jnp.float32

    def nrm(k, shape, s):
        return jax.random.normal(k, shape, f32) * s

    ev_in = sum(EVEN_SPLITS)
    od_in = sum(ODD_SPLITS)
    ssm_n = jnp.arange(SSM_STATE, dtype=f32)
    return {
        'x': nrm(ks[0], (BATCH, SEQ, D_MODEL), 1.0),
        'rel_bias': nrm(ks[1], (NUM_BUCKETS, A_HEADS), 0.5),
        'norm_g': 1.0 + nrm(ks[2], (DEPTH, D_MODEL), 0.02),
        'ev_w_in': nrm(ks[3], (N_EVEN, D_MODEL, ev_in), D_MODEL ** -0.5),
        'ev_w_out': nrm(ks[4], (N_EVEN, A_WIDTH + B_WIDTH, D_MODEL), (A_WIDTH + B_WIDTH) ** -0.5),
        'ev_q_norm_g': 1.0 + nrm(ks[5], (N_EVEN, A_HEAD_DIM), 0.02),
        'ev_k_norm_g': 1.0 + nrm(ks[6], (N_EVEN, A_HEAD_DIM), 0.02),
        'ev_sinks': nrm(ks[7], (N_EVEN, A_HEADS), 1.0),
        'ev_ssm_log_dt': jax.random.uniform(ks[8], (N_EVEN, SSM_GROUPS), f32, math.log(DT_MIN), math.log(DT_MAX)),
        'ev_ssm_a_re': -0.5 + nrm(ks[9], (N_EVEN, SSM_GROUPS, SSM_STATE), 0.01),
        'ev_ssm_a_im': jnp.pi * ssm_n + nrm(ks[10], (N_EVEN, SSM_GROUPS, SSM_STATE), 0.01),
        'ev_ssm_b_re': nrm(ks[11], (N_EVEN, SSM_GROUPS, SSM_STATE, SSM_GROUP), (2 * SSM_GROUP) ** -0.5),
        'ev_ssm_b_im': nrm(ks[12], (N_EVEN, SSM_GROUPS, SSM_STATE, SSM_GROUP), (2 * SSM_GROUP) ** -0.5),
        'ev_ssm_c_re': nrm(ks[13], (N_EVEN, SSM_GROUPS, SSM_GROUP, SSM_STATE), (2 * SSM_STATE) ** -0.5),
        'ev_ssm_c_im': nrm(ks[14], (N_EVEN, SSM_GROUPS, SSM_GROUP, SSM_STATE), (2 * SSM_STATE) ** -0.5),
        'ev_ssm_d': nrm(ks[15], (N_EVEN, SSM_GROUPS, SSM_GROUP), 1.0),
        'ev_glu_w': nrm(ks[16], (N_EVEN, B_WIDTH, 2 * B_WIDTH), B_WIDTH ** -0.5),
        'ev_glu_b': nrm(ks[17], (N_EVEN, 2 * B_WIDTH), 0.01),
        'od_w_in': nrm(ks[18], (N_ODD, D_MODEL, od_in), D_MODEL ** -0.5),
        'od_w_out': nrm(ks[19], (N_ODD, C_WIDTH, D_MODEL), C_WIDTH ** -0.5),
        'od_q_norm_g': 1.0 + nrm(ks[20], (N_ODD, C_HEAD_DIM), 0.02),
        'od_k_norm_g': 1.0 + nrm(ks[21], (N_ODD, C_HEAD_DIM), 0.02),
    }


def reference(x, rel_bias, norm_g, ev_w_in, ev_w_out, ev_q_norm_g, ev_k_norm_g, ev_sinks,
              ev_ssm_log_dt, ev_ssm_a_re, ev_ssm_a_im, ev_ssm_b_re, ev_ssm_b_im, ev_ssm_c_re,
              ev_ssm_c_im, ev_ssm_d, ev_glu_w, ev_glu_b, od_w_in, od_w_out, od_q_norm_g, od_k_norm_g):
    h = x
    for layer in range(DEPTH):
        hn = rms_norm(h, norm_g[layer])
        j = layer // 2
        if layer % 2 == 0:
            out = even_layer(hn, rel_bias, ev_w_in[j], ev_w_out[j], ev_q_norm_g[j], ev_k_norm_g[j],
                             ev_sinks[j], ev_ssm_log_dt[j], ev_ssm_a_re[j], ev_ssm_a_im[j],
                             ev_ssm_b_re[j], ev_ssm_b_im[j], ev_ssm_c_re[j], ev_ssm_c_im[j],
                             ev_ssm_d[j], ev_glu_w[j], ev_glu_b[j])
        else:
            out = odd_layer(hn, rel_bias, od_w_in[j], od_w_out[j], od_q_norm_g[j], od_k_norm_g[j])
        h = h + out
    return h
```

```python
import contextlib
import math
import numpy as np
import ml_dtypes
import concourse.bass as bass
import concourse.mybir as mybir
from concourse.bass_utils import run_bass_kernel_spmd

F32 = mybir.dt.float32
BF16 = mybir.dt.bfloat16
I32 = mybir.dt.int32
I8 = mybir.dt.int8
ALU = mybir.AluOpType
AF = mybir.ActivationFunctionType
AX = mybir.AxisListType

NCORES = 8
D = 1024
SEQ = 8192
BATCH = 2
TPC = 2048
NBLK = TPC // 128
EPS = 1e-6
NEG = -30000.0
DBG = {}
SEM_LIMIT = 30000


class Buf:
    __slots__ = ("name", "lw", "rd", "dsem", "dcnt")

    def __init__(self, name):
        self.name = name
        self.lw = None
        self.rd = {}
        self.dsem = {}
        self.dcnt = {}


class V:
    __slots__ = ("ap", "bufs")

    def __init__(self, ap, bufs):
        self.ap = ap
        self.bufs = bufs

    def __getitem__(self, idx):
        return V(self.ap[idx], self.bufs)

    def bc(self, shape):
        return V(self.ap.broadcast_to(list(shape)), self.bufs)

    def re(self, pat, **kw):
        return V(self.ap.rearrange(pat, **kw), self.bufs)

    def bitcast(self, dt):
        return V(self.ap.bitcast(dt), self.bufs)


class Tile:
    def __init__(self, P, name, shape, dtype, space="sbuf"):
        nc = P.nc
        if space == "sbuf":
            self.t = P.es.enter_context(nc.sbuf_tensor(name, list(shape), dtype))
        elif space == "psum":
            self.t = P.es.enter_context(nc.psum_tensor(name, list(shape), dtype))
        else:
            raise ValueError(space)
        self.buf = Buf(name)
        self.name = name
        self.shape = shape

    def __getitem__(self, idx):
        return V(self.t[idx], (self.buf,))

    def v(self, idx, buf):
        return V(self.t[idx], (buf,))

    def all(self):
        return V(self.t[:], (self.buf,))


class Prog:
    def __init__(self, nc):
        self.nc = nc
        self.es = contextlib.ExitStack()
        self.eng = {"pe": nc.tensor, "dve": nc.vector, "act": nc.scalar, "pool": nc.gpsimd, "sp": nc.sync}
        self.semh = {}
        self.esem = {}
        self.cnt = {}
        self.epoch = {}
        self.waited = {e: {} for e in self.eng}
        self.nsem = 0
        for e in ("pe", "dve", "act", "pool"):
            self.epoch[e] = 0
            self._new_eng_sem(e)
        self.out_waits = []
        self.n_instr = 0

    def _sem(self, name):
        h = self.es.enter_context(self.nc.semaphore(name))
        self.semh[name] = h
        self.nsem += 1
        return name

    def _new_eng_sem(self, e):
        name = "c_%s_%d" % (e, self.epoch[e])
        self._sem(name)
        self.esem[e] = name
        self.cnt[e] = 0
        self.epoch[e] += 1

    def sb(self, name, shape, dtype):
        return Tile(self, name, shape, dtype, "sbuf")

    def ps(self, name, shape, dtype=F32):
        return Tile(self, name, shape, dtype, "psum")

    def _deps(self, e, reads, writes):
        deps = {}

        def add(sn, val, src, kind):
            if src == e and e == "pe":
                return
            if deps.get(sn, 0) < val:
                deps[sn] = val

        for b in reads:
            if b.lw is not None:
                add(b.lw[0], b.lw[1], b.lw[2], "raw")
        for b in writes:
            if b.lw is not None:
                add(b.lw[0], b.lw[1], b.lw[2], "waw")
            for sn, (v, se) in b.rd.items():
                add(sn, v, se, "war")
        h = self.eng[e]
        w = self.waited[e]
        for sn, v in deps.items():
            if w.get(sn, 0) >= v:
                continue
            h.wait_ge(self.semh[sn], v)
            w[sn] = v
            self.n_instr += 1

    def op(self, e, fn, ins=(), outs=()):
        reads = []
        for x in ins:
            if isinstance(x, V):
                reads.extend(x.bufs)
        writes = []
        for x in outs:
            if isinstance(x, V):
                writes.extend(x.bufs)
        self._deps(e, reads, writes)
        i = fn(self.eng[e])
        self.cnt[e] += 1
        self.n_instr += 1
        sn = self.esem[e]
        v = self.cnt[e]
        i.then_inc(self.semh[sn], 1)
        for b in writes:
            b.lw = (sn, v, e)
            b.rd = {}
        for b in reads:
            if b not in writes:
                b.rd[sn] = (v, e)
        if v >= SEM_LIMIT:
            self._new_eng_sem(e)
        return i

    def dma(self, q, out, in_, primary=None, nc_kwargs=None):
        reads = list(in_.bufs)
        writes = list(out.bufs)
        self._deps(q, reads, writes)
        if primary is None:
            primary = writes[0] if writes else reads[0]
        qc = "sw" if q == "pool" else "hw"
        if qc not in primary.dsem:
            primary.dsem[qc] = self._sem("d%s_%s" % (qc, primary.name))
            primary.dcnt[qc] = 0
        kw = nc_kwargs or {}
        i = self.eng[q].dma_start(out=out.ap, in_=in_.ap, **kw)
        primary.dcnt[qc] += 16
        sn, val = primary.dsem[qc], primary.dcnt[qc]
        i.then_inc(self.semh[sn], 16)
        self.n_instr += 1
        for b in writes:
            b.lw = (sn, val, "dma")
            b.rd = {}
        for b in reads:
            b.rd[sn] = (val, "dma")
        return (sn, val)

    def finish(self, bufs):
        h = self.eng["sp"]
        done = {}
        for b in bufs:
            if b.lw is not None:
                done[b.lw[0]] = max(done.get(b.lw[0], 0), b.lw[1])
            for sn, (v, se) in b.rd.items():
                done[sn] = max(done.get(sn, 0), v)
        for sn, v in done.items():
            h.wait_ge(self.semh[sn], v)

    def mm(self, out, lhsT, rhs, start=True, stop=True):
        return self.op("pe", lambda h: h.matmul(out.ap, lhsT=lhsT.ap, rhs=rhs.ap, start=start, stop=stop),
                       ins=(lhsT, rhs), outs=(out,))

    def tr(self, out, in_, ident):
        return self.op("pe", lambda h: h.transpose(out.ap, in_.ap, ident.ap), ins=(in_, ident), outs=(out,))

    def act(self, out, in_, func, bias=None, scale=None, accum=None, e="act"):
        kw = {}
        ins = [in_]
        outs = [out]
        if bias is not None:
            kw["bias"] = bias.ap if isinstance(bias, V) else bias
            ins.append(bias)
        if scale is not None:
            kw["scale"] = scale.ap if isinstance(scale, V) else scale
            ins.append(scale)
        if accum is not None:
            kw["accum_out"] = accum.ap
            outs.append(accum)
        return self.op(e, lambda h: h.activation(out=out.ap, in_=in_.ap, func=func, **kw), ins=ins, outs=outs)

    def ts(self, out, in0, s1, s2=None, op0=ALU.mult, op1=None, accum=None, e="dve"):
        kw = {}
        ins = [in0, s1, s2]
        outs = [out]
        if op1 is not None:
            kw["op1"] = op1
        if accum is not None:
            kw["accum_out"] = accum.ap
            outs.append(accum)
        a1 = s1.ap if isinstance(s1, V) else s1
        a2 = s2.ap if isinstance(s2, V) else s2
        return self.op(e, lambda h: h.tensor_scalar(out=out.ap, in0=in0.ap, scalar1=a1, scalar2=a2, op0=op0, **kw),
                       ins=ins, outs=outs)

    def tt(self, out, in0, in1, op, e="dve"):
        return self.op(e, lambda h: h.tensor_tensor(out=out.ap, in0=in0.ap, in1=in1.ap, op=op),
                       ins=(in0, in1), outs=(out,))

    def stt(self, out, in0, s, in1, op0, op1):
        a = s.ap if isinstance(s, V) else s
        return self.op("dve", lambda h: h.scalar_tensor_tensor(out=out.ap, in0=in0.ap, scalar=a, in1=in1.ap,
                                                                op0=op0, op1=op1),
                       ins=(in0, s, in1), outs=(out,))

    def copy(self, out, in_, e="dve"):
        if e == "act":
            return self.op("act", lambda h: h.copy(out=out.ap, in_=in_.ap), ins=(in_,), outs=(out,))
        return self.op(e, lambda h: h.tensor_copy(out=out.ap, in_=in_.ap), ins=(in_,), outs=(out,))

    def recip(self, out, in_):
        return self.op("dve", lambda h: h.reciprocal(out=out.ap, in_=in_.ap), ins=(in_,), outs=(out,))

    def reduce(self, out, in_, op, axis=AX.X):
        return self.op("dve", lambda h: h.tensor_reduce(out=out.ap, in_=in_.ap, axis=axis, op=op),
                       ins=(in_,), outs=(out,))

    def memset(self, out, val, e="dve"):
        return self.op(e, lambda h: h.memset(out.ap, val), ins=(), outs=(out,))

    def iota(self, out, pattern, base, cm):
        return self.op("pool", lambda h: h.iota(out.ap, pattern=pattern, base=base, channel_multiplier=cm,
                                                allow_small_or_imprecise_dtypes=True), ins=(), outs=(out,))


def dram_in(nc, name, shape, dtype):
    return nc.dram_tensor(name, list(shape), dtype, kind="ExternalInput")


def dram_out(nc, name, shape, dtype):
    return nc.dram_tensor(name, list(shape), dtype, kind="ExternalOutput")


def DV(t, ap=None, buf=None):
    return V(t.ap() if ap is None else ap, (buf,) if buf is not None else ())


def dap(t, offset, pattern):
    return bass.AP(t, offset, [list(p) for p in pattern])


def barrier(P):
    tgt = {P.esem[e]: P.cnt[e] for e in ("pe", "dve", "act", "pool") if P.cnt[e] > 0}
    for e in ("pe", "dve", "act", "pool", "sp"):
        for sn, v in tgt.items():
            if P.waited[e].get(sn, 0) < v:
                P.eng[e].wait_ge(P.semh[sn], v)
                P.waited[e][sn] = v


def make_consts(P):
    c = {}
    c["ident"] = P.sb("c_ident", [128, 128], BF16)
    with contextlib.ExitStack() as es2:
        old_es, P.es = P.es, es2
        io = P.sb("c_iota", [128, 128], F32)
        P.iota(io.all(), [[1, 128]], 0, -1)
        P.ts(c["ident"].all(), io.all(), 0.0, None, op0=ALU.is_equal)
        barrier(P)
        P.es = old_es
    c["ones"] = P.sb("c_ones", [128, 128], BF16)
    P.memset(c["ones"].all(), 1.0)
    return c


def load_fm_vec(P, name, dram_t, n):
    t = P.sb(name, [128, n], F32)
    P.dma("sp", t.all(), DV(dram_t, dap(dram_t, 0, [[1, 128], [128, n]])),
          nc_kwargs={"allow_slow_non_contiguous": True})
    return t


def rmsnorm_to_fm(P, c, x_v, hnT_v, g_fm, wk, nfeat=1024):
    nk = nfeat // 128
    P.act(wk["junk"].all(), x_v, AF.Square, accum=wk["ss"].all())
    P.act(wk["sd"].all(), wk["ss"].all(), AF.Sqrt, bias=wk["epsb"].all(), scale=1.0 / nfeat)
    P.recip(wk["rstd"].all(), wk["sd"].all())
    P.ts(wk["xn"].all(), x_v, wk["rstd"].all(), None, op0=ALU.mult)
    if DBG.get('no_tr'):
        return
    pst = wk["pst"]
    for k in range(nk):
        P.tr(pst[:, k * 128:(k + 1) * 128], wk["xn"][:, k * 128:(k + 1) * 128], c["ident"].all())
    if DBG.get('no_tt'):
        return
    if DBG.get('tt_copy'):
        P.copy(hnT_v, pst.all().re("p (k t) -> p k t", k=nk))
        return
    for k in range(nk):
        if k % 2 == 0:
            P.ts(hnT_v[:, k, :], pst[:, k * 128:(k + 1) * 128], g_fm[:, k:k + 1], None, op0=ALU.mult)
        else:
            P.act(hnT_v[:, k, :], pst[:, k * 128:(k + 1) * 128], AF.Copy, scale=g_fm[:, k:k + 1])


def make_norm_work(P, pfx, pst):
    wk = {}
    wk["junk"] = P.sb(pfx + "junk", [128, 1024], BF16)
    wk["ss"] = P.sb(pfx + "ss", [128, 1], F32)
    wk["sd"] = P.sb(pfx + "sd", [128, 1], F32)
    wk["rstd"] = P.sb(pfx + "rstd", [128, 1], F32)
    wk["xn"] = P.sb(pfx + "xn", [128, 1024], BF16)
    wk["epsb"] = P.sb(pfx + "epsb", [128, 1], F32)
    P.memset(wk["epsb"].all(), EPS)
    wk["pst"] = pst
    return wk


OD_Q, OD_K, OD_V, OD_G, OD_QI, OD_KI, OD_WI = 0, 1024, 1280, 1536, 2560, 3072, 3136


def build_p2b(nsb=TPC // 512, do_tiles=True, do_blocks=True):
    nc = bass.Bass("TRN2", target_bir_lowering=False)
    h1 = dram_in(nc, "h1", [TPC, D], F32)
    w_in = dram_in(nc, "w_in", [D, 3144], F32)
    ng = dram_in(nc, "ng", [D], F32)
    qg = dram_in(nc, "qg", [128], F32)
    kg = dram_in(nc, "kg", [128], F32)
    q_out = dram_out(nc, "q_out", [NBLK, 128, 8, 128], BF16)
    g_out = dram_out(nc, "g_out", [NBLK, 128, 8, 128], BF16)
    qi_out = dram_out(nc, "qi_out", [NBLK, 128, 4, 128], BF16)
    wi_out = dram_out(nc, "wi_out", [TPC, 8], F32)
    kT_out = dram_out(nc, "kT_out", [128, 2, TPC], BF16)
    v_out = dram_out(nc, "v_out", [TPC, 256], BF16)
    ki_out = dram_out(nc, "ki_out", [128, TPC], BF16)
    P = Prog(nc)
    with P.es:
        c = make_consts(P)
        W = P.sb("W", [128, 8, 3200], BF16)
        Wwi = P.sb("Wwi", [128, 8, 8], BF16)
        wbufs = [Buf("Wk%d" % k) for k in range(8)]
        for k in range(8):
            wv = V(W.t[:, k, :], (wbufs[k],))
            P.dma("pool", wv[:, 0:3136], DV(w_in, w_in.ap()[k * 128:(k + 1) * 128, 0:3136]), primary=wbufs[k])
            P.dma("pool", wv[:, 3136:3200], DV(w_in, w_in.ap()[k * 128:(k + 1) * 128, OD_KI:OD_KI + 64]),
                  primary=wbufs[k])
        P.dma("pool", Wwi.all(), DV(w_in, dap(w_in, OD_WI, [[3144, 128], [128 * 3144, 8], [1, 8]])))

        def Wk(k, c0, n):
            return V(W.t[:, k, c0:c0 + n], (wbufs[k],))

        g_fm = load_fm_vec(P, "g_fm", ng, 8)
        gq = P.sb("gq", [128, 1], F32)
        gk = P.sb("gk", [128, 1], F32)
        P.dma("sp", gq.all(), DV(qg, dap(qg, 0, [[1, 128], [1, 1]])))
        P.dma("sp", gk.all(), DV(kg, dap(kg, 0, [[1, 128], [1, 1]])))
        P.ts(gq.all(), gq.all(), 128.0 ** -0.5, None, op0=ALU.mult)
        pst = P.ps("pst", [128, 1024], BF16)
        wk = make_norm_work(P, "n_", pst)
        xb = [P.sb("xb%d" % i, [128, 1024], F32) for i in range(2)]
        hnT = P.sb("hnT", [128, 8, 512], BF16)
        ring = [P.ps("pr%d" % i, [128, 512]) for i in range(3)]
        ring2 = [P.ps("ps2_%d" % i, [128, 512]) for i in range(2)]
        ptm = P.ps("ptm", [128, 512])
        ptm2 = P.ps("ptm2", [128, 512])
        sq = [P.sb("sq%d" % i, [128, 512], BF16) for i in range(2)]
        sd = [P.sb("sdq%d" % i, [128, 512], F32) for i in range(2)]
        ob = [P.sb("ob%d" % i, [128, 512], BF16) for i in range(4)]
        vb = [P.sb("vb%d" % i, [128, 256], BF16) for i in range(2)]
        wib = [P.sb("wib%d" % i, [128, 8], F32) for i in range(2)]
        rr = [0, 0, 0, 0]
        outs = []

        def nxt(lst, idx):
            t = lst[rr[idx] % len(lst)]
            rr[idx] += 1
            return t

        blk_sz = 128 * 8 * 128
        for sb_ in range(nsb):
            for bl in range(4 if do_blocks else 0):
                t0 = sb_ * 512 + bl * 128
                x = xb[bl % 2]
                P.dma("sp", x.all(), DV(h1, h1.ap()[t0:t0 + 128, :]))
                rmsnorm_to_fm(P, c, x.all(), hnT[:, :, bl * 128:(bl + 1) * 128], g_fm, wk)
                if DBG.get('no_tm'):
                    continue
                for k in range(8):
                    P.mm(ptm[:, 0:256], hnT[:, k, bl * 128:(bl + 1) * 128], Wk(k, OD_V, 256), start=(k == 0), stop=(k == 7))
                v_sb = vb[bl % 2]
                P.copy(v_sb.all(), ptm[:, 0:256], e="act")
                if not DBG.get('no_vst'):
                    P.dma("pool", DV(v_out, v_out.ap()[t0:t0 + 128, :]), v_sb.all(), primary=v_sb.buf)
                    outs += [v_sb.buf]
                if DBG.get('no_wi'):
                    continue
                for k in range(8):
                    P.mm(ptm2[:, 0:8], hnT[:, k, bl * 128:(bl + 1) * 128], Wwi[:, k, :], start=(k == 0), stop=(k == 7))
                w_sb = wib[bl % 2]
                P.ts(w_sb.all(), ptm2[:, 0:8], (8.0 ** -0.5) * (64.0 ** -0.5), None, op0=ALU.mult)
                P.dma("pool", DV(wi_out, wi_out.ap()[t0:t0 + 128, :]), w_sb.all(), primary=w_sb.buf)
                outs += [w_sb.buf]
            tiles = [("q", h, OD_Q + h * 128) for h in range(8)] + [("k", g, OD_K + g * 128) for g in range(2)] + \
                    [("g", h, OD_G + h * 128) for h in range(8)] + [("qi", pr, OD_QI + pr * 128) for pr in range(4)] + \
                    [("ki", 0, OD_KI)]
            for (kind, idx, c0) in (tiles if do_tiles else []):
                pr_ = nxt(ring, 0)
                for k in range(8):
                    P.mm(pr_.all(), Wk(k, c0, 128), hnT[:, k, :], start=(k == 0), stop=(k == 7))
                o = nxt(ob, 1)
                if kind in ("q", "k"):
                    s = nxt(sq, 2)
                    P.act(s.all(), pr_.all(), AF.Square)
                    p2 = nxt(ring2, 3)
                    P.mm(p2.all(), c["ones"].all(), s.all())
                    d_ = sd[(rr[3] - 1) % 2]
                    P.act(d_.all(), p2.all(), AF.Sqrt, bias=wk["epsb"].all(), scale=1.0 / 128)
                    P.recip(d_.all(), d_.all())
                    P.stt(o.all(), pr_.all(), (gq if kind == "q" else gk).all(), d_.all(), ALU.mult, ALU.mult)
                elif kind == "g":
                    P.act(o.all(), pr_.all(), AF.Silu)
                else:
                    P.copy(o.all(), pr_.all(), e="act")
                o3 = o.all().re("p (b t) -> p b t", b=4)
                if kind == "q":
                    dst = dap(q_out, sb_ * 4 * blk_sz + idx * 128, [[8 * 128, 128], [blk_sz, 4], [1, 128]])
                    P.dma("sp", DV(q_out, dst), o3, primary=o.buf)
                elif kind == "g":
                    dst = dap(g_out, sb_ * 4 * blk_sz + idx * 128, [[8 * 128, 128], [blk_sz, 4], [1, 128]])
                    P.dma("sp", DV(g_out, dst), o3, primary=o.buf)
                elif kind == "qi":
                    bs = 128 * 4 * 128
                    dst = dap(qi_out, sb_ * 4 * bs + idx * 128, [[4 * 128, 128], [bs, 4], [1, 128]])
                    P.dma("sp", DV(qi_out, dst), o3, primary=o.buf)
                elif kind == "k":
                    P.dma("sp", DV(kT_out, kT_out.ap()[:, idx, sb_ * 512:(sb_ + 1) * 512]), o.all(), primary=o.buf)
                else:
                    P.dma("sp", DV(ki_out, ki_out.ap()[:, sb_ * 512:(sb_ + 1) * 512]), o.all(), primary=o.buf)
                outs.append(o.buf)
        P.finish(outs)
    return nc, P


NPOS = 12
GLEN = NPOS * 128 + 128
BIS_ITERS = 23


def barrier(P):
    tgt = {P.esem[e]: P.cnt[e] for e in ("pe", "dve", "act", "pool") if P.cnt[e] > 0}
    for e in ("pe", "dve", "act", "pool", "sp"):
        for sn, v in tgt.items():
            if P.waited[e].get(sn, 0) < v:
                P.eng[e].wait_ge(P.semh[sn], v)
                P.waited[e][sn] = v


def build_p3(nblk=NBLK):
    nc = bass.Bass("TRN2", target_bir_lowering=False)
    q_blk = dram_in(nc, "q_blk", [NBLK, 128, 8, 128], BF16)
    g_blk = dram_in(nc, "g_blk", [NBLK, 128, 8, 128], BF16)
    qi_blk = dram_in(nc, "qi_blk", [NBLK, 128, 4, 128], BF16)
    wi_blk = dram_in(nc, "wi_blk", [NBLK, 128, 8], F32)
    h1_blk = dram_in(nc, "h1_blk", [NBLK, 128, D], F32)
    kT_all = dram_in(nc, "kT_all", [128, 2, SEQ], BF16)
    v_all = dram_in(nc, "v_all", [SEQ, 256], BF16)
    ki_all = dram_in(nc, "ki_all", [128, SEQ], BF16)
    cmask = dram_in(nc, "cmask", [128, 512], BF16)
    oh = dram_in(nc, "oh", [32, GLEN], F32)
    rel_bias = dram_in(nc, "rel_bias", [32, 8], F32)
    w_out = dram_in(nc, "w_out", [D, D], F32)
    y = dram_out(nc, "y", [NBLK, 128, D], F32)
    gtab = nc.dram_tensor("gtab", [8, GLEN], F32, kind="Internal")
    gtab_buf = Buf("gtab")
    P = Prog(nc)
    with P.es:
        c = make_consts(P)
        ident4 = P.sb("ident4", [128, 512], BF16)
        for h in range(4):
            P.copy(ident4[:, h * 128:(h + 1) * 128], c["ident"].all())
        kT = P.sb("kT", [128, 2, SEQ], BF16)
        Vs = P.sb("Vs", [128, 64, 256], BF16)
        ki = P.sb("ki", [128, SEQ], BF16)
        Wo = P.sb("Wo", [128, 8, D], BF16)
        cm = P.sb("cm", [128, 512], BF16)
        biasT = P.sb("biasT", [128, NPOS, 1024], BF16)
        for g in range(2):
            P.dma("sp", kT[:, g, :], DV(kT_all, kT_all.ap()[:, g, :]))
        for part in range(4):
            P.dma("sp", Vs[:, part * 16:(part + 1) * 16, :],
                  DV(v_all, dap(v_all, part * 16 * 128 * 256, [[256, 128], [128 * 256, 16], [1, 256]])))
        P.dma("sp", ki.all(), DV(ki_all))
        P.dma("sp", cm.all(), DV(cmask))
        for k in range(8):
            P.dma("pool", Wo[:, k, :], DV(w_out, w_out.ap()[k * 128:(k + 1) * 128, :]))
        ring = [P.ps("ring%d" % i, [128, 512]) for i in range(4)]
        num = [P.ps("num%d" % g, [128, 512]) for g in range(2)]
        den = [P.ps("den%d" % g, [128, 512]) for g in range(2)]
        rr = {"ring": 0}

        def nring():
            for _ in range(4):
                t = ring[rr["ring"] % 4]
                rr["ring"] += 1
                if t.buf.lw is None or t.buf.rd:
                    return t
            raise RuntimeError("PSUM ring exhausted: every bank holds unread results")

        with contextlib.ExitStack() as es2:
            old_es, P.es = P.es, es2
            Jf = P.sb("Jf", [128, 128], F32)
            tmpi = P.sb("tmpi", [128, 128], F32)
            P.iota(tmpi.all(), [[1, 128]], -127, 1)
            P.ts(Jf.all(), tmpi.all(), 0.0, None, op0=ALU.is_equal)
            rb = P.sb("rb", [32, 8], F32)
            rb31 = P.sb("rb31", [32, 8], F32)
            ohs = P.sb("ohs", [32, GLEN], F32)
            gsb = P.sb("gsb", [8, GLEN], F32)
            hk = [P.sb("hk%d" % i, [128, 8, 128], F32) for i in range(2)]
            P.dma("sp", rb.all(), DV(rel_bias))
            P.dma("sp", rb31.all(), DV(rel_bias, dap(rel_bias, 31 * 8, [[0, 32], [1, 8]])))
            P.dma("sp", ohs.all(), DV(oh))
            P.tt(rb.all(), rb.all(), rb31.all(), ALU.subtract)
            for ch in range((GLEN + 511) // 512):
                n = min(512, GLEN - ch * 512)
                pr_ = nring()
                P.mm(pr_[0:8, 0:n], rb.all(), ohs[:, ch * 512:ch * 512 + n])
                P.copy(gsb[:, ch * 512:ch * 512 + n], pr_[0:8, 0:n])
            P.dma("sp", DV(gtab, buf=gtab_buf), gsb.all(), primary=gtab_buf)
            for p in range(NPOS):
                hkt = hk[p % 2]
                P.dma("sp", hkt.all(), DV(gtab, dap(gtab, p * 128, [[1, 128], [GLEN, 8], [1, 128]]), buf=gtab_buf))
                for half in range(2):
                    pr_ = nring()
                    P.mm(pr_.all(), Jf.all(), hkt[:, half * 4:(half + 1) * 4, :])
                    P.copy(biasT[:, p, half * 512:(half + 1) * 512], pr_.all(), e=("act" if half else "dve"))
            barrier(P)
            P.es = old_es

        score = P.sb("score", [128, SEQ], F32)
        sc_bufs = [Buf("sc%d" % i) for i in range(SEQ // 512)]

        def scv(a, b):
            return V(score.t[:, a:b], tuple(sc_bufs[a // 512:(b + 511) // 512]))

        JW = SEQ
        nmall = [P.sb("nmall%d" % i, [128, SEQ], BF16) for i in range(2)]
        qT = [P.sb("qT%d" % i, [128, 8, 128], BF16) for i in range(2)]
        gT = [P.sb("gT%d" % i, [128, 8, 128], BF16) for i in range(2)]
        qiT = [P.sb("qiT%d" % i, [128, 4, 128], BF16) for i in range(2)]
        wi = [P.sb("wi%d" % i, [128, 8], F32) for i in range(2)]
        h1b = [P.sb("h1b%d" % i, [128, D], F32) for i in range(1)]
        Pm = [P.sb("Pm%d" % i, [128, 512], BF16) for i in range(2)]
        rd = [P.sb("rd%d" % i, [128, 512], F32) for i in range(1)]
        catT = P.sb("catT", [128, 8, 128], BF16)
        small = {n: P.sb("b_" + n, [128, 1], F32) for n in ("lo", "hi", "w", "mid", "nmid", "cnt", "cnt2", "sel")}
        nm_j = [(Buf("jD%d" % i), Buf("jA%d" % i)) for i in range(2)]
        pow2 = P.sb("pow2", [128, BIS_ITERS], F32)
        wall = P.sb("wall", [128, BIS_ITERS], F32)
        for k in range(BIS_ITERS):
            P.memset(pow2[:, k:k + 1], 2.0 ** -(k + 1))
        cn = {"Pm": 0}
        outs = []

        def stage1(i):
            steps = []
            nkt = 4 * i + 4
            nk = nkt * 128
            q_, g_, qi_, wi_ = qT[i % 2], gT[i % 2], qiT[i % 2], wi[i % 2]
            S = small
            nm = nmall[i % 2]
            jD, jA = nm_j[i % 2]
            nm8 = nm.t[:].bitcast(I8)
            nD = ((nk // 2 + 127) // 128) * 128
            nA = nk - nD

            def loads():
                P.dma("sp", qi_.all(), DV(qi_blk, qi_blk.ap()[i]))
                P.dma("sp", wi_.all(), DV(wi_blk, wi_blk.ap()[i]))
                P.dma("sp", q_.all(), DV(q_blk, q_blk.ap()[i]))
                P.dma("sp", g_.all(), DV(g_blk, g_blk.ap()[i]))
            steps.append(loads)

            def idx(chunks, h0):
                for h in range(h0, h0 + 4):
                    for c5 in chunks:
                        sc = scv(c5 * 512, (c5 + 1) * 512)
                        pI = nring()
                        lo_p = (h % 2) * 64
                        P.mm(pI.all(), qi_[lo_p:lo_p + 64, h // 2, :], ki[lo_p:lo_p + 64, c5 * 512:(c5 + 1) * 512])
                        P.act(pI.all(), pI.all(), AF.Relu)
                        if h == 0:
                            P.ts(sc, pI.all(), wi_[:, 0:1], None, op0=ALU.mult)
                        else:
                            P.stt(sc, pI.all(), wi_[:, h:h + 1], sc, ALU.mult, ALU.add)
            for c5 in range(0, (i + 1) if not DBG.get('no_idx') else 0, 2):
                chunks = [c5] + ([c5 + 1] if c5 + 1 <= i else [])
                for h0 in (0, 4):
                    steps.append(lambda chunks=chunks, h0=h0: idx(chunks, h0))

            def bis_init():
                P.reduce(S["lo"].all(), scv(0, nk), ALU.min)
                P.tt(scv(nk - 512, nk), scv(nk - 512, nk), cm.all(), ALU.add)
                P.reduce(S["hi"].all(), scv(0, nk), ALU.max)
                P.ts(S["w"].all(), S["hi"].all(), 1.0, S["lo"].all(), op0=ALU.add, op1=ALU.subtract)
                P.ts(wall.all(), pow2.all(), S["w"].all(), None, op0=ALU.mult)
                P.tt(S["mid"].all(), S["lo"].all(), wall[:, 0:1], ALU.add)
            steps.append(bis_init)

            def bis_iter(k):
                P.ts(V(nm8[:, 0:nk], (jD,)), scv(0, nk), S["mid"].all(), 0.0, op0=ALU.is_ge, op1=ALU.add,
                     accum=S["cnt"].all())
                P.stt(S["sel"].all(), S["cnt"].all(), 255.5, wall[:, k:k + 1], ALU.is_ge, ALU.mult)
                if k + 1 < BIS_ITERS:
                    P.ts(S["mid"].all(), S["sel"].all(), S["lo"].all(), wall[:, k + 1:k + 2], op0=ALU.add, op1=ALU.add)
                P.tt(S["lo"].all(), S["lo"].all(), S["sel"].all(), ALU.add)
            for it in range(BIS_ITERS):
                steps.append(lambda it=it: bis_iter(it))

            def negmask(a0, a1):
                P.ts(V(nm.t[:, a0:a1], (nm.buf, jD, jA)), scv(a0, a1), S["lo"].all(), NEG, op0=ALU.is_lt, op1=ALU.mult)
            for a0 in range(0, nk, 2048):
                steps.append(lambda a0=a0: negmask(a0, min(nk, a0 + 2048)))
            return steps

        def stage2(i):
            steps = []
            nkt = 4 * i + 4
            q_, g_, hb, nm = qT[i % 2], gT[i % 2], h1b[0], nmall[i % 2]

            def tile_(m):
                pos = nkt - 1 - m
                for g in range(2):
                    pL = nring()
                    P.mm(pL.all(), kT[:, g, m * 128:(m + 1) * 128], q_[:, 4 * g:4 * g + 4, :], start=True, stop=False)
                    P.mm(pL.all(), V(nm.t[:, m * 128:(m + 1) * 128], (nm.buf,) + nm_j[i % 2]), ident4.all(), start=False,
                         stop=(pos >= NPOS))
                    if pos < NPOS:
                        P.mm(pL.all(), c["ident"].all(), biasT[:, pos, g * 512:(g + 1) * 512], start=False, stop=True)
                    pm = Pm[cn["Pm"] % 2]
                    cn["Pm"] += 1
                    P.act(pm.all(), pL.all(), AF.Exp)
                    P.mm(num[g].all(), Vs[:, m, g * 128:(g + 1) * 128], pm.all(), start=(m == 0), stop=(m == nkt - 1))
                    P.mm(den[g].all(), c["ones"].all(), pm.all(), start=(m == 0), stop=(m == nkt - 1))
            for m in range(nkt if not DBG.get('no_att') else 1):
                steps.append(lambda m=m: tile_(m))

            def epi():
                P.dma("sp", hb.all(), DV(h1_blk, h1_blk.ap()[i]))
                for g in range(2):
                    r_ = rd[0]
                    P.recip(r_.all(), den[g].all())
                    tmp = nring()
                    P.tt(tmp.all(), num[g].all(), r_.all(), ALU.mult)
                    P.tt(catT[:, 4 * g:4 * g + 4, :], tmp.all().re("p (h t) -> p h t", h=4), g_[:, 4 * g:4 * g + 4, :], ALU.mult)
                for half in range(2):
                    po = nring()
                    for h in range(8):
                        P.mm(po.all(), catT[:, h, :], Wo[:, h, half * 512:(half + 1) * 512], start=(h == 0), stop=(h == 7))
                    P.tt(hb[:, half * 512:(half + 1) * 512], po.all(), hb[:, half * 512:(half + 1) * 512], ALU.add)
                P.dma("pool", DV(y, y.ap()[i]), hb.all(), primary=hb.buf)
                outs.append(hb.buf)
            steps.append(epi)
            return steps

        def interleave(sa, sb):
            na, nb = len(sa), len(sb)
            ia = ib = 0
            while ia < na or ib < nb:
                if ib >= nb or (ia < na and ia * nb <= ib * na):
                    sa[ia]()
                    ia += 1
                else:
                    sb[ib]()
                    ib += 1

        for st_ in stage1(0):
            st_()
        for i in range(nblk):
            s2 = stage2(i)
            s1 = stage1(i + 1) if i + 1 < nblk else []
            interleave(s1, s2)
        P.finish(outs)
    return nc, P


def t5_bucket_np(d):
    d = np.maximum(d, 0)
    nf = np.maximum(d, 1).astype(np.float32)
    large = 16 + (np.log(nf / np.float32(16)) / np.float32(math.log(1024 / 16)) * np.float32(16)).astype(np.int32)
    large = np.minimum(large, 31)
    return np.where(d < 16, d, large)


def p3_consts(j):
    e = np.arange(GLEN)
    dist = (j - 3) * 128 + e - 127
    b = t5_bucket_np(dist)
    ohm = np.zeros((32, GLEN), np.float32)
    ohm[b, e] = 1.0
    t = np.arange(128)[:, None]
    r = np.arange(512)[None, :]
    s_rel = r - j * 128
    cmask = np.where(s_rel <= t, 0.0, -1e30).astype(np.float32).astype(ml_dtypes.bfloat16)
    return ohm, cmask


EV_Q, EV_K, EV_V, EV_GA, EV_U, EV_GB = 0, 512, 640, 768, 1280, 1792
WQ, WKD, WVD, WGA, WU, WGB = 0, 512, 768, 1024, 1536, 2048
TWO_PI = 2.0 * math.pi
CW1 = 6.28125
CW2 = TWO_PI - CW1


def sincos(P, wk, ph, s_out, c_out):
    ki, kf, r, m = wk["ki"], wk["kf"], wk["r"], wk["m"]
    P.ts(ki, ph, 1.0 / TWO_PI, None, op0=ALU.mult)
    P.copy(kf, ki)
    P.stt(r, kf, -CW1, ph, ALU.mult, ALU.add)
    P.stt(r, kf, -CW2, r, ALU.mult, ALU.add)
    for (outv, shift) in ((s_out, 0.0), (c_out, math.pi / 2)):
        if shift:
            P.ts(r, r, shift, None, op0=ALU.add)
        P.ts(m, r, math.pi, -TWO_PI, op0=ALU.is_gt, op1=ALU.mult)
        P.tt(r, r, m, ALU.add)
        P.ts(m, r, -math.pi, TWO_PI, op0=ALU.is_lt, op1=ALU.mult)
        P.tt(r, r, m, ALU.add)
        P.ts(r, r, math.pi, -math.pi, op0=ALU.min, op1=ALU.max)
        P.act(outv, r, AF.Sin)


def cmul(P, o_re, o_im, a_re, a_im, b_re, b_im, t1, t2):
    P.tt(t1, a_re, b_re, ALU.mult)
    P.tt(t2, a_im, b_im, ALU.mult)
    P.tt(o_re, t1, t2, ALU.subtract)
    P.tt(t1, a_re, b_im, ALU.mult)
    P.tt(t2, a_im, b_re, ALU.mult)
    P.tt(o_im, t1, t2, ALU.add)


def interleave_steps(sa, sb):
    na, nb = len(sa), len(sb)
    ia = ib = 0
    while ia < na or ib < nb:
        if ib >= nb or (ia < na and ia * nb <= ib * na):
            sa[ia]()
            ia += 1
        else:
            sb[ib]()
            ib += 1


def build_p2(mode="full", nsb=TPC // 512):
    full = mode == "full"
    nc = bass.Bass("TRN2", target_bir_lowering=False)
    x_own = dram_in(nc, "x_own", [TPC, D], F32)
    w_in = dram_in(nc, "w_in", [D, 2304], F32)
    ng_fm_d = dram_in(nc, "ng_fm", [128, 8], F32)
    a_re_f = dram_in(nc, "a_re_f", [2048], F32)
    a_im_f = dram_in(nc, "a_im_f", [2048], F32)
    ldt_f = dram_in(nc, "ldt_f", [2048], F32)
    a_re_s = dram_in(nc, "a_re_s", [128, 16], F32)
    a_im_s = dram_in(nc, "a_im_s", [128, 16], F32)
    ldt_s = dram_in(nc, "ldt_s", [128, 16], F32)
    b_blk_re = dram_in(nc, "b_blk_re", [128, 4, 128], F32)
    b_blk_im = dram_in(nc, "b_blk_im", [128, 4, 128], F32)
    if full:
        x_halo = dram_in(nc, "x_halo", [128, D], F32)
        firstneg = dram_in(nc, "firstneg", [128, 1], F32)
        w_out = dram_in(nc, "w_out", [D, D], F32)
        glu_w = dram_in(nc, "glu_w", [512, 1024], F32)
        qg2 = dram_in(nc, "qg2", [128, 1], F32)
        kg2 = dram_in(nc, "kg2", [128, 1], F32)
        sinks_row = dram_in(nc, "sinks_row", [128, 8], F32)
        rel_bias = dram_in(nc, "rel_bias", [32, 8], F32)
        oh_swa = dram_in(nc, "oh_swa", [32, 384], F32)
        neg_swa = dram_in(nc, "neg_swa", [8, 384], F32)
        c_blk_re = dram_in(nc, "c_blk_re", [128, 16, 128], F32)
        c_blk_im = dram_in(nc, "c_blk_im", [128, 16, 128], F32)
        d_fm_d = dram_in(nc, "d_fm", [128, 4], F32)
        glu_b_fm_d = dram_in(nc, "glu_b_fm", [128, 8], F32)
        eprev = dram_in(nc, "eprev", [3, 128, 32], F32)
        h1_out = dram_out(nc, "h1", [TPC, D], F32)
        gtab = nc.dram_tensor("gtab0", [8, 384], F32, kind="Internal")
        gtab_buf = Buf("gtab0")
    else:
        eloc = dram_out(nc, "eloc", [128, 32], F32)
    P = Prog(nc)
    with P.es:
        c = make_consts(P)
        wu0 = WU if full else 0
        g_fm = P.sb("g_fm", [128, 8], F32)
        P.dma("sp", g_fm.all(), DV(ng_fm_d))
        Wr = P.sb("Wr", [128, 16, 128], F32)
        Ws = P.sb("Ws", [128, 16, 128], F32)
        BBp = P.sb("BBp", [128, 4, 512], BF16)
        P.memset(BBp.all(), 0.0)
        A128r = P.sb("A128r", [128, 16], F32)
        A128i = P.sb("A128i", [128, 16], F32)
        A127r = P.sb("A127r", [128, 16], F32)
        A127i = P.sb("A127i", [128, 16], F32)
        A1r = P.sb("A1r", [128, 16], F32)
        A1i = P.sb("A1i", [128, 16], F32)
        Zr = P.sb("Zr", [128, 16], F32)
        Zi = P.sb("Zi", [128, 16], F32)
        if full:
            Vr = P.sb("Vr", [128, 16, 128], F32)
            Vi = P.sb("Vi", [128, 16, 128], F32)
            d_fm = P.sb("d_fm_s", [128, 4], F32)
            glu_b = P.sb("glu_b", [128, 8], F32)
            BT = [P.sb("BT%d" % i, [128, 1024], F32) for i in range(3)]
            esink = P.sb("esink", [128, 8], F32)
            gq = P.sb("gq", [128, 1], F32)
            gk = P.sb("gk", [128, 1], F32)
            ones2 = P.sb("ones2", [128, 128], BF16)
            P.dma("sp", d_fm.all(), DV(d_fm_d))
            P.dma("sp", glu_b.all(), DV(glu_b_fm_d))
            P.dma("sp", gq.all(), DV(qg2))
            P.dma("sp", gk.all(), DV(kg2))
            P.ts(gq.all(), gq.all(), 64.0 ** -0.5, None, op0=ALU.mult)
            P.dma("sp", esink.all(), DV(sinks_row))
            P.act(esink.all(), esink.all(), AF.Exp)
            P.memset(ones2.all(), 0.0)
            P.memset(ones2[0:64, 0:64], 1.0)
            P.memset(ones2[64:128, 64:128], 1.0)

        NR = 6
        ring = [P.ps("ring%d" % i, [128, 512]) for i in range(NR)]
        py_bank = P.ps("py_bank", [128, 512]) if full else None
        pst = P.ps("pst", [128, 1024], BF16)
        rr = {"ring": 0}

        def nring():
            for _ in range(NR):
                t = ring[rr["ring"] % NR]
                rr["ring"] += 1
                if t.buf.lw is None or t.buf.rd:
                    return t
            raise RuntimeError("PSUM ring exhausted: every bank holds unread results")

        with contextlib.ExitStack() as es2:
            old_es, P.es = P.es, es2
            N = 2048
            T = {n: P.sb("t_" + n, [128, N], F32) for n in ("ard", "ang", "a", "b", "mag", "kf", "r", "m", "s", "c")}
            Tki = P.sb("t_ki", [128, N], I32)
            jf = P.sb("jf", [128, 1], F32)
            P.iota(jf.all(), [[1, 1]], 0, 1)
            io_i = P.sb("io_i", [128, 128], F32)
            P.iota(io_i.all(), [[1, 128]], 0, 0)

            def scw(n):
                return {"ki": Tki[:, 0:n], "kf": T["kf"][:, 0:n], "r": T["r"][:, 0:n], "m": T["m"][:, 0:n]}

            P.dma("sp", T["a"].all(), DV(ldt_f, dap(ldt_f, 0, [[0, 128], [1, N]])))
            P.act(T["a"].all(), T["a"].all(), AF.Exp)
            P.dma("sp", T["ard"].all(), DV(a_re_f, dap(a_re_f, 0, [[0, 128], [1, N]])))
            P.dma("sp", T["ang"].all(), DV(a_im_f, dap(a_im_f, 0, [[0, 128], [1, N]])))
            are_row = P.sb("are_row", [128, N], F32)
            aim_row = P.sb("aim_row", [128, N], F32)
            P.copy(are_row.all(), T["ard"].all())
            P.copy(aim_row.all(), T["ang"].all(), e="act")
            P.tt(T["ard"].all(), T["ard"].all(), T["a"].all(), ALU.mult)
            P.tt(T["ang"].all(), T["ang"].all(), T["a"].all(), ALU.mult)
            P.ts(T["b"].all(), T["ard"].all(), jf.all(), -1.0, op0=ALU.mult, op1=ALU.mult)
            P.act(T["mag"].all(), T["b"].all(), AF.Exp)
            P.ts(T["b"].all(), T["ang"].all(), jf.all(), None, op0=ALU.mult)
            sincos(P, scw(N), T["b"].all(), T["s"].all(), T["c"].all())
            P.tt(Wr.all().re("p a b -> p (a b)"), T["mag"].all(), T["c"].all(), ALU.mult)
            P.tt(Ws.all().re("p a b -> p (a b)"), T["mag"].all(), T["s"].all(), ALU.mult)
            Fr = P.sb("Fr", [128, N], F32)
            Fi = P.sb("Fi", [128, N], F32)
            P.act(T["mag"].all(), T["ard"].all(), AF.Exp)
            sincos(P, scw(N), T["ang"].all(), T["s"].all(), T["c"].all())
            P.tt(T["c"].all(), T["mag"].all(), T["c"].all(), ALU.mult)
            P.tt(T["s"].all(), T["mag"].all(), T["s"].all(), ALU.mult)
            P.ts(T["c"].all(), T["c"].all(), -1.0, None, op0=ALU.add)
            P.tt(T["a"].all(), are_row.all(), are_row.all(), ALU.mult)
            P.tt(T["b"].all(), aim_row.all(), aim_row.all(), ALU.mult)
            P.tt(T["a"].all(), T["a"].all(), T["b"].all(), ALU.add)
            P.recip(T["a"].all(), T["a"].all())
            P.tt(T["b"].all(), T["c"].all(), are_row.all(), ALU.mult)
            P.tt(T["m"].all(), T["s"].all(), aim_row.all(), ALU.mult)
            P.tt(T["b"].all(), T["b"].all(), T["m"].all(), ALU.add)
            P.tt(Fr.all(), T["b"].all(), T["a"].all(), ALU.mult)
            P.tt(T["b"].all(), T["s"].all(), are_row.all(), ALU.mult)
            P.tt(T["m"].all(), T["c"].all(), aim_row.all(), ALU.mult)
            P.tt(T["b"].all(), T["b"].all(), T["m"].all(), ALU.subtract)
            P.tt(Fi.all(), T["b"].all(), T["a"].all(), ALU.mult)
            bre = P.sb("bre", [128, 4, 128], F32)
            bim = P.sb("bim", [128, 4, 128], F32)
            P.dma("sp", bre.all(), DV(b_blk_re))
            P.dma("sp", bim.all(), DV(b_blk_im))
            t1 = P.sb("bt1", [128, 128], F32)
            t2 = P.sb("bt2", [128, 128], F32)
            for st4 in range(4):
                ps_ = slice(32 * st4, 32 * st4 + 32)
                for q in range(4):
                    st = 4 * q + st4
                    fr = Fr[ps_, st * 128:(st + 1) * 128]
                    fi = Fi[ps_, st * 128:(st + 1) * 128]
                    P.tt(t1[ps_, :], bre[ps_, q, :], fr, ALU.mult)
                    P.tt(t2[ps_, :], bim[ps_, q, :], fi, ALU.mult)
                    co = (st4 % 2) * 256
                    P.tt(BBp[ps_, q, co:co + 128], t1[ps_, :], t2[ps_, :], ALU.subtract)
                    P.tt(t1[ps_, :], bre[ps_, q, :], fi, ALU.mult)
                    P.tt(t2[ps_, :], bim[ps_, q, :], fr, ALU.mult)
                    P.tt(BBp[ps_, q, co + 128:co + 256], t1[ps_, :], t2[ps_, :], ALU.add)
            sp_ = {n: P.sb("sp_" + n, [128, 16], F32) for n in ("ard", "ang", "dt", "a", "b", "mag", "kf", "r", "m", "s", "c")}
            spki = P.sb("sp_ki", [128, 16], I32)
            P.dma("sp", sp_["dt"].all(), DV(ldt_s))
            P.act(sp_["dt"].all(), sp_["dt"].all(), AF.Exp)
            P.dma("sp", sp_["ard"].all(), DV(a_re_s))
            P.dma("sp", sp_["ang"].all(), DV(a_im_s))
            P.tt(sp_["ard"].all(), sp_["ard"].all(), sp_["dt"].all(), ALU.mult)
            P.tt(sp_["ang"].all(), sp_["ang"].all(), sp_["dt"].all(), ALU.mult)
            spw = {"ki": spki.all(), "kf": sp_["kf"].all(), "r": sp_["r"].all(), "m": sp_["m"].all()}
            for (mult_, orr, oii) in ((128.0, A128r, A128i), (127.0, A127r, A127i), (1.0, A1r, A1i)):
                P.ts(sp_["a"].all(), sp_["ard"].all(), mult_, None, op0=ALU.mult)
                P.act(sp_["mag"].all(), sp_["a"].all(), AF.Exp)
                P.ts(sp_["b"].all(), sp_["ang"].all(), mult_, None, op0=ALU.mult)
                sincos(P, spw, sp_["b"].all(), sp_["s"].all(), sp_["c"].all())
                P.tt(orr.all(), sp_["mag"].all(), sp_["c"].all(), ALU.mult)
                P.tt(oii.all(), sp_["mag"].all(), sp_["s"].all(), ALU.mult)
            if full:
                for st in range(16):
                    P.act(T["mag"][:, st * 128:(st + 1) * 128], io_i.all(), AF.Exp, scale=sp_["ard"][:, st:st + 1])
                    P.ts(T["b"][:, st * 128:(st + 1) * 128], io_i.all(), sp_["ang"][:, st:st + 1], None, op0=ALU.mult)
                sincos(P, scw(N), T["b"].all(), T["s"].all(), T["c"].all())
                P.tt(Vr.all().re("p a b -> p (a b)"), T["mag"].all(), T["c"].all(), ALU.mult)
                P.tt(Vi.all().re("p a b -> p (a b)"), T["mag"].all(), T["s"].all(), ALU.mult)
                ep = [P.sb("ep%d" % i, [128, 32], F32) for i in range(3)]
                for i in range(3):
                    P.dma("sp", ep[i].all(), DV(eprev, eprev.ap()[i]))
                pa = [P.sb("pa%d" % i, [128, 16], F32) for i in range(8)]
                cur_r, cur_i = A128r, A128i
                for sqi in range(4):
                    nr, ni = pa[2 * (sqi % 2)], pa[2 * (sqi % 2) + 1]
                    cmul(P, nr.all(), ni.all(), cur_r.all(), cur_i.all(), cur_r.all(), cur_i.all(), pa[4].all(), pa[5].all())
                    cur_r, cur_i = nr, ni
                ar, ai = pa[6], pa[7]
                xr = P.sb("xsr", [128, 16], F32)
                xi = P.sb("xsi", [128, 16], F32)
                cmul(P, ar.all(), ai.all(), cur_r.all(), cur_i.all(), ep[2][:, 0:16], ep[2][:, 16:32], pa[4].all(), pa[5].all())
                P.tt(ar.all(), ar.all(), ep[1][:, 0:16], ALU.add)
                P.tt(ai.all(), ai.all(), ep[1][:, 16:32], ALU.add)
                cmul(P, xr.all(), xi.all(), cur_r.all(), cur_i.all(), ar.all(), ai.all(), pa[4].all(), pa[5].all())
                P.tt(xr.all(), xr.all(), ep[0][:, 0:16], ALU.add)
                P.tt(xi.all(), xi.all(), ep[0][:, 16:32], ALU.add)
                cmul(P, Zr.all(), Zi.all(), A1r.all(), A1i.all(), xr.all(), xi.all(), pa[4].all(), pa[5].all())
                def swa_bias_setup():
                    Jf = P.sb("Jf", [128, 128], F32)
                    tmpi = P.sb("tmpi", [128, 128], F32)
                    P.iota(tmpi.all(), [[1, 128]], -127, 1)
                    P.ts(Jf.all(), tmpi.all(), 0.0, None, op0=ALU.is_equal)
                    rb = P.sb("rb", [32, 8], F32)
                    ohs = P.sb("ohs", [32, 384], F32)
                    ngs = P.sb("ngs", [8, 384], F32)
                    P.dma("sp", ngs.all(), DV(neg_swa))
                    gsb = P.sb("gsb", [8, 384], F32)
                    fneg = P.sb("fneg", [128, 1], F32)
                    hk = [P.sb("hk%d" % i, [128, 8, 128], F32) for i in range(2)]
                    P.dma("sp", rb.all(), DV(rel_bias))
                    P.dma("sp", ohs.all(), DV(oh_swa))
                    P.dma("sp", fneg.all(), DV(firstneg))
                    pr_ = nring()
                    P.mm(pr_[0:8, 0:384], rb.all(), ohs.all())
                    P.tt(gsb.all(), pr_[0:8, 0:384], ngs.all(), ALU.add)
                    P.dma("sp", DV(gtab, buf=gtab_buf), gsb.all(), primary=gtab_buf)
                    for kt in range(2):
                        hkt = hk[kt]
                        off = 128 if kt == 0 else 0
                        P.dma("sp", hkt.all(), DV(gtab, dap(gtab, off, [[1, 128], [384, 8], [1, 128]]), buf=gtab_buf))
                        for half in range(2):
                            pr_ = nring()
                            P.mm(pr_.all(), Jf.all(), hkt[:, half * 4:(half + 1) * 4, :])
                            P.copy(BT[kt][:, half * 512:(half + 1) * 512], pr_.all())
                    P.ts(BT[2].all(), BT[0].all(), fneg.all(), None, op0=ALU.add)
                if not DBG.get('no_bias'):
                    swa_bias_setup()
            else:
                P.memset(Zr.all(), 0.0)
                P.memset(Zi.all(), 0.0)
            barrier(P)
            P.es = old_es

        ncolW = 2560 if full else 512
        W = P.sb("W", [128, 8, ncolW], BF16)
        wbufs = [Buf("Wk%d" % k) for k in range(8)]

        def Wk(k, c0, n):
            return V(W.t[:, k, c0:c0 + n], (wbufs[k],))

        for k in range(8):
            rows = slice(k * 128, (k + 1) * 128)

            def ld(dst0, n, src0):
                P.dma("pool", V(W.t[:, k, dst0:dst0 + n], (wbufs[k],)), DV(w_in, w_in.ap()[rows, src0:src0 + n]),
                      primary=wbufs[k])
            if full:
                ld(WQ, 512, EV_Q)
                for g in range(2):
                    for dup in range(2):
                        ld(WKD + g * 128 + dup * 64, 64, EV_K + g * 64)
                        ld(WVD + g * 128 + dup * 64, 64, EV_V + g * 64)
                ld(WGA, 512, EV_GA)
                ld(WU, 512, EV_U)
                ld(WGB, 512, EV_GB)
            else:
                ld(0, 512, EV_U)
        if full:
            Cre = P.sb("Cre", [128, 16, 128], BF16)
            Cim = P.sb("Cim", [128, 16, 128], BF16)
            Wo = P.sb("Wo", [128, 8, D], BF16)
            Wg = P.sb("Wg", [128, 4, 1024], BF16)
            for k in range(8):
                P.dma("pool", Wo[:, k, :], DV(w_out, w_out.ap()[k * 128:(k + 1) * 128, :]))
            for k in range(4):
                P.dma("pool", Wg[:, k, :], DV(glu_w, glu_w.ap()[k * 128:(k + 1) * 128, :]))
            P.dma("pool", Cre.all(), DV(c_blk_re))
            P.dma("pool", Cim.all(), DV(c_blk_im))
            P.ts(Cim.all(), Cim.all(), -1.0, None, op0=ALU.mult)
        tri = P.sb("tri", [128, 128], BF16)
        tri_f = P.sb("tri_f", [128, 128], F32)
        P.iota(tri_f.all(), [[1, 128]], 0, -1)
        P.ts(tri.all(), tri_f.all(), 0.0, None, op0=ALU.is_ge)
        wk = make_norm_work(P, "n_", pst)
        nxb = 4 if full else 2
        xb = [P.sb("xb%d" % i, [128, D], F32) for i in range(nxb)]
        hnT = P.sb("hnT", [128, 8, 512], BF16)
        uT = P.sb("uT", [128, 4, 512], BF16)
        tm = [P.sb("tm%d" % i, [128, 512], F32) for i in range(4)]
        vre = [P.sb("vre%d" % i, [128, 4, 128], BF16) for i in range(2)]
        vim = [P.sb("vim%d" % i, [128, 4, 128], BF16) for i in range(2)]
        cnt = {"x": 0, "v": 0, "o": 0}
        outs = []
        if full:
            qT = P.sb("qTm", [128, 8, 512], BF16)
            P.memset(qT.all(), 0.0)
            kTd = P.sb("kTd", [128, 2, 640], BF16)
            Vd = P.sb("Vd", [128, 5, 256], BF16)
            gaT = P.sb("gaT", [128, 4, 512], BF16)
            gbT = P.sb("gbT", [128, 4, 512], BF16)
            caT = gaT
            cbT = gbT
            gyT = P.sb("gyT", [128, 4, 512], BF16)
            sqb = [P.sb("sqb%d" % i, [128, 512], BF16) for i in range(2)]
            sdb = [tm[2], tm[3]]
            Sb = [P.sb("Sb%d" % i, [128, 512], F32) for i in range(2)]
            PT = [P.sb("PT%d" % i, [128, 2, 512], BF16) for i in range(2)]
            dtot = Sb[1]
            xre = [P.sb("xre%d" % i, [128, 4, 128], BF16) for i in range(1)]
            xim = [P.sb("xim%d" % i, [128, 4, 128], BF16) for i in range(1)]
            ta = [P.sb("ta%d" % i, [128, 16], F32) for i in range(6)]
            gl = {"y": tm[1], "t": tm[0]}
            sg = [Sb[0]]
        else:
            esum = P.ps("esum", [128, 32])
            Sr = P.sb("Sr", [128, 16], F32)
            Si = P.sb("Si", [128, 16], F32)
            ta = [P.sb("ta%d" % i, [128, 16], F32) for i in range(6)]
            P.memset(Sr.all(), 0.0)
            P.memset(Si.all(), 0.0)

        def kv_project(blk_cols, kslot, vslot):
            for g in range(2):
                pr_ = nring()
                for k in range(8):
                    P.mm(pr_[:, 0:128], Wk(k, WKD + g * 128, 128), hnT[:, k, blk_cols], start=(k == 0), stop=(k == 7))
                s = sqb[g]
                P.act(s[:, 0:128], pr_[:, 0:128], AF.Square)
                p2 = nring()
                P.mm(p2[:, 0:128], ones2.all(), s[:, 0:128])
                d_ = sdb[g]
                P.act(d_[:, 0:128], p2[:, 0:128], AF.Sqrt, bias=wk["epsb"].all(), scale=1.0 / 64)
                P.recip(d_[:, 0:128], d_[:, 0:128])
                P.stt(kTd[:, g, kslot * 128:(kslot + 1) * 128], pr_[:, 0:128], gk.all(), d_[:, 0:128], ALU.mult, ALU.mult)
            pr_ = nring()
            for k in range(8):
                P.mm(pr_[:, 0:256], hnT[:, k, blk_cols], Wk(k, WVD, 256), start=(k == 0), stop=(k == 7))
            P.copy(Vd[:, vslot, :], pr_[:, 0:256], e="act")

        if full:
            xh = xb[nxb - 1]
            P.dma("sp", xh.all(), DV(x_halo))
            rmsnorm_to_fm(P, c, xh.all(), hnT[:, :, 0:128], g_fm, wk)
            kv_project(slice(0, 128), 0, 0)

        for sb_ in range(nsb):
            xs = []
            for bl in range(4):
                t0 = sb_ * 512 + bl * 128
                x = xb[cnt["x"] % nxb]
                cnt["x"] += 1
                xs.append(x)
                P.dma("sp", x.all(), DV(x_own, x_own.ap()[t0:t0 + 128, :]))
                rmsnorm_to_fm(P, c, x.all(), hnT[:, :, bl * 128:(bl + 1) * 128], g_fm, wk)
            for q in range(4):
                pr_ = nring()
                for k in range(8):
                    P.mm(pr_.all(), Wk(k, wu0 + q * 128, 128), hnT[:, k, :], start=(k == 0), stop=(k == 7))
                P.copy(uT[:, q, :], pr_.all(), e="act")
            if full:
                for t in range(4):
                    pr_ = nring()
                    for k in range(8):
                        P.mm(pr_.all(), Wk(k, WQ + t * 128, 128), hnT[:, k, :], start=(k == 0), stop=(k == 7))
                    s = sqb[t % 2]
                    P.act(s.all(), pr_.all(), AF.Square)
                    p2 = nring()
                    P.mm(p2.all(), ones2.all(), s.all())
                    d_ = sdb[t % 2]
                    P.act(d_.all(), p2.all(), AF.Sqrt, bias=wk["epsb"].all(), scale=1.0 / 64)
                    P.recip(d_.all(), d_.all())
                    P.stt(qT[0:64, 2 * t, :], pr_[0:64, :], gq[0:64, :], d_[0:64, :], ALU.mult, ALU.mult)
                    P.stt(qT[64:128, 2 * t + 1, :], pr_[64:128, :], gq[64:128, :], d_[64:128, :], ALU.mult, ALU.mult)
                for (dst, c0) in ((gaT, WGA), (gbT, WGB)):
                    for t in range(4):
                        pr_ = nring()
                        for k in range(8):
                            P.mm(pr_.all(), Wk(k, c0 + t * 128, 128), hnT[:, k, :], start=(k == 0), stop=(k == 7))
                        P.act(dst[:, t, :], pr_.all(), AF.Silu)
                for bl in range(4):
                    kv_project(slice(bl * 128, (bl + 1) * 128), bl + 1, bl + 1)
                swa_steps = []

                def swa_step(bl, g, sb_=sb_):
                    qcols = slice(bl * 128, (bl + 1) * 128)
                    first = (sb_ == 0 and bl == 0)
                    if True:
                        banks = [nring(), nring()]
                        for kt in range(2):
                            kslot = bl + kt
                            for hh in range(4):
                                h = 4 * g + hh
                                lp = slice((h % 2) * 64, (h % 2) * 64 + 64)
                                P.mm(banks[kt][:, hh * 128:(hh + 1) * 128], kTd[:, g, kslot * 128:(kslot + 1) * 128],
                                     qT[:, h, qcols])
                        pt = PT[g]
                        for kt in range(2):
                            bt = BT[2] if (first and kt == 0) else BT[kt]
                            P.tt(Sb[kt].all(), banks[kt].all(), bt[:, g * 512:(g + 1) * 512], ALU.add)
                            P.act(pt[:, kt, :], Sb[kt].all(), AF.Exp)
                        pn, pd = nring(), nring()
                        for kt in range(2):
                            P.mm(pn.all(), Vd[:, bl + kt, g * 128:(g + 1) * 128], pt[:, kt, :], start=(kt == 0), stop=(kt == 1))
                        for kt in range(2):
                            P.mm(pd.all(), c["ones"].all(), pt[:, kt, :], start=(kt == 0), stop=(kt == 1))
                        for hh in range(4):
                            h = 4 * g + hh
                            P.ts(dtot[:, hh * 128:(hh + 1) * 128], pd[:, hh * 128:(hh + 1) * 128], esink[:, h:h + 1], None,
                                 op0=ALU.add)
                        P.recip(dtot.all(), dtot.all())
                        for hh in range(4):
                            h = 4 * g + hh
                            lp = slice((h % 2) * 64, (h % 2) * 64 + 64)
                            o = caT[lp, h // 2, qcols]
                            tmo = Sb[0][lp, hh * 128:(hh + 1) * 128]
                            P.tt(tmo, pn[lp, hh * 128:(hh + 1) * 128], dtot[lp, hh * 128:(hh + 1) * 128], ALU.mult)
                            P.tt(o, tmo, gaT[lp, h // 2, qcols], ALU.mult)
                def halo_step():
                    P.copy(kTd[:, :, 0:128], kTd[:, :, 512:640], e="pool")
                    P.copy(Vd[:, 0, :], Vd[:, 4, :], e="pool")
                for bl in range(0 if DBG.get('no_swa') else 4):
                    for g in range(2):
                        swa_steps.append(lambda bl=bl, g=g: swa_step(bl, g))
                swa_steps.append(halo_step)
            ssm_steps = []
            pyd = {}

            def ssm_q_step(bl, q):
                tcols = slice(bl * 128, (bl + 1) * 128)
                if True:
                    bu = [nring(), nring()]
                    for hb_ in range(2):
                        ps_ = slice(64 * hb_, 64 * hb_ + 64)
                        P.mm(bu[hb_].all(), uT[ps_, q, tcols], BBp[ps_, q, :])
                    vr_, vi_ = vre[cnt["v"] % 2], vim[cnt["v"] % 2]
                    cnt["v"] += 1
                    for hb_ in range(2):
                        bre_ = bu[hb_].all().re("p (a c s) -> p a c s", a=2, c=2)[:, :, 0, :]
                        bim_ = bu[hb_].all().re("p (a c s) -> p a c s", a=2, c=2)[:, :, 1, :]
                        sts = slice(4 * q + 2 * hb_, 4 * q + 2 * hb_ + 2)
                        wr_, ws_ = Wr[:, sts, :], Ws[:, sts, :]
                        o = slice(hb_ * 256, hb_ * 256 + 256)
                        P.tt(tm[0][:, o].re("p (a s) -> p a s", a=2), bre_, wr_, ALU.mult)
                        P.tt(tm[1][:, o].re("p (a s) -> p a s", a=2), bim_, ws_, ALU.mult)
                        P.tt(tm[2][:, o].re("p (a s) -> p a s", a=2), bim_, wr_, ALU.mult)
                        P.tt(tm[3][:, o].re("p (a s) -> p a s", a=2), bre_, ws_, ALU.mult)
                    P.tt(vr_.all().re("p a s -> p (a s)"), tm[0].all(), tm[1].all(), ALU.add, e="pool")
                    P.tt(vi_.all().re("p a s -> p (a s)"), tm[2].all(), tm[3].all(), ALU.subtract, e="pool")
                    if full and DBG.get('no_ssm2'):
                        return
                    if not full:
                        for st4 in range(4):
                            st = 4 * q + st4
                            P.mm(esum[:, st:st + 1], vr_[:, st4, :], c["ones"][:, 0:1])
                            P.mm(esum[:, 16 + st:17 + st], vi_[:, st4, :], c["ones"][:, 0:1])
                        return
                    csr, csi = nring(), nring()
                    for st4 in range(4):
                        P.mm(csr[:, st4 * 128:(st4 + 1) * 128], vr_[:, st4, :], tri.all())
                        P.mm(csi[:, st4 * 128:(st4 + 1) * 128], vi_[:, st4, :], tri.all())
                    xr_, xi_ = xre[0], xim[0]
                    for st4 in range(4):
                        st = 4 * q + st4
                        cr = csr[:, st4 * 128:(st4 + 1) * 128]
                        ci = csi[:, st4 * 128:(st4 + 1) * 128]
                        o = slice(st4 * 128, (st4 + 1) * 128)
                        P.stt(tm[0][:, o], cr, Zr[:, st:st + 1], Vr[:, st, :], ALU.add, ALU.mult)
                        P.stt(tm[1][:, o], ci, Zi[:, st:st + 1], Vi[:, st, :], ALU.add, ALU.mult)
                        P.stt(tm[2][:, o], cr, Zr[:, st:st + 1], Vi[:, st, :], ALU.add, ALU.mult)
                        P.stt(tm[3][:, o], ci, Zi[:, st:st + 1], Vr[:, st, :], ALU.add, ALU.mult)
                    sq_ = slice(4 * q, 4 * q + 4)
                    cr127 = csr.all().re("p (a s) -> p a s", a=4)[:, :, 127]
                    ci127 = csi.all().re("p (a s) -> p a s", a=4)[:, :, 127]
                    P.tt(ta[0][:, 0:4], Zr[:, sq_], cr127, ALU.add)
                    P.tt(ta[1][:, 0:4], Zi[:, sq_], ci127, ALU.add)
                    cmul(P, Zr[:, sq_], Zi[:, sq_], A128r[:, sq_], A128i[:, sq_], ta[0][:, 0:4], ta[1][:, 0:4],
                         ta[2][:, 0:4], ta[3][:, 0:4])
                    P.tt(xr_.all().re("p a s -> p (a s)"), tm[0].all(), tm[1].all(), ALU.subtract, e="pool")
                    P.tt(xi_.all().re("p a s -> p (a s)"), tm[2].all(), tm[3].all(), ALU.add, e="pool")
                    pyd['py'] = py_bank
                    py = py_bank
                    for st4 in range(4):
                        st = 4 * q + st4
                        P.mm(py[:, q * 128:(q + 1) * 128], Cre[:, st, :], xr_[:, st4, :], start=(st4 == 0), stop=False)
                        P.mm(py[:, q * 128:(q + 1) * 128], Cim[:, st, :], xi_[:, st4, :], start=False, stop=(st4 == 3))
            def ssm_tail_step(bl):
                tcols = slice(bl * 128, (bl + 1) * 128)
                if not full:
                    P.copy(ta[4].all(), esum[:, 0:16])
                    P.copy(ta[5].all(), esum[:, 16:32])
                    cmul(P, ta[0].all(), ta[1].all(), A127r.all(), A127i.all(), ta[4].all(), ta[5].all(), ta[2].all(), ta[3].all())
                    cmul(P, ta[4].all(), ta[5].all(), A128r.all(), A128i.all(), Sr.all(), Si.all(), ta[2].all(), ta[3].all())
                    P.tt(Sr.all(), ta[0].all(), ta[4].all(), ALU.add)
                    P.tt(Si.all(), ta[1].all(), ta[5].all(), ALU.add)
                    return
                if DBG.get('no_ssm2'):
                    return
                for q in range(4):
                    P.stt(gl["y"][:, q * 128:(q + 1) * 128], uT[:, q, tcols], d_fm[:, q:q + 1], pyd['py'][:, q * 128:(q + 1) * 128],
                          ALU.mult, ALU.add)
                P.act(gyT[:, :, tcols], gl["y"].all().re("p (q t) -> p q t", q=4), AF.Gelu_apprx_tanh)
            for bl in range(4):
                for q in range(4):
                    ssm_steps.append(lambda bl=bl, q=q: ssm_q_step(bl, q))
                ssm_steps.append(lambda bl=bl: ssm_tail_step(bl))
            interleave_steps(swa_steps if full else [], ssm_steps)
            if not full:
                continue
            for f in range(0 if DBG.get('no_glu') else 4):
                pa_, pb_ = nring(), nring()
                for q in range(4):
                    P.mm(pa_.all(), Wg[:, q, f * 128:(f + 1) * 128], gyT[:, q, :], start=(q == 0), stop=(q == 3))
                for q in range(4):
                    P.mm(pb_.all(), Wg[:, q, 512 + f * 128:512 + (f + 1) * 128], gyT[:, q, :], start=(q == 0), stop=(q == 3))
                s_ = sg[0]
                P.act(s_.all(), pb_.all(), AF.Sigmoid, bias=glu_b[:, 4 + f:5 + f])
                P.stt(gl["t"].all(), pa_.all(), glu_b[:, f:f + 1], s_.all(), ALU.add, ALU.mult)
                P.tt(cbT[:, f, :], gl["t"].all(), gbT[:, f, :], ALU.mult)
            for bl in range(4):
                tcols = slice(bl * 128, (bl + 1) * 128)
                t0 = sb_ * 512 + bl * 128
                o_ = xs[bl]
                for half in range(0 if DBG.get('no_out') else 2):
                    po = nring()
                    for k in range(8):
                        lhs = caT[:, k, tcols] if k < 4 else cbT[:, k - 4, tcols]
                        P.mm(po.all(), lhs, Wo[:, k, half * 512:(half + 1) * 512], start=(k == 0), stop=(k == 7))
                    P.tt(o_[:, half * 512:(half + 1) * 512], po.all(), xs[bl][:, half * 512:(half + 1) * 512], ALU.add)
                P.dma("pool", DV(h1_out, h1_out.ap()[t0:t0 + 128, :]), o_.all(), primary=o_.buf)
                outs.append(o_.buf)
        if not full:
            eo = P.sb("eo", [128, 32], F32)
            P.copy(eo[:, 0:16], Sr.all())
            P.copy(eo[:, 16:32], Si.all())
            P.dma("pool", DV(eloc), eo.all(), primary=eo.buf)
            outs.append(eo.buf)
        P.finish(outs)
    return nc, P


def swa_onehot():
    e = np.arange(384)
    d = e - 127
    valid = (d >= 0) & (d < 128)
    oh = np.zeros((32, 384), np.float32)
    oh[t5_bucket_np(d)[valid], e[valid]] = 1.0
    neg = np.tile(np.where(valid, 0.0, NEG).astype(np.float32)[None, :], (8, 1))
    return oh, neg


def l0_inputs(inp, b, r, eprev=None, full=True):
    f32 = np.float32
    x = inp["x"][b]
    d = {}
    d["x_own"] = np.ascontiguousarray(x[r * TPC:(r + 1) * TPC])
    d["w_in"] = np.ascontiguousarray(inp["ev_w_in"][0])
    d["ng_fm"] = np.ascontiguousarray(inp["norm_g"][0].reshape(8, 128).T)
    a_re = inp["ev_ssm_a_re"][0]
    a_im = inp["ev_ssm_a_im"][0]
    ldt = np.repeat(inp["ev_ssm_log_dt"][0], 64)
    d["a_re_f"] = np.ascontiguousarray(a_re.reshape(2048))
    d["a_im_f"] = np.ascontiguousarray(a_im.reshape(2048))
    d["ldt_f"] = np.ascontiguousarray(ldt)
    d["a_re_s"] = np.ascontiguousarray(a_re.reshape(16, 128).T)
    d["a_im_s"] = np.ascontiguousarray(a_im.reshape(16, 128).T)
    d["ldt_s"] = np.ascontiguousarray(ldt.reshape(16, 128).T)
    for nm, src in (("b_blk_re", inp["ev_ssm_b_re"][0]), ("b_blk_im", inp["ev_ssm_b_im"][0])):
        blk = np.zeros((4, 2, 16, 4, 2, 64), f32)
        s6 = src.reshape(4, 4, 2, 64, 16)
        for g2 in range(2):
            blk[:, g2, :, :, g2, :] = s6[:, :, g2].transpose(1, 3, 0, 2)
        d[nm] = np.ascontiguousarray(blk.reshape(128, 4, 128))
    if not full:
        return d
    d["x_halo"] = np.ascontiguousarray(x[r * TPC - 128:r * TPC]) if r > 0 else np.zeros((128, D), f32)
    d["firstneg"] = np.full((128, 1), NEG if r == 0 else 0.0, f32)
    d["w_out"] = np.ascontiguousarray(inp["ev_w_out"][0])
    d["glu_w"] = np.ascontiguousarray(inp["ev_glu_w"][0])
    d["qg2"] = np.ascontiguousarray(np.tile(inp["ev_q_norm_g"][0], 2)[:, None])
    d["kg2"] = np.ascontiguousarray(np.tile(inp["ev_k_norm_g"][0], 2)[:, None])
    d["sinks_row"] = np.ascontiguousarray(np.tile(inp["ev_sinks"][0][None, :], (128, 1)))
    d["rel_bias"] = np.ascontiguousarray(inp["rel_bias"])
    d["oh_swa"], d["neg_swa"] = swa_onehot()
    for nm, src in (("c_blk_re", inp["ev_ssm_c_re"][0]), ("c_blk_im", inp["ev_ssm_c_im"][0])):
        blk = np.zeros((2, 64, 16, 8, 16), f32)
        s5 = src.reshape(16, 2, 16, 64)
        for st in range(16):
            for g2 in range(2):
                blk[g2, :, st, 2 * (st % 4) + g2, :] = s5[st, g2].T
        d[nm] = np.ascontiguousarray(blk.reshape(128, 16, 128))
    d["d_fm"] = np.ascontiguousarray(inp["ev_ssm_d"][0].reshape(4, 128).T)
    d["glu_b_fm"] = np.ascontiguousarray(inp["ev_glu_b"][0].reshape(8, 128).T)
    d["eprev"] = np.zeros((3, 128, 32), f32) if eprev is None else np.ascontiguousarray(eprev)
    return d


_PROGS = {}


def _prog(name):
    if name not in _PROGS:
        if name == "p1":
            _PROGS[name] = build_p2("p1")[0]
        elif name == "p2":
            _PROGS[name] = build_p2("full")[0]
        elif name == "p2b":
            _PROGS[name] = build_p2b()[0]
        elif name == "p3":
            _PROGS[name] = build_p3()[0]
    return _PROGS[name]


def _run(name, maps):
    return run_bass_kernel_spmd(_prog(name), maps, core_ids=list(range(NCORES))).results


def kernel(**inputs):
    inp = {k: np.asarray(v) for k, v in inputs.items()}
    f32 = np.float32
    r1 = _run("p1", [l0_inputs(inp, c // 4, c % 4, full=False) for c in range(NCORES)])
    eloc = [np.asarray(r1[c]["eloc"], f32) for c in range(NCORES)]
    maps = []
    for c in range(NCORES):
        b, r = c // 4, c % 4
        ep = np.zeros((3, 128, 32), f32)
        for kk in range(min(r, 3)):
            ep[kk] = eloc[4 * b + r - 1 - kk]
        maps.append(l0_inputs(inp, b, r, eprev=ep))
    r2 = _run("p2", maps)
    h1 = [np.asarray(r2[c]["h1"], f32) for c in range(NCORES)]
    maps = [{"h1": h1[c], "w_in": np.ascontiguousarray(inp["od_w_in"][0]), "ng": np.ascontiguousarray(inp["norm_g"][1]),
             "qg": np.ascontiguousarray(inp["od_q_norm_g"][0]), "kg": np.ascontiguousarray(inp["od_k_norm_g"][0])}
            for c in range(NCORES)]
    r3 = _run("p2b", maps)
    maps = []
    for c in range(NCORES):
        b, j = c // 4, c % 4
        cat = lambda nm, ax: np.concatenate([np.asarray(r3[4 * b + r][nm]) for r in range(4)], axis=ax)
        ohm, cmask = p3_consts(j)
        h1b = np.concatenate([h1[4 * b + r] for r in range(4)], axis=0).reshape(64, 128, D)
        maps.append({
            "q_blk": np.ascontiguousarray(cat("q_out", 0)[j::4]),
            "g_blk": np.ascontiguousarray(cat("g_out", 0)[j::4]),
            "qi_blk": np.ascontiguousarray(cat("qi_out", 0)[j::4]),
            "wi_blk": np.ascontiguousarray(cat("wi_out", 0).reshape(64, 128, 8)[j::4]),
            "h1_blk": np.ascontiguousarray(h1b[j::4]),
            "kT_all": np.ascontiguousarray(cat("kT_out", 2)),
            "v_all": np.ascontiguousarray(cat("v_out", 0)),
            "ki_all": np.ascontiguousarray(cat("ki_out", 1)),
            "cmask": cmask, "oh": ohm,
            "rel_bias": np.ascontiguousarray(inp["rel_bias"]),
            "w_out": np.ascontiguousarray(inp["od_w_out"][0]),
        })
    r4 = _run("p3", maps)
    out = np.zeros((BATCH, SEQ // 128, 128, D), f32)
    for c in range(NCORES):
        b, j = c // 4, c % 4
        out[b, j::4] = np.asarray(r4[c]["y"], f32)
    return out.reshape(BATCH, SEQ, D)
```

```python
import contextlib
import math
import numpy as np
import ml_dtypes
import concourse.bass as bass
import concourse.mybir as mybir
from concourse.bass_utils import run_bass_kernel_spmd

F32 = mybir.dt.float32
BF16 = mybir.dt.bfloat16
I32 = mybir.dt.int32
I8 = mybir.dt.int8
ALU = mybir.AluOpType
AF = mybir.ActivationFunctionType
AX = mybir.AxisListType

NCORES = 8
D = 1024
SEQ = 8192
BATCH = 2
TPC = 2048
NBLK = TPC // 128
EPS = 1e-6
NEG = -30000.0
DBG = {}
SEM_LIMIT = 30000


class Buf:
    __slots__ = ("name", "lw", "rd", "dsem", "dcnt")

    def __init__(self, name):
        self.name = name
        self.lw = None
        self.rd = {}
        self.dsem = {}
        self.dcnt = {}


class V:
    __slots__ = ("ap", "bufs")

    def __init__(self, ap, bufs):
        self.ap = ap
        self.bufs = bufs

    def __getitem__(self, idx):
        return V(self.ap[idx], self.bufs)

    def bc(self, shape):
        return V(self.ap.broadcast_to(list(shape)), self.bufs)

    def re(self, pat, **kw):
        return V(self.ap.rearrange(pat, **kw), self.bufs)

    def bitcast(self, dt):
        return V(self.ap.bitcast(dt), self.bufs)


class Tile:
    def __init__(self, P, name, shape, dtype, space="sbuf"):
        nc = P.nc
        if space == "sbuf":
            self.t = P.es.enter_context(nc.sbuf_tensor(name, list(shape), dtype))
        elif space == "psum":
            self.t = P.es.enter_context(nc.psum_tensor(name, list(shape), dtype))
        else:
            raise ValueError(space)
        self.buf = Buf(name)
        self.name = name
        self.shape = shape

    def __getitem__(self, idx):
        return V(self.t[idx], (self.buf,))

    def v(self, idx, buf):
        return V(self.t[idx], (buf,))

    def all(self):
        return V(self.t[:], (self.buf,))


class Prog:
    def __init__(self, nc):
        self.nc = nc
        self.es = contextlib.ExitStack()
        self.eng = {"pe": nc.tensor, "dve": nc.vector, "act": nc.scalar, "pool": nc.gpsimd, "sp": nc.sync}
        self.semh = {}
        self.esem = {}
        self.cnt = {}
        self.epoch = {}
        self.waited = {e: {} for e in self.eng}
        self.nsem = 0
        for e in ("pe", "dve", "act", "pool"):
            self.epoch[e] = 0
            self._new_eng_sem(e)
        self.out_waits = []
        self.n_instr = 0

    def _sem(self, name):
        h = self.es.enter_context(self.nc.semaphore(name))
        self.semh[name] = h
        self.nsem += 1
        return name

    def _new_eng_sem(self, e):
        name = "c_%s_%d" % (e, self.epoch[e])
        self._sem(name)
        self.esem[e] = name
        self.cnt[e] = 0
        self.epoch[e] += 1

    def sb(self, name, shape, dtype):
        return Tile(self, name, shape, dtype, "sbuf")

    def ps(self, name, shape, dtype=F32):
        return Tile(self, name, shape, dtype, "psum")

    def _deps(self, e, reads, writes):
        deps = {}

        def add(sn, val, src, kind):
            if src == e and e == "pe":
                return
            if deps.get(sn, 0) < val:
                deps[sn] = val

        for b in reads:
            if b.lw is not None:
                add(b.lw[0], b.lw[1], b.lw[2], "raw")
        for b in writes:
            if b.lw is not None:
                add(b.lw[0], b.lw[1], b.lw[2], "waw")
            for sn, (v, se) in b.rd.items():
                add(sn, v, se, "war")
        h = self.eng[e]
        w = self.waited[e]
        for sn, v in deps.items():
            if w.get(sn, 0) >= v:
                continue
            h.wait_ge(self.semh[sn], v)
            w[sn] = v
            self.n_instr += 1

    def op(self, e, fn, ins=(), outs=()):
        reads = []
        for x in ins:
            if isinstance(x, V):
                reads.extend(x.bufs)
        writes = []
        for x in outs:
            if isinstance(x, V):
                writes.extend(x.bufs)
        self._deps(e, reads, writes)
        i = fn(self.eng[e])
        self.cnt[e] += 1
        self.n_instr += 1
        sn = self.esem[e]
        v = self.cnt[e]
        i.then_inc(self.semh[sn], 1)
        for b in writes:
            b.lw = (sn, v, e)
            b.rd = {}
        for b in reads:
            if b not in writes:
                b.rd[sn] = (v, e)
        if v >= SEM_LIMIT:
            self._new_eng_sem(e)
        return i

    def dma(self, q, out, in_, primary=None, nc_kwargs=None):
        reads = list(in_.bufs)
        writes = list(out.bufs)
        self._deps(q, reads, writes)
        if primary is None:
            primary = writes[0] if writes else reads[0]
        qc = "sw" if q == "pool" else "hw"
        if qc not in primary.dsem:
            primary.dsem[qc] = self._sem("d%s_%s" % (qc, primary.name))
            primary.dcnt[qc] = 0
        kw = nc_kwargs or {}
        i = self.eng[q].dma_start(out=out.ap, in_=in_.ap, **kw)
        primary.dcnt[qc] += 16
        sn, val = primary.dsem[qc], primary.dcnt[qc]
        i.then_inc(self.semh[sn], 16)
        self.n_instr += 1
        for b in writes:
            b.lw = (sn, val, "dma")
            b.rd = {}
        for b in reads:
            b.rd[sn] = (val, "dma")
        return (sn, val)

    def finish(self, bufs):
        h = self.eng["sp"]
        done = {}
        for b in bufs:
            if b.lw is not None:
                done[b.lw[0]] = max(done.get(b.lw[0], 0), b.lw[1])
            for sn, (v, se) in b.rd.items():
                done[sn] = max(done.get(sn, 0), v)
        for sn, v in done.items():
            h.wait_ge(self.semh[sn], v)

    def mm(self, out, lhsT, rhs, start=True, stop=True):
        return self.op("pe", lambda h: h.matmul(out.ap, lhsT=lhsT.ap, rhs=rhs.ap, start=start, stop=stop),
                       ins=(lhsT, rhs), outs=(out,))

    def tr(self, out, in_, ident):
        return self.op("pe", lambda h: h.transpose(out.ap, in_.ap, ident.ap), ins=(in_, ident), outs=(out,))

    def act(self, out, in_, func, bias=None, scale=None, accum=None, e="act"):
        kw = {}
        ins = [in_]
        outs = [out]
        if bias is not None:
            kw["bias"] = bias.ap if isinstance(bias, V) else bias
            ins.append(bias)
        if scale is not None:
            kw["scale"] = scale.ap if isinstance(scale, V) else scale
            ins.append(scale)
        if accum is not None:
            kw["accum_out"] = accum.ap
            outs.append(accum)
        return self.op(e, lambda h: h.activation(out=out.ap, in_=in_.ap, func=func, **kw), ins=ins, outs=outs)

    def ts(self, out, in0, s1, s2=None, op0=ALU.mult, op1=None, accum=None, e="dve"):
        kw = {}
        ins = [in0, s1, s2]
        outs = [out]
        if op1 is not None:
            kw["op1"] = op1
        if accum is not None:
            kw["accum_out"] = accum.ap
            outs.append(accum)
        a1 = s1.ap if isinstance(s1, V) else s1
        a2 = s2.ap if isinstance(s2, V) else s2
        return self.op(e, lambda h: h.tensor_scalar(out=out.ap, in0=in0.ap, scalar1=a1, scalar2=a2, op0=op0, **kw),
                       ins=ins, outs=outs)

    def tt(self, out, in0, in1, op, e="dve"):
        return self.op(e, lambda h: h.tensor_tensor(out=out.ap, in0=in0.ap, in1=in1.ap, op=op),
                       ins=(in0, in1), outs=(out,))

    def stt(self, out, in0, s, in1, op0, op1):
        a = s.ap if isinstance(s, V) else s
        return self.op("dve", lambda h: h.scalar_tensor_tensor(out=out.ap, in0=in0.ap, scalar=a, in1=in1.ap,
                                                                op0=op0, op1=op1),
                       ins=(in0, s, in1), outs=(out,))

    def copy(self, out, in_, e="dve"):
        if e == "act":
            return self.op("act", lambda h: h.copy(out=out.ap, in_=in_.ap), ins=(in_,), outs=(out,))
        return self.op(e, lambda h: h.tensor_copy(out=out.ap, in_=in_.ap), ins=(in_,), outs=(out,))

    def recip(self, out, in_):
        return self.op("dve", lambda h: h.reciprocal(out=out.ap, in_=in_.ap), ins=(in_,), outs=(out,))

    def reduce(self, out, in_, op, axis=AX.X):
        return self.op("dve", lambda h: h.tensor_reduce(out=out.ap, in_=in_.ap, axis=axis, op=op),
                       ins=(in_,), outs=(out,))

    def memset(self, out, val, e="dve"):
        return self.op(e, lambda h: h.memset(out.ap, val), ins=(), outs=(out,))

    def iota(self, out, pattern, base, cm):
        return self.op("pool", lambda h: h.iota(out.ap, pattern=pattern, base=base, channel_multiplier=cm,
                                                allow_small_or_imprecise_dtypes=True), ins=(), outs=(out,))


def dram_in(nc, name, shape, dtype):
    return nc.dram_tensor(name, list(shape), dtype, kind="ExternalInput")


def dram_out(nc, name, shape, dtype):
    return nc.dram_tensor(name, list(shape), dtype, kind="ExternalOutput")


def DV(t, ap=None, buf=None):
    return V(t.ap() if ap is None else ap, (buf,) if buf is not None else ())


def dap(t, offset, pattern):
    return bass.AP(t, offset, [list(p) for p in pattern])


def barrier(P):
    tgt = {P.esem[e]: P.cnt[e] for e in ("pe", "dve", "act", "pool") if P.cnt[e] > 0}
    for e in ("pe", "dve", "act", "pool", "sp"):
        for sn, v in tgt.items():
            if P.waited[e].get(sn, 0) < v:
                P.eng[e].wait_ge(P.semh[sn], v)
                P.waited[e][sn] = v


def make_consts(P):
    c = {}
    c["ident"] = P.sb("c_ident", [128, 128], BF16)
    with contextlib.ExitStack() as es2:
        old_es, P.es = P.es, es2
        io = P.sb("c_iota", [128, 128], F32)
        P.iota(io.all(), [[1, 128]], 0, -1)
        P.ts(c["ident"].all(), io.all(), 0.0, None, op0=ALU.is_equal)
        barrier(P)
        P.es = old_es
    c["ones"] = P.sb("c_ones", [128, 128], BF16)
    P.memset(c["ones"].all(), 1.0)
    return c


def load_fm_vec(P, name, dram_t, n):
    t = P.sb(name, [128, n], F32)
    P.dma("sp", t.all(), DV(dram_t, dap(dram_t, 0, [[1, 128], [128, n]])),
          nc_kwargs={"allow_slow_non_contiguous": True})
    return t


def rmsnorm_to_fm(P, c, x_v, hnT_v, g_fm, wk, nfeat=1024):
    nk = nfeat // 128
    P.act(wk["junk"].all(), x_v, AF.Square, accum=wk["ss"].all())
    P.act(wk["sd"].all(), wk["ss"].all(), AF.Sqrt, bias=wk["epsb"].all(), scale=1.0 / nfeat)
    P.recip(wk["rstd"].all(), wk["sd"].all())
    P.ts(wk["xn"].all(), x_v, wk["rstd"].all(), None, op0=ALU.mult)
    if DBG.get('no_tr'):
        return
    pst = wk["pst"]
    for k in range(nk):
        P.tr(pst[:, k * 128:(k + 1) * 128], wk["xn"][:, k * 128:(k + 1) * 128], c["ident"].all())
    if DBG.get('no_tt'):
        return
    if DBG.get('tt_copy'):
        P.copy(hnT_v, pst.all().re("p (k t) -> p k t", k=nk))
        return
    for k in range(nk):
        if k % 2 == 0:
            P.ts(hnT_v[:, k, :], pst[:, k * 128:(k + 1) * 128], g_fm[:, k:k + 1], None, op0=ALU.mult)
        else:
            P.act(hnT_v[:, k, :], pst[:, k * 128:(k + 1) * 128], AF.Copy, scale=g_fm[:, k:k + 1])


def make_norm_work(P, pfx, pst):
    wk = {}
    wk["junk"] = P.sb(pfx + "junk", [128, 1024], BF16)
    wk["ss"] = P.sb(pfx + "ss", [128, 1], F32)
    wk["sd"] = P.sb(pfx + "sd", [128, 1], F32)
    wk["rstd"] = P.sb(pfx + "rstd", [128, 1], F32)
    wk["xn"] = P.sb(pfx + "xn", [128, 1024], BF16)
    wk["epsb"] = P.sb(pfx + "epsb", [128, 1], F32)
    P.memset(wk["epsb"].all(), EPS)
    wk["pst"] = pst
    return wk


OD_Q, OD_K, OD_V, OD_G, OD_QI, OD_KI, OD_WI = 0, 1024, 1280, 1536, 2560, 3072, 3136


def build_p2b(nsb=TPC // 512, do_tiles=True, do_blocks=True):
    nc = bass.Bass("TRN2", target_bir_lowering=False)
    h1 = dram_in(nc, "h1", [TPC, D], F32)
    w_in = dram_in(nc, "w_in", [D, 3144], F32)
    ng = dram_in(nc, "ng", [D], F32)
    qg = dram_in(nc, "qg", [128], F32)
    kg = dram_in(nc, "kg", [128], F32)
    q_out = dram_out(nc, "q_out", [NBLK, 128, 8, 128], BF16)
    g_out = dram_out(nc, "g_out", [NBLK, 128, 8, 128], BF16)
    qi_out = dram_out(nc, "qi_out", [NBLK, 128, 4, 128], BF16)
    wi_out = dram_out(nc, "wi_out", [TPC, 8], F32)
    kT_out = dram_out(nc, "kT_out", [128, 2, TPC], BF16)
    v_out = dram_out(nc, "v_out", [TPC, 256], BF16)
    ki_out = dram_out(nc, "ki_out", [128, TPC], BF16)
    P = Prog(nc)
    with P.es:
        c = make_consts(P)
        W = P.sb("W", [128, 8, 3200], BF16)
        Wwi = P.sb("Wwi", [128, 8, 8], BF16)
        wbufs = [Buf("Wk%d" % k) for k in range(8)]
        for k in range(8):
            wv = V(W.t[:, k, :], (wbufs[k],))
            P.dma("pool", wv[:, 0:3136], DV(w_in, w_in.ap()[k * 128:(k + 1) * 128, 0:3136]), primary=wbufs[k])
            P.dma("pool", wv[:, 3136:3200], DV(w_in, w_in.ap()[k * 128:(k + 1) * 128, OD_KI:OD_KI + 64]),
                  primary=wbufs[k])
        P.dma("pool", Wwi.all(), DV(w_in, dap(w_in, OD_WI, [[3144, 128], [128 * 3144, 8], [1, 8]])))

        def Wk(k, c0, n):
            return V(W.t[:, k, c0:c0 + n], (wbufs[k],))

        g_fm = load_fm_vec(P, "g_fm", ng, 8)
        gq = P.sb("gq", [128, 1], F32)
        gk = P.sb("gk", [128, 1], F32)
        P.dma("sp", gq.all(), DV(qg, dap(qg, 0, [[1, 128], [1, 1]])))
        P.dma("sp", gk.all(), DV(kg, dap(kg, 0, [[1, 128], [1, 1]])))
        P.ts(gq.all(), gq.all(), 128.0 ** -0.5, None, op0=ALU.mult)
        pst = P.ps("pst", [128, 1024], BF16)
        wk = make_norm_work(P, "n_", pst)
        xb = [P.sb("xb%d" % i, [128, 1024], F32) for i in range(2)]
        hnT = P.sb("hnT", [128, 8, 512], BF16)
        ring = [P.ps("pr%d" % i, [128, 512]) for i in range(3)]
        ring2 = [P.ps("ps2_%d" % i, [128, 512]) for i in range(2)]
        ptm = P.ps("ptm", [128, 512])
        ptm2 = P.ps("ptm2", [128, 512])
        sq = [P.sb("sq%d" % i, [128, 512], BF16) for i in range(2)]
        sd = [P.sb("sdq%d" % i, [128, 512], F32) for i in range(2)]
        ob = [P.sb("ob%d" % i, [128, 512], BF16) for i in range(4)]
        vb = [P.sb("vb%d" % i, [128, 256], BF16) for i in range(2)]
        wib = [P.sb("wib%d" % i, [128, 8], F32) for i in range(2)]
        rr = [0, 0, 0, 0]
        outs = []

        def nxt(lst, idx):
            t = lst[rr[idx] % len(lst)]
            rr[idx] += 1
            return t

        blk_sz = 128 * 8 * 128
        for sb_ in range(nsb):
            for bl in range(4 if do_blocks else 0):
                t0 = sb_ * 512 + bl * 128
                x = xb[bl % 2]
                P.dma("sp", x.all(), DV(h1, h1.ap()[t0:t0 + 128, :]))
                rmsnorm_to_fm(P, c, x.all(), hnT[:, :, bl * 128:(bl + 1) * 128], g_fm, wk)
                if DBG.get('no_tm'):
                    continue
                for k in range(8):
                    P.mm(ptm[:, 0:256], hnT[:, k, bl * 128:(bl + 1) * 128], Wk(k, OD_V, 256), start=(k == 0), stop=(k == 7))
                v_sb = vb[bl % 2]
                P.copy(v_sb.all(), ptm[:, 0:256], e="act")
                if not DBG.get('no_vst'):
                    P.dma("pool", DV(v_out, v_out.ap()[t0:t0 + 128, :]), v_sb.all(), primary=v_sb.buf)
                    outs += [v_sb.buf]
                if DBG.get('no_wi'):
                    continue
                for k in range(8):
                    P.mm(ptm2[:, 0:8], hnT[:, k, bl * 128:(bl + 1) * 128], Wwi[:, k, :], start=(k == 0), stop=(k == 7))
                w_sb = wib[bl % 2]
                P.ts(w_sb.all(), ptm2[:, 0:8], (8.0 ** -0.5) * (64.0 ** -0.5), None, op0=ALU.mult)
                P.dma("pool", DV(wi_out, wi_out.ap()[t0:t0 + 128, :]), w_sb.all(), primary=w_sb.buf)
                outs += [w_sb.buf]
            tiles = [("q", h, OD_Q + h * 128) for h in range(8)] + [("k", g, OD_K + g * 128) for g in range(2)] + \
                    [("g", h, OD_G + h * 128) for h in range(8)] + [("qi", pr, OD_QI + pr * 128) for pr in range(4)] + \
                    [("ki", 0, OD_KI)]
            for (kind, idx, c0) in (tiles if do_tiles else []):
                pr_ = nxt(ring, 0)
                for k in range(8):
                    P.mm(pr_.all(), Wk(k, c0, 128), hnT[:, k, :], start=(k == 0), stop=(k == 7))
                o = nxt(ob, 1)
                if kind in ("q", "k"):
                    s = nxt(sq, 2)
                    P.act(s.all(), pr_.all(), AF.Square)
                    p2 = nxt(ring2, 3)
                    P.mm(p2.all(), c["ones"].all(), s.all())
                    d_ = sd[(rr[3] - 1) % 2]
                    P.act(d_.all(), p2.all(), AF.Sqrt, bias=wk["epsb"].all(), scale=1.0 / 128)
                    P.recip(d_.all(), d_.all())
                    P.stt(o.all(), pr_.all(), (gq if kind == "q" else gk).all(), d_.all(), ALU.mult, ALU.mult)
                elif kind == "g":
                    P.act(o.all(), pr_.all(), AF.Silu)
                else:
                    P.copy(o.all(), pr_.all(), e="act")
                o3 = o.all().re("p (b t) -> p b t", b=4)
                if kind == "q":
                    dst = dap(q_out, sb_ * 4 * blk_sz + idx * 128, [[8 * 128, 128], [blk_sz, 4], [1, 128]])
                    P.dma("sp", DV(q_out, dst), o3, primary=o.buf)
                elif kind == "g":
                    dst = dap(g_out, sb_ * 4 * blk_sz + idx * 128, [[8 * 128, 128], [blk_sz, 4], [1, 128]])
                    P.dma("sp", DV(g_out, dst), o3, primary=o.buf)
                elif kind == "qi":
                    bs = 128 * 4 * 128
                    dst = dap(qi_out, sb_ * 4 * bs + idx * 128, [[4 * 128, 128], [bs, 4], [1, 128]])
                    P.dma("sp", DV(qi_out, dst), o3, primary=o.buf)
                elif kind == "k":
                    P.dma("sp", DV(kT_out, kT_out.ap()[:, idx, sb_ * 512:(sb_ + 1) * 512]), o.all(), primary=o.buf)
                else:
                    P.dma("sp", DV(ki_out, ki_out.ap()[:, sb_ * 512:(sb_ + 1) * 512]), o.all(), primary=o.buf)
                outs.append(o.buf)
        P.finish(outs)
    return nc, P


NPOS = 12
GLEN = NPOS * 128 + 128
BIS_ITERS = 21


def barrier(P):
    tgt = {P.esem[e]: P.cnt[e] for e in ("pe", "dve", "act", "pool") if P.cnt[e] > 0}
    for e in ("pe", "dve", "act", "pool", "sp"):
        for sn, v in tgt.items():
            if P.waited[e].get(sn, 0) < v:
                P.eng[e].wait_ge(P.semh[sn], v)
                P.waited[e][sn] = v


def build_p3(nblk=NBLK):
    nc = bass.Bass("TRN2", target_bir_lowering=False)
    q_blk = dram_in(nc, "q_blk", [NBLK, 128, 8, 128], BF16)
    g_blk = dram_in(nc, "g_blk", [NBLK, 128, 8, 128], BF16)
    qi_blk = dram_in(nc, "qi_blk", [NBLK, 128, 4, 128], BF16)
    wi_blk = dram_in(nc, "wi_blk", [NBLK, 128, 8], F32)
    h1_blk = dram_in(nc, "h1_blk", [NBLK, 128, D], F32)
    kT_all = dram_in(nc, "kT_all", [128, 2, SEQ], BF16)
    v_all = dram_in(nc, "v_all", [SEQ, 256], BF16)
    ki_all = dram_in(nc, "ki_all", [128, SEQ], BF16)
    cmask = dram_in(nc, "cmask", [128, 512], BF16)
    oh = dram_in(nc, "oh", [32, GLEN], F32)
    rel_bias = dram_in(nc, "rel_bias", [32, 8], F32)
    w_out = dram_in(nc, "w_out", [D, D], F32)
    y = dram_out(nc, "y", [NBLK, 128, D], F32)
    gtab = nc.dram_tensor("gtab", [8, GLEN], F32, kind="Internal")
    gtab_buf = Buf("gtab")
    P = Prog(nc)
    with P.es:
        c = make_consts(P)
        ident4 = P.sb("ident4", [128, 512], BF16)
        for h in range(4):
            P.copy(ident4[:, h * 128:(h + 1) * 128], c["ident"].all())
        kT = P.sb("kT", [128, 2, SEQ], BF16)
        Vs = P.sb("Vs", [128, 64, 256], BF16)
        ki = P.sb("ki", [128, SEQ], BF16)
        Wo = P.sb("Wo", [128, 8, D], BF16)
        cm = P.sb("cm", [128, 512], BF16)
        biasT = P.sb("biasT", [128, NPOS, 1024], BF16)
        for g in range(2):
            P.dma("sp", kT[:, g, :], DV(kT_all, kT_all.ap()[:, g, :]))
        for part in range(4):
            P.dma("sp", Vs[:, part * 16:(part + 1) * 16, :],
                  DV(v_all, dap(v_all, part * 16 * 128 * 256, [[256, 128], [128 * 256, 16], [1, 256]])))
        P.dma("sp", ki.all(), DV(ki_all))
        P.dma("sp", cm.all(), DV(cmask))
        for k in range(8):
            P.dma("pool", Wo[:, k, :], DV(w_out, w_out.ap()[k * 128:(k + 1) * 128, :]))
        ring = [P.ps("ring%d" % i, [128, 512]) for i in range(4)]
        num = [P.ps("num%d" % g, [128, 512]) for g in range(2)]
        den = [P.ps("den%d" % g, [128, 512]) for g in range(2)]
        rr = {"ring": 0}

        def nring():
            for _ in range(4):
                t = ring[rr["ring"] % 4]
                rr["ring"] += 1
                if t.buf.lw is None or t.buf.rd:
                    return t
            raise RuntimeError("PSUM ring exhausted: every bank holds unread results")

        with contextlib.ExitStack() as es2:
            old_es, P.es = P.es, es2
            Jf = P.sb("Jf", [128, 128], F32)
            tmpi = P.sb("tmpi", [128, 128], F32)
            P.iota(tmpi.all(), [[1, 128]], -127, 1)
            P.ts(Jf.all(), tmpi.all(), 0.0, None, op0=ALU.is_equal)
            rb = P.sb("rb", [32, 8], F32)
            rb31 = P.sb("rb31", [32, 8], F32)
            ohs = P.sb("ohs", [32, GLEN], F32)
            gsb = P.sb("gsb", [8, GLEN], F32)
            hk = [P.sb("hk%d" % i, [128, 8, 128], F32) for i in range(2)]
            P.dma("sp", rb.all(), DV(rel_bias))
            P.dma("sp", rb31.all(), DV(rel_bias, dap(rel_bias, 31 * 8, [[0, 32], [1, 8]])))
            P.dma("sp", ohs.all(), DV(oh))
            P.tt(rb.all(), rb.all(), rb31.all(), ALU.subtract)
            for ch in range((GLEN + 511) // 512):
                n = min(512, GLEN - ch * 512)
                pr_ = nring()
                P.mm(pr_[0:8, 0:n], rb.all(), ohs[:, ch * 512:ch * 512 + n])
                P.copy(gsb[:, ch * 512:ch * 512 + n], pr_[0:8, 0:n])
            P.dma("sp", DV(gtab, buf=gtab_buf), gsb.all(), primary=gtab_buf)
            for p in range(NPOS):
                hkt = hk[p % 2]
                P.dma("sp", hkt.all(), DV(gtab, dap(gtab, p * 128, [[1, 128], [GLEN, 8], [1, 128]]), buf=gtab_buf))
                for half in range(2):
                    pr_ = nring()
                    P.mm(pr_.all(), Jf.all(), hkt[:, half * 4:(half + 1) * 4, :])
                    P.copy(biasT[:, p, half * 512:(half + 1) * 512], pr_.all(), e=("act" if half else "dve"))
            barrier(P)
            P.es = old_es

        score = P.sb("score", [128, SEQ], F32)
        sc_bufs = [Buf("sc%d" % i) for i in range(SEQ // 512)]

        def scv(a, b):
            return V(score.t[:, a:b], tuple(sc_bufs[a // 512:(b + 511) // 512]))

        JW = SEQ
        nmall = [P.sb("nmall%d" % i, [128, SEQ], BF16) for i in range(2)]
        qT = [P.sb("qT%d" % i, [128, 8, 128], BF16) for i in range(2)]
        gT = [P.sb("gT%d" % i, [128, 8, 128], BF16) for i in range(1)]
        PmSum = [P.sb("PmSum%d" % g, [128, 512], F32) for g in range(2)]
        ones_f = P.sb("ones_f", [128, 128], F32)
        P.memset(ones_f.all(), 1.0)
        qiT = [P.sb("qiT%d" % i, [128, 4, 128], BF16) for i in range(2)]
        wi = [P.sb("wi%d" % i, [128, 8], F32) for i in range(2)]
        h1b = [P.sb("h1b%d" % i, [128, 512], F32) for i in range(1)]
        Pm = [P.sb("Pm%d" % i, [128, 512], BF16) for i in range(2)]
        rd = [P.sb("rd%d" % i, [128, 512], F32) for i in range(1)]
        catT = P.sb("catT", [128, 8, 128], BF16)
        small = {n: P.sb("b_" + n, [128, 1], F32) for n in ("lo", "hi", "w", "mid", "nmid", "cnt", "cnt2", "sel")}
        nm_j = [(Buf("jD%d" % i), Buf("jA%d" % i)) for i in range(2)]
        pow2 = P.sb("pow2", [128, BIS_ITERS], F32)
        wall = P.sb("wall", [128, BIS_ITERS], F32)
        for k in range(BIS_ITERS):
            P.memset(pow2[:, k:k + 1], 2.0 ** -(k + 1))
        cn = {"Pm": 0}
        outs = []

        def stage1(i):
            steps = []
            nkt = 4 * i + 4
            nk = nkt * 128
            q_, qi_, wi_ = qT[i % 2], qiT[i % 2], wi[i % 2]
            S = small
            nm = nmall[i % 2]
            jD, jA = nm_j[i % 2]
            nm8 = nm.t[:].bitcast(I8)
            nD = ((nk // 2 + 127) // 128) * 128
            nA = nk - nD

            def loads():
                P.dma("sp", qi_.all(), DV(qi_blk, qi_blk.ap()[i]))
                P.dma("sp", wi_.all(), DV(wi_blk, wi_blk.ap()[i]))
                P.dma("sp", q_.all(), DV(q_blk, q_blk.ap()[i]))
            steps.append(loads)

            def idx(chunks, h0):
                for h in range(h0, h0 + 4):
                    for c5 in chunks:
                        sc = scv(c5 * 512, (c5 + 1) * 512)
                        pI = nring()
                        lo_p = (h % 2) * 64
                        P.mm(pI.all(), qi_[lo_p:lo_p + 64, h // 2, :], ki[lo_p:lo_p + 64, c5 * 512:(c5 + 1) * 512])
                        P.act(pI.all(), pI.all(), AF.Relu)
                        if h == 0:
                            P.ts(sc, pI.all(), wi_[:, 0:1], None, op0=ALU.mult)
                        else:
                            P.stt(sc, pI.all(), wi_[:, h:h + 1], sc, ALU.mult, ALU.add)
            for c5 in range(0, (i + 1) if not DBG.get('no_idx') else 0, 2):
                chunks = [c5] + ([c5 + 1] if c5 + 1 <= i else [])
                for h0 in (0, 4):
                    steps.append(lambda chunks=chunks, h0=h0: idx(chunks, h0))

            def bis_init():
                P.reduce(S["lo"].all(), scv(0, nk), ALU.min)
                P.tt(scv(nk - 512, nk), scv(nk - 512, nk), cm.all(), ALU.add)
                P.reduce(S["hi"].all(), scv(0, nk), ALU.max)
                P.ts(S["w"].all(), S["hi"].all(), 1.0, S["lo"].all(), op0=ALU.add, op1=ALU.subtract)
                P.ts(wall.all(), pow2.all(), S["w"].all(), None, op0=ALU.mult)
                P.tt(S["mid"].all(), S["lo"].all(), wall[:, 0:1], ALU.add)
            steps.append(bis_init)

            def bis_iter(k):
                P.ts(V(nm8[:, 0:nk], (jD,)), scv(0, nk), S["mid"].all(), 0.0, op0=ALU.is_ge, op1=ALU.add,
                     accum=S["cnt"].all())
                P.stt(S["sel"].all(), S["cnt"].all(), 255.5, wall[:, k:k + 1], ALU.is_ge, ALU.mult)
                if k + 1 < BIS_ITERS:
                    P.ts(S["mid"].all(), S["sel"].all(), S["lo"].all(), wall[:, k + 1:k + 2], op0=ALU.add, op1=ALU.add)
                P.tt(S["lo"].all(), S["lo"].all(), S["sel"].all(), ALU.add)
            for it in range(BIS_ITERS):
                steps.append(lambda it=it: bis_iter(it))

            def negmask(a0, a1):
                P.ts(V(nm.t[:, a0:a1], (nm.buf, jD, jA)), scv(a0, a1), S["lo"].all(), NEG, op0=ALU.is_lt, op1=ALU.mult)
            for a0 in range(0, nk, 2048):
                steps.append(lambda a0=a0: negmask(a0, min(nk, a0 + 2048)))
            return steps

        def stage2(i):
            steps = []
            nkt = 4 * i + 4
            q_, g_, hb, nm = qT[i % 2], gT[0], h1b[0], nmall[i % 2]

            def tile_(m):
                pos = nkt - 1 - m
                for g in range(2):
                    pL = nring()
                    P.mm(pL.all(), kT[:, g, m * 128:(m + 1) * 128], q_[:, 4 * g:4 * g + 4, :], start=True, stop=False)
                    P.mm(pL.all(), V(nm.t[:, m * 128:(m + 1) * 128], (nm.buf,) + nm_j[i % 2]), ident4.all(), start=False,
                         stop=(pos >= NPOS))
                    if pos < NPOS:
                        P.mm(pL.all(), c["ident"].all(), biasT[:, pos, g * 512:(g + 1) * 512], start=False, stop=True)
                    pm = Pm[cn["Pm"] % 2]
                    cn["Pm"] += 1
                    P.act(pm.all(), pL.all(), AF.Exp)
                    P.mm(num[g].all(), Vs[:, m, g * 128:(g + 1) * 128], pm.all(), start=(m == 0), stop=(m == nkt - 1))
                    if m == 0:
                        P.copy(PmSum[g].all(), pm.all(), e="pool")
                    else:
                        P.tt(PmSum[g].all(), PmSum[g].all(), pm.all(), ALU.add, e="pool")
            for m in range(nkt if not DBG.get('no_att') else 1):
                steps.append(lambda m=m: tile_(m))

            def epi():
                P.dma("sp", g_.all(), DV(g_blk, g_blk.ap()[i]))
                for g in range(2):
                    P.mm(den[g].all(), ones_f.all(), PmSum[g].all())
                for g in range(2):
                    r_ = rd[0]
                    P.recip(r_.all(), den[g].all())
                    tmp = nring()
                    P.tt(tmp.all(), num[g].all(), r_.all(), ALU.mult)
                    P.tt(catT[:, 4 * g:4 * g + 4, :], tmp.all().re("p (h t) -> p h t", h=4), g_[:, 4 * g:4 * g + 4, :], ALU.mult)
                for half in range(2):
                    cs_ = slice(half * 512, (half + 1) * 512)
                    P.dma("sp", hb.all(), DV(h1_blk, h1_blk.ap()[i][:, cs_]))
                    po = nring()
                    for h in range(8):
                        P.mm(po.all(), catT[:, h, :], Wo[:, h, cs_], start=(h == 0), stop=(h == 7))
                    P.tt(hb.all(), po.all(), hb.all(), ALU.add)
                    P.dma("pool", DV(y, y.ap()[i][:, cs_]), hb.all(), primary=hb.buf)
                outs.append(hb.buf)
            steps.append(epi)
            return steps

        def interleave(sa, sb):
            na, nb = len(sa), len(sb)
            ia = ib = 0
            while ia < na or ib < nb:
                if ib >= nb or (ia < na and ia * nb <= ib * na):
                    sa[ia]()
                    ia += 1
                else:
                    sb[ib]()
                    ib += 1

        order = list(range(nblk - 1, -1, -1))
        for st_ in stage1(order[0]):
            st_()
        for oi, i in enumerate(order):
            s2 = stage2(i)
            s1 = stage1(order[oi + 1]) if oi + 1 < nblk else []
            interleave(s1, s2)
        P.finish(outs)
    return nc, P


def t5_bucket_np(d):
    d = np.maximum(d, 0)
    nf = np.maximum(d, 1).astype(np.float32)
    large = 16 + (np.log(nf / np.float32(16)) / np.float32(math.log(1024 / 16)) * np.float32(16)).astype(np.int32)
    large = np.minimum(large, 31)
    return np.where(d < 16, d, large)


def p3_consts(j):
    e = np.arange(GLEN)
    dist = (j - 3) * 128 + e - 127
    b = t5_bucket_np(dist)
    ohm = np.zeros((32, GLEN), np.float32)
    ohm[b, e] = 1.0
    t = np.arange(128)[:, None]
    r = np.arange(512)[None, :]
    s_rel = r - j * 128
    cmask = np.where(s_rel <= t, 0.0, -1e30).astype(np.float32).astype(ml_dtypes.bfloat16)
    return ohm, cmask


EV_Q, EV_K, EV_V, EV_GA, EV_U, EV_GB = 0, 512, 640, 768, 1280, 1792
WQ, WKD, WVD, WGA, WU, WGB = 0, 512, 768, 1024, 1536, 2048
TWO_PI = 2.0 * math.pi
CW1 = 6.28125
CW2 = TWO_PI - CW1


def sincos(P, wk, ph, s_out, c_out):
    ki, kf, r, m = wk["ki"], wk["kf"], wk["r"], wk["m"]
    P.ts(ki, ph, 1.0 / TWO_PI, None, op0=ALU.mult)
    P.copy(kf, ki)
    P.stt(r, kf, -CW1, ph, ALU.mult, ALU.add)
    P.stt(r, kf, -CW2, r, ALU.mult, ALU.add)
    for (outv, shift) in ((s_out, 0.0), (c_out, math.pi / 2)):
        if shift:
            P.ts(r, r, shift, None, op0=ALU.add)
        P.ts(m, r, math.pi, -TWO_PI, op0=ALU.is_gt, op1=ALU.mult)
        P.tt(r, r, m, ALU.add)
        P.ts(m, r, -math.pi, TWO_PI, op0=ALU.is_lt, op1=ALU.mult)
        P.tt(r, r, m, ALU.add)
        P.ts(r, r, math.pi, -math.pi, op0=ALU.min, op1=ALU.max)
        P.act(outv, r, AF.Sin)


def cmul(P, o_re, o_im, a_re, a_im, b_re, b_im, t1, t2):
    P.tt(t1, a_re, b_re, ALU.mult)
    P.tt(t2, a_im, b_im, ALU.mult)
    P.tt(o_re, t1, t2, ALU.subtract)
    P.tt(t1, a_re, b_im, ALU.mult)
    P.tt(t2, a_im, b_re, ALU.mult)
    P.tt(o_im, t1, t2, ALU.add)


def interleave_steps(sa, sb):
    na, nb = len(sa), len(sb)
    ia = ib = 0
    while ia < na or ib < nb:
        if ib >= nb or (ia < na and ia * nb <= ib * na):
            sa[ia]()
            ia += 1
        else:
            sb[ib]()
            ib += 1


def build_p2(mode="full", nsb=TPC // 512):
    full = mode == "full"
    nc = bass.Bass("TRN2", target_bir_lowering=False)
    x_own = dram_in(nc, "x_own", [TPC, D], F32)
    w_in = dram_in(nc, "w_in", [D, 2304], F32)
    ng_fm_d = dram_in(nc, "ng_fm", [128, 8], F32)
    a_re_f = dram_in(nc, "a_re_f", [2048], F32)
    a_im_f = dram_in(nc, "a_im_f", [2048], F32)
    ldt_f = dram_in(nc, "ldt_f", [2048], F32)
    a_re_s = dram_in(nc, "a_re_s", [128, 16], F32)
    a_im_s = dram_in(nc, "a_im_s", [128, 16], F32)
    ldt_s = dram_in(nc, "ldt_s", [128, 16], F32)
    b_blk_re = dram_in(nc, "b_blk_re", [128, 4, 128], F32)
    b_blk_im = dram_in(nc, "b_blk_im", [128, 4, 128], F32)
    if full:
        x_halo = dram_in(nc, "x_halo", [128, D], F32)
        firstneg = dram_in(nc, "firstneg", [128, 1], F32)
        w_out = dram_in(nc, "w_out", [D, D], F32)
        glu_w = dram_in(nc, "glu_w", [512, 1024], F32)
        qg2 = dram_in(nc, "qg2", [128, 1], F32)
        kg2 = dram_in(nc, "kg2", [128, 1], F32)
        sinks_row = dram_in(nc, "sinks_row", [128, 8], F32)
        rel_bias = dram_in(nc, "rel_bias", [32, 8], F32)
        oh_swa = dram_in(nc, "oh_swa", [32, 384], F32)
        neg_swa = dram_in(nc, "neg_swa", [8, 384], F32)
        c_blk_re = dram_in(nc, "c_blk_re", [128, 16, 128], F32)
        c_blk_im = dram_in(nc, "c_blk_im", [128, 16, 128], F32)
        d_fm_d = dram_in(nc, "d_fm", [128, 4], F32)
        glu_b_fm_d = dram_in(nc, "glu_b_fm", [128, 8], F32)
        eprev = dram_in(nc, "eprev", [3, 128, 32], F32)
        h1_out = dram_out(nc, "h1", [TPC, D], F32)
        gtab = nc.dram_tensor("gtab0", [8, 384], F32, kind="Internal")
        gtab_buf = Buf("gtab0")
    else:
        eloc = dram_out(nc, "eloc", [128, 32], F32)
    P = Prog(nc)
    with P.es:
        c = make_consts(P)
        wu0 = WU if full else 0
        g_fm = P.sb("g_fm", [128, 8], F32)
        P.dma("sp", g_fm.all(), DV(ng_fm_d))
        Wr = P.sb("Wr", [128, 16, 128], F32)
        Ws = P.sb("Ws", [128, 16, 128], F32)
        BBp = P.sb("BBp", [128, 4, 512], BF16)
        P.memset(BBp.all(), 0.0)
        A128r = P.sb("A128r", [128, 16], F32)
        A128i = P.sb("A128i", [128, 16], F32)
        A127r = P.sb("A127r", [128, 16], F32)
        A127i = P.sb("A127i", [128, 16], F32)
        A1r = P.sb("A1r", [128, 16], F32)
        A1i = P.sb("A1i", [128, 16], F32)
        Zr = P.sb("Zr", [128, 16], F32)
        Zi = P.sb("Zi", [128, 16], F32)
        if full:
            Vr = P.sb("Vr", [128, 16, 128], F32)
            Vi = P.sb("Vi", [128, 16, 128], F32)
            d_fm = P.sb("d_fm_s", [128, 4], F32)
            glu_b = P.sb("glu_b", [128, 8], F32)
            BT = [P.sb("BT%d" % i, [128, 1024], F32) for i in range(3)]
            esink = P.sb("esink", [128, 8], F32)
            gq = P.sb("gq", [128, 1], F32)
            gk = P.sb("gk", [128, 1], F32)
            ones2 = P.sb("ones2", [128, 128], BF16)
            P.dma("sp", d_fm.all(), DV(d_fm_d))
            P.dma("sp", glu_b.all(), DV(glu_b_fm_d))
            P.dma("sp", gq.all(), DV(qg2))
            P.dma("sp", gk.all(), DV(kg2))
            P.ts(gq.all(), gq.all(), 64.0 ** -0.5, None, op0=ALU.mult)
            P.dma("sp", esink.all(), DV(sinks_row))
            P.act(esink.all(), esink.all(), AF.Exp)
            P.memset(ones2.all(), 0.0)
            P.memset(ones2[0:64, 0:64], 1.0)
            P.memset(ones2[64:128, 64:128], 1.0)

        NR = 6
        ring = [P.ps("ring%d" % i, [128, 512]) for i in range(NR)]
        py_bank = P.ps("py_bank", [128, 512]) if full else None
        pst = P.ps("pst", [128, 1024], BF16)
        rr = {"ring": 0}

        def nring():
            for _ in range(NR):
                t = ring[rr["ring"] % NR]
                rr["ring"] += 1
                if t.buf.lw is None or t.buf.rd:
                    return t
            raise RuntimeError("PSUM ring exhausted: every bank holds unread results")

        with contextlib.ExitStack() as es2:
            old_es, P.es = P.es, es2
            N = 2048
            T = {n: P.sb("t_" + n, [128, N], F32) for n in ("ard", "ang", "a", "b", "mag", "kf", "r", "m", "s", "c")}
            Tki = P.sb("t_ki", [128, N], I32)
            jf = P.sb("jf", [128, 1], F32)
            P.iota(jf.all(), [[1, 1]], 0, 1)
            io_i = P.sb("io_i", [128, 128], F32)
            P.iota(io_i.all(), [[1, 128]], 0, 0)

            def scw(n):
                return {"ki": Tki[:, 0:n], "kf": T["kf"][:, 0:n], "r": T["r"][:, 0:n], "m": T["m"][:, 0:n]}

            P.dma("sp", T["a"].all(), DV(ldt_f, dap(ldt_f, 0, [[0, 128], [1, N]])))
            P.act(T["a"].all(), T["a"].all(), AF.Exp)
            P.dma("sp", T["ard"].all(), DV(a_re_f, dap(a_re_f, 0, [[0, 128], [1, N]])))
            P.dma("sp", T["ang"].all(), DV(a_im_f, dap(a_im_f, 0, [[0, 128], [1, N]])))
            are_row = P.sb("are_row", [128, N], F32)
            aim_row = P.sb("aim_row", [128, N], F32)
            P.copy(are_row.all(), T["ard"].all())
            P.copy(aim_row.all(), T["ang"].all(), e="act")
            P.tt(T["ard"].all(), T["ard"].all(), T["a"].all(), ALU.mult)
            P.tt(T["ang"].all(), T["ang"].all(), T["a"].all(), ALU.mult)
            P.ts(T["b"].all(), T["ard"].all(), jf.all(), -1.0, op0=ALU.mult, op1=ALU.mult)
            P.act(T["mag"].all(), T["b"].all(), AF.Exp)
            P.ts(T["b"].all(), T["ang"].all(), jf.all(), None, op0=ALU.mult)
            sincos(P, scw(N), T["b"].all(), T["s"].all(), T["c"].all())
            P.tt(Wr.all().re("p a b -> p (a b)"), T["mag"].all(), T["c"].all(), ALU.mult)
            P.tt(Ws.all().re("p a b -> p (a b)"), T["mag"].all(), T["s"].all(), ALU.mult)
            Fr = P.sb("Fr", [128, N], F32)
            Fi = P.sb("Fi", [128, N], F32)
            P.act(T["mag"].all(), T["ard"].all(), AF.Exp)
            sincos(P, scw(N), T["ang"].all(), T["s"].all(), T["c"].all())
            P.tt(T["c"].all(), T["mag"].all(), T["c"].all(), ALU.mult)
            P.tt(T["s"].all(), T["mag"].all(), T["s"].all(), ALU.mult)
            P.ts(T["c"].all(), T["c"].all(), -1.0, None, op0=ALU.add)
            P.tt(T["a"].all(), are_row.all(), are_row.all(), ALU.mult)
            P.tt(T["b"].all(), aim_row.all(), aim_row.all(), ALU.mult)
            P.tt(T["a"].all(), T["a"].all(), T["b"].all(), ALU.add)
            P.recip(T["a"].all(), T["a"].all())
            P.tt(T["b"].all(), T["c"].all(), are_row.all(), ALU.mult)
            P.tt(T["m"].all(), T["s"].all(), aim_row.all(), ALU.mult)
            P.tt(T["b"].all(), T["b"].all(), T["m"].all(), ALU.add)
            P.tt(Fr.all(), T["b"].all(), T["a"].all(), ALU.mult)
            P.tt(T["b"].all(), T["s"].all(), are_row.all(), ALU.mult)
            P.tt(T["m"].all(), T["c"].all(), aim_row.all(), ALU.mult)
            P.tt(T["b"].all(), T["b"].all(), T["m"].all(), ALU.subtract)
            P.tt(Fi.all(), T["b"].all(), T["a"].all(), ALU.mult)
            bre = P.sb("bre", [128, 4, 128], F32)
            bim = P.sb("bim", [128, 4, 128], F32)
            P.dma("sp", bre.all(), DV(b_blk_re))
            P.dma("sp", bim.all(), DV(b_blk_im))
            t1 = P.sb("bt1", [128, 128], F32)
            t2 = P.sb("bt2", [128, 128], F32)
            for st4 in range(4):
                ps_ = slice(32 * st4, 32 * st4 + 32)
                for q in range(4):
                    st = 4 * q + st4
                    fr = Fr[ps_, st * 128:(st + 1) * 128]
                    fi = Fi[ps_, st * 128:(st + 1) * 128]
                    P.tt(t1[ps_, :], bre[ps_, q, :], fr, ALU.mult)
                    P.tt(t2[ps_, :], bim[ps_, q, :], fi, ALU.mult)
                    co = (st4 % 2) * 256
                    P.tt(BBp[ps_, q, co:co + 128], t1[ps_, :], t2[ps_, :], ALU.subtract)
                    P.tt(t1[ps_, :], bre[ps_, q, :], fi, ALU.mult)
                    P.tt(t2[ps_, :], bim[ps_, q, :], fr, ALU.mult)
                    P.tt(BBp[ps_, q, co + 128:co + 256], t1[ps_, :], t2[ps_, :], ALU.add)
            sp_ = {n: P.sb("sp_" + n, [128, 16], F32) for n in ("ard", "ang", "dt", "a", "b", "mag", "kf", "r", "m", "s", "c")}
            spki = P.sb("sp_ki", [128, 16], I32)
            P.dma("sp", sp_["dt"].all(), DV(ldt_s))
            P.act(sp_["dt"].all(), sp_["dt"].all(), AF.Exp)
            P.dma("sp", sp_["ard"].all(), DV(a_re_s))
            P.dma("sp", sp_["ang"].all(), DV(a_im_s))
            P.tt(sp_["ard"].all(), sp_["ard"].all(), sp_["dt"].all(), ALU.mult)
            P.tt(sp_["ang"].all(), sp_["ang"].all(), sp_["dt"].all(), ALU.mult)
            spw = {"ki": spki.all(), "kf": sp_["kf"].all(), "r": sp_["r"].all(), "m": sp_["m"].all()}
            for (mult_, orr, oii) in ((128.0, A128r, A128i), (127.0, A127r, A127i), (1.0, A1r, A1i)):
                P.ts(sp_["a"].all(), sp_["ard"].all(), mult_, None, op0=ALU.mult)
                P.act(sp_["mag"].all(), sp_["a"].all(), AF.Exp)
                P.ts(sp_["b"].all(), sp_["ang"].all(), mult_, None, op0=ALU.mult)
                sincos(P, spw, sp_["b"].all(), sp_["s"].all(), sp_["c"].all())
                P.tt(orr.all(), sp_["mag"].all(), sp_["c"].all(), ALU.mult)
                P.tt(oii.all(), sp_["mag"].all(), sp_["s"].all(), ALU.mult)
            if full:
                for st in range(16):
                    P.act(T["mag"][:, st * 128:(st + 1) * 128], io_i.all(), AF.Exp, scale=sp_["ard"][:, st:st + 1])
                    P.ts(T["b"][:, st * 128:(st + 1) * 128], io_i.all(), sp_["ang"][:, st:st + 1], None, op0=ALU.mult)
                sincos(P, scw(N), T["b"].all(), T["s"].all(), T["c"].all())
                P.tt(Vr.all().re("p a b -> p (a b)"), T["mag"].all(), T["c"].all(), ALU.mult)
                P.tt(Vi.all().re("p a b -> p (a b)"), T["mag"].all(), T["s"].all(), ALU.mult)
                ep = [P.sb("ep%d" % i, [128, 32], F32) for i in range(3)]
                for i in range(3):
                    P.dma("sp", ep[i].all(), DV(eprev, eprev.ap()[i]))
                pa = [P.sb("pa%d" % i, [128, 16], F32) for i in range(8)]
                cur_r, cur_i = A128r, A128i
                for sqi in range(4):
                    nr, ni = pa[2 * (sqi % 2)], pa[2 * (sqi % 2) + 1]
                    cmul(P, nr.all(), ni.all(), cur_r.all(), cur_i.all(), cur_r.all(), cur_i.all(), pa[4].all(), pa[5].all())
                    cur_r, cur_i = nr, ni
                ar, ai = pa[6], pa[7]
                xr = P.sb("xsr", [128, 16], F32)
                xi = P.sb("xsi", [128, 16], F32)
                cmul(P, ar.all(), ai.all(), cur_r.all(), cur_i.all(), ep[2][:, 0:16], ep[2][:, 16:32], pa[4].all(), pa[5].all())
                P.tt(ar.all(), ar.all(), ep[1][:, 0:16], ALU.add)
                P.tt(ai.all(), ai.all(), ep[1][:, 16:32], ALU.add)
                cmul(P, xr.all(), xi.all(), cur_r.all(), cur_i.all(), ar.all(), ai.all(), pa[4].all(), pa[5].all())
                P.tt(xr.all(), xr.all(), ep[0][:, 0:16], ALU.add)
                P.tt(xi.all(), xi.all(), ep[0][:, 16:32], ALU.add)
                cmul(P, Zr.all(), Zi.all(), A1r.all(), A1i.all(), xr.all(), xi.all(), pa[4].all(), pa[5].all())
                def swa_bias_setup():
                    Jf = P.sb("Jf", [128, 128], F32)
                    tmpi = P.sb("tmpi", [128, 128], F32)
                    P.iota(tmpi.all(), [[1, 128]], -127, 1)
                    P.ts(Jf.all(), tmpi.all(), 0.0, None, op0=ALU.is_equal)
                    rb = P.sb("rb", [32, 8], F32)
                    ohs = P.sb("ohs", [32, 384], F32)
                    ngs = P.sb("ngs", [8, 384], F32)
                    P.dma("sp", ngs.all(), DV(neg_swa))
                    gsb = P.sb("gsb", [8, 384], F32)
                    fneg = P.sb("fneg", [128, 1], F32)
                    hk = [P.sb("hk%d" % i, [128, 8, 128], F32) for i in range(2)]
                    P.dma("sp", rb.all(), DV(rel_bias))
                    P.dma("sp", ohs.all(), DV(oh_swa))
                    P.dma("sp", fneg.all(), DV(firstneg))
                    pr_ = nring()
                    P.mm(pr_[0:8, 0:384], rb.all(), ohs.all())
                    P.tt(gsb.all(), pr_[0:8, 0:384], ngs.all(), ALU.add)
                    P.dma("sp", DV(gtab, buf=gtab_buf), gsb.all(), primary=gtab_buf)
                    for kt in range(2):
                        hkt = hk[kt]
                        off = 128 if kt == 0 else 0
                        P.dma("sp", hkt.all(), DV(gtab, dap(gtab, off, [[1, 128], [384, 8], [1, 128]]), buf=gtab_buf))
                        for half in range(2):
                            pr_ = nring()
                            P.mm(pr_.all(), Jf.all(), hkt[:, half * 4:(half + 1) * 4, :])
                            P.copy(BT[kt][:, half * 512:(half + 1) * 512], pr_.all())
                    P.ts(BT[2].all(), BT[0].all(), fneg.all(), None, op0=ALU.add)
                if not DBG.get('no_bias'):
                    swa_bias_setup()
            else:
                P.memset(Zr.all(), 0.0)
                P.memset(Zi.all(), 0.0)
            barrier(P)
            P.es = old_es

        ncolW = 2560 if full else 512
        W = P.sb("W", [128, 8, ncolW], BF16)
        wbufs = [Buf("Wk%d" % k) for k in range(8)]

        def Wk(k, c0, n):
            return V(W.t[:, k, c0:c0 + n], (wbufs[k],))

        for k in range(8):
            rows = slice(k * 128, (k + 1) * 128)

            def ld(dst0, n, src0):
                P.dma("pool", V(W.t[:, k, dst0:dst0 + n], (wbufs[k],)), DV(w_in, w_in.ap()[rows, src0:src0 + n]),
                      primary=wbufs[k])
            if full:
                ld(WQ, 512, EV_Q)
                for g in range(2):
                    for dup in range(2):
                        ld(WKD + g * 128 + dup * 64, 64, EV_K + g * 64)
                        ld(WVD + g * 128 + dup * 64, 64, EV_V + g * 64)
                ld(WGA, 512, EV_GA)
                ld(WU, 512, EV_U)
                ld(WGB, 512, EV_GB)
            else:
                ld(0, 512, EV_U)
        if full:
            Cre = P.sb("Cre", [128, 16, 128], BF16)
            Cim = P.sb("Cim", [128, 16, 128], BF16)
            Wo = P.sb("Wo", [128, 8, D], BF16)
            Wg = P.sb("Wg", [128, 4, 1024], BF16)
            for k in range(8):
                P.dma("pool", Wo[:, k, :], DV(w_out, w_out.ap()[k * 128:(k + 1) * 128, :]))
            for k in range(4):
                P.dma("pool", Wg[:, k, :], DV(glu_w, glu_w.ap()[k * 128:(k + 1) * 128, :]))
            P.dma("pool", Cre.all(), DV(c_blk_re))
            P.dma("pool", Cim.all(), DV(c_blk_im))
            P.ts(Cim.all(), Cim.all(), -1.0, None, op0=ALU.mult)
        tri = P.sb("tri", [128, 128], BF16)
        tri_f = P.sb("tri_f", [128, 128], F32)
        P.iota(tri_f.all(), [[1, 128]], 0, -1)
        P.ts(tri.all(), tri_f.all(), 0.0, None, op0=ALU.is_ge)
        wk = make_norm_work(P, "n_", pst)
        nxb = 4 if full else 2
        xb = [P.sb("xb%d" % i, [128, D], F32) for i in range(nxb)]
        hnT = P.sb("hnT", [128, 8, 512], BF16)
        uT = P.sb("uT", [128, 4, 512], BF16)
        tm = [P.sb("tm%d" % i, [128, 512], F32) for i in range(4)]
        vre = [P.sb("vre%d" % i, [128, 4, 128], BF16) for i in range(2)]
        vim = [P.sb("vim%d" % i, [128, 4, 128], BF16) for i in range(2)]
        cnt = {"x": 0, "v": 0, "o": 0}
        outs = []
        if full:
            qT = P.sb("qTm", [128, 8, 512], BF16)
            P.memset(qT.all(), 0.0)
            kTd = P.sb("kTd", [128, 2, 640], BF16)
            Vd = P.sb("Vd", [128, 5, 256], BF16)
            gaT = P.sb("gaT", [128, 4, 512], BF16)
            gbT = P.sb("gbT", [128, 4, 512], BF16)
            caT = gaT
            cbT = gbT
            gyT = P.sb("gyT", [128, 4, 512], BF16)
            sqb = [P.sb("sqb%d" % i, [128, 512], BF16) for i in range(2)]
            sdb = [tm[2], tm[3]]
            Sb = [P.sb("Sb%d" % i, [128, 512], F32) for i in range(2)]
            PT = [P.sb("PT%d" % i, [128, 2, 512], BF16) for i in range(2)]
            dtot = Sb[1]
            xre = [P.sb("xre%d" % i, [128, 4, 128], BF16) for i in range(1)]
            xim = [P.sb("xim%d" % i, [128, 4, 128], BF16) for i in range(1)]
            ta = [P.sb("ta%d" % i, [128, 16], F32) for i in range(6)]
            gl = {"y": tm[1], "t": tm[0]}
            sg = [Sb[0]]
        else:
            esum = P.ps("esum", [128, 32])
            Sr = P.sb("Sr", [128, 16], F32)
            Si = P.sb("Si", [128, 16], F32)
            ta = [P.sb("ta%d" % i, [128, 16], F32) for i in range(6)]
            P.memset(Sr.all(), 0.0)
            P.memset(Si.all(), 0.0)

        def kv_project(blk_cols, kslot, vslot):
            for g in range(2):
                pr_ = nring()
                for k in range(8):
                    P.mm(pr_[:, 0:128], Wk(k, WKD + g * 128, 128), hnT[:, k, blk_cols], start=(k == 0), stop=(k == 7))
                s = sqb[g]
                P.act(s[:, 0:128], pr_[:, 0:128], AF.Square)
                p2 = nring()
                P.mm(p2[:, 0:128], ones2.all(), s[:, 0:128])
                d_ = sdb[g]
                P.act(d_[:, 0:128], p2[:, 0:128], AF.Sqrt, bias=wk["epsb"].all(), scale=1.0 / 64)
                P.recip(d_[:, 0:128], d_[:, 0:128])
                P.stt(kTd[:, g, kslot * 128:(kslot + 1) * 128], pr_[:, 0:128], gk.all(), d_[:, 0:128], ALU.mult, ALU.mult)
            pr_ = nring()
            for k in range(8):
                P.mm(pr_[:, 0:256], hnT[:, k, blk_cols], Wk(k, WVD, 256), start=(k == 0), stop=(k == 7))
            P.copy(Vd[:, vslot, :], pr_[:, 0:256], e="act")

        if full:
            xh = xb[nxb - 1]
            P.dma("sp", xh.all(), DV(x_halo))
            rmsnorm_to_fm(P, c, xh.all(), hnT[:, :, 0:128], g_fm, wk)
            kv_project(slice(0, 128), 0, 0)

        for sb_ in range(nsb):
            xs = []
            for bl in range(4):
                t0 = sb_ * 512 + bl * 128
                x = xb[cnt["x"] % nxb]
                cnt["x"] += 1
                xs.append(x)
                P.dma("sp", x.all(), DV(x_own, x_own.ap()[t0:t0 + 128, :]))
                rmsnorm_to_fm(P, c, x.all(), hnT[:, :, bl * 128:(bl + 1) * 128], g_fm, wk)
            for q in range(4):
                pr_ = nring()
                for k in range(8):
                    P.mm(pr_.all(), Wk(k, wu0 + q * 128, 128), hnT[:, k, :], start=(k == 0), stop=(k == 7))
                P.copy(uT[:, q, :], pr_.all(), e="act")
            if full:
                for t in range(4):
                    pr_ = nring()
                    for k in range(8):
                        P.mm(pr_.all(), Wk(k, WQ + t * 128, 128), hnT[:, k, :], start=(k == 0), stop=(k == 7))
                    s = sqb[t % 2]
                    P.act(s.all(), pr_.all(), AF.Square)
                    p2 = nring()
                    P.mm(p2.all(), ones2.all(), s.all())
                    d_ = sdb[t % 2]
                    P.act(d_.all(), p2.all(), AF.Sqrt, bias=wk["epsb"].all(), scale=1.0 / 64)
                    P.recip(d_.all(), d_.all())
                    P.stt(qT[0:64, 2 * t, :], pr_[0:64, :], gq[0:64, :], d_[0:64, :], ALU.mult, ALU.mult)
                    P.stt(qT[64:128, 2 * t + 1, :], pr_[64:128, :], gq[64:128, :], d_[64:128, :], ALU.mult, ALU.mult)
                for (dst, c0) in ((gaT, WGA), (gbT, WGB)):
                    for t in range(4):
                        pr_ = nring()
                        for k in range(8):
                            P.mm(pr_.all(), Wk(k, c0 + t * 128, 128), hnT[:, k, :], start=(k == 0), stop=(k == 7))
                        P.act(dst[:, t, :], pr_.all(), AF.Silu)
                for bl in range(4):
                    kv_project(slice(bl * 128, (bl + 1) * 128), bl + 1, bl + 1)
                swa_steps = []

                def swa_step(bl, g, sb_=sb_):
                    qcols = slice(bl * 128, (bl + 1) * 128)
                    first = (sb_ == 0 and bl == 0)
                    if True:
                        banks = [nring(), nring()]
                        for kt in range(2):
                            kslot = bl + kt
                            for hh in range(4):
                                h = 4 * g + hh
                                lp = slice((h % 2) * 64, (h % 2) * 64 + 64)
                                P.mm(banks[kt][:, hh * 128:(hh + 1) * 128], kTd[:, g, kslot * 128:(kslot + 1) * 128],
                                     qT[:, h, qcols])
                        pt = PT[g]
                        for kt in range(2):
                            bt = BT[2] if (first and kt == 0) else BT[kt]
                            P.tt(Sb[kt].all(), banks[kt].all(), bt[:, g * 512:(g + 1) * 512], ALU.add)
                            P.act(pt[:, kt, :], Sb[kt].all(), AF.Exp)
                        pn, pd = nring(), nring()
                        for kt in range(2):
                            P.mm(pn.all(), Vd[:, bl + kt, g * 128:(g + 1) * 128], pt[:, kt, :], start=(kt == 0), stop=(kt == 1))
                        for kt in range(2):
                            P.mm(pd.all(), c["ones"].all(), pt[:, kt, :], start=(kt == 0), stop=(kt == 1))
                        for hh in range(4):
                            h = 4 * g + hh
                            P.ts(dtot[:, hh * 128:(hh + 1) * 128], pd[:, hh * 128:(hh + 1) * 128], esink[:, h:h + 1], None,
                                 op0=ALU.add)
                        P.recip(dtot.all(), dtot.all())
                        for hh in range(4):
                            h = 4 * g + hh
                            lp = slice((h % 2) * 64, (h % 2) * 64 + 64)
                            o = caT[lp, h // 2, qcols]
                            tmo = Sb[0][lp, hh * 128:(hh + 1) * 128]
                            P.tt(tmo, pn[lp, hh * 128:(hh + 1) * 128], dtot[lp, hh * 128:(hh + 1) * 128], ALU.mult)
                            P.tt(o, tmo, gaT[lp, h // 2, qcols], ALU.mult)
                def halo_step():
                    P.copy(kTd[:, :, 0:128], kTd[:, :, 512:640], e="pool")
                    P.copy(Vd[:, 0, :], Vd[:, 4, :], e="pool")
                for bl in range(0 if DBG.get('no_swa') else 4):
                    for g in range(2):
                        swa_steps.append(lambda bl=bl, g=g: swa_step(bl, g))
                swa_steps.append(halo_step)
            ssm_steps = []
            pyd = {}

            def ssm_q_step(bl, q):
                tcols = slice(bl * 128, (bl + 1) * 128)
                if True:
                    bu = [nring(), nring()]
                    for hb_ in range(2):
                        ps_ = slice(64 * hb_, 64 * hb_ + 64)
                        P.mm(bu[hb_].all(), uT[ps_, q, tcols], BBp[ps_, q, :])
                    vr_, vi_ = vre[cnt["v"] % 2], vim[cnt["v"] % 2]
                    cnt["v"] += 1
                    for hb_ in range(2):
                        bre_ = bu[hb_].all().re("p (a c s) -> p a c s", a=2, c=2)[:, :, 0, :]
                        bim_ = bu[hb_].all().re("p (a c s) -> p a c s", a=2, c=2)[:, :, 1, :]
                        sts = slice(4 * q + 2 * hb_, 4 * q + 2 * hb_ + 2)
                        wr_, ws_ = Wr[:, sts, :], Ws[:, sts, :]
                        o = slice(hb_ * 256, hb_ * 256 + 256)
                        P.tt(tm[0][:, o].re("p (a s) -> p a s", a=2), bre_, wr_, ALU.mult)
                        P.tt(tm[1][:, o].re("p (a s) -> p a s", a=2), bim_, ws_, ALU.mult)
                        P.tt(tm[2][:, o].re("p (a s) -> p a s", a=2), bim_, wr_, ALU.mult)
                        P.tt(tm[3][:, o].re("p (a s) -> p a s", a=2), bre_, ws_, ALU.mult)
                    P.tt(vr_.all().re("p a s -> p (a s)"), tm[0].all(), tm[1].all(), ALU.add, e="pool")
                    P.tt(vi_.all().re("p a s -> p (a s)"), tm[2].all(), tm[3].all(), ALU.subtract, e="pool")
                    if full and DBG.get('no_ssm2'):
                        return
                    if not full:
                        for st4 in range(4):
                            st = 4 * q + st4
                            P.mm(esum[:, st:st + 1], vr_[:, st4, :], c["ones"][:, 0:1])
                            P.mm(esum[:, 16 + st:17 + st], vi_[:, st4, :], c["ones"][:, 0:1])
                        return
                    csr, csi = nring(), nring()
                    for st4 in range(4):
                        P.mm(csr[:, st4 * 128:(st4 + 1) * 128], vr_[:, st4, :], tri.all())
                        P.mm(csi[:, st4 * 128:(st4 + 1) * 128], vi_[:, st4, :], tri.all())
                    xr_, xi_ = xre[0], xim[0]
                    for st4 in range(4):
                        st = 4 * q + st4
                        cr = csr[:, st4 * 128:(st4 + 1) * 128]
                        ci = csi[:, st4 * 128:(st4 + 1) * 128]
                        o = slice(st4 * 128, (st4 + 1) * 128)
                        P.stt(tm[0][:, o], cr, Zr[:, st:st + 1], Vr[:, st, :], ALU.add, ALU.mult)
                        P.stt(tm[1][:, o], ci, Zi[:, st:st + 1], Vi[:, st, :], ALU.add, ALU.mult)
                        P.stt(tm[2][:, o], cr, Zr[:, st:st + 1], Vi[:, st, :], ALU.add, ALU.mult)
                        P.stt(tm[3][:, o], ci, Zi[:, st:st + 1], Vr[:, st, :], ALU.add, ALU.mult)
                    sq_ = slice(4 * q, 4 * q + 4)
                    cr127 = csr.all().re("p (a s) -> p a s", a=4)[:, :, 127]
                    ci127 = csi.all().re("p (a s) -> p a s", a=4)[:, :, 127]
                    P.tt(ta[0][:, 0:4], Zr[:, sq_], cr127, ALU.add)
                    P.tt(ta[1][:, 0:4], Zi[:, sq_], ci127, ALU.add)
                    cmul(P, Zr[:, sq_], Zi[:, sq_], A128r[:, sq_], A128i[:, sq_], ta[0][:, 0:4], ta[1][:, 0:4],
                         ta[2][:, 0:4], ta[3][:, 0:4])
                    P.tt(xr_.all().re("p a s -> p (a s)"), tm[0].all(), tm[1].all(), ALU.subtract, e="pool")
                    P.tt(xi_.all().re("p a s -> p (a s)"), tm[2].all(), tm[3].all(), ALU.add, e="pool")
                    pyd['py'] = py_bank
                    py = py_bank
                    for st4 in range(4):
                        st = 4 * q + st4
                        P.mm(py[:, q * 128:(q + 1) * 128], Cre[:, st, :], xr_[:, st4, :], start=(st4 == 0), stop=False)
                        P.mm(py[:, q * 128:(q + 1) * 128], Cim[:, st, :], xi_[:, st4, :], start=False, stop=(st4 == 3))
            def ssm_tail_step(bl):
                tcols = slice(bl * 128, (bl + 1) * 128)
                if not full:
                    P.copy(ta[4].all(), esum[:, 0:16])
                    P.copy(ta[5].all(), esum[:, 16:32])
                    cmul(P, ta[0].all(), ta[1].all(), A127r.all(), A127i.all(), ta[4].all(), ta[5].all(), ta[2].all(), ta[3].all())
                    cmul(P, ta[4].all(), ta[5].all(), A128r.all(), A128i.all(), Sr.all(), Si.all(), ta[2].all(), ta[3].all())
                    P.tt(Sr.all(), ta[0].all(), ta[4].all(), ALU.add)
                    P.tt(Si.all(), ta[1].all(), ta[5].all(), ALU.add)
                    return
                if DBG.get('no_ssm2'):
                    return
                for q in range(4):
                    P.stt(gl["y"][:, q * 128:(q + 1) * 128], uT[:, q, tcols], d_fm[:, q:q + 1], pyd['py'][:, q * 128:(q + 1) * 128],
                          ALU.mult, ALU.add)
                P.act(gyT[:, :, tcols], gl["y"].all().re("p (q t) -> p q t", q=4), AF.Gelu_apprx_tanh)
            for bl in range(4):
                for q in range(4):
                    ssm_steps.append(lambda bl=bl, q=q: ssm_q_step(bl, q))
                ssm_steps.append(lambda bl=bl: ssm_tail_step(bl))
            interleave_steps(swa_steps if full else [], ssm_steps)
            if not full:
                continue
            for f in range(0 if DBG.get('no_glu') else 4):
                pa_, pb_ = nring(), nring()
                for q in range(4):
                    P.mm(pa_.all(), Wg[:, q, f * 128:(f + 1) * 128], gyT[:, q, :], start=(q == 0), stop=(q == 3))
                for q in range(4):
                    P.mm(pb_.all(), Wg[:, q, 512 + f * 128:512 + (f + 1) * 128], gyT[:, q, :], start=(q == 0), stop=(q == 3))
                s_ = sg[0]
                P.act(s_.all(), pb_.all(), AF.Sigmoid, bias=glu_b[:, 4 + f:5 + f])
                P.stt(gl["t"].all(), pa_.all(), glu_b[:, f:f + 1], s_.all(), ALU.add, ALU.mult)
                P.tt(cbT[:, f, :], gl["t"].all(), gbT[:, f, :], ALU.mult)
            for bl in range(4):
                tcols = slice(bl * 128, (bl + 1) * 128)
                t0 = sb_ * 512 + bl * 128
                o_ = xs[bl]
                for half in range(0 if DBG.get('no_out') else 2):
                    po = nring()
                    for k in range(8):
                        lhs = caT[:, k, tcols] if k < 4 else cbT[:, k - 4, tcols]
                        P.mm(po.all(), lhs, Wo[:, k, half * 512:(half + 1) * 512], start=(k == 0), stop=(k == 7))
                    P.tt(o_[:, half * 512:(half + 1) * 512], po.all(), xs[bl][:, half * 512:(half + 1) * 512], ALU.add)
                P.dma("pool", DV(h1_out, h1_out.ap()[t0:t0 + 128, :]), o_.all(), primary=o_.buf)
                outs.append(o_.buf)
        if not full:
            eo = P.sb("eo", [128, 32], F32)
            P.copy(eo[:, 0:16], Sr.all())
            P.copy(eo[:, 16:32], Si.all())
            P.dma("pool", DV(eloc), eo.all(), primary=eo.buf)
            outs.append(eo.buf)
        P.finish(outs)
    return nc, P


def swa_onehot():
    e = np.arange(384)
    d = e - 127
    valid = (d >= 0) & (d < 128)
    oh = np.zeros((32, 384), np.float32)
    oh[t5_bucket_np(d)[valid], e[valid]] = 1.0
    neg = np.tile(np.where(valid, 0.0, NEG).astype(np.float32)[None, :], (8, 1))
    return oh, neg


def l0_inputs(inp, b, r, eprev=None, full=True):
    f32 = np.float32
    x = inp["x"][b]
    d = {}
    d["x_own"] = np.ascontiguousarray(x[r * TPC:(r + 1) * TPC])
    d["w_in"] = np.ascontiguousarray(inp["ev_w_in"][0])
    d["ng_fm"] = np.ascontiguousarray(inp["norm_g"][0].reshape(8, 128).T)
    a_re = inp["ev_ssm_a_re"][0]
    a_im = inp["ev_ssm_a_im"][0]
    ldt = np.repeat(inp["ev_ssm_log_dt"][0], 64)
    d["a_re_f"] = np.ascontiguousarray(a_re.reshape(2048))
    d["a_im_f"] = np.ascontiguousarray(a_im.reshape(2048))
    d["ldt_f"] = np.ascontiguousarray(ldt)
    d["a_re_s"] = np.ascontiguousarray(a_re.reshape(16, 128).T)
    d["a_im_s"] = np.ascontiguousarray(a_im.reshape(16, 128).T)
    d["ldt_s"] = np.ascontiguousarray(ldt.reshape(16, 128).T)
    for nm, src in (("b_blk_re", inp["ev_ssm_b_re"][0]), ("b_blk_im", inp["ev_ssm_b_im"][0])):
        blk = np.zeros((4, 2, 16, 4, 2, 64), f32)
        s6 = src.reshape(4, 4, 2, 64, 16)
        for g2 in range(2):
            blk[:, g2, :, :, g2, :] = s6[:, :, g2].transpose(1, 3, 0, 2)
        d[nm] = np.ascontiguousarray(blk.reshape(128, 4, 128))
    if not full:
        return d
    d["x_halo"] = np.ascontiguousarray(x[r * TPC - 128:r * TPC]) if r > 0 else np.zeros((128, D), f32)
    d["firstneg"] = np.full((128, 1), NEG if r == 0 else 0.0, f32)
    d["w_out"] = np.ascontiguousarray(inp["ev_w_out"][0])
    d["glu_w"] = np.ascontiguousarray(inp["ev_glu_w"][0])
    d["qg2"] = np.ascontiguousarray(np.tile(inp["ev_q_norm_g"][0], 2)[:, None])
    d["kg2"] = np.ascontiguousarray(np.tile(inp["ev_k_norm_g"][0], 2)[:, None])
    d["sinks_row"] = np.ascontiguousarray(np.tile(inp["ev_sinks"][0][None, :], (128, 1)))
    d["rel_bias"] = np.ascontiguousarray(inp["rel_bias"])
    d["oh_swa"], d["neg_swa"] = swa_onehot()
    for nm, src in (("c_blk_re", inp["ev_ssm_c_re"][0]), ("c_blk_im", inp["ev_ssm_c_im"][0])):
        blk = np.zeros((2, 64, 16, 8, 16), f32)
        s5 = src.reshape(16, 2, 16, 64)
        for st in range(16):
            for g2 in range(2):
                blk[g2, :, st, 2 * (st % 4) + g2, :] = s5[st, g2].T
        d[nm] = np.ascontiguousarray(blk.reshape(128, 16, 128))
    d["d_fm"] = np.ascontiguousarray(inp["ev_ssm_d"][0].reshape(4, 128).T)
    d["glu_b_fm"] = np.ascontiguousarray(inp["ev_glu_b"][0].reshape(8, 128).T)
    d["eprev"] = np.zeros((3, 128, 32), f32) if eprev is None else np.ascontiguousarray(eprev)
    return d


_PROGS = {}


def _prog(name):
    if name not in _PROGS:
        if name == "p1":
            _PROGS[name] = build_p2("p1")[0]
        elif name == "p2":
            _PROGS[name] = build_p2("full")[0]
        elif name == "p2b":
            _PROGS[name] = build_p2b()[0]
        elif name == "p3":
            _PROGS[name] = build_p3()[0]
    return _PROGS[name]


def _run(name, maps):
    return run_bass_kernel_spmd(_prog(name), maps, core_ids=list(range(NCORES))).results


def kernel(**inputs):
    inp = {k: np.asarray(v) for k, v in inputs.items()}
    f32 = np.float32
    r1 = _run("p1", [l0_inputs(inp, c // 4, c % 4, full=False) for c in range(NCORES)])
    eloc = [np.asarray(r1[c]["eloc"], f32) for c in range(NCORES)]
    maps = []
    for c in range(NCORES):
        b, r = c // 4, c % 4
        ep = np.zeros((3, 128, 32), f32)
        for kk in range(min(r, 3)):
            ep[kk] = eloc[4 * b + r - 1 - kk]
        maps.append(l0_inputs(inp, b, r, eprev=ep))
    r2 = _run("p2", maps)
    h1 = [np.asarray(r2[c]["h1"], f32) for c in range(NCORES)]
    maps = [{"h1": h1[c], "w_in": np.ascontiguousarray(inp["od_w_in"][0]), "ng": np.ascontiguousarray(inp["norm_g"][1]),
             "qg": np.ascontiguousarray(inp["od_q_norm_g"][0]), "kg": np.ascontiguousarray(inp["od_k_norm_g"][0])}
            for c in range(NCORES)]
    r3 = _run("p2b", maps)
    maps = []
    for c in range(NCORES):
        b, j = c // 4, c % 4
        cat = lambda nm, ax: np.concatenate([np.asarray(r3[4 * b + r][nm]) for r in range(4)], axis=ax)
        ohm, cmask = p3_consts(j)
        h1b = np.concatenate([h1[4 * b + r] for r in range(4)], axis=0).reshape(64, 128, D)
        maps.append({
            "q_blk": np.ascontiguousarray(cat("q_out", 0)[j::4]),
            "g_blk": np.ascontiguousarray(cat("g_out", 0)[j::4]),
            "qi_blk": np.ascontiguousarray(cat("qi_out", 0)[j::4]),
            "wi_blk": np.ascontiguousarray(cat("wi_out", 0).reshape(64, 128, 8)[j::4]),
            "h1_blk": np.ascontiguousarray(h1b[j::4]),
            "kT_all": np.ascontiguousarray(cat("kT_out", 2)),
            "v_all": np.ascontiguousarray(cat("v_out", 0)),
            "ki_all": np.ascontiguousarray(cat("ki_out", 1)),
            "cmask": cmask, "oh": ohm,
            "rel_bias": np.ascontiguousarray(inp["rel_bias"]),
            "w_out": np.ascontiguousarray(inp["od_w_out"][0]),
        })
    r4 = _run("p3", maps)
    out = np.zeros((BATCH, SEQ // 128, 128, D), f32)
    for c in range(NCORES):
        b, j = c // 4, c % 4
        out[b, j::4] = np.asarray(r4[c]["y"], f32)
    return out.reshape(BATCH, SEQ, D)
```

```python
import contextlib
import math
import numpy as np
import ml_dtypes
import concourse.bass as bass
import concourse.mybir as mybir
from concourse.bass_utils import run_bass_kernel_spmd

F32 = mybir.dt.float32
BF16 = mybir.dt.bfloat16
I32 = mybir.dt.int32
I8 = mybir.dt.int8
ALU = mybir.AluOpType
AF = mybir.ActivationFunctionType
AX = mybir.AxisListType

NCORES = 8
D = 1024
SEQ = 8192
BATCH = 2
TPC = 2048
NBLK = TPC // 128
EPS = 1e-6
NEG = -30000.0
DBG = {}
SEM_LIMIT = 30000


class Buf:
    __slots__ = ("name", "lw", "rd", "dsem", "dcnt")

    def __init__(self, name):
        self.name = name
        self.lw = None
        self.rd = {}
        self.dsem = {}
        self.dcnt = {}


class V:
    __slots__ = ("ap", "bufs")

    def __init__(self, ap, bufs):
        self.ap = ap
        self.bufs = bufs

    def __getitem__(self, idx):
        return V(self.ap[idx], self.bufs)

    def bc(self, shape):
        return V(self.ap.broadcast_to(list(shape)), self.bufs)

    def re(self, pat, **kw):
        return V(self.ap.rearrange(pat, **kw), self.bufs)

    def bitcast(self, dt):
        return V(self.ap.bitcast(dt), self.bufs)


class Tile:
    def __init__(self, P, name, shape, dtype, space="sbuf"):
        nc = P.nc
        if space == "sbuf":
            self.t = P.es.enter_context(nc.sbuf_tensor(name, list(shape), dtype))
        elif space == "psum":
            self.t = P.es.enter_context(nc.psum_tensor(name, list(shape), dtype))
        else:
            raise ValueError(space)
        self.buf = Buf(name)
        self.name = name
        self.shape = shape

    def __getitem__(self, idx):
        return V(self.t[idx], (self.buf,))

    def v(self, idx, buf):
        return V(self.t[idx], (buf,))

    def all(self):
        return V(self.t[:], (self.buf,))


class Prog:
    def __init__(self, nc):
        self.nc = nc
        self.es = contextlib.ExitStack()
        self.eng = {"pe": nc.tensor, "dve": nc.vector, "act": nc.scalar, "pool": nc.gpsimd, "sp": nc.sync}
        self.semh = {}
        self.esem = {}
        self.cnt = {}
        self.epoch = {}
        self.waited = {e: {} for e in self.eng}
        self.nsem = 0
        for e in ("pe", "dve", "act", "pool"):
            self.epoch[e] = 0
            self._new_eng_sem(e)
        self.out_waits = []
        self.n_instr = 0

    def _sem(self, name):
        h = self.es.enter_context(self.nc.semaphore(name))
        self.semh[name] = h
        self.nsem += 1
        return name

    def _new_eng_sem(self, e):
        name = "c_%s_%d" % (e, self.epoch[e])
        self._sem(name)
        self.esem[e] = name
        self.cnt[e] = 0
        self.epoch[e] += 1

    def sb(self, name, shape, dtype):
        return Tile(self, name, shape, dtype, "sbuf")

    def ps(self, name, shape, dtype=F32):
        return Tile(self, name, shape, dtype, "psum")

    def _deps(self, e, reads, writes):
        deps = {}

        def add(sn, val, src, kind):
            if src == e and e == "pe":
                return
            if deps.get(sn, 0) < val:
                deps[sn] = val

        for b in reads:
            if b.lw is not None:
                add(b.lw[0], b.lw[1], b.lw[2], "raw")
        for b in writes:
            if b.lw is not None:
                add(b.lw[0], b.lw[1], b.lw[2], "waw")
            for sn, (v, se) in b.rd.items():
                add(sn, v, se, "war")
        h = self.eng[e]
        w = self.waited[e]
        for sn, v in deps.items():
            if w.get(sn, 0) >= v:
                continue
            h.wait_ge(self.semh[sn], v)
            w[sn] = v
            self.n_instr += 1

    def op(self, e, fn, ins=(), outs=()):
        reads = []
        for x in ins:
            if isinstance(x, V):
                reads.extend(x.bufs)
        writes = []
        for x in outs:
            if isinstance(x, V):
                writes.extend(x.bufs)
        self._deps(e, reads, writes)
        i = fn(self.eng[e])
        self.cnt[e] += 1
        self.n_instr += 1
        sn = self.esem[e]
        v = self.cnt[e]
        i.then_inc(self.semh[sn], 1)
        for b in writes:
            b.lw = (sn, v, e)
            b.rd = {}
        for b in reads:
            if b not in writes:
                b.rd[sn] = (v, e)
        if v >= SEM_LIMIT:
            self._new_eng_sem(e)
        return i

    def dma(self, q, out, in_, primary=None, nc_kwargs=None):
        reads = list(in_.bufs)
        writes = list(out.bufs)
        self._deps(q, reads, writes)
        if primary is None:
            primary = writes[0] if writes else reads[0]
        qc = "sw" if q == "pool" else "hw"
        if qc not in primary.dsem:
            primary.dsem[qc] = self._sem("d%s_%s" % (qc, primary.name))
            primary.dcnt[qc] = 0
        kw = nc_kwargs or {}
        i = self.eng[q].dma_start(out=out.ap, in_=in_.ap, **kw)
        primary.dcnt[qc] += 16
        sn, val = primary.dsem[qc], primary.dcnt[qc]
        i.then_inc(self.semh[sn], 16)
        self.n_instr += 1
        for b in writes:
            b.lw = (sn, val, "dma")
            b.rd = {}
        for b in reads:
            b.rd[sn] = (val, "dma")
        return (sn, val)

    def finish(self, bufs):
        h = self.eng["sp"]
        done = {}
        for b in bufs:
            if b.lw is not None:
                done[b.lw[0]] = max(done.get(b.lw[0], 0), b.lw[1])
            for sn, (v, se) in b.rd.items():
                done[sn] = max(done.get(sn, 0), v)
        for sn, v in done.items():
            h.wait_ge(self.semh[sn], v)

    def mm(self, out, lhsT, rhs, start=True, stop=True):
        return self.op("pe", lambda h: h.matmul(out.ap, lhsT=lhsT.ap, rhs=rhs.ap, start=start, stop=stop),
                       ins=(lhsT, rhs), outs=(out,))

    def tr(self, out, in_, ident):
        return self.op("pe", lambda h: h.transpose(out.ap, in_.ap, ident.ap), ins=(in_, ident), outs=(out,))

    def act(self, out, in_, func, bias=None, scale=None, accum=None, e="act"):
        kw = {}
        ins = [in_]
        outs = [out]
        if bias is not None:
            kw["bias"] = bias.ap if isinstance(bias, V) else bias
            ins.append(bias)
        if scale is not None:
            kw["scale"] = scale.ap if isinstance(scale, V) else scale
            ins.append(scale)
        if accum is not None:
            kw["accum_out"] = accum.ap
            outs.append(accum)
        return self.op(e, lambda h: h.activation(out=out.ap, in_=in_.ap, func=func, **kw), ins=ins, outs=outs)

    def ts(self, out, in0, s1, s2=None, op0=ALU.mult, op1=None, accum=None, e="dve"):
        kw = {}
        ins = [in0, s1, s2]
        outs = [out]
        if op1 is not None:
            kw["op1"] = op1
        if accum is not None:
            kw["accum_out"] = accum.ap
            outs.append(accum)
        a1 = s1.ap if isinstance(s1, V) else s1
        a2 = s2.ap if isinstance(s2, V) else s2
        return self.op(e, lambda h: h.tensor_scalar(out=out.ap, in0=in0.ap, scalar1=a1, scalar2=a2, op0=op0, **kw),
                       ins=ins, outs=outs)

    def tt(self, out, in0, in1, op, e="dve"):
        return self.op(e, lambda h: h.tensor_tensor(out=out.ap, in0=in0.ap, in1=in1.ap, op=op),
                       ins=(in0, in1), outs=(out,))

    def stt(self, out, in0, s, in1, op0, op1):
        a = s.ap if isinstance(s, V) else s
        return self.op("dve", lambda h: h.scalar_tensor_tensor(out=out.ap, in0=in0.ap, scalar=a, in1=in1.ap,
                                                                op0=op0, op1=op1),
                       ins=(in0, s, in1), outs=(out,))

    def copy(self, out, in_, e="dve"):
        if e == "act":
            return self.op("act", lambda h: h.copy(out=out.ap, in_=in_.ap), ins=(in_,), outs=(out,))
        return self.op(e, lambda h: h.tensor_copy(out=out.ap, in_=in_.ap), ins=(in_,), outs=(out,))

    def recip(self, out, in_):
        return self.op("dve", lambda h: h.reciprocal(out=out.ap, in_=in_.ap), ins=(in_,), outs=(out,))

    def reduce(self, out, in_, op, axis=AX.X):
        return self.op("dve", lambda h: h.tensor_reduce(out=out.ap, in_=in_.ap, axis=axis, op=op),
                       ins=(in_,), outs=(out,))

    def memset(self, out, val, e="dve"):
        return self.op(e, lambda h: h.memset(out.ap, val), ins=(), outs=(out,))

    def iota(self, out, pattern, base, cm):
        return self.op("pool", lambda h: h.iota(out.ap, pattern=pattern, base=base, channel_multiplier=cm,
                                                allow_small_or_imprecise_dtypes=True), ins=(), outs=(out,))


def dram_in(nc, name, shape, dtype):
    return nc.dram_tensor(name, list(shape), dtype, kind="ExternalInput")


def dram_out(nc, name, shape, dtype):
    return nc.dram_tensor(name, list(shape), dtype, kind="ExternalOutput")


def DV(t, ap=None, buf=None):
    return V(t.ap() if ap is None else ap, (buf,) if buf is not None else ())


def dap(t, offset, pattern):
    return bass.AP(t, offset, [list(p) for p in pattern])


def barrier(P):
    tgt = {P.esem[e]: P.cnt[e] for e in ("pe", "dve", "act", "pool") if P.cnt[e] > 0}
    for e in ("pe", "dve", "act", "pool", "sp"):
        for sn, v in tgt.items():
            if P.waited[e].get(sn, 0) < v:
                P.eng[e].wait_ge(P.semh[sn], v)
                P.waited[e][sn] = v


def make_consts(P):
    c = {}
    c["ident"] = P.sb("c_ident", [128, 128], BF16)
    with contextlib.ExitStack() as es2:
        old_es, P.es = P.es, es2
        io = P.sb("c_iota", [128, 128], F32)
        P.iota(io.all(), [[1, 128]], 0, -1)
        P.ts(c["ident"].all(), io.all(), 0.0, None, op0=ALU.is_equal)
        barrier(P)
        P.es = old_es
    c["ones"] = P.sb("c_ones", [128, 128], BF16)
    P.memset(c["ones"].all(), 1.0)
    return c


def load_fm_vec(P, name, dram_t, n):
    t = P.sb(name, [128, n], F32)
    P.dma("sp", t.all(), DV(dram_t, dap(dram_t, 0, [[1, 128], [128, n]])),
          nc_kwargs={"allow_slow_non_contiguous": True})
    return t


def rmsnorm_to_fm(P, c, x_v, hnT_v, g_fm, wk, nfeat=1024):
    nk = nfeat // 128
    P.act(wk["junk"].all(), x_v, AF.Square, accum=wk["ss"].all())
    P.act(wk["sd"].all(), wk["ss"].all(), AF.Sqrt, bias=wk["epsb"].all(), scale=1.0 / nfeat)
    P.recip(wk["rstd"].all(), wk["sd"].all())
    P.ts(wk["xn"].all(), x_v, wk["rstd"].all(), None, op0=ALU.mult)
    if DBG.get('no_tr'):
        return
    pst = wk["pst"]
    for k in range(nk):
        P.tr(pst[:, k * 128:(k + 1) * 128], wk["xn"][:, k * 128:(k + 1) * 128], c["ident"].all())
    if DBG.get('no_tt'):
        return
    if DBG.get('tt_copy'):
        P.copy(hnT_v, pst.all().re("p (k t) -> p k t", k=nk))
        return
    for k in range(nk):
        if k % 2 == 0:
            P.ts(hnT_v[:, k, :], pst[:, k * 128:(k + 1) * 128], g_fm[:, k:k + 1], None, op0=ALU.mult)
        else:
            P.act(hnT_v[:, k, :], pst[:, k * 128:(k + 1) * 128], AF.Copy, scale=g_fm[:, k:k + 1])


def make_norm_work(P, pfx, pst):
    wk = {}
    wk["junk"] = P.sb(pfx + "junk", [128, 1024], BF16)
    wk["ss"] = P.sb(pfx + "ss", [128, 1], F32)
    wk["sd"] = P.sb(pfx + "sd", [128, 1], F32)
    wk["rstd"] = P.sb(pfx + "rstd", [128, 1], F32)
    wk["xn"] = P.sb(pfx + "xn", [128, 1024], BF16)
    wk["epsb"] = P.sb(pfx + "epsb", [128, 1], F32)
    P.memset(wk["epsb"].all(), EPS)
    wk["pst"] = pst
    return wk


OD_Q, OD_K, OD_V, OD_G, OD_QI, OD_KI, OD_WI = 0, 1024, 1280, 1536, 2560, 3072, 3136


def build_p2b(nsb=TPC // 512, do_tiles=True, do_blocks=True):
    nc = bass.Bass("TRN2", target_bir_lowering=False)
    h1 = dram_in(nc, "h1", [TPC, D], F32)
    w_in = dram_in(nc, "w_in", [D, 3144], F32)
    ng = dram_in(nc, "ng", [D], F32)
    qg = dram_in(nc, "qg", [128], F32)
    kg = dram_in(nc, "kg", [128], F32)
    q_out = dram_out(nc, "q_out", [NBLK, 128, 8, 128], BF16)
    g_out = dram_out(nc, "g_out", [NBLK, 128, 8, 128], BF16)
    qi_out = dram_out(nc, "qi_out", [NBLK, 128, 4, 128], BF16)
    wi_out = dram_out(nc, "wi_out", [TPC, 8], F32)
    kT_out = dram_out(nc, "kT_out", [128, 2, TPC], BF16)
    v_out = dram_out(nc, "v_out", [TPC, 256], BF16)
    ki_out = dram_out(nc, "ki_out", [128, TPC], BF16)
    P = Prog(nc)
    with P.es:
        c = make_consts(P)
        W = P.sb("W", [128, 8, 3200], BF16)
        Wwi = P.sb("Wwi", [128, 8, 8], BF16)
        wbufs = [Buf("Wk%d" % k) for k in range(8)]
        for k in range(8):
            wv = V(W.t[:, k, :], (wbufs[k],))
            P.dma("pool", wv[:, 0:3136], DV(w_in, w_in.ap()[k * 128:(k + 1) * 128, 0:3136]), primary=wbufs[k])
            P.dma("pool", wv[:, 3136:3200], DV(w_in, w_in.ap()[k * 128:(k + 1) * 128, OD_KI:OD_KI + 64]),
                  primary=wbufs[k])
        P.dma("pool", Wwi.all(), DV(w_in, dap(w_in, OD_WI, [[3144, 128], [128 * 3144, 8], [1, 8]])))

        def Wk(k, c0, n):
            return V(W.t[:, k, c0:c0 + n], (wbufs[k],))

        g_fm = load_fm_vec(P, "g_fm", ng, 8)
        gq = P.sb("gq", [128, 1], F32)
        gk = P.sb("gk", [128, 1], F32)
        P.dma("sp", gq.all(), DV(qg, dap(qg, 0, [[1, 128], [1, 1]])))
        P.dma("sp", gk.all(), DV(kg, dap(kg, 0, [[1, 128], [1, 1]])))
        P.ts(gq.all(), gq.all(), 128.0 ** -0.5, None, op0=ALU.mult)
        pst = P.ps("pst", [128, 1024], BF16)
        wk = make_norm_work(P, "n_", pst)
        xb = [P.sb("xb%d" % i, [128, 1024], F32) for i in range(2)]
        hnT = P.sb("hnT", [128, 8, 512], BF16)
        ring = [P.ps("pr%d" % i, [128, 512]) for i in range(3)]
        ring2 = [P.ps("ps2_%d" % i, [128, 512]) for i in range(2)]
        ptm = P.ps("ptm", [128, 512])
        ptm2 = P.ps("ptm2", [128, 512])
        sq = [P.sb("sq%d" % i, [128, 512], BF16) for i in range(2)]
        sd = [P.sb("sdq%d" % i, [128, 512], F32) for i in range(2)]
        ob = [P.sb("ob%d" % i, [128, 512], BF16) for i in range(4)]
        vb = [P.sb("vb%d" % i, [128, 256], BF16) for i in range(2)]
        wib = [P.sb("wib%d" % i, [128, 8], F32) for i in range(2)]
        rr = [0, 0, 0, 0]
        outs = []

        def nxt(lst, idx):
            t = lst[rr[idx] % len(lst)]
            rr[idx] += 1
            return t

        blk_sz = 128 * 8 * 128
        for sb_ in range(nsb):
            for bl in range(4 if do_blocks else 0):
                t0 = sb_ * 512 + bl * 128
                x = xb[bl % 2]
                P.dma("sp", x.all(), DV(h1, h1.ap()[t0:t0 + 128, :]))
                rmsnorm_to_fm(P, c, x.all(), hnT[:, :, bl * 128:(bl + 1) * 128], g_fm, wk)
                if DBG.get('no_tm'):
                    continue
                for k in range(8):
                    P.mm(ptm[:, 0:256], hnT[:, k, bl * 128:(bl + 1) * 128], Wk(k, OD_V, 256), start=(k == 0), stop=(k == 7))
                v_sb = vb[bl % 2]
                P.copy(v_sb.all(), ptm[:, 0:256], e="act")
                if not DBG.get('no_vst'):
                    P.dma("pool", DV(v_out, v_out.ap()[t0:t0 + 128, :]), v_sb.all(), primary=v_sb.buf)
                    outs += [v_sb.buf]
                if DBG.get('no_wi'):
                    continue
                for k in range(8):
                    P.mm(ptm2[:, 0:8], hnT[:, k, bl * 128:(bl + 1) * 128], Wwi[:, k, :], start=(k == 0), stop=(k == 7))
                w_sb = wib[bl % 2]
                P.ts(w_sb.all(), ptm2[:, 0:8], (8.0 ** -0.5) * (64.0 ** -0.5), None, op0=ALU.mult)
                P.dma("pool", DV(wi_out, wi_out.ap()[t0:t0 + 128, :]), w_sb.all(), primary=w_sb.buf)
                outs += [w_sb.buf]
            tiles = [("q", h, OD_Q + h * 128) for h in range(8)] + [("k", g, OD_K + g * 128) for g in range(2)] + \
                    [("g", h, OD_G + h * 128) for h in range(8)] + [("qi", pr, OD_QI + pr * 128) for pr in range(4)] + \
                    [("ki", 0, OD_KI)]
            for (kind, idx, c0) in (tiles if do_tiles else []):
                pr_ = nxt(ring, 0)
                for k in range(8):
                    P.mm(pr_.all(), Wk(k, c0, 128), hnT[:, k, :], start=(k == 0), stop=(k == 7))
                o = nxt(ob, 1)
                if kind in ("q", "k"):
                    s = nxt(sq, 2)
                    P.act(s.all(), pr_.all(), AF.Square)
                    p2 = nxt(ring2, 3)
                    P.mm(p2.all(), c["ones"].all(), s.all())
                    d_ = sd[(rr[3] - 1) % 2]
                    P.act(d_.all(), p2.all(), AF.Sqrt, bias=wk["epsb"].all(), scale=1.0 / 128)
                    P.recip(d_.all(), d_.all())
                    P.stt(o.all(), pr_.all(), (gq if kind == "q" else gk).all(), d_.all(), ALU.mult, ALU.mult)
                elif kind == "g":
                    P.act(o.all(), pr_.all(), AF.Silu)
                else:
                    P.copy(o.all(), pr_.all(), e="act")
                o3 = o.all().re("p (b t) -> p b t", b=4)
                if kind == "q":
                    dst = dap(q_out, sb_ * 4 * blk_sz + idx * 128, [[8 * 128, 128], [blk_sz, 4], [1, 128]])
                    P.dma("sp", DV(q_out, dst), o3, primary=o.buf)
                elif kind == "g":
                    dst = dap(g_out, sb_ * 4 * blk_sz + idx * 128, [[8 * 128, 128], [blk_sz, 4], [1, 128]])
                    P.dma("sp", DV(g_out, dst), o3, primary=o.buf)
                elif kind == "qi":
                    bs = 128 * 4 * 128
                    dst = dap(qi_out, sb_ * 4 * bs + idx * 128, [[4 * 128, 128], [bs, 4], [1, 128]])
                    P.dma("sp", DV(qi_out, dst), o3, primary=o.buf)
                elif kind == "k":
                    P.dma("sp", DV(kT_out, kT_out.ap()[:, idx, sb_ * 512:(sb_ + 1) * 512]), o.all(), primary=o.buf)
                else:
                    P.dma("sp", DV(ki_out, ki_out.ap()[:, sb_ * 512:(sb_ + 1) * 512]), o.all(), primary=o.buf)
                outs.append(o.buf)
        P.finish(outs)
    return nc, P


NPOS = 12
GLEN = NPOS * 128 + 128
BIS_ITERS = 21


def barrier(P):
    tgt = {P.esem[e]: P.cnt[e] for e in ("pe", "dve", "act", "pool") if P.cnt[e] > 0}
    for e in ("pe", "dve", "act", "pool", "sp"):
        for sn, v in tgt.items():
            if P.waited[e].get(sn, 0) < v:
                P.eng[e].wait_ge(P.semh[sn], v)
                P.waited[e][sn] = v


def build_p3(nblk=NBLK):
    nc = bass.Bass("TRN2", target_bir_lowering=False)
    q_blk = dram_in(nc, "q_blk", [NBLK, 128, 8, 128], BF16)
    g_blk = dram_in(nc, "g_blk", [NBLK, 128, 8, 128], BF16)
    qi_blk = dram_in(nc, "qi_blk", [NBLK, 128, 4, 128], BF16)
    wi_blk = dram_in(nc, "wi_blk", [NBLK, 128, 8], F32)
    h1_blk = dram_in(nc, "h1_blk", [NBLK, 128, D], F32)
    kT_all = dram_in(nc, "kT_all", [128, 2, SEQ], BF16)
    v_all = dram_in(nc, "v_all", [SEQ, 256], BF16)
    ki_all = dram_in(nc, "ki_all", [128, SEQ], BF16)
    cmask = dram_in(nc, "cmask", [128, 512], BF16)
    oh = dram_in(nc, "oh", [32, GLEN], F32)
    rel_bias = dram_in(nc, "rel_bias", [32, 8], F32)
    w_out = dram_in(nc, "w_out", [D, D], F32)
    y = dram_out(nc, "y", [NBLK, 128, D], F32)
    gtab = nc.dram_tensor("gtab", [8, GLEN], F32, kind="Internal")
    gtab_buf = Buf("gtab")
    P = Prog(nc)
    with P.es:
        c = make_consts(P)
        ident4 = P.sb("ident4", [128, 512], BF16)
        for h in range(4):
            P.copy(ident4[:, h * 128:(h + 1) * 128], c["ident"].all())
        kT = P.sb("kT", [128, 2, SEQ], BF16)
        Vs = P.sb("Vs", [128, 64, 256], BF16)
        ki = P.sb("ki", [128, SEQ], BF16)
        Wo = P.sb("Wo", [128, 8, D], BF16)
        cm = P.sb("cm", [128, 512], BF16)
        biasT = P.sb("biasT", [128, NPOS, 1024], BF16)
        for g in range(2):
            P.dma("sp", kT[:, g, :], DV(kT_all, kT_all.ap()[:, g, :]))
        for part in range(4):
            P.dma("sp", Vs[:, part * 16:(part + 1) * 16, :],
                  DV(v_all, dap(v_all, part * 16 * 128 * 256, [[256, 128], [128 * 256, 16], [1, 256]])))
        P.dma("sp", ki.all(), DV(ki_all))
        P.dma("sp", cm.all(), DV(cmask))
        for k in range(8):
            P.dma("pool", Wo[:, k, :], DV(w_out, w_out.ap()[k * 128:(k + 1) * 128, :]))
        ring = [P.ps("ring%d" % i, [128, 512]) for i in range(4)]
        num = [P.ps("num%d" % g, [128, 512]) for g in range(2)]
        den = [P.ps("den%d" % g, [128, 512]) for g in range(2)]
        rr = {"ring": 0}

        def nring():
            for _ in range(4):
                t = ring[rr["ring"] % 4]
                rr["ring"] += 1
                if t.buf.lw is None or t.buf.rd:
                    return t
            raise RuntimeError("PSUM ring exhausted: every bank holds unread results")

        with contextlib.ExitStack() as es2:
            old_es, P.es = P.es, es2
            Jf = P.sb("Jf", [128, 128], F32)
            tmpi = P.sb("tmpi", [128, 128], F32)
            P.iota(tmpi.all(), [[1, 128]], -127, 1)
            P.ts(Jf.all(), tmpi.all(), 0.0, None, op0=ALU.is_equal)
            rb = P.sb("rb", [32, 8], F32)
            rb31 = P.sb("rb31", [32, 8], F32)
            ohs = P.sb("ohs", [32, GLEN], F32)
            gsb = P.sb("gsb", [8, GLEN], F32)
            hk = [P.sb("hk%d" % i, [128, 8, 128], F32) for i in range(2)]
            P.dma("sp", rb.all(), DV(rel_bias))
            P.dma("sp", rb31.all(), DV(rel_bias, dap(rel_bias, 31 * 8, [[0, 32], [1, 8]])))
            P.dma("sp", ohs.all(), DV(oh))
            P.tt(rb.all(), rb.all(), rb31.all(), ALU.subtract)
            for ch in range((GLEN + 511) // 512):
                n = min(512, GLEN - ch * 512)
                pr_ = nring()
                P.mm(pr_[0:8, 0:n], rb.all(), ohs[:, ch * 512:ch * 512 + n])
                P.copy(gsb[:, ch * 512:ch * 512 + n], pr_[0:8, 0:n])
            P.dma("sp", DV(gtab, buf=gtab_buf), gsb.all(), primary=gtab_buf)
            for p in range(NPOS):
                hkt = hk[p % 2]
                P.dma("sp", hkt.all(), DV(gtab, dap(gtab, p * 128, [[1, 128], [GLEN, 8], [1, 128]]), buf=gtab_buf))
                for half in range(2):
                    pr_ = nring()
                    P.mm(pr_.all(), Jf.all(), hkt[:, half * 4:(half + 1) * 4, :])
                    P.copy(biasT[:, p, half * 512:(half + 1) * 512], pr_.all(), e=("act" if half else "dve"))
            barrier(P)
            P.es = old_es

        score = P.sb("score", [128, SEQ], F32)
        sc_bufs = [Buf("sc%d" % i) for i in range(SEQ // 512)]

        def scv(a, b):
            return V(score.t[:, a:b], tuple(sc_bufs[a // 512:(b + 511) // 512]))

        JW = SEQ
        nmall = [P.sb("nmall%d" % i, [128, SEQ], BF16) for i in range(2)]
        qT = [P.sb("qT%d" % i, [128, 8, 128], BF16) for i in range(2)]
        gT = [P.sb("gT%d" % i, [128, 8, 128], BF16) for i in range(1)]
        PmSum = [P.sb("PmSum%d" % g, [128, 512], F32) for g in range(2)]
        ones_f = P.sb("ones_f", [128, 128], F32)
        P.memset(ones_f.all(), 1.0)
        qiT = [P.sb("qiT%d" % i, [128, 4, 128], BF16) for i in range(2)]
        wi = [P.sb("wi%d" % i, [128, 8], F32) for i in range(2)]
        h1b = [P.sb("h1b%d" % i, [128, 512], F32) for i in range(1)]
        Pm = [P.sb("Pm%d" % i, [128, 512], BF16) for i in range(2)]
        rd = [P.sb("rd%d" % i, [128, 512], F32) for i in range(1)]
        catT = P.sb("catT", [128, 8, 128], BF16)
        small = {n: P.sb("b_" + n, [128, 1], F32) for n in ("lo", "hi", "w", "mid", "nmid", "cnt", "cnt2", "sel")}
        nm_j = [(Buf("jD%d" % i), Buf("jA%d" % i)) for i in range(2)]
        pow2 = P.sb("pow2", [128, BIS_ITERS], F32)
        wall = P.sb("wall", [128, BIS_ITERS], F32)
        for k in range(BIS_ITERS):
            P.memset(pow2[:, k:k + 1], 2.0 ** -(k + 1))
        cn = {"Pm": 0}
        outs = []

        def stage1(i):
            steps = []
            nkt = 4 * i + 4
            nk = nkt * 128
            q_, qi_, wi_ = qT[i % 2], qiT[i % 2], wi[i % 2]
            S = small
            nm = nmall[i % 2]
            jD, jA = nm_j[i % 2]
            nm8 = nm.t[:].bitcast(I8)
            nD = ((nk // 2 + 127) // 128) * 128
            nA = nk - nD

            def loads():
                P.dma("sp", qi_.all(), DV(qi_blk, qi_blk.ap()[i]))
                P.dma("sp", wi_.all(), DV(wi_blk, wi_blk.ap()[i]))
                P.dma("sp", q_.all(), DV(q_blk, q_blk.ap()[i]))
            steps.append(loads)

            def idx(chunks, h0):
                for h in range(h0, h0 + 4):
                    for c5 in chunks:
                        sc = scv(c5 * 512, (c5 + 1) * 512)
                        pI = nring()
                        lo_p = (h % 2) * 64
                        P.mm(pI.all(), qi_[lo_p:lo_p + 64, h // 2, :], ki[lo_p:lo_p + 64, c5 * 512:(c5 + 1) * 512])
                        P.act(pI.all(), pI.all(), AF.Relu)
                        if h == 0:
                            P.ts(sc, pI.all(), wi_[:, 0:1], None, op0=ALU.mult)
                        else:
                            P.stt(sc, pI.all(), wi_[:, h:h + 1], sc, ALU.mult, ALU.add)
            for c5 in range(0, (i + 1) if not DBG.get('no_idx') else 0, 2):
                chunks = [c5] + ([c5 + 1] if c5 + 1 <= i else [])
                for h0 in (0, 4):
                    steps.append(lambda chunks=chunks, h0=h0: idx(chunks, h0))

            def bis_init():
                P.reduce(S["lo"].all(), scv(0, nk), ALU.min)
                P.tt(scv(nk - 512, nk), scv(nk - 512, nk), cm.all(), ALU.add)
                P.reduce(S["hi"].all(), scv(0, nk), ALU.max)
                P.ts(S["w"].all(), S["hi"].all(), 1.0, S["lo"].all(), op0=ALU.add, op1=ALU.subtract)
                P.ts(wall.all(), pow2.all(), S["w"].all(), None, op0=ALU.mult)
                P.tt(S["mid"].all(), S["lo"].all(), wall[:, 0:1], ALU.add)
            steps.append(bis_init)

            def bis_iter(k):
                P.ts(V(nm8[:, 0:nk], (jD,)), scv(0, nk), S["mid"].all(), 0.0, op0=ALU.is_ge, op1=ALU.add,
                     accum=S["cnt"].all())
                P.stt(S["sel"].all(), S["cnt"].all(), 255.5, wall[:, k:k + 1], ALU.is_ge, ALU.mult)
                if k + 1 < BIS_ITERS:
                    P.ts(S["mid"].all(), S["sel"].all(), S["lo"].all(), wall[:, k + 1:k + 2], op0=ALU.add, op1=ALU.add)
                P.tt(S["lo"].all(), S["lo"].all(), S["sel"].all(), ALU.add)
            for it in range(BIS_ITERS):
                steps.append(lambda it=it: bis_iter(it))

            def negmask(a0, a1):
                P.ts(V(nm.t[:, a0:a1], (nm.buf, jD, jA)), scv(a0, a1), S["lo"].all(), NEG, op0=ALU.is_lt, op1=ALU.mult)
            for a0 in range(0, nk, 2048):
                steps.append(lambda a0=a0: negmask(a0, min(nk, a0 + 2048)))
            return steps

        def stage2(i):
            steps = []
            nkt = 4 * i + 4
            q_, g_, hb, nm = qT[i % 2], gT[0], h1b[0], nmall[i % 2]

            def tile_(m):
                pos = nkt - 1 - m
                for g in range(2):
                    pL = nring()
                    P.mm(pL.all(), kT[:, g, m * 128:(m + 1) * 128], q_[:, 4 * g:4 * g + 4, :], start=True, stop=False)
                    P.mm(pL.all(), V(nm.t[:, m * 128:(m + 1) * 128], (nm.buf,) + nm_j[i % 2]), ident4.all(), start=False,
                         stop=(pos >= NPOS))
                    if pos < NPOS:
                        P.mm(pL.all(), c["ident"].all(), biasT[:, pos, g * 512:(g + 1) * 512], start=False, stop=True)
                    pm = Pm[cn["Pm"] % 2]
                    cn["Pm"] += 1
                    P.act(pm.all(), pL.all(), AF.Exp)
                    P.mm(num[g].all(), Vs[:, m, g * 128:(g + 1) * 128], pm.all(), start=(m == 0), stop=(m == nkt - 1))
                    if m == 0:
                        P.copy(PmSum[g].all(), pm.all(), e="pool")
                    else:
                        P.tt(PmSum[g].all(), PmSum[g].all(), pm.all(), ALU.add, e="pool")
            for m in range(nkt if not DBG.get('no_att') else 1):
                steps.append(lambda m=m: tile_(m))

            def epi():
                P.dma("sp", g_.all(), DV(g_blk, g_blk.ap()[i]))
                for g in range(2):
                    P.mm(den[g].all(), ones_f.all(), PmSum[g].all())
                for g in range(2):
                    r_ = rd[0]
                    P.recip(r_.all(), den[g].all())
                    tmp = nring()
                    P.tt(tmp.all(), num[g].all(), r_.all(), ALU.mult)
                    P.tt(catT[:, 4 * g:4 * g + 4, :], tmp.all().re("p (h t) -> p h t", h=4), g_[:, 4 * g:4 * g + 4, :], ALU.mult)
                for half in range(2):
                    cs_ = slice(half * 512, (half + 1) * 512)
                    P.dma("sp", hb.all(), DV(h1_blk, h1_blk.ap()[i][:, cs_]))
                    po = nring()
                    for h in range(8):
                        P.mm(po.all(), catT[:, h, :], Wo[:, h, cs_], start=(h == 0), stop=(h == 7))
                    P.tt(hb.all(), po.all(), hb.all(), ALU.add)
                    P.dma("pool", DV(y, y.ap()[i][:, cs_]), hb.all(), primary=hb.buf)
                outs.append(hb.buf)
            steps.append(epi)
            return steps

        def interleave(sa, sb):
            na, nb = len(sa), len(sb)
            ia = ib = 0
            while ia < na or ib < nb:
                if ib >= nb or (ia < na and ia * nb <= ib * na):
                    sa[ia]()
                    ia += 1
                else:
                    sb[ib]()
                    ib += 1

        order = list(range(nblk - 1, -1, -1))
        for st_ in stage1(order[0]):
            st_()
        for oi, i in enumerate(order):
            s2 = stage2(i)
            s1 = stage1(order[oi + 1]) if oi + 1 < nblk else []
            interleave(s1, s2)
        P.finish(outs)
    return nc, P


def t5_bucket_np(d):
    d = np.maximum(d, 0)
    nf = np.maximum(d, 1).astype(np.float32)
    large = 16 + (np.log(nf / np.float32(16)) / np.float32(math.log(1024 / 16)) * np.float32(16)).astype(np.int32)
    large = np.minimum(large, 31)
    return np.where(d < 16, d, large)


def p3_consts(j):
    e = np.arange(GLEN)
    dist = (j - 3) * 128 + e - 127
    b = t5_bucket_np(dist)
    ohm = np.zeros((32, GLEN), np.float32)
    ohm[b, e] = 1.0
    t = np.arange(128)[:, None]
    r = np.arange(512)[None, :]
    s_rel = r - j * 128
    cmask = np.where(s_rel <= t, 0.0, -1e30).astype(np.float32).astype(ml_dtypes.bfloat16)
    return ohm, cmask


EV_Q, EV_K, EV_V, EV_GA, EV_U, EV_GB = 0, 512, 640, 768, 1280, 1792
WQ, WKD, WVD, WGA, WU, WGB = 0, 512, 768, 1024, 1536, 2048
TWO_PI = 2.0 * math.pi
CW1 = 6.28125
CW2 = TWO_PI - CW1


def sincos(P, wk, ph, s_out, c_out):
    ki, kf, r, m = wk["ki"], wk["kf"], wk["r"], wk["m"]
    P.ts(ki, ph, 1.0 / TWO_PI, None, op0=ALU.mult)
    P.copy(kf, ki)
    P.stt(r, kf, -CW1, ph, ALU.mult, ALU.add)
    P.stt(r, kf, -CW2, r, ALU.mult, ALU.add)
    for (outv, shift) in ((s_out, 0.0), (c_out, math.pi / 2)):
        if shift:
            P.ts(r, r, shift, None, op0=ALU.add)
        P.ts(m, r, math.pi, -TWO_PI, op0=ALU.is_gt, op1=ALU.mult)
        P.tt(r, r, m, ALU.add)
        P.ts(m, r, -math.pi, TWO_PI, op0=ALU.is_lt, op1=ALU.mult)
        P.tt(r, r, m, ALU.add)
        P.ts(r, r, math.pi, -math.pi, op0=ALU.min, op1=ALU.max)
        P.act(outv, r, AF.Sin)


def cmul(P, o_re, o_im, a_re, a_im, b_re, b_im, t1, t2):
    P.tt(t1, a_re, b_re, ALU.mult)
    P.tt(t2, a_im, b_im, ALU.mult)
    P.tt(o_re, t1, t2, ALU.subtract)
    P.tt(t1, a_re, b_im, ALU.mult)
    P.tt(t2, a_im, b_re, ALU.mult)
    P.tt(o_im, t1, t2, ALU.add)


def interleave_steps(sa, sb):
    na, nb = len(sa), len(sb)
    ia = ib = 0
    while ia < na or ib < nb:
        if ib >= nb or (ia < na and ia * nb <= ib * na):
            sa[ia]()
            ia += 1
        else:
            sb[ib]()
            ib += 1


def build_p2(mode="full", nsb=TPC // 512):
    full = mode == "full"
    nc = bass.Bass("TRN2", target_bir_lowering=False)
    x_own = dram_in(nc, "x_own", [TPC, D], F32)
    w_in = dram_in(nc, "w_in", [D, 2304], F32)
    ng_fm_d = dram_in(nc, "ng_fm", [128, 8], F32)
    a_re_f = dram_in(nc, "a_re_f", [2048], F32)
    a_im_f = dram_in(nc, "a_im_f", [2048], F32)
    ldt_f = dram_in(nc, "ldt_f", [2048], F32)
    a_re_s = dram_in(nc, "a_re_s", [128, 16], F32)
    a_im_s = dram_in(nc, "a_im_s", [128, 16], F32)
    ldt_s = dram_in(nc, "ldt_s", [128, 16], F32)
    b_blk_re = dram_in(nc, "b_blk_re", [128, 4, 128], F32)
    b_blk_im = dram_in(nc, "b_blk_im", [128, 4, 128], F32)
    if full:
        x_halo = dram_in(nc, "x_halo", [128, D], F32)
        firstneg = dram_in(nc, "firstneg", [128, 1], F32)
        w_out = dram_in(nc, "w_out", [D, D], F32)
        glu_w = dram_in(nc, "glu_w", [512, 1024], F32)
        qg2 = dram_in(nc, "qg2", [128, 1], F32)
        kg2 = dram_in(nc, "kg2", [128, 1], F32)
        sinks_row = dram_in(nc, "sinks_row", [128, 8], F32)
        rel_bias = dram_in(nc, "rel_bias", [32, 8], F32)
        oh_swa = dram_in(nc, "oh_swa", [32, 384], F32)
        neg_swa = dram_in(nc, "neg_swa", [8, 384], F32)
        c_blk_re = dram_in(nc, "c_blk_re", [128, 16, 128], F32)
        c_blk_im = dram_in(nc, "c_blk_im", [128, 16, 128], F32)
        d_fm_d = dram_in(nc, "d_fm", [128, 4], F32)
        glu_b_fm_d = dram_in(nc, "glu_b_fm", [128, 8], F32)
        eprev = dram_in(nc, "eprev", [3, 128, 32], F32)
        h1_out = dram_out(nc, "h1", [TPC, D], F32)
        gtab = nc.dram_tensor("gtab0", [8, 384], F32, kind="Internal")
        gtab_buf = Buf("gtab0")
    else:
        eloc = dram_out(nc, "eloc", [128, 32], F32)
    P = Prog(nc)
    with P.es:
        c = make_consts(P)
        wu0 = WU if full else 0
        g_fm = P.sb("g_fm", [128, 8], F32)
        P.dma("sp", g_fm.all(), DV(ng_fm_d))
        Wr = P.sb("Wr", [128, 16, 128], BF16)
        Ws = P.sb("Ws", [128, 16, 128], BF16)
        BBp = P.sb("BBp", [128, 4, 512], BF16)
        P.memset(BBp.all(), 0.0)
        A128r = P.sb("A128r", [128, 16], F32)
        A128i = P.sb("A128i", [128, 16], F32)
        A127r = P.sb("A127r", [128, 16], F32)
        A127i = P.sb("A127i", [128, 16], F32)
        A1r = P.sb("A1r", [128, 16], F32)
        A1i = P.sb("A1i", [128, 16], F32)
        Zr = P.sb("Zr", [128, 16], F32)
        Zi = P.sb("Zi", [128, 16], F32)
        if full:
            Vr = P.sb("Vr", [128, 16, 128], F32)
            Vi = P.sb("Vi", [128, 16, 128], F32)
            d_fm = P.sb("d_fm_s", [128, 4], F32)
            glu_b = P.sb("glu_b", [128, 8], F32)
            BT = [P.sb("BT%d" % i, [128, 1024], F32) for i in range(3)]
            esink = P.sb("esink", [128, 8], F32)
            gq = P.sb("gq", [128, 1], F32)
            gk = P.sb("gk", [128, 1], F32)
            ones2 = P.sb("ones2", [128, 128], BF16)
            P.dma("sp", d_fm.all(), DV(d_fm_d))
            P.dma("sp", glu_b.all(), DV(glu_b_fm_d))
            P.dma("sp", gq.all(), DV(qg2))
            P.dma("sp", gk.all(), DV(kg2))
            P.ts(gq.all(), gq.all(), 64.0 ** -0.5, None, op0=ALU.mult)
            P.dma("sp", esink.all(), DV(sinks_row))
            P.act(esink.all(), esink.all(), AF.Exp)
            P.memset(ones2.all(), 0.0)
            P.memset(ones2[0:64, 0:64], 1.0)
            P.memset(ones2[64:128, 64:128], 1.0)

        NR = 6
        ring = [P.ps("ring%d" % i, [128, 512]) for i in range(NR)]
        py_bank = P.ps("py_bank", [128, 512]) if full else None
        pst = P.ps("pst", [128, 1024], BF16)
        rr = {"ring": 0}

        def nring():
            for _ in range(NR):
                t = ring[rr["ring"] % NR]
                rr["ring"] += 1
                if t.buf.lw is None or t.buf.rd:
                    return t
            raise RuntimeError("PSUM ring exhausted: every bank holds unread results")

        jf = P.sb("jf", [128, 1], F32)
        P.iota(jf.all(), [[1, 1]], 0, 1)
        io_i = P.sb("io_i", [128, 128], F32)
        P.iota(io_i.all(), [[1, 128]], 0, 0)
        tri_f = P.sb("tri_f", [128, 128], F32)
        P.iota(tri_f.all(), [[1, 128]], 0, -1)
        tmpi_e = P.sb("tmpi_e", [128, 128], F32)
        P.iota(tmpi_e.all(), [[1, 128]], -127, 1)
        ncolW = 2560 if full else 512
        W = P.sb("W", [128, 8, ncolW], BF16)
        wbufs = [Buf("Wk%d" % k) for k in range(8)]

        def Wk(k, c0, n):
            return V(W.t[:, k, c0:c0 + n], (wbufs[k],))

        if full:
            Cre = P.sb("Cre", [128, 16, 128], BF16)
            Cim = P.sb("Cim", [128, 16, 128], BF16)
            Wo = P.sb("Wo", [128, 8, D], BF16)
            Wg = P.sb("Wg", [128, 4, 1024], BF16)

        def issue_weight_loads():
            for k in range(8):
                rows = slice(k * 128, (k + 1) * 128)

                def ld(dst0, n, src0):
                    P.dma("pool", V(W.t[:, k, dst0:dst0 + n], (wbufs[k],)), DV(w_in, w_in.ap()[rows, src0:src0 + n]),
                          primary=wbufs[k])
                if full:
                    ld(WU, 512, EV_U)
                    ld(WQ, 512, EV_Q)
                    for g in range(2):
                        for dup in range(2):
                            ld(WKD + g * 128 + dup * 64, 64, EV_K + g * 64)
                            ld(WVD + g * 128 + dup * 64, 64, EV_V + g * 64)
                    ld(WGA, 512, EV_GA)
                    ld(WGB, 512, EV_GB)
                else:
                    ld(0, 512, EV_U)
            if full:
                P.dma("pool", Cre.all(), DV(c_blk_re))
                P.dma("pool", Cim.all(), DV(c_blk_im))
                for k in range(4):
                    P.dma("pool", Wg[:, k, :], DV(glu_w, glu_w.ap()[k * 128:(k + 1) * 128, :]))
                for k in range(8):
                    P.dma("pool", Wo[:, k, :], DV(w_out, w_out.ap()[k * 128:(k + 1) * 128, :]))
                P.ts(Cim.all(), Cim.all(), -1.0, None, op0=ALU.mult, e="pool")
        tri = P.sb("tri", [128, 128], BF16)
        P.ts(tri.all(), tri_f.all(), 0.0, None, op0=ALU.is_ge)
        with contextlib.ExitStack() as es2:
            old_es, P.es = P.es, es2
            N = 2048
            T = {n: P.sb("t_" + n, [128, N], F32) for n in ("ard", "ang", "a", "b", "mag", "kf", "r", "m", "s", "c")}
            Tki = P.sb("t_ki", [128, N], I32)

            def scw(n):
                return {"ki": Tki[:, 0:n], "kf": T["kf"][:, 0:n], "r": T["r"][:, 0:n], "m": T["m"][:, 0:n]}

            P.dma("sp", T["a"].all(), DV(ldt_f, dap(ldt_f, 0, [[0, 128], [1, N]])))
            P.act(T["a"].all(), T["a"].all(), AF.Exp)
            P.dma("sp", T["ard"].all(), DV(a_re_f, dap(a_re_f, 0, [[0, 128], [1, N]])))
            P.dma("sp", T["ang"].all(), DV(a_im_f, dap(a_im_f, 0, [[0, 128], [1, N]])))
            issue_weight_loads()
            P.tt(T["ard"].all(), T["ard"].all(), T["a"].all(), ALU.mult)
            P.tt(T["ang"].all(), T["ang"].all(), T["a"].all(), ALU.mult)
            P.ts(T["b"].all(), T["ard"].all(), jf.all(), -1.0, op0=ALU.mult, op1=ALU.mult)
            P.act(T["mag"].all(), T["b"].all(), AF.Exp)
            P.ts(T["b"].all(), T["ang"].all(), jf.all(), None, op0=ALU.mult)
            sincos(P, scw(N), T["b"].all(), T["s"].all(), T["c"].all())
            P.tt(Wr.all().re("p a b -> p (a b)"), T["mag"].all(), T["c"].all(), ALU.mult)
            P.tt(Ws.all().re("p a b -> p (a b)"), T["mag"].all(), T["s"].all(), ALU.mult)
            Fr, Fi = T["ard"], T["ang"]
            P.act(T["mag"].all(), T["ard"].all(), AF.Exp)
            sincos(P, scw(N), T["ang"].all(), T["s"].all(), T["c"].all())
            are_row, aim_row = T["kf"], T["r"]
            P.dma("sp", are_row.all(), DV(a_re_f, dap(a_re_f, 0, [[0, 128], [1, N]])))
            P.dma("sp", aim_row.all(), DV(a_im_f, dap(a_im_f, 0, [[0, 128], [1, N]])))
            P.tt(T["c"].all(), T["mag"].all(), T["c"].all(), ALU.mult)
            P.tt(T["s"].all(), T["mag"].all(), T["s"].all(), ALU.mult)
            P.ts(T["c"].all(), T["c"].all(), -1.0, None, op0=ALU.add)
            P.tt(T["a"].all(), are_row.all(), are_row.all(), ALU.mult)
            P.tt(T["b"].all(), aim_row.all(), aim_row.all(), ALU.mult)
            P.tt(T["a"].all(), T["a"].all(), T["b"].all(), ALU.add)
            P.recip(T["a"].all(), T["a"].all())
            P.tt(T["b"].all(), T["c"].all(), are_row.all(), ALU.mult)
            P.tt(T["m"].all(), T["s"].all(), aim_row.all(), ALU.mult)
            P.tt(T["b"].all(), T["b"].all(), T["m"].all(), ALU.add)
            P.tt(Fr.all(), T["b"].all(), T["a"].all(), ALU.mult)
            P.tt(T["b"].all(), T["s"].all(), are_row.all(), ALU.mult)
            P.tt(T["m"].all(), T["c"].all(), aim_row.all(), ALU.mult)
            P.tt(T["b"].all(), T["b"].all(), T["m"].all(), ALU.subtract)
            P.tt(Fi.all(), T["b"].all(), T["a"].all(), ALU.mult)
            bre = T["s"].all()[:, 0:512].re("p (q s) -> p q s", q=4)
            bim = T["c"].all()[:, 0:512].re("p (q s) -> p q s", q=4)
            P.dma("sp", bre, DV(b_blk_re))
            P.dma("sp", bim, DV(b_blk_im))
            t1 = T["m"].all()[:, 0:128]
            t2 = T["m"].all()[:, 128:256]
            for st4 in range(4):
                ps_ = slice(32 * st4, 32 * st4 + 32)
                for q in range(4):
                    st = 4 * q + st4
                    fr = Fr[ps_, st * 128:(st + 1) * 128]
                    fi = Fi[ps_, st * 128:(st + 1) * 128]
                    P.tt(t1[ps_, :], bre[ps_, q, :], fr, ALU.mult)
                    P.tt(t2[ps_, :], bim[ps_, q, :], fi, ALU.mult)
                    co = (st4 % 2) * 256
                    P.tt(BBp[ps_, q, co:co + 128], t1[ps_, :], t2[ps_, :], ALU.subtract)
                    P.tt(t1[ps_, :], bre[ps_, q, :], fi, ALU.mult)
                    P.tt(t2[ps_, :], bim[ps_, q, :], fr, ALU.mult)
                    P.tt(BBp[ps_, q, co + 128:co + 256], t1[ps_, :], t2[ps_, :], ALU.add)
            sp_ = {n: P.sb("sp_" + n, [128, 16], F32) for n in ("ard", "ang", "dt", "a", "b", "mag", "kf", "r", "m", "s", "c")}
            spki = P.sb("sp_ki", [128, 16], I32)
            P.dma("sp", sp_["dt"].all(), DV(ldt_s))
            P.act(sp_["dt"].all(), sp_["dt"].all(), AF.Exp)
            P.dma("sp", sp_["ard"].all(), DV(a_re_s))
            P.dma("sp", sp_["ang"].all(), DV(a_im_s))
            P.tt(sp_["ard"].all(), sp_["ard"].all(), sp_["dt"].all(), ALU.mult)
            P.tt(sp_["ang"].all(), sp_["ang"].all(), sp_["dt"].all(), ALU.mult)
            spw = {"ki": spki.all(), "kf": sp_["kf"].all(), "r": sp_["r"].all(), "m": sp_["m"].all()}
            for (mult_, orr, oii) in ((128.0, A128r, A128i), (127.0, A127r, A127i), (1.0, A1r, A1i)):
                P.ts(sp_["a"].all(), sp_["ard"].all(), mult_, None, op0=ALU.mult)
                P.act(sp_["mag"].all(), sp_["a"].all(), AF.Exp)
                P.ts(sp_["b"].all(), sp_["ang"].all(), mult_, None, op0=ALU.mult)
                sincos(P, spw, sp_["b"].all(), sp_["s"].all(), sp_["c"].all())
                P.tt(orr.all(), sp_["mag"].all(), sp_["c"].all(), ALU.mult)
                P.tt(oii.all(), sp_["mag"].all(), sp_["s"].all(), ALU.mult)
            if full:
                for st in range(16):
                    P.act(T["mag"][:, st * 128:(st + 1) * 128], io_i.all(), AF.Exp, scale=sp_["ard"][:, st:st + 1])
                    P.ts(T["b"][:, st * 128:(st + 1) * 128], io_i.all(), sp_["ang"][:, st:st + 1], None, op0=ALU.mult)
                sincos(P, scw(N), T["b"].all(), T["s"].all(), T["c"].all())
                P.tt(Vr.all().re("p a b -> p (a b)"), T["mag"].all(), T["c"].all(), ALU.mult)
                P.tt(Vi.all().re("p a b -> p (a b)"), T["mag"].all(), T["s"].all(), ALU.mult)
                ep = [P.sb("ep%d" % i, [128, 32], F32) for i in range(3)]
                for i in range(3):
                    P.dma("sp", ep[i].all(), DV(eprev, eprev.ap()[i]))
                pa = [P.sb("pa%d" % i, [128, 16], F32) for i in range(8)]
                cur_r, cur_i = A128r, A128i
                for sqi in range(4):
                    nr, ni = pa[2 * (sqi % 2)], pa[2 * (sqi % 2) + 1]
                    cmul(P, nr.all(), ni.all(), cur_r.all(), cur_i.all(), cur_r.all(), cur_i.all(), pa[4].all(), pa[5].all())
                    cur_r, cur_i = nr, ni
                ar, ai = pa[6], pa[7]
                xr = P.sb("xsr", [128, 16], F32)
                xi = P.sb("xsi", [128, 16], F32)
                cmul(P, ar.all(), ai.all(), cur_r.all(), cur_i.all(), ep[2][:, 0:16], ep[2][:, 16:32], pa[4].all(), pa[5].all())
                P.tt(ar.all(), ar.all(), ep[1][:, 0:16], ALU.add)
                P.tt(ai.all(), ai.all(), ep[1][:, 16:32], ALU.add)
                cmul(P, xr.all(), xi.all(), cur_r.all(), cur_i.all(), ar.all(), ai.all(), pa[4].all(), pa[5].all())
                P.tt(xr.all(), xr.all(), ep[0][:, 0:16], ALU.add)
                P.tt(xi.all(), xi.all(), ep[0][:, 16:32], ALU.add)
                cmul(P, Zr.all(), Zi.all(), A1r.all(), A1i.all(), xr.all(), xi.all(), pa[4].all(), pa[5].all())
                def swa_bias_setup():
                    Jf = P.sb("Jf", [128, 128], F32)
                    P.ts(Jf.all(), tmpi_e.all(), 0.0, None, op0=ALU.is_equal)
                    rb = P.sb("rb", [32, 8], F32)
                    ohs = P.sb("ohs", [32, 384], F32)
                    ngs = P.sb("ngs", [8, 384], F32)
                    P.dma("sp", ngs.all(), DV(neg_swa))
                    gsb = P.sb("gsb", [8, 384], F32)
                    fneg = P.sb("fneg", [128, 1], F32)
                    hk = [P.sb("hk%d" % i, [128, 8, 128], F32) for i in range(2)]
                    P.dma("sp", rb.all(), DV(rel_bias))
                    P.dma("sp", ohs.all(), DV(oh_swa))
                    P.dma("sp", fneg.all(), DV(firstneg))
                    pr_ = nring()
                    P.mm(pr_[0:8, 0:384], rb.all(), ohs.all())
                    P.tt(gsb.all(), pr_[0:8, 0:384], ngs.all(), ALU.add)
                    P.dma("sp", DV(gtab, buf=gtab_buf), gsb.all(), primary=gtab_buf)
                    for kt in range(2):
                        hkt = hk[kt]
                        off = 128 if kt == 0 else 0
                        P.dma("sp", hkt.all(), DV(gtab, dap(gtab, off, [[1, 128], [384, 8], [1, 128]]), buf=gtab_buf))
                        for half in range(2):
                            pr_ = nring()
                            P.mm(pr_.all(), Jf.all(), hkt[:, half * 4:(half + 1) * 4, :])
                            P.copy(BT[kt][:, half * 512:(half + 1) * 512], pr_.all())
                    P.ts(BT[2].all(), BT[0].all(), fneg.all(), None, op0=ALU.add)
            else:
                P.memset(Zr.all(), 0.0)
                P.memset(Zi.all(), 0.0)
            barrier(P)
            P.es = old_es
        if full and not DBG.get('no_bias'):
            with contextlib.ExitStack() as es3:
                old_es, P.es = P.es, es3
                swa_bias_setup()
                barrier(P)
                P.es = old_es

        wk = make_norm_work(P, "n_", pst)
        nxb = 4 if full else 2
        xb = [P.sb("xb%d" % i, [128, D], F32) for i in range(nxb)]
        hnT = P.sb("hnT", [128, 8, 512], BF16)
        uT = P.sb("uT", [128, 4, 512], BF16)
        tm = [P.sb("tm%d" % i, [128, 512], F32) for i in range(4)]
        tmB = [P.sb("tmB%d" % i, [128, 512], F32) for i in range(4)]
        tmsets = [tm, tm] if full else [tm, tmB]
        vre = [P.sb("vre%d" % i, [128, 4, 128], BF16) for i in range(2)]
        vim = [P.sb("vim%d" % i, [128, 4, 128], BF16) for i in range(2)]
        cnt = {"x": 0, "v": 0, "o": 0}
        outs = []
        if full:
            qT = P.sb("qTm", [128, 8, 512], BF16)
            P.memset(qT.all(), 0.0)
            kTd = P.sb("kTd", [128, 2, 640], BF16)
            Vd = P.sb("Vd", [128, 5, 256], BF16)
            gaT = P.sb("gaT", [128, 4, 512], BF16)
            gbT = P.sb("gbT", [128, 4, 512], BF16)
            caT = gaT
            cbT = gbT
            gyT = P.sb("gyT", [128, 4, 512], BF16)
            sqb = [P.sb("sqb%d" % i, [128, 512], BF16) for i in range(2)]
            sdb = [tm[2], tm[3]]
            Sb = [P.sb("Sb%d" % i, [128, 512], F32) for i in range(2)]
            PT = [P.sb("PT%d" % i, [128, 2, 512], BF16) for i in range(2)]
            dtot = Sb[1]
            xre = [P.sb("xre%d" % i, [128, 4, 128], BF16) for i in range(1)]
            xim = [P.sb("xim%d" % i, [128, 4, 128], BF16) for i in range(1)]
            ta = [P.sb("ta%d" % i, [128, 16], F32) for i in range(6)]
            gl = {"y": tm[1], "t": tm[0]}
            sg = [Sb[0]]
        else:
            esum = P.ps("esum", [128, 32])
            Sr = P.sb("Sr", [128, 16], F32)
            Si = P.sb("Si", [128, 16], F32)
            ta = [P.sb("ta%d" % i, [128, 16], F32) for i in range(6)]
            P.memset(Sr.all(), 0.0)
            P.memset(Si.all(), 0.0)

        def kv_project(blk_cols, kslot, vslot):
            for g in range(2):
                pr_ = nring()
                for k in range(8):
                    P.mm(pr_[:, 0:128], Wk(k, WKD + g * 128, 128), hnT[:, k, blk_cols], start=(k == 0), stop=(k == 7))
                s = sqb[g]
                P.act(s[:, 0:128], pr_[:, 0:128], AF.Square)
                p2 = nring()
                P.mm(p2[:, 0:128], ones2.all(), s[:, 0:128])
                d_ = sdb[g]
                P.act(d_[:, 0:128], p2[:, 0:128], AF.Sqrt, bias=wk["epsb"].all(), scale=1.0 / 64)
                P.recip(d_[:, 0:128], d_[:, 0:128])
                P.stt(kTd[:, g, kslot * 128:(kslot + 1) * 128], pr_[:, 0:128], gk.all(), d_[:, 0:128], ALU.mult, ALU.mult)
            pr_ = nring()
            for k in range(8):
                P.mm(pr_[:, 0:256], hnT[:, k, blk_cols], Wk(k, WVD, 256), start=(k == 0), stop=(k == 7))
            P.copy(Vd[:, vslot, :], pr_[:, 0:256], e="act")

        if full:
            xh = xb[nxb - 1]
            P.dma("sp", xh.all(), DV(x_halo))
            rmsnorm_to_fm(P, c, xh.all(), hnT[:, :, 0:128], g_fm, wk)
            kv_project(slice(0, 128), 0, 0)

        for sb_ in range(nsb):
            xs = []
            for bl in range(4):
                t0 = sb_ * 512 + bl * 128
                x = xb[cnt["x"] % nxb]
                cnt["x"] += 1
                xs.append(x)
                P.dma("sp", x.all(), DV(x_own, x_own.ap()[t0:t0 + 128, :]))
                rmsnorm_to_fm(P, c, x.all(), hnT[:, :, bl * 128:(bl + 1) * 128], g_fm, wk)
            for q in range(4):
                pr_ = nring()
                for k in range(8):
                    P.mm(pr_.all(), Wk(k, wu0 + q * 128, 128), hnT[:, k, :], start=(k == 0), stop=(k == 7))
                P.copy(uT[:, q, :], pr_.all(), e="act")
            if full:
                for t in range(4):
                    pr_ = nring()
                    for k in range(8):
                        P.mm(pr_.all(), Wk(k, WQ + t * 128, 128), hnT[:, k, :], start=(k == 0), stop=(k == 7))
                    s = sqb[t % 2]
                    P.act(s.all(), pr_.all(), AF.Square)
                    p2 = nring()
                    P.mm(p2.all(), ones2.all(), s.all())
                    d_ = sdb[t % 2]
                    P.act(d_.all(), p2.all(), AF.Sqrt, bias=wk["epsb"].all(), scale=1.0 / 64)
                    P.recip(d_.all(), d_.all())
                    P.stt(qT[0:64, 2 * t, :], pr_[0:64, :], gq[0:64, :], d_[0:64, :], ALU.mult, ALU.mult)
                    P.stt(qT[64:128, 2 * t + 1, :], pr_[64:128, :], gq[64:128, :], d_[64:128, :], ALU.mult, ALU.mult)
                for (dst, c0) in ((gaT, WGA), (gbT, WGB)):
                    for t in range(4):
                        pr_ = nring()
                        for k in range(8):
                            P.mm(pr_.all(), Wk(k, c0 + t * 128, 128), hnT[:, k, :], start=(k == 0), stop=(k == 7))
                        P.act(dst[:, t, :], pr_.all(), AF.Silu)
                for bl in range(4):
                    kv_project(slice(bl * 128, (bl + 1) * 128), bl + 1, bl + 1)
                swa_steps = []

                def swa_step(bl, g, sb_=sb_):
                    qcols = slice(bl * 128, (bl + 1) * 128)
                    first = (sb_ == 0 and bl == 0)
                    if True:
                        banks = [nring(), nring()]
                        for kt in range(2):
                            kslot = bl + kt
                            for hh in range(4):
                                h = 4 * g + hh
                                lp = slice((h % 2) * 64, (h % 2) * 64 + 64)
                                P.mm(banks[kt][:, hh * 128:(hh + 1) * 128], kTd[:, g, kslot * 128:(kslot + 1) * 128],
                                     qT[:, h, qcols])
                        pt = PT[g]
                        for kt in range(2):
                            bt = BT[2] if (first and kt == 0) else BT[kt]
                            P.tt(Sb[kt].all(), banks[kt].all(), bt[:, g * 512:(g + 1) * 512], ALU.add)
                            P.act(pt[:, kt, :], Sb[kt].all(), AF.Exp)
                        pn, pd = nring(), nring()
                        for kt in range(2):
                            P.mm(pn.all(), Vd[:, bl + kt, g * 128:(g + 1) * 128], pt[:, kt, :], start=(kt == 0), stop=(kt == 1))
                        for kt in range(2):
                            P.mm(pd.all(), c["ones"].all(), pt[:, kt, :], start=(kt == 0), stop=(kt == 1))
                        for hh in range(4):
                            h = 4 * g + hh
                            P.ts(dtot[:, hh * 128:(hh + 1) * 128], pd[:, hh * 128:(hh + 1) * 128], esink[:, h:h + 1], None,
                                 op0=ALU.add)
                        P.recip(dtot.all(), dtot.all())
                        for hh in range(4):
                            h = 4 * g + hh
                            lp = slice((h % 2) * 64, (h % 2) * 64 + 64)
                            o = caT[lp, h // 2, qcols]
                            tmo = Sb[0][lp, hh * 128:(hh + 1) * 128]
                            P.tt(tmo, pn[lp, hh * 128:(hh + 1) * 128], dtot[lp, hh * 128:(hh + 1) * 128], ALU.mult)
                            P.tt(o, tmo, gaT[lp, h // 2, qcols], ALU.mult)
                def halo_step():
                    P.copy(kTd[:, :, 0:128], kTd[:, :, 512:640], e="pool")
                    P.copy(Vd[:, 0, :], Vd[:, 4, :], e="pool")
                for bl in range(0 if DBG.get('no_swa') else 4):
                    for g in range(2):
                        swa_steps.append(lambda bl=bl, g=g: swa_step(bl, g))
                swa_steps.append(halo_step)
            ssm_steps = []
            pyd = {}

            def ssm_A_step(bl, q):
                tcols = slice(bl * 128, (bl + 1) * 128)
                if True:
                    bu = [nring(), nring()]
                    for hb_ in range(2):
                        ps_ = slice(64 * hb_, 64 * hb_ + 64)
                        P.mm(bu[hb_].all(), uT[ps_, q, tcols], BBp[ps_, q, :])
                    vr_, vi_ = vre[(4 * bl + q) % 2], vim[(4 * bl + q) % 2]
                    tmA = tmsets[(4 * bl + q) % 2]
                    for hb_ in range(2):
                        bre_ = bu[hb_].all().re("p (a c s) -> p a c s", a=2, c=2)[:, :, 0, :]
                        bim_ = bu[hb_].all().re("p (a c s) -> p a c s", a=2, c=2)[:, :, 1, :]
                        sts = slice(4 * q + 2 * hb_, 4 * q + 2 * hb_ + 2)
                        wr_, ws_ = Wr[:, sts, :], Ws[:, sts, :]
                        o = slice(hb_ * 256, hb_ * 256 + 256)
                        P.tt(tmA[0][:, o].re("p (a s) -> p a s", a=2), bre_, wr_, ALU.mult)
                        P.tt(tmA[1][:, o].re("p (a s) -> p a s", a=2), bim_, ws_, ALU.mult)
                        P.tt(tmA[2][:, o].re("p (a s) -> p a s", a=2), bim_, wr_, ALU.mult)
                        P.tt(tmA[3][:, o].re("p (a s) -> p a s", a=2), bre_, ws_, ALU.mult)
                    P.tt(vr_.all().re("p a s -> p (a s)"), tmA[0].all(), tmA[1].all(), ALU.add, e="pool")
                    P.tt(vi_.all().re("p a s -> p (a s)"), tmA[2].all(), tmA[3].all(), ALU.subtract, e="pool")
                    if full and DBG.get('no_ssm2'):
                        return
                    if not full:
                        for st4 in range(4):
                            st = 4 * q + st4
                            P.mm(esum[:, st:st + 1], vr_[:, st4, :], c["ones"][:, 0:1])
                            P.mm(esum[:, 16 + st:17 + st], vi_[:, st4, :], c["ones"][:, 0:1])
                        return
            def ssm_B_step(bl, q):
                tcols = slice(bl * 128, (bl + 1) * 128)
                vr_, vi_ = vre[(4 * bl + q) % 2], vim[(4 * bl + q) % 2]
                if True:
                    csr, csi = nring(), nring()
                    for st4 in range(4):
                        P.mm(csr[:, st4 * 128:(st4 + 1) * 128], vr_[:, st4, :], tri.all())
                        P.mm(csi[:, st4 * 128:(st4 + 1) * 128], vi_[:, st4, :], tri.all())
                    xr_, xi_ = xre[0], xim[0]
                    for st4 in range(4):
                        st = 4 * q + st4
                        cr = csr[:, st4 * 128:(st4 + 1) * 128]
                        ci = csi[:, st4 * 128:(st4 + 1) * 128]
                        o = slice(st4 * 128, (st4 + 1) * 128)
                        P.stt(tmB[0][:, o], cr, Zr[:, st:st + 1], Vr[:, st, :], ALU.add, ALU.mult)
                        P.stt(tmB[1][:, o], ci, Zi[:, st:st + 1], Vi[:, st, :], ALU.add, ALU.mult)
                        P.stt(tmB[2][:, o], cr, Zr[:, st:st + 1], Vi[:, st, :], ALU.add, ALU.mult)
                        P.stt(tmB[3][:, o], ci, Zi[:, st:st + 1], Vr[:, st, :], ALU.add, ALU.mult)
                    sq_ = slice(4 * q, 4 * q + 4)
                    cr127 = csr.all().re("p (a s) -> p a s", a=4)[:, :, 127]
                    ci127 = csi.all().re("p (a s) -> p a s", a=4)[:, :, 127]
                    P.tt(ta[0][:, 0:4], Zr[:, sq_], cr127, ALU.add)
                    P.tt(ta[1][:, 0:4], Zi[:, sq_], ci127, ALU.add)
                    cmul(P, Zr[:, sq_], Zi[:, sq_], A128r[:, sq_], A128i[:, sq_], ta[0][:, 0:4], ta[1][:, 0:4],
                         ta[2][:, 0:4], ta[3][:, 0:4])
                    P.tt(xr_.all().re("p a s -> p (a s)"), tmB[0].all(), tmB[1].all(), ALU.subtract, e="pool")
                    P.tt(xi_.all().re("p a s -> p (a s)"), tmB[2].all(), tmB[3].all(), ALU.add, e="pool")
                    pyd['py'] = py_bank
                    py = py_bank
                    for st4 in range(4):
                        st = 4 * q + st4
                        P.mm(py[:, q * 128:(q + 1) * 128], Cre[:, st, :], xr_[:, st4, :], start=(st4 == 0), stop=False)
                        P.mm(py[:, q * 128:(q + 1) * 128], Cim[:, st, :], xi_[:, st4, :], start=False, stop=(st4 == 3))
            def ssm_tail_step(bl):
                tcols = slice(bl * 128, (bl + 1) * 128)
                if not full:
                    P.copy(ta[4].all(), esum[:, 0:16])
                    P.copy(ta[5].all(), esum[:, 16:32])
                    cmul(P, ta[0].all(), ta[1].all(), A127r.all(), A127i.all(), ta[4].all(), ta[5].all(), ta[2].all(), ta[3].all())
                    cmul(P, ta[4].all(), ta[5].all(), A128r.all(), A128i.all(), Sr.all(), Si.all(), ta[2].all(), ta[3].all())
                    P.tt(Sr.all(), ta[0].all(), ta[4].all(), ALU.add)
                    P.tt(Si.all(), ta[1].all(), ta[5].all(), ALU.add)
                    return
                if DBG.get('no_ssm2'):
                    return
                for q in range(4):
                    P.stt(gl["y"][:, q * 128:(q + 1) * 128], uT[:, q, tcols], d_fm[:, q:q + 1], pyd['py'][:, q * 128:(q + 1) * 128],
                          ALU.mult, ALU.add)
                P.act(gyT[:, :, tcols], gl["y"].all().re("p (q t) -> p q t", q=4), AF.Gelu_apprx_tanh)
            items = [(bl, q) for bl in range(4) for q in range(4)]
            if full and not DBG.get('no_ssm2'):
                for k in range(2):
                    ssm_steps.append(lambda k=k: ssm_A_step(*items[k]))
                for k in range(16):
                    ssm_steps.append(lambda k=k: ssm_B_step(*items[k]))
                    if k + 2 < 16:
                        ssm_steps.append(lambda k=k: ssm_A_step(*items[k + 2]))
                    if items[k][1] == 3:
                        ssm_steps.append(lambda k=k: ssm_tail_step(items[k][0]))
            else:
                for bl in range(4):
                    for q in range(4):
                        ssm_steps.append(lambda bl=bl, q=q: ssm_A_step(bl, q))
                    ssm_steps.append(lambda bl=bl: ssm_tail_step(bl))
            interleave_steps(swa_steps if full else [], ssm_steps)
            if not full:
                continue
            for f in range(0 if DBG.get('no_glu') else 4):
                pa_, pb_ = nring(), nring()
                for q in range(4):
                    P.mm(pa_.all(), Wg[:, q, f * 128:(f + 1) * 128], gyT[:, q, :], start=(q == 0), stop=(q == 3))
                for q in range(4):
                    P.mm(pb_.all(), Wg[:, q, 512 + f * 128:512 + (f + 1) * 128], gyT[:, q, :], start=(q == 0), stop=(q == 3))
                s_ = sg[0]
                P.act(s_.all(), pb_.all(), AF.Sigmoid, bias=glu_b[:, 4 + f:5 + f])
                P.stt(gl["t"].all(), pa_.all(), glu_b[:, f:f + 1], s_.all(), ALU.add, ALU.mult)
                P.tt(cbT[:, f, :], gl["t"].all(), gbT[:, f, :], ALU.mult)
            for bl in range(4):
                tcols = slice(bl * 128, (bl + 1) * 128)
                t0 = sb_ * 512 + bl * 128
                o_ = xs[bl]
                for half in range(0 if DBG.get('no_out') else 2):
                    po = nring()
                    for k in range(8):
                        lhs = caT[:, k, tcols] if k < 4 else cbT[:, k - 4, tcols]
                        P.mm(po.all(), lhs, Wo[:, k, half * 512:(half + 1) * 512], start=(k == 0), stop=(k == 7))
                    P.tt(o_[:, half * 512:(half + 1) * 512], po.all(), xs[bl][:, half * 512:(half + 1) * 512], ALU.add)
                P.dma("pool", DV(h1_out, h1_out.ap()[t0:t0 + 128, :]), o_.all(), primary=o_.buf)
                outs.append(o_.buf)
        if not full:
            eo = P.sb("eo", [128, 32], F32)
            P.copy(eo[:, 0:16], Sr.all())
            P.copy(eo[:, 16:32], Si.all())
            P.dma("pool", DV(eloc), eo.all(), primary=eo.buf)
            outs.append(eo.buf)
        P.finish(outs)
    return nc, P


def swa_onehot():
    e = np.arange(384)
    d = e - 127
    valid = (d >= 0) & (d < 128)
    oh = np.zeros((32, 384), np.float32)
    oh[t5_bucket_np(d)[valid], e[valid]] = 1.0
    neg = np.tile(np.where(valid, 0.0, NEG).astype(np.float32)[None, :], (8, 1))
    return oh, neg


def l0_inputs(inp, b, r, eprev=None, full=True):
    f32 = np.float32
    x = inp["x"][b]
    d = {}
    d["x_own"] = np.ascontiguousarray(x[r * TPC:(r + 1) * TPC])
    d["w_in"] = np.ascontiguousarray(inp["ev_w_in"][0])
    d["ng_fm"] = np.ascontiguousarray(inp["norm_g"][0].reshape(8, 128).T)
    a_re = inp["ev_ssm_a_re"][0]
    a_im = inp["ev_ssm_a_im"][0]
    ldt = np.repeat(inp["ev_ssm_log_dt"][0], 64)
    d["a_re_f"] = np.ascontiguousarray(a_re.reshape(2048))
    d["a_im_f"] = np.ascontiguousarray(a_im.reshape(2048))
    d["ldt_f"] = np.ascontiguousarray(ldt)
    d["a_re_s"] = np.ascontiguousarray(a_re.reshape(16, 128).T)
    d["a_im_s"] = np.ascontiguousarray(a_im.reshape(16, 128).T)
    d["ldt_s"] = np.ascontiguousarray(ldt.reshape(16, 128).T)
    for nm, src in (("b_blk_re", inp["ev_ssm_b_re"][0]), ("b_blk_im", inp["ev_ssm_b_im"][0])):
        blk = np.zeros((4, 2, 16, 4, 2, 64), f32)
        s6 = src.reshape(4, 4, 2, 64, 16)
        for g2 in range(2):
            blk[:, g2, :, :, g2, :] = s6[:, :, g2].transpose(1, 3, 0, 2)
        d[nm] = np.ascontiguousarray(blk.reshape(128, 4, 128))
    if not full:
        return d
    d["x_halo"] = np.ascontiguousarray(x[r * TPC - 128:r * TPC]) if r > 0 else np.zeros((128, D), f32)
    d["firstneg"] = np.full((128, 1), NEG if r == 0 else 0.0, f32)
    d["w_out"] = np.ascontiguousarray(inp["ev_w_out"][0])
    d["glu_w"] = np.ascontiguousarray(inp["ev_glu_w"][0])
    d["qg2"] = np.ascontiguousarray(np.tile(inp["ev_q_norm_g"][0], 2)[:, None])
    d["kg2"] = np.ascontiguousarray(np.tile(inp["ev_k_norm_g"][0], 2)[:, None])
    d["sinks_row"] = np.ascontiguousarray(np.tile(inp["ev_sinks"][0][None, :], (128, 1)))
    d["rel_bias"] = np.ascontiguousarray(inp["rel_bias"])
    d["oh_swa"], d["neg_swa"] = swa_onehot()
    for nm, src in (("c_blk_re", inp["ev_ssm_c_re"][0]), ("c_blk_im", inp["ev_ssm_c_im"][0])):
        blk = np.zeros((2, 64, 16, 8, 16), f32)
        s5 = src.reshape(16, 2, 16, 64)
        for st in range(16):
            for g2 in range(2):
                blk[g2, :, st, 2 * (st % 4) + g2, :] = s5[st, g2].T
        d[nm] = np.ascontiguousarray(blk.reshape(128, 16, 128))
    d["d_fm"] = np.ascontiguousarray(inp["ev_ssm_d"][0].reshape(4, 128).T)
    d["glu_b_fm"] = np.ascontiguousarray(inp["ev_glu_b"][0].reshape(8, 128).T)
    d["eprev"] = np.zeros((3, 128, 32), f32) if eprev is None else np.ascontiguousarray(eprev)
    return d


_PROGS = {}


def _prog(name):
    if name not in _PROGS:
        if name == "p1":
            _PROGS[name] = build_p2("p1")[0]
        elif name == "p2":
            _PROGS[name] = build_p2("full")[0]
        elif name == "p2b":
            _PROGS[name] = build_p2b()[0]
        elif name == "p3":
            _PROGS[name] = build_p3()[0]
    return _PROGS[name]


def _run(name, maps):
    return run_bass_kernel_spmd(_prog(name), maps, core_ids=list(range(NCORES))).results


def kernel(**inputs):
    inp = {k: np.asarray(v) for k, v in inputs.items()}
    f32 = np.float32
    r1 = _run("p1", [l0_inputs(inp, c // 4, c % 4, full=False) for c in range(NCORES)])
    eloc = [np.asarray(r1[c]["eloc"], f32) for c in range(NCORES)]
    maps = []
    for c in range(NCORES):
        b, r = c // 4, c % 4
        ep = np.zeros((3, 128, 32), f32)
        for kk in range(min(r, 3)):
            ep[kk] = eloc[4 * b + r - 1 - kk]
        maps.append(l0_inputs(inp, b, r, eprev=ep))
    r2 = _run("p2", maps)
    h1 = [np.asarray(r2[c]["h1"], f32) for c in range(NCORES)]
    maps = [{"h1": h1[c], "w_in": np.ascontiguousarray(inp["od_w_in"][0]), "ng": np.ascontiguousarray(inp["norm_g"][1]),
             "qg": np.ascontiguousarray(inp["od_q_norm_g"][0]), "kg": np.ascontiguousarray(inp["od_k_norm_g"][0])}
            for c in range(NCORES)]
    r3 = _run("p2b", maps)
    maps = []
    for c in range(NCORES):
        b, j = c // 4, c % 4
        cat = lambda nm, ax: np.concatenate([np.asarray(r3[4 * b + r][nm]) for r in range(4)], axis=ax)
        ohm, cmask = p3_consts(j)
        h1b = np.concatenate([h1[4 * b + r] for r in range(4)], axis=0).reshape(64, 128, D)
        maps.append({
            "q_blk": np.ascontiguousarray(cat("q_out", 0)[j::4]),
            "g_blk": np.ascontiguousarray(cat("g_out", 0)[j::4]),
            "qi_blk": np.ascontiguousarray(cat("qi_out", 0)[j::4]),
            "wi_blk": np.ascontiguousarray(cat("wi_out", 0).reshape(64, 128, 8)[j::4]),
            "h1_blk": np.ascontiguousarray(h1b[j::4]),
            "kT_all": np.ascontiguousarray(cat("kT_out", 2)),
            "v_all": np.ascontiguousarray(cat("v_out", 0)),
            "ki_all": np.ascontiguousarray(cat("ki_out", 1)),
            "cmask": cmask, "oh": ohm,
            "rel_bias": np.ascontiguousarray(inp["rel_bias"]),
            "w_out": np.ascontiguousarray(inp["od_w_out"][0]),
        })
    r4 = _run("p3", maps)
    out = np.zeros((BATCH, SEQ // 128, 128, D), f32)
    for c in range(NCORES):
        b, j = c // 4, c % 4
        out[b, j::4] = np.asarray(r4[c]["y"], f32)
    return out.reshape(BATCH, SEQ, D)
```

```python
import contextlib
import math
import numpy as np
import ml_dtypes
import concourse.bass as bass
import concourse.mybir as mybir
from concourse.bass_utils import run_bass_kernel_spmd

F32 = mybir.dt.float32
BF16 = mybir.dt.bfloat16
I32 = mybir.dt.int32
I8 = mybir.dt.int8
ALU = mybir.AluOpType
AF = mybir.ActivationFunctionType
AX = mybir.AxisListType

NCORES = 8
D = 1024
SEQ = 8192
BATCH = 2
TPC = 2048
NBLK = TPC // 128
EPS = 1e-6
NEG = -30000.0
DBG = {}
SEM_LIMIT = 30000


class Buf:
    __slots__ = ("name", "lw", "rd", "dsem", "dcnt")

    def __init__(self, name):
        self.name = name
        self.lw = None
        self.rd = {}
        self.dsem = {}
        self.dcnt = {}


class V:
    __slots__ = ("ap", "bufs")

    def __init__(self, ap, bufs):
        self.ap = ap
        self.bufs = bufs

    def __getitem__(self, idx):
        return V(self.ap[idx], self.bufs)

    def bc(self, shape):
        return V(self.ap.broadcast_to(list(shape)), self.bufs)

    def re(self, pat, **kw):
        return V(self.ap.rearrange(pat, **kw), self.bufs)

    def bitcast(self, dt):
        return V(self.ap.bitcast(dt), self.bufs)


class Tile:
    def __init__(self, P, name, shape, dtype, space="sbuf"):
        nc = P.nc
        if space == "sbuf":
            self.t = P.es.enter_context(nc.sbuf_tensor(name, list(shape), dtype))
        elif space == "psum":
            self.t = P.es.enter_context(nc.psum_tensor(name, list(shape), dtype))
        else:
            raise ValueError(space)
        self.buf = Buf(name)
        self.name = name
        self.shape = shape

    def __getitem__(self, idx):
        return V(self.t[idx], (self.buf,))

    def v(self, idx, buf):
        return V(self.t[idx], (buf,))

    def all(self):
        return V(self.t[:], (self.buf,))


class Prog:
    def __init__(self, nc):
        self.nc = nc
        self.es = contextlib.ExitStack()
        self.eng = {"pe": nc.tensor, "dve": nc.vector, "act": nc.scalar, "pool": nc.gpsimd, "sp": nc.sync}
        self.semh = {}
        self.esem = {}
        self.cnt = {}
        self.epoch = {}
        self.waited = {e: {} for e in self.eng}
        self.nsem = 0
        for e in ("pe", "dve", "act", "pool"):
            self.epoch[e] = 0
            self._new_eng_sem(e)
        self.out_waits = []
        self.n_instr = 0

    def _sem(self, name):
        h = self.es.enter_context(self.nc.semaphore(name))
        self.semh[name] = h
        self.nsem += 1
        return name

    def _new_eng_sem(self, e):
        name = "c_%s_%d" % (e, self.epoch[e])
        self._sem(name)
        self.esem[e] = name
        self.cnt[e] = 0
        self.epoch[e] += 1

    def sb(self, name, shape, dtype):
        return Tile(self, name, shape, dtype, "sbuf")

    def ps(self, name, shape, dtype=F32):
        return Tile(self, name, shape, dtype, "psum")

    def _deps(self, e, reads, writes):
        deps = {}

        def add(sn, val, src, kind):
            if src == e and e == "pe":
                return
            if deps.get(sn, 0) < val:
                deps[sn] = val

        for b in reads:
            if b.lw is not None:
                add(b.lw[0], b.lw[1], b.lw[2], "raw")
        for b in writes:
            if b.lw is not None:
                add(b.lw[0], b.lw[1], b.lw[2], "waw")
            for sn, (v, se) in b.rd.items():
                add(sn, v, se, "war")
        h = self.eng[e]
        w = self.waited[e]
        for sn, v in deps.items():
            if w.get(sn, 0) >= v:
                continue
            h.wait_ge(self.semh[sn], v)
            w[sn] = v
            self.n_instr += 1

    def op(self, e, fn, ins=(), outs=()):
        reads = []
        for x in ins:
            if isinstance(x, V):
                reads.extend(x.bufs)
        writes = []
        for x in outs:
            if isinstance(x, V):
                writes.extend(x.bufs)
        self._deps(e, reads, writes)
        i = fn(self.eng[e])
        self.cnt[e] += 1
        self.n_instr += 1
        sn = self.esem[e]
        v = self.cnt[e]
        i.then_inc(self.semh[sn], 1)
        for b in writes:
            b.lw = (sn, v, e)
            b.rd = {}
        for b in reads:
            if b not in writes:
                b.rd[sn] = (v, e)
        if v >= SEM_LIMIT:
            self._new_eng_sem(e)
        return i

    def dma(self, q, out, in_, primary=None, nc_kwargs=None):
        reads = list(in_.bufs)
        writes = list(out.bufs)
        self._deps(q, reads, writes)
        if primary is None:
            primary = writes[0] if writes else reads[0]
        qc = "sw" if q == "pool" else "hw"
        if qc not in primary.dsem:
            primary.dsem[qc] = self._sem("d%s_%s" % (qc, primary.name))
            primary.dcnt[qc] = 0
        kw = nc_kwargs or {}
        i = self.eng[q].dma_start(out=out.ap, in_=in_.ap, **kw)
        primary.dcnt[qc] += 16
        sn, val = primary.dsem[qc], primary.dcnt[qc]
        i.then_inc(self.semh[sn], 16)
        self.n_instr += 1
        for b in writes:
            b.lw = (sn, val, "dma")
            b.rd = {}
        for b in reads:
            b.rd[sn] = (val, "dma")
        return (sn, val)

    def finish(self, bufs):
        h = self.eng["sp"]
        done = {}
        for b in bufs:
            if b.lw is not None:
                done[b.lw[0]] = max(done.get(b.lw[0], 0), b.lw[1])
            for sn, (v, se) in b.rd.items():
                done[sn] = max(done.get(sn, 0), v)
        for sn, v in done.items():
            h.wait_ge(self.semh[sn], v)

    def mm(self, out, lhsT, rhs, start=True, stop=True):
        return self.op("pe", lambda h: h.matmul(out.ap, lhsT=lhsT.ap, rhs=rhs.ap, start=start, stop=stop),
                       ins=(lhsT, rhs), outs=(out,))

    def tr(self, out, in_, ident):
        return self.op("pe", lambda h: h.transpose(out.ap, in_.ap, ident.ap), ins=(in_, ident), outs=(out,))

    def act(self, out, in_, func, bias=None, scale=None, accum=None, e="act"):
        kw = {}
        ins = [in_]
        outs = [out]
        if bias is not None:
            kw["bias"] = bias.ap if isinstance(bias, V) else bias
            ins.append(bias)
        if scale is not None:
            kw["scale"] = scale.ap if isinstance(scale, V) else scale
            ins.append(scale)
        if accum is not None:
            kw["accum_out"] = accum.ap
            outs.append(accum)
        return self.op(e, lambda h: h.activation(out=out.ap, in_=in_.ap, func=func, **kw), ins=ins, outs=outs)

    def ts(self, out, in0, s1, s2=None, op0=ALU.mult, op1=None, accum=None, e="dve"):
        kw = {}
        ins = [in0, s1, s2]
        outs = [out]
        if op1 is not None:
            kw["op1"] = op1
        if accum is not None:
            kw["accum_out"] = accum.ap
            outs.append(accum)
        a1 = s1.ap if isinstance(s1, V) else s1
        a2 = s2.ap if isinstance(s2, V) else s2
        return self.op(e, lambda h: h.tensor_scalar(out=out.ap, in0=in0.ap, scalar1=a1, scalar2=a2, op0=op0, **kw),
                       ins=ins, outs=outs)

    def tt(self, out, in0, in1, op, e="dve"):
        return self.op(e, lambda h: h.tensor_tensor(out=out.ap, in0=in0.ap, in1=in1.ap, op=op),
                       ins=(in0, in1), outs=(out,))

    def stt(self, out, in0, s, in1, op0, op1):
        a = s.ap if isinstance(s, V) else s
        return self.op("dve", lambda h: h.scalar_tensor_tensor(out=out.ap, in0=in0.ap, scalar=a, in1=in1.ap,
                                                                op0=op0, op1=op1),
                       ins=(in0, s, in1), outs=(out,))

    def copy(self, out, in_, e="dve"):
        if e == "act":
            return self.op("act", lambda h: h.copy(out=out.ap, in_=in_.ap), ins=(in_,), outs=(out,))
        return self.op(e, lambda h: h.tensor_copy(out=out.ap, in_=in_.ap), ins=(in_,), outs=(out,))

    def recip(self, out, in_):
        return self.op("dve", lambda h: h.reciprocal(out=out.ap, in_=in_.ap), ins=(in_,), outs=(out,))

    def reduce(self, out, in_, op, axis=AX.X):
        return self.op("dve", lambda h: h.tensor_reduce(out=out.ap, in_=in_.ap, axis=axis, op=op),
                       ins=(in_,), outs=(out,))

    def memset(self, out, val, e="dve"):
        return self.op(e, lambda h: h.memset(out.ap, val), ins=(), outs=(out,))

    def iota(self, out, pattern, base, cm):
        return self.op("pool", lambda h: h.iota(out.ap, pattern=pattern, base=base, channel_multiplier=cm,
                                                allow_small_or_imprecise_dtypes=True), ins=(), outs=(out,))


def dram_in(nc, name, shape, dtype):
    return nc.dram_tensor(name, list(shape), dtype, kind="ExternalInput")


def dram_out(nc, name, shape, dtype):
    return nc.dram_tensor(name, list(shape), dtype, kind="ExternalOutput")


def DV(t, ap=None, buf=None):
    return V(t.ap() if ap is None else ap, (buf,) if buf is not None else ())


def dap(t, offset, pattern):
    return bass.AP(t, offset, [list(p) for p in pattern])


def barrier(P):
    tgt = {P.esem[e]: P.cnt[e] for e in ("pe", "dve", "act", "pool") if P.cnt[e] > 0}
    for e in ("pe", "dve", "act", "pool", "sp"):
        for sn, v in tgt.items():
            if P.waited[e].get(sn, 0) < v:
                P.eng[e].wait_ge(P.semh[sn], v)
                P.waited[e][sn] = v


def make_consts(P):
    c = {}
    c["ident"] = P.sb("c_ident", [128, 128], BF16)
    with contextlib.ExitStack() as es2:
        old_es, P.es = P.es, es2
        io = P.sb("c_iota", [128, 128], F32)
        P.iota(io.all(), [[1, 128]], 0, -1)
        P.ts(c["ident"].all(), io.all(), 0.0, None, op0=ALU.is_equal)
        barrier(P)
        P.es = old_es
    c["ones"] = P.sb("c_ones", [128, 128], BF16)
    P.memset(c["ones"].all(), 1.0)
    return c


def load_fm_vec(P, name, dram_t, n):
    t = P.sb(name, [128, n], F32)
    P.dma("sp", t.all(), DV(dram_t, dap(dram_t, 0, [[1, 128], [128, n]])),
          nc_kwargs={"allow_slow_non_contiguous": True})
    return t


def rmsnorm_to_fm(P, c, x_v, hnT_v, g_fm, wk, nfeat=1024):
    nk = nfeat // 128
    P.act(wk["junk"].all(), x_v, AF.Square, accum=wk["ss"].all())
    P.act(wk["sd"].all(), wk["ss"].all(), AF.Sqrt, bias=wk["epsb"].all(), scale=1.0 / nfeat)
    P.recip(wk["rstd"].all(), wk["sd"].all())
    P.ts(wk["xn"].all(), x_v, wk["rstd"].all(), None, op0=ALU.mult)
    if DBG.get('no_tr'):
        return
    pst = wk["pst"]
    for k in range(nk):
        P.tr(pst[:, k * 128:(k + 1) * 128], wk["xn"][:, k * 128:(k + 1) * 128], c["ident"].all())
    if DBG.get('no_tt'):
        return
    if DBG.get('tt_copy'):
        P.copy(hnT_v, pst.all().re("p (k t) -> p k t", k=nk))
        return
    for k in range(nk):
        if k % 2 == 0:
            P.ts(hnT_v[:, k, :], pst[:, k * 128:(k + 1) * 128], g_fm[:, k:k + 1], None, op0=ALU.mult)
        else:
            P.act(hnT_v[:, k, :], pst[:, k * 128:(k + 1) * 128], AF.Copy, scale=g_fm[:, k:k + 1])


def make_norm_work(P, pfx, pst):
    wk = {}
    wk["junk"] = P.sb(pfx + "junk", [128, 1024], BF16)
    wk["ss"] = P.sb(pfx + "ss", [128, 1], F32)
    wk["sd"] = P.sb(pfx + "sd", [128, 1], F32)
    wk["rstd"] = P.sb(pfx + "rstd", [128, 1], F32)
    wk["xn"] = P.sb(pfx + "xn", [128, 1024], BF16)
    wk["epsb"] = P.sb(pfx + "epsb", [128, 1], F32)
    P.memset(wk["epsb"].all(), EPS)
    wk["pst"] = pst
    return wk


OD_Q, OD_K, OD_V, OD_G, OD_QI, OD_KI, OD_WI = 0, 1024, 1280, 1536, 2560, 3072, 3136


def build_p2b(nsb=TPC // 512, do_tiles=True, do_blocks=True):
    nc = bass.Bass("TRN2", target_bir_lowering=False)
    h1 = dram_in(nc, "h1", [TPC, D], F32)
    w_in = dram_in(nc, "w_in", [D, 3144], F32)
    ng = dram_in(nc, "ng", [D], F32)
    qg = dram_in(nc, "qg", [128], F32)
    kg = dram_in(nc, "kg", [128], F32)
    q_out = dram_out(nc, "q_out", [NBLK, 128, 8, 128], BF16)
    g_out = dram_out(nc, "g_out", [NBLK, 128, 8, 128], BF16)
    qi_out = dram_out(nc, "qi_out", [NBLK, 128, 4, 128], BF16)
    wi_out = dram_out(nc, "wi_out", [TPC, 8], F32)
    kT_out = dram_out(nc, "kT_out", [128, 2, TPC], BF16)
    v_out = dram_out(nc, "v_out", [TPC, 256], BF16)
    ki_out = dram_out(nc, "ki_out", [128, TPC], BF16)
    P = Prog(nc)
    with P.es:
        c = make_consts(P)
        W = P.sb("W", [128, 8, 3200], BF16)
        Wwi = P.sb("Wwi", [128, 8, 8], BF16)
        wbufs = [Buf("Wk%d" % k) for k in range(8)]
        stg = [P.sb("stg%d" % i, [128, 3136], F32) for i in range(2)]
        for k in range(8):
            wv = V(W.t[:, k, :], (wbufs[k],))
            if k % 2 == 0:
                P.dma("pool", wv[:, 0:3136], DV(w_in, w_in.ap()[k * 128:(k + 1) * 128, 0:3136]), primary=wbufs[k])
                P.dma("pool", wv[:, 3136:3200], DV(w_in, w_in.ap()[k * 128:(k + 1) * 128, OD_KI:OD_KI + 64]),
                      primary=wbufs[k])
            else:
                st_ = stg[(k // 2) % 2]
                P.dma("sp", st_.all(), DV(w_in, w_in.ap()[k * 128:(k + 1) * 128, 0:3136]))
                P.copy(wv[:, 0:3136], st_.all(), e="act")
                P.copy(wv[:, 3136:3200], st_[:, OD_KI:OD_KI + 64], e="act")
        P.dma("pool", Wwi.all(), DV(w_in, dap(w_in, OD_WI, [[3144, 128], [128 * 3144, 8], [1, 8]])))

        def Wk(k, c0, n):
            return V(W.t[:, k, c0:c0 + n], (wbufs[k],))

        g_fm = load_fm_vec(P, "g_fm", ng, 8)
        gq = P.sb("gq", [128, 1], F32)
        gk = P.sb("gk", [128, 1], F32)
        P.dma("sp", gq.all(), DV(qg, dap(qg, 0, [[1, 128], [1, 1]])))
        P.dma("sp", gk.all(), DV(kg, dap(kg, 0, [[1, 128], [1, 1]])))
        P.ts(gq.all(), gq.all(), 128.0 ** -0.5, None, op0=ALU.mult)
        pst = P.ps("pst", [128, 1024], BF16)
        wk = make_norm_work(P, "n_", pst)
        xb = [P.sb("xb%d" % i, [128, 1024], F32) for i in range(2)]
        hnT = P.sb("hnT", [128, 8, 512], BF16)
        ring = [P.ps("pr%d" % i, [128, 512]) for i in range(3)]
        ring2 = [P.ps("ps2_%d" % i, [128, 512]) for i in range(2)]
        ptm = P.ps("ptm", [128, 512])
        ptm2 = P.ps("ptm2", [128, 512])
        sq = [P.sb("sq%d" % i, [128, 512], BF16) for i in range(2)]
        sd = [P.sb("sdq%d" % i, [128, 512], F32) for i in range(2)]
        ob = [P.sb("ob%d" % i, [128, 512], BF16) for i in range(4)]
        vb = [P.sb("vb%d" % i, [128, 256], BF16) for i in range(2)]
        wib = [P.sb("wib%d" % i, [128, 8], F32) for i in range(2)]
        rr = [0, 0, 0, 0]
        outs = []

        def nxt(lst, idx):
            t = lst[rr[idx] % len(lst)]
            rr[idx] += 1
            return t

        blk_sz = 128 * 8 * 128
        for sb_ in range(nsb):
            for bl in range(4 if do_blocks else 0):
                t0 = sb_ * 512 + bl * 128
                x = xb[bl % 2]
                P.dma("sp", x.all(), DV(h1, h1.ap()[t0:t0 + 128, :]))
                rmsnorm_to_fm(P, c, x.all(), hnT[:, :, bl * 128:(bl + 1) * 128], g_fm, wk)
                if DBG.get('no_tm'):
                    continue
                for k in range(8):
                    P.mm(ptm[:, 0:256], hnT[:, k, bl * 128:(bl + 1) * 128], Wk(k, OD_V, 256), start=(k == 0), stop=(k == 7))
                v_sb = vb[bl % 2]
                P.copy(v_sb.all(), ptm[:, 0:256], e="act")
                if not DBG.get('no_vst'):
                    P.dma("pool", DV(v_out, v_out.ap()[t0:t0 + 128, :]), v_sb.all(), primary=v_sb.buf)
                    outs += [v_sb.buf]
                if DBG.get('no_wi'):
                    continue
                for k in range(8):
                    P.mm(ptm2[:, 0:8], hnT[:, k, bl * 128:(bl + 1) * 128], Wwi[:, k, :], start=(k == 0), stop=(k == 7))
                w_sb = wib[bl % 2]
                P.ts(w_sb.all(), ptm2[:, 0:8], (8.0 ** -0.5) * (64.0 ** -0.5), None, op0=ALU.mult)
                P.dma("pool", DV(wi_out, wi_out.ap()[t0:t0 + 128, :]), w_sb.all(), primary=w_sb.buf)
                outs += [w_sb.buf]
            tiles = [("q", h, OD_Q + h * 128) for h in range(8)] + [("k", g, OD_K + g * 128) for g in range(2)] + \
                    [("g", h, OD_G + h * 128) for h in range(8)] + [("qi", pr, OD_QI + pr * 128) for pr in range(4)] + \
                    [("ki", 0, OD_KI)]
            for (kind, idx, c0) in (tiles if do_tiles else []):
                pr_ = nxt(ring, 0)
                for k in range(8):
                    P.mm(pr_.all(), Wk(k, c0, 128), hnT[:, k, :], start=(k == 0), stop=(k == 7))
                o = nxt(ob, 1)
                if kind in ("q", "k"):
                    s = nxt(sq, 2)
                    P.act(s.all(), pr_.all(), AF.Square)
                    p2 = nxt(ring2, 3)
                    P.mm(p2.all(), c["ones"].all(), s.all())
                    d_ = sd[(rr[3] - 1) % 2]
                    P.act(d_.all(), p2.all(), AF.Ln, bias=wk["epsb"].all(), scale=1.0 / 128)
                    P.act(d_.all(), d_.all(), AF.Exp, scale=-0.5)
                    P.stt(o.all(), pr_.all(), (gq if kind == "q" else gk).all(), d_.all(), ALU.mult, ALU.mult)
                elif kind == "g":
                    P.act(o.all(), pr_.all(), AF.Silu)
                else:
                    P.copy(o.all(), pr_.all(), e="act")
                o3 = o.all().re("p (b t) -> p b t", b=4)
                if kind == "q":
                    dst = dap(q_out, sb_ * 4 * blk_sz + idx * 128, [[8 * 128, 128], [blk_sz, 4], [1, 128]])
                    P.dma("sp", DV(q_out, dst), o3, primary=o.buf)
                elif kind == "g":
                    dst = dap(g_out, sb_ * 4 * blk_sz + idx * 128, [[8 * 128, 128], [blk_sz, 4], [1, 128]])
                    P.dma("sp", DV(g_out, dst), o3, primary=o.buf)
                elif kind == "qi":
                    bs = 128 * 4 * 128
                    dst = dap(qi_out, sb_ * 4 * bs + idx * 128, [[4 * 128, 128], [bs, 4], [1, 128]])
                    P.dma("sp", DV(qi_out, dst), o3, primary=o.buf)
                elif kind == "k":
                    P.dma("sp", DV(kT_out, kT_out.ap()[:, idx, sb_ * 512:(sb_ + 1) * 512]), o.all(), primary=o.buf)
                else:
                    P.dma("sp", DV(ki_out, ki_out.ap()[:, sb_ * 512:(sb_ + 1) * 512]), o.all(), primary=o.buf)
                outs.append(o.buf)
        P.finish(outs)
    return nc, P


NPOS = 12
GLEN = NPOS * 128 + 128
BIS_ITERS = 21


def barrier(P):
    tgt = {P.esem[e]: P.cnt[e] for e in ("pe", "dve", "act", "pool") if P.cnt[e] > 0}
    for e in ("pe", "dve", "act", "pool", "sp"):
        for sn, v in tgt.items():
            if P.waited[e].get(sn, 0) < v:
                P.eng[e].wait_ge(P.semh[sn], v)
                P.waited[e][sn] = v


def build_p3(nblk=NBLK):
    nc = bass.Bass("TRN2", target_bir_lowering=False)
    q_blk = dram_in(nc, "q_blk", [NBLK, 128, 8, 128], BF16)
    g_blk = dram_in(nc, "g_blk", [NBLK, 128, 8, 128], BF16)
    qi_blk = dram_in(nc, "qi_blk", [NBLK, 128, 4, 128], BF16)
    wi_blk = dram_in(nc, "wi_blk", [NBLK, 128, 8], F32)
    h1_blk = dram_in(nc, "h1_blk", [NBLK, 128, D], F32)
    kT_all = dram_in(nc, "kT_all", [128, 2, SEQ], BF16)
    v_all = dram_in(nc, "v_all", [SEQ, 256], BF16)
    ki_all = dram_in(nc, "ki_all", [128, SEQ], BF16)
    cmask = dram_in(nc, "cmask", [128, 512], BF16)
    oh = dram_in(nc, "oh", [32, GLEN], F32)
    rel_bias = dram_in(nc, "rel_bias", [32, 8], F32)
    w_out = dram_in(nc, "w_out", [D, D], F32)
    y = dram_out(nc, "y", [NBLK, 128, D], F32)
    gtab = nc.dram_tensor("gtab", [8, GLEN], F32, kind="Internal")
    gtab_buf = Buf("gtab")
    P = Prog(nc)
    with P.es:
        c = make_consts(P)
        ident4 = P.sb("ident4", [128, 512], BF16)
        for h in range(4):
            P.copy(ident4[:, h * 128:(h + 1) * 128], c["ident"].all())
        kT = P.sb("kT", [128, 2, SEQ], BF16)
        Vs = P.sb("Vs", [128, 64, 256], BF16)
        ki = P.sb("ki", [128, SEQ], BF16)
        Wo = P.sb("Wo", [128, 8, D], BF16)
        cm = P.sb("cm", [128, 512], BF16)
        biasT = P.sb("biasT", [128, NPOS, 1024], BF16)
        for g in range(2):
            P.dma("sp", kT[:, g, :], DV(kT_all, kT_all.ap()[:, g, :]))
        for part in range(4):
            P.dma("sp", Vs[:, part * 16:(part + 1) * 16, :],
                  DV(v_all, dap(v_all, part * 16 * 128 * 256, [[256, 128], [128 * 256, 16], [1, 256]])))
        P.dma("sp", ki.all(), DV(ki_all))
        P.dma("sp", cm.all(), DV(cmask))
        for k in range(8):
            P.dma("pool", Wo[:, k, :], DV(w_out, w_out.ap()[k * 128:(k + 1) * 128, :]))
        ring = [P.ps("ring%d" % i, [128, 512]) for i in range(4)]
        num = [P.ps("num%d" % g, [128, 512]) for g in range(2)]
        den = [P.ps("den%d" % g, [128, 512]) for g in range(2)]
        rr = {"ring": 0}

        def nring():
            for _ in range(4):
                t = ring[rr["ring"] % 4]
                rr["ring"] += 1
                if t.buf.lw is None or t.buf.rd:
                    return t
            raise RuntimeError("PSUM ring exhausted: every bank holds unread results")

        with contextlib.ExitStack() as es2:
            old_es, P.es = P.es, es2
            Jf = P.sb("Jf", [128, 128], F32)
            tmpi = P.sb("tmpi", [128, 128], F32)
            P.iota(tmpi.all(), [[1, 128]], -127, 1)
            P.ts(Jf.all(), tmpi.all(), 0.0, None, op0=ALU.is_equal)
            rb = P.sb("rb", [32, 8], F32)
            rb31 = P.sb("rb31", [32, 8], F32)
            ohs = P.sb("ohs", [32, GLEN], F32)
            gsb = P.sb("gsb", [8, GLEN], F32)
            hk = [P.sb("hk%d" % i, [128, 8, 128], F32) for i in range(2)]
            P.dma("sp", rb.all(), DV(rel_bias))
            P.dma("sp", rb31.all(), DV(rel_bias, dap(rel_bias, 31 * 8, [[0, 32], [1, 8]])))
            P.dma("sp", ohs.all(), DV(oh))
            P.tt(rb.all(), rb.all(), rb31.all(), ALU.subtract)
            for ch in range((GLEN + 511) // 512):
                n = min(512, GLEN - ch * 512)
                pr_ = nring()
                P.mm(pr_[0:8, 0:n], rb.all(), ohs[:, ch * 512:ch * 512 + n])
                P.copy(gsb[:, ch * 512:ch * 512 + n], pr_[0:8, 0:n])
            P.dma("sp", DV(gtab, buf=gtab_buf), gsb.all(), primary=gtab_buf)
            for p in range(NPOS):
                hkt = hk[p % 2]
                P.dma("sp", hkt.all(), DV(gtab, dap(gtab, p * 128, [[1, 128], [GLEN, 8], [1, 128]]), buf=gtab_buf))
                for half in range(2):
                    pr_ = nring()
                    P.mm(pr_.all(), Jf.all(), hkt[:, half * 4:(half + 1) * 4, :])
                    P.copy(biasT[:, p, half * 512:(half + 1) * 512], pr_.all(), e=("act" if half else "dve"))
            barrier(P)
            P.es = old_es

        score = P.sb("score", [128, SEQ], F32)
        sc_bufs = [Buf("sc%d" % i) for i in range(SEQ // 512)]

        def scv(a, b):
            return V(score.t[:, a:b], tuple(sc_bufs[a // 512:(b + 511) // 512]))

        JW = SEQ
        nmall = [P.sb("nmall%d" % i, [128, SEQ], BF16) for i in range(2)]
        qT = [P.sb("qT%d" % i, [128, 8, 128], BF16) for i in range(2)]
        gT = [P.sb("gT%d" % i, [128, 8, 128], BF16) for i in range(1)]
        PmSum = [P.sb("PmSum%d" % g, [128, 512], F32) for g in range(2)]
        ones_f = P.sb("ones_f", [128, 128], F32)
        P.memset(ones_f.all(), 1.0)
        qiT = [P.sb("qiT%d" % i, [128, 4, 128], BF16) for i in range(2)]
        wi = [P.sb("wi%d" % i, [128, 8], F32) for i in range(2)]
        h1b = [P.sb("h1b%d" % i, [128, 512], F32) for i in range(1)]
        Pm = [P.sb("Pm%d" % i, [128, 512], BF16) for i in range(2)]
        rd = [P.sb("rd%d" % i, [128, 512], F32) for i in range(1)]
        catT = P.sb("catT", [128, 8, 128], BF16)
        small = {n: P.sb("b_" + n, [128, 1], F32) for n in ("lo", "hi", "w", "mid", "nmid", "cnt", "cnt2", "sel")}
        nm_j = [(Buf("jD%d" % i), Buf("jA%d" % i)) for i in range(2)]
        pow2 = P.sb("pow2", [128, BIS_ITERS], F32)
        wall = P.sb("wall", [128, BIS_ITERS], F32)
        for k in range(BIS_ITERS):
            P.memset(pow2[:, k:k + 1], 2.0 ** -(k + 1))
        cn = {"Pm": 0}
        outs = []

        def stage1(i):
            steps = []
            nkt = 4 * i + 4
            nk = nkt * 128
            q_, qi_, wi_ = qT[i % 2], qiT[i % 2], wi[i % 2]
            S = small
            nm = nmall[i % 2]
            jD, jA = nm_j[i % 2]
            nm8 = nm.t[:].bitcast(I8)
            nD = ((nk // 2 + 127) // 128) * 128
            nA = nk - nD

            def loads():
                P.dma("sp", qi_.all(), DV(qi_blk, qi_blk.ap()[i]))
                P.dma("sp", wi_.all(), DV(wi_blk, wi_blk.ap()[i]))
                P.dma("sp", q_.all(), DV(q_blk, q_blk.ap()[i]))
            steps.append(loads)

            def idx(chunks, h0):
                for h in range(h0, h0 + 4):
                    for c5 in chunks:
                        sc = scv(c5 * 512, (c5 + 1) * 512)
                        pI = nring()
                        lo_p = (h % 2) * 64
                        P.mm(pI.all(), qi_[lo_p:lo_p + 64, h // 2, :], ki[lo_p:lo_p + 64, c5 * 512:(c5 + 1) * 512])
                        P.act(pI.all(), pI.all(), AF.Relu)
                        if h == 0:
                            P.ts(sc, pI.all(), wi_[:, 0:1], None, op0=ALU.mult)
                        else:
                            P.stt(sc, pI.all(), wi_[:, h:h + 1], sc, ALU.mult, ALU.add)
            for c5 in range(0, (i + 1) if not DBG.get('no_idx') else 0, 2):
                chunks = [c5] + ([c5 + 1] if c5 + 1 <= i else [])
                for h0 in (0, 4):
                    steps.append(lambda chunks=chunks, h0=h0: idx(chunks, h0))

            def bis_init():
                P.reduce(S["lo"].all(), scv(0, nk), ALU.min)
                P.tt(scv(nk - 512, nk), scv(nk - 512, nk), cm.all(), ALU.add)
                P.reduce(S["hi"].all(), scv(0, nk), ALU.max)
                P.ts(S["w"].all(), S["hi"].all(), 1.0, S["lo"].all(), op0=ALU.add, op1=ALU.subtract)
                P.ts(wall.all(), pow2.all(), S["w"].all(), None, op0=ALU.mult)
                P.tt(S["mid"].all(), S["lo"].all(), wall[:, 0:1], ALU.add)
            steps.append(bis_init)

            def bis_iter(k):
                P.ts(V(nm8[:, 0:nk], (jD,)), scv(0, nk), S["mid"].all(), 0.0, op0=ALU.is_ge, op1=ALU.add,
                     accum=S["cnt"].all())
                P.stt(S["sel"].all(), S["cnt"].all(), 255.5, wall[:, k:k + 1], ALU.is_ge, ALU.mult)
                if k + 1 < BIS_ITERS:
                    P.ts(S["mid"].all(), S["sel"].all(), S["lo"].all(), wall[:, k + 1:k + 2], op0=ALU.add, op1=ALU.add)
                P.tt(S["lo"].all(), S["lo"].all(), S["sel"].all(), ALU.add)
            for it in range(BIS_ITERS):
                steps.append(lambda it=it: bis_iter(it))

            def negmask(a0, a1):
                P.ts(V(nm.t[:, a0:a1], (nm.buf, jD, jA)), scv(a0, a1), S["lo"].all(), NEG, op0=ALU.is_lt, op1=ALU.mult)
            for a0 in range(0, nk, 2048):
                steps.append(lambda a0=a0: negmask(a0, min(nk, a0 + 2048)))
            return steps

        def stage2(i):
            steps = []
            nkt = 4 * i + 4
            q_, g_, hb, nm = qT[i % 2], gT[0], h1b[0], nmall[i % 2]

            def tile_(m):
                pos = nkt - 1 - m
                for g in range(2):
                    pL = nring()
                    P.mm(pL.all(), kT[:, g, m * 128:(m + 1) * 128], q_[:, 4 * g:4 * g + 4, :], start=True, stop=False)
                    P.mm(pL.all(), V(nm.t[:, m * 128:(m + 1) * 128], (nm.buf,) + nm_j[i % 2]), ident4.all(), start=False,
                         stop=(pos >= NPOS))
                    if pos < NPOS:
                        P.mm(pL.all(), c["ident"].all(), biasT[:, pos, g * 512:(g + 1) * 512], start=False, stop=True)
                    pm = Pm[cn["Pm"] % 2]
                    cn["Pm"] += 1
                    P.act(pm.all(), pL.all(), AF.Exp)
                    P.mm(num[g].all(), Vs[:, m, g * 128:(g + 1) * 128], pm.all(), start=(m == 0), stop=(m == nkt - 1))
                    if m == 0:
                        P.copy(PmSum[g].all(), pm.all(), e="pool")
                    else:
                        P.tt(PmSum[g].all(), PmSum[g].all(), pm.all(), ALU.add, e="pool")
            for m in range(nkt if not DBG.get('no_att') else 1):
                steps.append(lambda m=m: tile_(m))

            def epi():
                P.dma("sp", g_.all(), DV(g_blk, g_blk.ap()[i]))
                for g in range(2):
                    P.mm(den[g].all(), ones_f.all(), PmSum[g].all())
                for g in range(2):
                    r_ = rd[0]
                    P.act(r_.all(), den[g].all(), AF.Ln)
                    P.act(r_.all(), r_.all(), AF.Exp, scale=-1.0)
                    tmp = nring()
                    P.tt(tmp.all(), num[g].all(), r_.all(), ALU.mult)
                    P.tt(catT[:, 4 * g:4 * g + 4, :], tmp.all().re("p (h t) -> p h t", h=4), g_[:, 4 * g:4 * g + 4, :], ALU.mult)
                for half in range(2):
                    cs_ = slice(half * 512, (half + 1) * 512)
                    P.dma("sp", hb.all(), DV(h1_blk, h1_blk.ap()[i][:, cs_]))
                    po = nring()
                    for h in range(8):
                        P.mm(po.all(), catT[:, h, :], Wo[:, h, cs_], start=(h == 0), stop=(h == 7))
                    P.tt(hb.all(), po.all(), hb.all(), ALU.add)
                    P.dma("pool", DV(y, y.ap()[i][:, cs_]), hb.all(), primary=hb.buf)
                outs.append(hb.buf)
            steps.append(epi)
            return steps

        def interleave(sa, sb):
            na, nb = len(sa), len(sb)
            ia = ib = 0
            while ia < na or ib < nb:
                if ib >= nb or (ia < na and ia * nb <= ib * na):
                    sa[ia]()
                    ia += 1
                else:
                    sb[ib]()
                    ib += 1

        order = list(range(nblk - 1, -1, -1))
        for st_ in stage1(order[0]):
            st_()
        for oi, i in enumerate(order):
            s2 = stage2(i)
            s1 = stage1(order[oi + 1]) if oi + 1 < nblk else []
            interleave(s1, s2)
        P.finish(outs)
    return nc, P


def t5_bucket_np(d):
    d = np.maximum(d, 0)
    nf = np.maximum(d, 1).astype(np.float32)
    large = 16 + (np.log(nf / np.float32(16)) / np.float32(math.log(1024 / 16)) * np.float32(16)).astype(np.int32)
    large = np.minimum(large, 31)
    return np.where(d < 16, d, large)


def p3_consts(j):
    e = np.arange(GLEN)
    dist = (j - 3) * 128 + e - 127
    b = t5_bucket_np(dist)
    ohm = np.zeros((32, GLEN), np.float32)
    ohm[b, e] = 1.0
    t = np.arange(128)[:, None]
    r = np.arange(512)[None, :]
    s_rel = r - j * 128
    cmask = np.where(s_rel <= t, 0.0, -1e30).astype(np.float32).astype(ml_dtypes.bfloat16)
    return ohm, cmask


EV_Q, EV_K, EV_V, EV_GA, EV_U, EV_GB = 0, 512, 640, 768, 1280, 1792
WQ, WKD, WVD, WGA, WU, WGB = 0, 512, 768, 1024, 1536, 2048
TWO_PI = 2.0 * math.pi
CW1 = 6.28125
CW2 = TWO_PI - CW1


def sincos(P, wk, ph, s_out, c_out):
    ki, kf, r, m = wk["ki"], wk["kf"], wk["r"], wk["m"]
    P.ts(ki, ph, 1.0 / TWO_PI, None, op0=ALU.mult)
    P.copy(kf, ki)
    P.stt(r, kf, -CW1, ph, ALU.mult, ALU.add)
    P.stt(r, kf, -CW2, r, ALU.mult, ALU.add)
    for (outv, shift) in ((s_out, 0.0), (c_out, math.pi / 2)):
        if shift:
            P.ts(r, r, shift, None, op0=ALU.add)
        P.ts(m, r, math.pi, -TWO_PI, op0=ALU.is_gt, op1=ALU.mult)
        P.tt(r, r, m, ALU.add)
        P.ts(m, r, -math.pi, TWO_PI, op0=ALU.is_lt, op1=ALU.mult)
        P.tt(r, r, m, ALU.add)
        P.ts(r, r, math.pi, -math.pi, op0=ALU.min, op1=ALU.max)
        P.act(outv, r, AF.Sin)


def cmul(P, o_re, o_im, a_re, a_im, b_re, b_im, t1, t2):
    P.tt(t1, a_re, b_re, ALU.mult)
    P.tt(t2, a_im, b_im, ALU.mult)
    P.tt(o_re, t1, t2, ALU.subtract)
    P.tt(t1, a_re, b_im, ALU.mult)
    P.tt(t2, a_im, b_re, ALU.mult)
    P.tt(o_im, t1, t2, ALU.add)


def interleave_steps(sa, sb):
    na, nb = len(sa), len(sb)
    ia = ib = 0
    while ia < na or ib < nb:
        if ib >= nb or (ia < na and ia * nb <= ib * na):
            sa[ia]()
            ia += 1
        else:
            sb[ib]()
            ib += 1


def build_p2(mode="full", nsb=TPC // 512):
    full = mode == "full"
    nc = bass.Bass("TRN2", target_bir_lowering=False)
    x_own = dram_in(nc, "x_own", [TPC, D], F32)
    w_in = dram_in(nc, "w_in", [D, 2304], F32)
    ng_fm_d = dram_in(nc, "ng_fm", [128, 8], F32)
    a_re_f = dram_in(nc, "a_re_f", [2048], F32)
    a_im_f = dram_in(nc, "a_im_f", [2048], F32)
    ldt_f = dram_in(nc, "ldt_f", [2048], F32)
    a_re_s = dram_in(nc, "a_re_s", [128, 16], F32)
    a_im_s = dram_in(nc, "a_im_s", [128, 16], F32)
    ldt_s = dram_in(nc, "ldt_s", [128, 16], F32)
    b_blk_re = dram_in(nc, "b_blk_re", [128, 4, 128], F32)
    b_blk_im = dram_in(nc, "b_blk_im", [128, 4, 128], F32)
    if full:
        x_halo = dram_in(nc, "x_halo", [128, D], F32)
        firstneg = dram_in(nc, "firstneg", [128, 1], F32)
        w_out = dram_in(nc, "w_out", [D, D], F32)
        glu_w = dram_in(nc, "glu_w", [512, 1024], F32)
        qg2 = dram_in(nc, "qg2", [128, 1], F32)
        kg2 = dram_in(nc, "kg2", [128, 1], F32)
        sinks_row = dram_in(nc, "sinks_row", [128, 8], F32)
        rel_bias = dram_in(nc, "rel_bias", [32, 8], F32)
        oh_swa = dram_in(nc, "oh_swa", [32, 384], F32)
        neg_swa = dram_in(nc, "neg_swa", [8, 384], F32)
        c_blk_re = dram_in(nc, "c_blk_re", [128, 16, 128], F32)
        c_blk_im = dram_in(nc, "c_blk_im", [128, 16, 128], F32)
        d_fm_d = dram_in(nc, "d_fm", [128, 4], F32)
        glu_b_fm_d = dram_in(nc, "glu_b_fm", [128, 8], F32)
        eprev = dram_in(nc, "eprev", [3, 128, 32], F32)
        h1_out = dram_out(nc, "h1", [TPC, D], F32)
        gtab = nc.dram_tensor("gtab0", [8, 384], F32, kind="Internal")
        gtab_buf = Buf("gtab0")
    else:
        eloc = dram_out(nc, "eloc", [128, 32], F32)
    P = Prog(nc)
    with P.es:
        c = make_consts(P)
        wu0 = WU if full else 0
        g_fm = P.sb("g_fm", [128, 8], F32)
        P.dma("sp", g_fm.all(), DV(ng_fm_d))
        Wr = P.sb("Wr", [128, 16, 128], BF16)
        Ws = P.sb("Ws", [128, 16, 128], BF16)
        BBp = P.sb("BBp", [128, 4, 512], BF16)
        P.memset(BBp.all(), 0.0)
        A128r = P.sb("A128r", [128, 16], F32)
        A128i = P.sb("A128i", [128, 16], F32)
        A127r = P.sb("A127r", [128, 16], F32)
        A127i = P.sb("A127i", [128, 16], F32)
        A1r = P.sb("A1r", [128, 16], F32)
        A1i = P.sb("A1i", [128, 16], F32)
        Zr = P.sb("Zr", [128, 16], F32)
        Zi = P.sb("Zi", [128, 16], F32)
        if full:
            Vr = P.sb("Vr", [128, 16, 128], F32)
            Vi = P.sb("Vi", [128, 16, 128], F32)
            d_fm = P.sb("d_fm_s", [128, 4], F32)
            glu_b = P.sb("glu_b", [128, 8], F32)
            BT = [P.sb("BT%d" % i, [128, 1024], F32) for i in range(3)]
            esink = P.sb("esink", [128, 8], F32)
            gq = P.sb("gq", [128, 1], F32)
            gk = P.sb("gk", [128, 1], F32)
            ones2 = P.sb("ones2", [128, 128], BF16)
            P.dma("sp", d_fm.all(), DV(d_fm_d))
            P.dma("sp", glu_b.all(), DV(glu_b_fm_d))
            P.dma("sp", gq.all(), DV(qg2))
            P.dma("sp", gk.all(), DV(kg2))
            P.ts(gq.all(), gq.all(), 64.0 ** -0.5, None, op0=ALU.mult)
            P.dma("sp", esink.all(), DV(sinks_row))
            P.act(esink.all(), esink.all(), AF.Exp)
            P.memset(ones2.all(), 0.0)
            P.memset(ones2[0:64, 0:64], 1.0)
            P.memset(ones2[64:128, 64:128], 1.0)

        NR = 6
        ring = [P.ps("ring%d" % i, [128, 512]) for i in range(NR)]
        py_bank = P.ps("py_bank", [128, 512]) if full else None
        pst = P.ps("pst", [128, 1024], BF16)
        rr = {"ring": 0}

        def nring():
            for _ in range(NR):
                t = ring[rr["ring"] % NR]
                rr["ring"] += 1
                if t.buf.lw is None or t.buf.rd:
                    return t
            raise RuntimeError("PSUM ring exhausted: every bank holds unread results")

        jf = P.sb("jf", [128, 1], F32)
        P.iota(jf.all(), [[1, 1]], 0, 1)
        io_i = P.sb("io_i", [128, 128], F32)
        P.iota(io_i.all(), [[1, 128]], 0, 0)
        tri_f = P.sb("tri_f", [128, 128], F32)
        P.iota(tri_f.all(), [[1, 128]], 0, -1)
        tmpi_e = P.sb("tmpi_e", [128, 128], F32)
        P.iota(tmpi_e.all(), [[1, 128]], -127, 1)
        ncolW = 2560 if full else 512
        W = P.sb("W", [128, 8, ncolW], BF16)
        wbufs = [Buf("Wk%d" % k) for k in range(8)]

        def Wk(k, c0, n):
            return V(W.t[:, k, c0:c0 + n], (wbufs[k],))

        if full:
            Cre = P.sb("Cre", [128, 16, 128], BF16)
            Cim = P.sb("Cim", [128, 16, 128], BF16)
            Wo = P.sb("Wo", [128, 8, D], BF16)
            Wg = P.sb("Wg", [128, 4, 1024], BF16)

        def issue_weight_loads():
            for k in range(8):
                rows = slice(k * 128, (k + 1) * 128)

                def ld(dst0, n, src0):
                    P.dma("pool", V(W.t[:, k, dst0:dst0 + n], (wbufs[k],)), DV(w_in, w_in.ap()[rows, src0:src0 + n]),
                          primary=wbufs[k])
                if full:
                    ld(WU, 512, EV_U)
                    ld(WQ, 512, EV_Q)
                    for g in range(2):
                        for dup in range(2):
                            ld(WKD + g * 128 + dup * 64, 64, EV_K + g * 64)
                            ld(WVD + g * 128 + dup * 64, 64, EV_V + g * 64)
                    ld(WGA, 512, EV_GA)
                    ld(WGB, 512, EV_GB)
                else:
                    ld(0, 512, EV_U)
            if full:
                P.dma("pool", Cre.all(), DV(c_blk_re))
                P.dma("pool", Cim.all(), DV(c_blk_im))
                for k in range(4):
                    P.dma("pool", Wg[:, k, :], DV(glu_w, glu_w.ap()[k * 128:(k + 1) * 128, :]))
                for k in range(8):
                    P.dma("pool", Wo[:, k, :], DV(w_out, w_out.ap()[k * 128:(k + 1) * 128, :]))
                P.ts(Cim.all(), Cim.all(), -1.0, None, op0=ALU.mult, e="pool")
        tri = P.sb("tri", [128, 128], BF16)
        P.ts(tri.all(), tri_f.all(), 0.0, None, op0=ALU.is_ge)
        with contextlib.ExitStack() as es2:
            old_es, P.es = P.es, es2
            N = 2048
            T = {n: P.sb("t_" + n, [128, N], F32) for n in ("ard", "ang", "a", "b", "mag", "kf", "r", "m", "s", "c")}
            Tki = P.sb("t_ki", [128, N], I32)

            def scw(n):
                return {"ki": Tki[:, 0:n], "kf": T["kf"][:, 0:n], "r": T["r"][:, 0:n], "m": T["m"][:, 0:n]}

            P.dma("sp", T["a"].all(), DV(ldt_f, dap(ldt_f, 0, [[0, 128], [1, N]])))
            P.act(T["a"].all(), T["a"].all(), AF.Exp)
            P.dma("sp", T["ard"].all(), DV(a_re_f, dap(a_re_f, 0, [[0, 128], [1, N]])))
            P.dma("sp", T["ang"].all(), DV(a_im_f, dap(a_im_f, 0, [[0, 128], [1, N]])))
            issue_weight_loads()
            P.tt(T["ard"].all(), T["ard"].all(), T["a"].all(), ALU.mult)
            P.tt(T["ang"].all(), T["ang"].all(), T["a"].all(), ALU.mult)
            P.ts(T["b"].all(), T["ard"].all(), jf.all(), -1.0, op0=ALU.mult, op1=ALU.mult)
            P.act(T["mag"].all(), T["b"].all(), AF.Exp)
            P.ts(T["b"].all(), T["ang"].all(), jf.all(), None, op0=ALU.mult)
            sincos(P, scw(N), T["b"].all(), T["s"].all(), T["c"].all())
            P.tt(Wr.all().re("p a b -> p (a b)"), T["mag"].all(), T["c"].all(), ALU.mult)
            P.tt(Ws.all().re("p a b -> p (a b)"), T["mag"].all(), T["s"].all(), ALU.mult)
            Fr, Fi = T["ard"], T["ang"]
            P.act(T["mag"].all(), T["ard"].all(), AF.Exp)
            sincos(P, scw(N), T["ang"].all(), T["s"].all(), T["c"].all())
            are_row, aim_row = T["kf"], T["r"]
            P.dma("sp", are_row.all(), DV(a_re_f, dap(a_re_f, 0, [[0, 128], [1, N]])))
            P.dma("sp", aim_row.all(), DV(a_im_f, dap(a_im_f, 0, [[0, 128], [1, N]])))
            P.tt(T["c"].all(), T["mag"].all(), T["c"].all(), ALU.mult)
            P.tt(T["s"].all(), T["mag"].all(), T["s"].all(), ALU.mult)
            P.ts(T["c"].all(), T["c"].all(), -1.0, None, op0=ALU.add)
            P.tt(T["a"].all(), are_row.all(), are_row.all(), ALU.mult)
            P.tt(T["b"].all(), aim_row.all(), aim_row.all(), ALU.mult)
            P.tt(T["a"].all(), T["a"].all(), T["b"].all(), ALU.add)
            P.recip(T["a"].all(), T["a"].all())
            P.tt(T["b"].all(), T["c"].all(), are_row.all(), ALU.mult)
            P.tt(T["m"].all(), T["s"].all(), aim_row.all(), ALU.mult)
            P.tt(T["b"].all(), T["b"].all(), T["m"].all(), ALU.add)
            P.tt(Fr.all(), T["b"].all(), T["a"].all(), ALU.mult)
            P.tt(T["b"].all(), T["s"].all(), are_row.all(), ALU.mult)
            P.tt(T["m"].all(), T["c"].all(), aim_row.all(), ALU.mult)
            P.tt(T["b"].all(), T["b"].all(), T["m"].all(), ALU.subtract)
            P.tt(Fi.all(), T["b"].all(), T["a"].all(), ALU.mult)
            bre = T["s"].all()[:, 0:512].re("p (q s) -> p q s", q=4)
            bim = T["c"].all()[:, 0:512].re("p (q s) -> p q s", q=4)
            P.dma("sp", bre, DV(b_blk_re))
            P.dma("sp", bim, DV(b_blk_im))
            t1 = T["m"].all()[:, 0:128]
            t2 = T["m"].all()[:, 128:256]
            for st4 in range(4):
                ps_ = slice(32 * st4, 32 * st4 + 32)
                for q in range(4):
                    st = 4 * q + st4
                    fr = Fr[ps_, st * 128:(st + 1) * 128]
                    fi = Fi[ps_, st * 128:(st + 1) * 128]
                    P.tt(t1[ps_, :], bre[ps_, q, :], fr, ALU.mult)
                    P.tt(t2[ps_, :], bim[ps_, q, :], fi, ALU.mult)
                    co = (st4 % 2) * 256
                    P.tt(BBp[ps_, q, co:co + 128], t1[ps_, :], t2[ps_, :], ALU.subtract)
                    P.tt(t1[ps_, :], bre[ps_, q, :], fi, ALU.mult)
                    P.tt(t2[ps_, :], bim[ps_, q, :], fr, ALU.mult)
                    P.tt(BBp[ps_, q, co + 128:co + 256], t1[ps_, :], t2[ps_, :], ALU.add)
            sp_ = {n: P.sb("sp_" + n, [128, 16], F32) for n in ("ard", "ang", "dt", "a", "b", "mag", "kf", "r", "m", "s", "c")}
            spki = P.sb("sp_ki", [128, 16], I32)
            P.dma("sp", sp_["dt"].all(), DV(ldt_s))
            P.act(sp_["dt"].all(), sp_["dt"].all(), AF.Exp)
            P.dma("sp", sp_["ard"].all(), DV(a_re_s))
            P.dma("sp", sp_["ang"].all(), DV(a_im_s))
            P.tt(sp_["ard"].all(), sp_["ard"].all(), sp_["dt"].all(), ALU.mult)
            P.tt(sp_["ang"].all(), sp_["ang"].all(), sp_["dt"].all(), ALU.mult)
            spw = {"ki": spki.all(), "kf": sp_["kf"].all(), "r": sp_["r"].all(), "m": sp_["m"].all()}
            for (mult_, orr, oii) in ((128.0, A128r, A128i), (127.0, A127r, A127i), (1.0, A1r, A1i)):
                P.ts(sp_["a"].all(), sp_["ard"].all(), mult_, None, op0=ALU.mult)
                P.act(sp_["mag"].all(), sp_["a"].all(), AF.Exp)
                P.ts(sp_["b"].all(), sp_["ang"].all(), mult_, None, op0=ALU.mult)
                sincos(P, spw, sp_["b"].all(), sp_["s"].all(), sp_["c"].all())
                P.tt(orr.all(), sp_["mag"].all(), sp_["c"].all(), ALU.mult)
                P.tt(oii.all(), sp_["mag"].all(), sp_["s"].all(), ALU.mult)
            if full:
                for st in range(16):
                    P.act(T["mag"][:, st * 128:(st + 1) * 128], io_i.all(), AF.Exp, scale=sp_["ard"][:, st:st + 1])
                    P.ts(T["b"][:, st * 128:(st + 1) * 128], io_i.all(), sp_["ang"][:, st:st + 1], None, op0=ALU.mult)
                sincos(P, scw(N), T["b"].all(), T["s"].all(), T["c"].all())
                P.tt(Vr.all().re("p a b -> p (a b)"), T["mag"].all(), T["c"].all(), ALU.mult)
                P.tt(Vi.all().re("p a b -> p (a b)"), T["mag"].all(), T["s"].all(), ALU.mult)
                ep = [P.sb("ep%d" % i, [128, 32], F32) for i in range(3)]
                for i in range(3):
                    P.dma("sp", ep[i].all(), DV(eprev, eprev.ap()[i]))
                pa = [P.sb("pa%d" % i, [128, 16], F32) for i in range(8)]
                cur_r, cur_i = A128r, A128i
                for sqi in range(4):
                    nr, ni = pa[2 * (sqi % 2)], pa[2 * (sqi % 2) + 1]
                    cmul(P, nr.all(), ni.all(), cur_r.all(), cur_i.all(), cur_r.all(), cur_i.all(), pa[4].all(), pa[5].all())
                    cur_r, cur_i = nr, ni
                ar, ai = pa[6], pa[7]
                xr = P.sb("xsr", [128, 16], F32)
                xi = P.sb("xsi", [128, 16], F32)
                cmul(P, ar.all(), ai.all(), cur_r.all(), cur_i.all(), ep[2][:, 0:16], ep[2][:, 16:32], pa[4].all(), pa[5].all())
                P.tt(ar.all(), ar.all(), ep[1][:, 0:16], ALU.add)
                P.tt(ai.all(), ai.all(), ep[1][:, 16:32], ALU.add)
                cmul(P, xr.all(), xi.all(), cur_r.all(), cur_i.all(), ar.all(), ai.all(), pa[4].all(), pa[5].all())
                P.tt(xr.all(), xr.all(), ep[0][:, 0:16], ALU.add)
                P.tt(xi.all(), xi.all(), ep[0][:, 16:32], ALU.add)
                cmul(P, Zr.all(), Zi.all(), A1r.all(), A1i.all(), xr.all(), xi.all(), pa[4].all(), pa[5].all())
                def swa_bias_setup():
                    Jf = P.sb("Jf", [128, 128], F32)
                    P.ts(Jf.all(), tmpi_e.all(), 0.0, None, op0=ALU.is_equal)
                    rb = P.sb("rb", [32, 8], F32)
                    ohs = P.sb("ohs", [32, 384], F32)
                    ngs = P.sb("ngs", [8, 384], F32)
                    P.dma("sp", ngs.all(), DV(neg_swa))
                    gsb = P.sb("gsb", [8, 384], F32)
                    fneg = P.sb("fneg", [128, 1], F32)
                    hk = [P.sb("hk%d" % i, [128, 8, 128], F32) for i in range(2)]
                    P.dma("sp", rb.all(), DV(rel_bias))
                    P.dma("sp", ohs.all(), DV(oh_swa))
                    P.dma("sp", fneg.all(), DV(firstneg))
                    pr_ = nring()
                    P.mm(pr_[0:8, 0:384], rb.all(), ohs.all())
                    P.tt(gsb.all(), pr_[0:8, 0:384], ngs.all(), ALU.add)
                    P.dma("sp", DV(gtab, buf=gtab_buf), gsb.all(), primary=gtab_buf)
                    for kt in range(2):
                        hkt = hk[kt]
                        off = 128 if kt == 0 else 0
                        P.dma("sp", hkt.all(), DV(gtab, dap(gtab, off, [[1, 128], [384, 8], [1, 128]]), buf=gtab_buf))
                        for half in range(2):
                            pr_ = nring()
                            P.mm(pr_.all(), Jf.all(), hkt[:, half * 4:(half + 1) * 4, :])
                            P.copy(BT[kt][:, half * 512:(half + 1) * 512], pr_.all())
                    P.ts(BT[2].all(), BT[0].all(), fneg.all(), None, op0=ALU.add)
            else:
                P.memset(Zr.all(), 0.0)
                P.memset(Zi.all(), 0.0)
            barrier(P)
            P.es = old_es
        if full and not DBG.get('no_bias'):
            with contextlib.ExitStack() as es3:
                old_es, P.es = P.es, es3
                swa_bias_setup()
                barrier(P)
                P.es = old_es

        wk = make_norm_work(P, "n_", pst)
        nxb = 4 if full else 2
        xb = [P.sb("xb%d" % i, [128, D], F32) for i in range(nxb)]
        hnT = P.sb("hnT", [128, 8, 512], BF16)
        uT = P.sb("uT", [128, 4, 512], BF16)
        tm = [P.sb("tm%d" % i, [128, 512], F32) for i in range(4)]
        tmB = [P.sb("tmB%d" % i, [128, 512], F32) for i in range(4)]
        tmsets = [tm, tm] if full else [tm, tmB]
        vre = [P.sb("vre%d" % i, [128, 4, 128], BF16) for i in range(2)]
        vim = [P.sb("vim%d" % i, [128, 4, 128], BF16) for i in range(2)]
        cnt = {"x": 0, "v": 0, "o": 0}
        outs = []
        if full:
            qT = P.sb("qTm", [128, 8, 512], BF16)
            P.memset(qT.all(), 0.0)
            kTd = P.sb("kTd", [128, 2, 640], BF16)
            Vd = P.sb("Vd", [128, 5, 256], BF16)
            gaT = P.sb("gaT", [128, 4, 512], BF16)
            gbT = P.sb("gbT", [128, 4, 512], BF16)
            caT = gaT
            cbT = gbT
            gyT = P.sb("gyT", [128, 4, 512], BF16)
            sqb = [P.sb("sqb%d" % i, [128, 512], BF16) for i in range(2)]
            sdb = [tm[2], tm[3]]
            Sb = [P.sb("Sb%d" % i, [128, 512], F32) for i in range(2)]
            PT = [P.sb("PT%d" % i, [128, 2, 512], BF16) for i in range(2)]
            dtot = Sb[1]
            xre = [P.sb("xre%d" % i, [128, 4, 128], BF16) for i in range(1)]
            xim = [P.sb("xim%d" % i, [128, 4, 128], BF16) for i in range(1)]
            ta = [P.sb("ta%d" % i, [128, 16], F32) for i in range(6)]
            gl = {"y": tm[1], "t": tm[0]}
            sg = [Sb[0]]
        else:
            esum = P.ps("esum", [128, 32])
            Sr = P.sb("Sr", [128, 16], F32)
            Si = P.sb("Si", [128, 16], F32)
            ta = [P.sb("ta%d" % i, [128, 16], F32) for i in range(6)]
            P.memset(Sr.all(), 0.0)
            P.memset(Si.all(), 0.0)

        def kv_project(blk_cols, kslot, vslot):
            for g in range(2):
                pr_ = nring()
                for k in range(8):
                    P.mm(pr_[:, 0:128], Wk(k, WKD + g * 128, 128), hnT[:, k, blk_cols], start=(k == 0), stop=(k == 7))
                s = sqb[g]
                P.act(s[:, 0:128], pr_[:, 0:128], AF.Square)
                p2 = nring()
                P.mm(p2[:, 0:128], ones2.all(), s[:, 0:128])
                d_ = sdb[g]
                P.act(d_[:, 0:128], p2[:, 0:128], AF.Ln, bias=wk["epsb"].all(), scale=1.0 / 64)
                P.act(d_[:, 0:128], d_[:, 0:128], AF.Exp, scale=-0.5)
                P.stt(kTd[:, g, kslot * 128:(kslot + 1) * 128], pr_[:, 0:128], gk.all(), d_[:, 0:128], ALU.mult, ALU.mult)
            pr_ = nring()
            for k in range(8):
                P.mm(pr_[:, 0:256], hnT[:, k, blk_cols], Wk(k, WVD, 256), start=(k == 0), stop=(k == 7))
            P.copy(Vd[:, vslot, :], pr_[:, 0:256], e="act")

        if full:
            xh = xb[nxb - 1]
            P.dma("sp", xh.all(), DV(x_halo))
            rmsnorm_to_fm(P, c, xh.all(), hnT[:, :, 0:128], g_fm, wk)
            kv_project(slice(0, 128), 0, 0)

        for sb_ in range(nsb):
            xs = []
            for bl in range(4):
                t0 = sb_ * 512 + bl * 128
                x = xb[cnt["x"] % nxb]
                cnt["x"] += 1
                xs.append(x)
                P.dma("sp", x.all(), DV(x_own, x_own.ap()[t0:t0 + 128, :]))
                rmsnorm_to_fm(P, c, x.all(), hnT[:, :, bl * 128:(bl + 1) * 128], g_fm, wk)
            for q in range(4):
                pr_ = nring()
                for k in range(8):
                    P.mm(pr_.all(), Wk(k, wu0 + q * 128, 128), hnT[:, k, :], start=(k == 0), stop=(k == 7))
                P.copy(uT[:, q, :], pr_.all(), e="act")
            if full:
                for t in range(4):
                    pr_ = nring()
                    for k in range(8):
                        P.mm(pr_.all(), Wk(k, WQ + t * 128, 128), hnT[:, k, :], start=(k == 0), stop=(k == 7))
                    s = sqb[t % 2]
                    P.act(s.all(), pr_.all(), AF.Square)
                    p2 = nring()
                    P.mm(p2.all(), ones2.all(), s.all())
                    d_ = sdb[t % 2]
                    P.act(d_.all(), p2.all(), AF.Ln, bias=wk["epsb"].all(), scale=1.0 / 64)
                    P.act(d_.all(), d_.all(), AF.Exp, scale=-0.5)
                    P.stt(qT[0:64, 2 * t, :], pr_[0:64, :], gq[0:64, :], d_[0:64, :], ALU.mult, ALU.mult)
                    P.stt(qT[64:128, 2 * t + 1, :], pr_[64:128, :], gq[64:128, :], d_[64:128, :], ALU.mult, ALU.mult)
                for (dst, c0) in ((gaT, WGA), (gbT, WGB)):
                    for t in range(4):
                        pr_ = nring()
                        for k in range(8):
                            P.mm(pr_.all(), Wk(k, c0 + t * 128, 128), hnT[:, k, :], start=(k == 0), stop=(k == 7))
                        P.act(dst[:, t, :], pr_.all(), AF.Silu)
                for bl in range(4):
                    kv_project(slice(bl * 128, (bl + 1) * 128), bl + 1, bl + 1)
                swa_steps = []

                def swa_step(bl, g, sb_=sb_):
                    qcols = slice(bl * 128, (bl + 1) * 128)
                    first = (sb_ == 0 and bl == 0)
                    if True:
                        banks = [nring(), nring()]
                        for kt in range(2):
                            kslot = bl + kt
                            for hh in range(4):
                                h = 4 * g + hh
                                lp = slice((h % 2) * 64, (h % 2) * 64 + 64)
                                P.mm(banks[kt][:, hh * 128:(hh + 1) * 128], kTd[:, g, kslot * 128:(kslot + 1) * 128],
                                     qT[:, h, qcols])
                        pt = PT[g]
                        for kt in range(2):
                            bt = BT[2] if (first and kt == 0) else BT[kt]
                            P.tt(Sb[kt].all(), banks[kt].all(), bt[:, g * 512:(g + 1) * 512], ALU.add)
                            P.act(pt[:, kt, :], Sb[kt].all(), AF.Exp)
                        pn, pd = nring(), nring()
                        for kt in range(2):
                            P.mm(pn.all(), Vd[:, bl + kt, g * 128:(g + 1) * 128], pt[:, kt, :], start=(kt == 0), stop=(kt == 1))
                        for kt in range(2):
                            P.mm(pd.all(), c["ones"].all(), pt[:, kt, :], start=(kt == 0), stop=(kt == 1))
                        for hh in range(4):
                            h = 4 * g + hh
                            P.ts(dtot[:, hh * 128:(hh + 1) * 128], pd[:, hh * 128:(hh + 1) * 128], esink[:, h:h + 1], None,
                                 op0=ALU.add)
                        P.act(dtot.all(), dtot.all(), AF.Ln)
                        P.act(dtot.all(), dtot.all(), AF.Exp, scale=-1.0)
                        for hh in range(4):
                            h = 4 * g + hh
                            lp = slice((h % 2) * 64, (h % 2) * 64 + 64)
                            o = caT[lp, h // 2, qcols]
                            tmo = Sb[0][lp, hh * 128:(hh + 1) * 128]
                            P.tt(tmo, pn[lp, hh * 128:(hh + 1) * 128], dtot[lp, hh * 128:(hh + 1) * 128], ALU.mult)
                            P.tt(o, tmo, gaT[lp, h // 2, qcols], ALU.mult)
                def halo_step():
                    P.copy(kTd[:, :, 0:128], kTd[:, :, 512:640], e="pool")
                    P.copy(Vd[:, 0, :], Vd[:, 4, :], e="pool")
                for bl in range(0 if DBG.get('no_swa') else 4):
                    for g in range(2):
                        swa_steps.append(lambda bl=bl, g=g: swa_step(bl, g))
                swa_steps.append(halo_step)
            ssm_steps = []
            pyd = {}

            def ssm_A_step(bl, q):
                tcols = slice(bl * 128, (bl + 1) * 128)
                if True:
                    bu = [nring(), nring()]
                    for hb_ in range(2):
                        ps_ = slice(64 * hb_, 64 * hb_ + 64)
                        P.mm(bu[hb_].all(), uT[ps_, q, tcols], BBp[ps_, q, :])
                    vr_, vi_ = vre[(4 * bl + q) % 2], vim[(4 * bl + q) % 2]
                    tmA = tmsets[(4 * bl + q) % 2]
                    for hb_ in range(2):
                        bre_ = bu[hb_].all().re("p (a c s) -> p a c s", a=2, c=2)[:, :, 0, :]
                        bim_ = bu[hb_].all().re("p (a c s) -> p a c s", a=2, c=2)[:, :, 1, :]
                        sts = slice(4 * q + 2 * hb_, 4 * q + 2 * hb_ + 2)
                        wr_, ws_ = Wr[:, sts, :], Ws[:, sts, :]
                        o = slice(hb_ * 256, hb_ * 256 + 256)
                        P.tt(tmA[0][:, o].re("p (a s) -> p a s", a=2), bre_, wr_, ALU.mult)
                        P.tt(tmA[1][:, o].re("p (a s) -> p a s", a=2), bim_, ws_, ALU.mult)
                        P.tt(tmA[2][:, o].re("p (a s) -> p a s", a=2), bim_, wr_, ALU.mult)
                        P.tt(tmA[3][:, o].re("p (a s) -> p a s", a=2), bre_, ws_, ALU.mult)
                    P.tt(vr_.all().re("p a s -> p (a s)"), tmA[0].all(), tmA[1].all(), ALU.add, e="pool")
                    P.tt(vi_.all().re("p a s -> p (a s)"), tmA[2].all(), tmA[3].all(), ALU.subtract, e="pool")
                    if full and DBG.get('no_ssm2'):
                        return
                    if not full:
                        return
            def esum_step(bl, q):
                vr_, vi_ = vre[(4 * bl + q) % 2], vim[(4 * bl + q) % 2]
                for st4 in range(4):
                    st = 4 * q + st4
                    P.mm(esum[:, st:st + 1], vr_[:, st4, :], c["ones"][:, 0:1])
                    P.mm(esum[:, 16 + st:17 + st], vi_[:, st4, :], c["ones"][:, 0:1])

            def ssm_B_step(bl, q):
                tcols = slice(bl * 128, (bl + 1) * 128)
                vr_, vi_ = vre[(4 * bl + q) % 2], vim[(4 * bl + q) % 2]
                if True:
                    csr, csi = nring(), nring()
                    for st4 in range(4):
                        P.mm(csr[:, st4 * 128:(st4 + 1) * 128], vr_[:, st4, :], tri.all())
                        P.mm(csi[:, st4 * 128:(st4 + 1) * 128], vi_[:, st4, :], tri.all())
                    xr_, xi_ = xre[0], xim[0]
                    for st4 in range(4):
                        st = 4 * q + st4
                        cr = csr[:, st4 * 128:(st4 + 1) * 128]
                        ci = csi[:, st4 * 128:(st4 + 1) * 128]
                        o = slice(st4 * 128, (st4 + 1) * 128)
                        P.stt(tmB[0][:, o], cr, Zr[:, st:st + 1], Vr[:, st, :], ALU.add, ALU.mult)
                        P.stt(tmB[1][:, o], ci, Zi[:, st:st + 1], Vi[:, st, :], ALU.add, ALU.mult)
                        P.stt(tmB[2][:, o], cr, Zr[:, st:st + 1], Vi[:, st, :], ALU.add, ALU.mult)
                        P.stt(tmB[3][:, o], ci, Zi[:, st:st + 1], Vr[:, st, :], ALU.add, ALU.mult)
                    sq_ = slice(4 * q, 4 * q + 4)
                    cr127 = csr.all().re("p (a s) -> p a s", a=4)[:, :, 127]
                    ci127 = csi.all().re("p (a s) -> p a s", a=4)[:, :, 127]
                    P.tt(ta[0][:, 0:4], Zr[:, sq_], cr127, ALU.add)
                    P.tt(ta[1][:, 0:4], Zi[:, sq_], ci127, ALU.add)
                    cmul(P, Zr[:, sq_], Zi[:, sq_], A128r[:, sq_], A128i[:, sq_], ta[0][:, 0:4], ta[1][:, 0:4],
                         ta[2][:, 0:4], ta[3][:, 0:4])
                    P.tt(xr_.all().re("p a s -> p (a s)"), tmB[0].all(), tmB[1].all(), ALU.subtract, e="pool")
                    P.tt(xi_.all().re("p a s -> p (a s)"), tmB[2].all(), tmB[3].all(), ALU.add, e="pool")
                    pyd['py'] = py_bank
                    py = py_bank
                    for st4 in range(4):
                        st = 4 * q + st4
                        P.mm(py[:, q * 128:(q + 1) * 128], Cre[:, st, :], xr_[:, st4, :], start=(st4 == 0), stop=False)
                        P.mm(py[:, q * 128:(q + 1) * 128], Cim[:, st, :], xi_[:, st4, :], start=False, stop=(st4 == 3))
            def ssm_tail_step(bl):
                tcols = slice(bl * 128, (bl + 1) * 128)
                if not full:
                    P.copy(ta[4].all(), esum[:, 0:16])
                    P.copy(ta[5].all(), esum[:, 16:32])
                    cmul(P, ta[0].all(), ta[1].all(), A127r.all(), A127i.all(), ta[4].all(), ta[5].all(), ta[2].all(), ta[3].all())
                    cmul(P, ta[4].all(), ta[5].all(), A128r.all(), A128i.all(), Sr.all(), Si.all(), ta[2].all(), ta[3].all())
                    P.tt(Sr.all(), ta[0].all(), ta[4].all(), ALU.add)
                    P.tt(Si.all(), ta[1].all(), ta[5].all(), ALU.add)
                    return
                if DBG.get('no_ssm2'):
                    return
                for q in range(4):
                    P.stt(gl["y"][:, q * 128:(q + 1) * 128], uT[:, q, tcols], d_fm[:, q:q + 1], pyd['py'][:, q * 128:(q + 1) * 128],
                          ALU.mult, ALU.add)
                P.act(gyT[:, :, tcols], gl["y"].all().re("p (q t) -> p q t", q=4), AF.Gelu_apprx_tanh)
            items = [(bl, q) for bl in range(4) for q in range(4)]
            if full and not DBG.get('no_ssm2'):
                for k in range(2):
                    ssm_steps.append(lambda k=k: ssm_A_step(*items[k]))
                for k in range(16):
                    ssm_steps.append(lambda k=k: ssm_B_step(*items[k]))
                    if k + 2 < 16:
                        ssm_steps.append(lambda k=k: ssm_A_step(*items[k + 2]))
                    if items[k][1] == 3:
                        ssm_steps.append(lambda k=k: ssm_tail_step(items[k][0]))
            elif not full:
                ssm_steps.append(lambda: ssm_A_step(*items[0]))
                for k in range(16):
                    if k + 1 < 16:
                        ssm_steps.append(lambda k=k: ssm_A_step(*items[k + 1]))
                    ssm_steps.append(lambda k=k: esum_step(*items[k]))
                    if items[k][1] == 3:
                        ssm_steps.append(lambda k=k: ssm_tail_step(items[k][0]))
            else:
                for bl in range(4):
                    for q in range(4):
                        ssm_steps.append(lambda bl=bl, q=q: ssm_A_step(bl, q))
            interleave_steps(swa_steps if full else [], ssm_steps)
            if not full:
                continue
            for f in range(0 if DBG.get('no_glu') else 4):
                pa_, pb_ = nring(), nring()
                for q in range(4):
                    P.mm(pa_.all(), Wg[:, q, f * 128:(f + 1) * 128], gyT[:, q, :], start=(q == 0), stop=(q == 3))
                for q in range(4):
                    P.mm(pb_.all(), Wg[:, q, 512 + f * 128:512 + (f + 1) * 128], gyT[:, q, :], start=(q == 0), stop=(q == 3))
                s_ = sg[0]
                P.act(s_.all(), pb_.all(), AF.Sigmoid, bias=glu_b[:, 4 + f:5 + f])
                P.stt(gl["t"].all(), pa_.all(), glu_b[:, f:f + 1], s_.all(), ALU.add, ALU.mult)
                P.tt(cbT[:, f, :], gl["t"].all(), gbT[:, f, :], ALU.mult)
            for bl in range(4):
                tcols = slice(bl * 128, (bl + 1) * 128)
                t0 = sb_ * 512 + bl * 128
                o_ = xs[bl]
                for half in range(0 if DBG.get('no_out') else 2):
                    po = nring()
                    for k in range(8):
                        lhs = caT[:, k, tcols] if k < 4 else cbT[:, k - 4, tcols]
                        P.mm(po.all(), lhs, Wo[:, k, half * 512:(half + 1) * 512], start=(k == 0), stop=(k == 7))
                    P.tt(o_[:, half * 512:(half + 1) * 512], po.all(), xs[bl][:, half * 512:(half + 1) * 512], ALU.add)
                P.dma("pool", DV(h1_out, h1_out.ap()[t0:t0 + 128, :]), o_.all(), primary=o_.buf)
                outs.append(o_.buf)
        if not full:
            eo = P.sb("eo", [128, 32], F32)
            P.copy(eo[:, 0:16], Sr.all())
            P.copy(eo[:, 16:32], Si.all())
            P.dma("pool", DV(eloc), eo.all(), primary=eo.buf)
            outs.append(eo.buf)
        P.finish(outs)
    return nc, P


def swa_onehot():
    e = np.arange(384)
    d = e - 127
    valid = (d >= 0) & (d < 128)
    oh = np.zeros((32, 384), np.float32)
    oh[t5_bucket_np(d)[valid], e[valid]] = 1.0
    neg = np.tile(np.where(valid, 0.0, NEG).astype(np.float32)[None, :], (8, 1))
    return oh, neg


def l0_inputs(inp, b, r, eprev=None, full=True):
    f32 = np.float32
    x = inp["x"][b]
    d = {}
    d["x_own"] = np.ascontiguousarray(x[r * TPC:(r + 1) * TPC])
    d["w_in"] = np.ascontiguousarray(inp["ev_w_in"][0])
    d["ng_fm"] = np.ascontiguousarray(inp["norm_g"][0].reshape(8, 128).T)
    a_re = inp["ev_ssm_a_re"][0]
    a_im = inp["ev_ssm_a_im"][0]
    ldt = np.repeat(inp["ev_ssm_log_dt"][0], 64)
    d["a_re_f"] = np.ascontiguousarray(a_re.reshape(2048))
    d["a_im_f"] = np.ascontiguousarray(a_im.reshape(2048))
    d["ldt_f"] = np.ascontiguousarray(ldt)
    d["a_re_s"] = np.ascontiguousarray(a_re.reshape(16, 128).T)
    d["a_im_s"] = np.ascontiguousarray(a_im.reshape(16, 128).T)
    d["ldt_s"] = np.ascontiguousarray(ldt.reshape(16, 128).T)
    for nm, src in (("b_blk_re", inp["ev_ssm_b_re"][0]), ("b_blk_im", inp["ev_ssm_b_im"][0])):
        blk = np.zeros((4, 2, 16, 4, 2, 64), f32)
        s6 = src.reshape(4, 4, 2, 64, 16)
        for g2 in range(2):
            blk[:, g2, :, :, g2, :] = s6[:, :, g2].transpose(1, 3, 0, 2)
        d[nm] = np.ascontiguousarray(blk.reshape(128, 4, 128))
    if not full:
        return d
    d["x_halo"] = np.ascontiguousarray(x[r * TPC - 128:r * TPC]) if r > 0 else np.zeros((128, D), f32)
    d["firstneg"] = np.full((128, 1), NEG if r == 0 else 0.0, f32)
    d["w_out"] = np.ascontiguousarray(inp["ev_w_out"][0])
    d["glu_w"] = np.ascontiguousarray(inp["ev_glu_w"][0])
    d["qg2"] = np.ascontiguousarray(np.tile(inp["ev_q_norm_g"][0], 2)[:, None])
    d["kg2"] = np.ascontiguousarray(np.tile(inp["ev_k_norm_g"][0], 2)[:, None])
    d["sinks_row"] = np.ascontiguousarray(np.tile(inp["ev_sinks"][0][None, :], (128, 1)))
    d["rel_bias"] = np.ascontiguousarray(inp["rel_bias"])
    d["oh_swa"], d["neg_swa"] = swa_onehot()
    for nm, src in (("c_blk_re", inp["ev_ssm_c_re"][0]), ("c_blk_im", inp["ev_ssm_c_im"][0])):
        blk = np.zeros((2, 64, 16, 8, 16), f32)
        s5 = src.reshape(16, 2, 16, 64)
        for st in range(16):
            for g2 in range(2):
                blk[g2, :, st, 2 * (st % 4) + g2, :] = s5[st, g2].T
        d[nm] = np.ascontiguousarray(blk.reshape(128, 16, 128))
    d["d_fm"] = np.ascontiguousarray(inp["ev_ssm_d"][0].reshape(4, 128).T)
    d["glu_b_fm"] = np.ascontiguousarray(inp["ev_glu_b"][0].reshape(8, 128).T)
    d["eprev"] = np.zeros((3, 128, 32), f32) if eprev is None else np.ascontiguousarray(eprev)
    return d


_PROGS = {}


def _prog(name):
    if name not in _PROGS:
        if name == "p1":
            _PROGS[name] = build_p2("p1")[0]
        elif name == "p2":
            _PROGS[name] = build_p2("full")[0]
        elif name == "p2b":
            _PROGS[name] = build_p2b()[0]
        elif name == "p3":
            _PROGS[name] = build_p3()[0]
    return _PROGS[name]


def _run(name, maps):
    return run_bass_kernel_spmd(_prog(name), maps, core_ids=list(range(NCORES))).results


def kernel(**inputs):
    inp = {k: np.asarray(v) for k, v in inputs.items()}
    f32 = np.float32
    r1 = _run("p1", [l0_inputs(inp, c // 4, c % 4, full=False) for c in range(NCORES)])
    eloc = [np.asarray(r1[c]["eloc"], f32) for c in range(NCORES)]
    maps = []
    for c in range(NCORES):
        b, r = c // 4, c % 4
        ep = np.zeros((3, 128, 32), f32)
        for kk in range(min(r, 3)):
            ep[kk] = eloc[4 * b + r - 1 - kk]
        maps.append(l0_inputs(inp, b, r, eprev=ep))
    r2 = _run("p2", maps)
    h1 = [np.asarray(r2[c]["h1"], f32) for c in range(NCORES)]
    maps = [{"h1": h1[c], "w_in": np.ascontiguousarray(inp["od_w_in"][0]), "ng": np.ascontiguousarray(inp["norm_g"][1]),
             "qg": np.ascontiguousarray(inp["od_q_norm_g"][0]), "kg": np.ascontiguousarray(inp["od_k_norm_g"][0])}
            for c in range(NCORES)]
    r3 = _run("p2b", maps)
    maps = []
    for c in range(NCORES):
        b, j = c // 4, c % 4
        cat = lambda nm, ax: np.concatenate([np.asarray(r3[4 * b + r][nm]) for r in range(4)], axis=ax)
        ohm, cmask = p3_consts(j)
        h1b = np.concatenate([h1[4 * b + r] for r in range(4)], axis=0).reshape(64, 128, D)
        maps.append({
            "q_blk": np.ascontiguousarray(cat("q_out", 0)[j::4]),
            "g_blk": np.ascontiguousarray(cat("g_out", 0)[j::4]),
            "qi_blk": np.ascontiguousarray(cat("qi_out", 0)[j::4]),
            "wi_blk": np.ascontiguousarray(cat("wi_out", 0).reshape(64, 128, 8)[j::4]),
            "h1_blk": np.ascontiguousarray(h1b[j::4]),
            "kT_all": np.ascontiguousarray(cat("kT_out", 2)),
            "v_all": np.ascontiguousarray(cat("v_out", 0)),
            "ki_all": np.ascontiguousarray(cat("ki_out", 1)),
            "cmask": cmask, "oh": ohm,
            "rel_bias": np.ascontiguousarray(inp["rel_bias"]),
            "w_out": np.ascontiguousarray(inp["od_w_out"][0]),
        })
    r4 = _run("p3", maps)
    out = np.zeros((BATCH, SEQ // 128, 128, D), f32)
    for c in range(NCORES):
        b, j = c // 4, c % 4
        out[b, j::4] = np.asarray(r4[c]["y"], f32)
    return out.reshape(BATCH, SEQ, D)
```

```python
import contextlib
import math
import numpy as np
import ml_dtypes
import concourse.bass as bass
import concourse.mybir as mybir
from concourse.bass_utils import run_bass_kernel_spmd

F32 = mybir.dt.float32
BF16 = mybir.dt.bfloat16
I32 = mybir.dt.int32
I8 = mybir.dt.int8
ALU = mybir.AluOpType
AF = mybir.ActivationFunctionType
AX = mybir.AxisListType

NCORES = 8
D = 1024
SEQ = 8192
BATCH = 2
TPC = 2048
NBLK = TPC // 128
EPS = 1e-6
NEG = -30000.0
DBG = {}
SEM_LIMIT = 30000


class Buf:
    __slots__ = ("name", "lw", "rd", "dsem", "dcnt")

    def __init__(self, name):
        self.name = name
        self.lw = None
        self.rd = {}
        self.dsem = {}
        self.dcnt = {}


class V:
    __slots__ = ("ap", "bufs")

    def __init__(self, ap, bufs):
        self.ap = ap
        self.bufs = bufs

    def __getitem__(self, idx):
        return V(self.ap[idx], self.bufs)

    def bc(self, shape):
        return V(self.ap.broadcast_to(list(shape)), self.bufs)

    def re(self, pat, **kw):
        return V(self.ap.rearrange(pat, **kw), self.bufs)

    def bitcast(self, dt):
        return V(self.ap.bitcast(dt), self.bufs)


class Tile:
    def __init__(self, P, name, shape, dtype, space="sbuf"):
        nc = P.nc
        if space == "sbuf":
            self.t = P.es.enter_context(nc.sbuf_tensor(name, list(shape), dtype))
        elif space == "psum":
            self.t = P.es.enter_context(nc.psum_tensor(name, list(shape), dtype))
        else:
            raise ValueError(space)
        self.buf = Buf(name)
        self.name = name
        self.shape = shape

    def __getitem__(self, idx):
        return V(self.t[idx], (self.buf,))

    def v(self, idx, buf):
        return V(self.t[idx], (buf,))

    def all(self):
        return V(self.t[:], (self.buf,))


class Prog:
    def __init__(self, nc):
        self.nc = nc
        self.es = contextlib.ExitStack()
        self.eng = {"pe": nc.tensor, "dve": nc.vector, "act": nc.scalar, "pool": nc.gpsimd, "sp": nc.sync}
        self.semh = {}
        self.esem = {}
        self.cnt = {}
        self.epoch = {}
        self.waited = {e: {} for e in self.eng}
        self.nsem = 0
        for e in ("pe", "dve", "act", "pool"):
            self.epoch[e] = 0
            self._new_eng_sem(e)
        self.out_waits = []
        self.n_instr = 0

    def _sem(self, name):
        h = self.es.enter_context(self.nc.semaphore(name))
        self.semh[name] = h
        self.nsem += 1
        return name

    def _new_eng_sem(self, e):
        name = "c_%s_%d" % (e, self.epoch[e])
        self._sem(name)
        self.esem[e] = name
        self.cnt[e] = 0
        self.epoch[e] += 1

    def sb(self, name, shape, dtype):
        return Tile(self, name, shape, dtype, "sbuf")

    def ps(self, name, shape, dtype=F32):
        return Tile(self, name, shape, dtype, "psum")

    def _deps(self, e, reads, writes):
        deps = {}

        def add(sn, val, src, kind):
            if src == e and e == "pe":
                return
            if deps.get(sn, 0) < val:
                deps[sn] = val

        for b in reads:
            if b.lw is not None:
                add(b.lw[0], b.lw[1], b.lw[2], "raw")
        for b in writes:
            if b.lw is not None:
                add(b.lw[0], b.lw[1], b.lw[2], "waw")
            for sn, (v, se) in b.rd.items():
                add(sn, v, se, "war")
        h = self.eng[e]
        w = self.waited[e]
        for sn, v in deps.items():
            if w.get(sn, 0) >= v:
                continue
            h.wait_ge(self.semh[sn], v)
            w[sn] = v
            self.n_instr += 1

    def op(self, e, fn, ins=(), outs=()):
        reads = []
        for x in ins:
            if isinstance(x, V):
                reads.extend(x.bufs)
        writes = []
        for x in outs:
            if isinstance(x, V):
                writes.extend(x.bufs)
        self._deps(e, reads, writes)
        i = fn(self.eng[e])
        self.cnt[e] += 1
        self.n_instr += 1
        sn = self.esem[e]
        v = self.cnt[e]
        i.then_inc(self.semh[sn], 1)
        for b in writes:
            b.lw = (sn, v, e)
            b.rd = {}
        for b in reads:
            if b not in writes:
                b.rd[sn] = (v, e)
        if v >= SEM_LIMIT:
            self._new_eng_sem(e)
        return i

    def dma(self, q, out, in_, primary=None, nc_kwargs=None):
        reads = list(in_.bufs)
        writes = list(out.bufs)
        self._deps(q, reads, writes)
        if primary is None:
            primary = writes[0] if writes else reads[0]
        qc = "sw" if q == "pool" else "hw"
        if qc not in primary.dsem:
            primary.dsem[qc] = self._sem("d%s_%s" % (qc, primary.name))
            primary.dcnt[qc] = 0
        kw = nc_kwargs or {}
        i = self.eng[q].dma_start(out=out.ap, in_=in_.ap, **kw)
        primary.dcnt[qc] += 16
        sn, val = primary.dsem[qc], primary.dcnt[qc]
        i.then_inc(self.semh[sn], 16)
        self.n_instr += 1
        for b in writes:
            b.lw = (sn, val, "dma")
            b.rd = {}
        for b in reads:
            b.rd[sn] = (val, "dma")
        return (sn, val)

    def finish(self, bufs):
        h = self.eng["sp"]
        done = {}
        for b in bufs:
            if b.lw is not None:
                done[b.lw[0]] = max(done.get(b.lw[0], 0), b.lw[1])
            for sn, (v, se) in b.rd.items():
                done[sn] = max(done.get(sn, 0), v)
        for sn, v in done.items():
            h.wait_ge(self.semh[sn], v)

    def mm(self, out, lhsT, rhs, start=True, stop=True):
        return self.op("pe", lambda h: h.matmul(out.ap, lhsT=lhsT.ap, rhs=rhs.ap, start=start, stop=stop),
                       ins=(lhsT, rhs), outs=(out,))

    def tr(self, out, in_, ident):
        return self.op("pe", lambda h: h.transpose(out.ap, in_.ap, ident.ap), ins=(in_, ident), outs=(out,))

    def act(self, out, in_, func, bias=None, scale=None, accum=None, e="act"):
        kw = {}
        ins = [in_]
        outs = [out]
        if bias is not None:
            kw["bias"] = bias.ap if isinstance(bias, V) else bias
            ins.append(bias)
        if scale is not None:
            kw["scale"] = scale.ap if isinstance(scale, V) else scale
            ins.append(scale)
        if accum is not None:
            kw["accum_out"] = accum.ap
            outs.append(accum)
        return self.op(e, lambda h: h.activation(out=out.ap, in_=in_.ap, func=func, **kw), ins=ins, outs=outs)

    def ts(self, out, in0, s1, s2=None, op0=ALU.mult, op1=None, accum=None, e="dve"):
        kw = {}
        ins = [in0, s1, s2]
        outs = [out]
        if op1 is not None:
            kw["op1"] = op1
        if accum is not None:
            kw["accum_out"] = accum.ap
            outs.append(accum)
        a1 = s1.ap if isinstance(s1, V) else s1
        a2 = s2.ap if isinstance(s2, V) else s2
        return self.op(e, lambda h: h.tensor_scalar(out=out.ap, in0=in0.ap, scalar1=a1, scalar2=a2, op0=op0, **kw),
                       ins=ins, outs=outs)

    def tt(self, out, in0, in1, op, e="dve"):
        return self.op(e, lambda h: h.tensor_tensor(out=out.ap, in0=in0.ap, in1=in1.ap, op=op),
                       ins=(in0, in1), outs=(out,))

    def stt(self, out, in0, s, in1, op0, op1):
        a = s.ap if isinstance(s, V) else s
        return self.op("dve", lambda h: h.scalar_tensor_tensor(out=out.ap, in0=in0.ap, scalar=a, in1=in1.ap,
                                                                op0=op0, op1=op1),
                       ins=(in0, s, in1), outs=(out,))

    def copy(self, out, in_, e="dve"):
        if e == "act":
            return self.op("act", lambda h: h.copy(out=out.ap, in_=in_.ap), ins=(in_,), outs=(out,))
        return self.op(e, lambda h: h.tensor_copy(out=out.ap, in_=in_.ap), ins=(in_,), outs=(out,))

    def recip(self, out, in_):
        return self.op("dve", lambda h: h.reciprocal(out=out.ap, in_=in_.ap), ins=(in_,), outs=(out,))

    def reduce(self, out, in_, op, axis=AX.X):
        return self.op("dve", lambda h: h.tensor_reduce(out=out.ap, in_=in_.ap, axis=axis, op=op),
                       ins=(in_,), outs=(out,))

    def memset(self, out, val, e="dve"):
        return self.op(e, lambda h: h.memset(out.ap, val), ins=(), outs=(out,))

    def iota(self, out, pattern, base, cm):
        return self.op("pool", lambda h: h.iota(out.ap, pattern=pattern, base=base, channel_multiplier=cm,
                                                allow_small_or_imprecise_dtypes=True), ins=(), outs=(out,))


def dram_in(nc, name, shape, dtype):
    return nc.dram_tensor(name, list(shape), dtype, kind="ExternalInput")


def dram_out(nc, name, shape, dtype):
    return nc.dram_tensor(name, list(shape), dtype, kind="ExternalOutput")


def DV(t, ap=None, buf=None):
    return V(t.ap() if ap is None else ap, (buf,) if buf is not None else ())


def dap(t, offset, pattern):
    return bass.AP(t, offset, [list(p) for p in pattern])


def barrier(P):
    tgt = {P.esem[e]: P.cnt[e] for e in ("pe", "dve", "act", "pool") if P.cnt[e] > 0}
    for e in ("pe", "dve", "act", "pool", "sp"):
        for sn, v in tgt.items():
            if P.waited[e].get(sn, 0) < v:
                P.eng[e].wait_ge(P.semh[sn], v)
                P.waited[e][sn] = v


def make_consts(P):
    c = {}
    c["ident"] = P.sb("c_ident", [128, 128], BF16)
    with contextlib.ExitStack() as es2:
        old_es, P.es = P.es, es2
        io = P.sb("c_iota", [128, 128], F32)
        P.iota(io.all(), [[1, 128]], 0, -1)
        P.ts(c["ident"].all(), io.all(), 0.0, None, op0=ALU.is_equal)
        barrier(P)
        P.es = old_es
    c["ones"] = P.sb("c_ones", [128, 128], BF16)
    P.memset(c["ones"].all(), 1.0)
    return c


def load_fm_vec(P, name, dram_t, n):
    t = P.sb(name, [128, n], F32)
    P.dma("sp", t.all(), DV(dram_t, dap(dram_t, 0, [[1, 128], [128, n]])),
          nc_kwargs={"allow_slow_non_contiguous": True})
    return t


def rmsnorm_to_fm(P, c, x_v, hnT_v, g_fm, wk, nfeat=1024):
    nk = nfeat // 128
    P.act(wk["junk"].all(), x_v, AF.Square, accum=wk["ss"].all())
    P.act(wk["sd"].all(), wk["ss"].all(), AF.Sqrt, bias=wk["epsb"].all(), scale=1.0 / nfeat)
    P.recip(wk["rstd"].all(), wk["sd"].all())
    P.ts(wk["xn"].all(), x_v, wk["rstd"].all(), None, op0=ALU.mult)
    if DBG.get('no_tr'):
        return
    pst = wk["pst"]
    for k in range(nk):
        P.tr(pst[:, k * 128:(k + 1) * 128], wk["xn"][:, k * 128:(k + 1) * 128], c["ident"].all())
    if DBG.get('no_tt'):
        return
    if DBG.get('tt_copy'):
        P.copy(hnT_v, pst.all().re("p (k t) -> p k t", k=nk))
        return
    for k in range(nk):
        if k % 2 == 0:
            P.ts(hnT_v[:, k, :], pst[:, k * 128:(k + 1) * 128], g_fm[:, k:k + 1], None, op0=ALU.mult)
        else:
            P.act(hnT_v[:, k, :], pst[:, k * 128:(k + 1) * 128], AF.Copy, scale=g_fm[:, k:k + 1])


def make_norm_work(P, pfx, pst):
    wk = {}
    wk["junk"] = P.sb(pfx + "junk", [128, 1024], BF16)
    wk["ss"] = P.sb(pfx + "ss", [128, 1], F32)
    wk["sd"] = P.sb(pfx + "sd", [128, 1], F32)
    wk["rstd"] = P.sb(pfx + "rstd", [128, 1], F32)
    wk["xn"] = P.sb(pfx + "xn", [128, 1024], BF16)
    wk["epsb"] = P.sb(pfx + "epsb", [128, 1], F32)
    P.memset(wk["epsb"].all(), EPS)
    wk["pst"] = pst
    return wk


OD_Q, OD_K, OD_V, OD_G, OD_QI, OD_KI, OD_WI = 0, 1024, 1280, 1536, 2560, 3072, 3136


def build_p2b(nsb=TPC // 512, do_tiles=True, do_blocks=True):
    nc = bass.Bass("TRN2", target_bir_lowering=False)
    h1 = dram_in(nc, "h1", [TPC, D], F32)
    w_in = dram_in(nc, "w_in", [D, 3144], F32)
    ng = dram_in(nc, "ng", [D], F32)
    qg = dram_in(nc, "qg", [128], F32)
    kg = dram_in(nc, "kg", [128], F32)
    q_out = dram_out(nc, "q_out", [NBLK, 128, 8, 128], BF16)
    g_out = dram_out(nc, "g_out", [NBLK, 128, 8, 128], BF16)
    qi_out = dram_out(nc, "qi_out", [NBLK, 128, 4, 128], BF16)
    wi_out = dram_out(nc, "wi_out", [TPC, 8], F32)
    kT_out = dram_out(nc, "kT_out", [128, 2, TPC], BF16)
    v_out = dram_out(nc, "v_out", [TPC, 256], BF16)
    ki_out = dram_out(nc, "ki_out", [128, TPC], BF16)
    P = Prog(nc)
    with P.es:
        c = make_consts(P)
        W = P.sb("W", [128, 8, 3200], BF16)
        Wwi = P.sb("Wwi", [128, 8, 8], BF16)
        wbufs = [Buf("Wk%d" % k) for k in range(8)]
        stg = [P.sb("stg%d" % i, [128, 3136], F32) for i in range(2)]
        for k in range(8):
            wv = V(W.t[:, k, :], (wbufs[k],))
            if k % 2 == 0:
                P.dma("pool", wv[:, 0:3136], DV(w_in, w_in.ap()[k * 128:(k + 1) * 128, 0:3136]), primary=wbufs[k])
                P.dma("pool", wv[:, 3136:3200], DV(w_in, w_in.ap()[k * 128:(k + 1) * 128, OD_KI:OD_KI + 64]),
                      primary=wbufs[k])
            else:
                st_ = stg[(k // 2) % 2]
                P.dma("sp", st_.all(), DV(w_in, w_in.ap()[k * 128:(k + 1) * 128, 0:3136]))
                P.copy(wv[:, 0:3136], st_.all(), e="act")
                P.copy(wv[:, 3136:3200], st_[:, OD_KI:OD_KI + 64], e="act")
        P.dma("pool", Wwi.all(), DV(w_in, dap(w_in, OD_WI, [[3144, 128], [128 * 3144, 8], [1, 8]])))

        def Wk(k, c0, n):
            return V(W.t[:, k, c0:c0 + n], (wbufs[k],))

        g_fm = load_fm_vec(P, "g_fm", ng, 8)
        gq = P.sb("gq", [128, 1], F32)
        gk = P.sb("gk", [128, 1], F32)
        P.dma("sp", gq.all(), DV(qg, dap(qg, 0, [[1, 128], [1, 1]])))
        P.dma("sp", gk.all(), DV(kg, dap(kg, 0, [[1, 128], [1, 1]])))
        P.ts(gq.all(), gq.all(), 128.0 ** -0.5, None, op0=ALU.mult)
        pst = P.ps("pst", [128, 1024], BF16)
        wk = make_norm_work(P, "n_", pst)
        xb = [P.sb("xb%d" % i, [128, 1024], F32) for i in range(2)]
        hnT = P.sb("hnT", [128, 8, 512], BF16)
        ring = [P.ps("pr%d" % i, [128, 512]) for i in range(3)]
        ring2 = [P.ps("ps2_%d" % i, [128, 512]) for i in range(2)]
        ptm = P.ps("ptm", [128, 512])
        ptm2 = P.ps("ptm2", [128, 512])
        sq = [P.sb("sq%d" % i, [128, 512], BF16) for i in range(2)]
        sd = [P.sb("sdq%d" % i, [128, 512], F32) for i in range(2)]
        ob = [P.sb("ob%d" % i, [128, 512], BF16) for i in range(4)]
        vb = [P.sb("vb%d" % i, [128, 256], BF16) for i in range(2)]
        wib = [P.sb("wib%d" % i, [128, 8], F32) for i in range(2)]
        rr = [0, 0, 0, 0]
        outs = []

        def nxt(lst, idx):
            t = lst[rr[idx] % len(lst)]
            rr[idx] += 1
            return t

        blk_sz = 128 * 8 * 128
        for sb_ in range(nsb):
            for bl in range(4 if do_blocks else 0):
                t0 = sb_ * 512 + bl * 128
                x = xb[bl % 2]
                P.dma("sp", x.all(), DV(h1, h1.ap()[t0:t0 + 128, :]))
                rmsnorm_to_fm(P, c, x.all(), hnT[:, :, bl * 128:(bl + 1) * 128], g_fm, wk)
                if DBG.get('no_tm'):
                    continue
                for k in range(8):
                    P.mm(ptm[:, 0:256], hnT[:, k, bl * 128:(bl + 1) * 128], Wk(k, OD_V, 256), start=(k == 0), stop=(k == 7))
                v_sb = vb[bl % 2]
                P.copy(v_sb.all(), ptm[:, 0:256], e="act")
                if not DBG.get('no_vst'):
                    P.dma("pool", DV(v_out, v_out.ap()[t0:t0 + 128, :]), v_sb.all(), primary=v_sb.buf)
                    outs += [v_sb.buf]
                if DBG.get('no_wi'):
                    continue
                for k in range(8):
                    P.mm(ptm2[:, 0:8], hnT[:, k, bl * 128:(bl + 1) * 128], Wwi[:, k, :], start=(k == 0), stop=(k == 7))
                w_sb = wib[bl % 2]
                P.ts(w_sb.all(), ptm2[:, 0:8], (8.0 ** -0.5) * (64.0 ** -0.5), None, op0=ALU.mult)
                P.dma("pool", DV(wi_out, wi_out.ap()[t0:t0 + 128, :]), w_sb.all(), primary=w_sb.buf)
                outs += [w_sb.buf]
            tiles = [("q", h, OD_Q + h * 128) for h in range(8)] + [("k", g, OD_K + g * 128) for g in range(2)] + \
                    [("g", h, OD_G + h * 128) for h in range(8)] + [("qi", pr, OD_QI + pr * 128) for pr in range(4)] + \
                    [("ki", 0, OD_KI)]
            for (kind, idx, c0) in (tiles if do_tiles else []):
                pr_ = nxt(ring, 0)
                for k in range(8):
                    P.mm(pr_.all(), Wk(k, c0, 128), hnT[:, k, :], start=(k == 0), stop=(k == 7))
                o = nxt(ob, 1)
                if kind in ("q", "k"):
                    s = nxt(sq, 2)
                    P.act(s.all(), pr_.all(), AF.Square)
                    p2 = nxt(ring2, 3)
                    P.mm(p2.all(), c["ones"].all(), s.all())
                    d_ = sd[(rr[3] - 1) % 2]
                    P.act(d_.all(), p2.all(), AF.Ln, bias=wk["epsb"].all(), scale=1.0 / 128)
                    P.act(d_.all(), d_.all(), AF.Exp, scale=-0.5)
                    P.stt(o.all(), pr_.all(), (gq if kind == "q" else gk).all(), d_.all(), ALU.mult, ALU.mult)
                elif kind == "g":
                    P.act(o.all(), pr_.all(), AF.Silu)
                else:
                    P.copy(o.all(), pr_.all(), e="act")
                o3 = o.all().re("p (b t) -> p b t", b=4)
                if kind == "q":
                    dst = dap(q_out, sb_ * 4 * blk_sz + idx * 128, [[8 * 128, 128], [blk_sz, 4], [1, 128]])
                    P.dma("sp", DV(q_out, dst), o3, primary=o.buf)
                elif kind == "g":
                    dst = dap(g_out, sb_ * 4 * blk_sz + idx * 128, [[8 * 128, 128], [blk_sz, 4], [1, 128]])
                    P.dma("sp", DV(g_out, dst), o3, primary=o.buf)
                elif kind == "qi":
                    bs = 128 * 4 * 128
                    dst = dap(qi_out, sb_ * 4 * bs + idx * 128, [[4 * 128, 128], [bs, 4], [1, 128]])
                    P.dma("sp", DV(qi_out, dst), o3, primary=o.buf)
                elif kind == "k":
                    P.dma("sp", DV(kT_out, kT_out.ap()[:, idx, sb_ * 512:(sb_ + 1) * 512]), o.all(), primary=o.buf)
                else:
                    P.dma("sp", DV(ki_out, ki_out.ap()[:, sb_ * 512:(sb_ + 1) * 512]), o.all(), primary=o.buf)
                outs.append(o.buf)
        P.finish(outs)
    return nc, P


NPOS = 12
GLEN = NPOS * 128 + 128
BIS_ITERS = 21


def barrier(P):
    tgt = {P.esem[e]: P.cnt[e] for e in ("pe", "dve", "act", "pool") if P.cnt[e] > 0}
    for e in ("pe", "dve", "act", "pool", "sp"):
        for sn, v in tgt.items():
            if P.waited[e].get(sn, 0) < v:
                P.eng[e].wait_ge(P.semh[sn], v)
                P.waited[e][sn] = v


def build_p3(nblk=NBLK):
    nc = bass.Bass("TRN2", target_bir_lowering=False)
    q_blk = dram_in(nc, "q_blk", [NBLK, 128, 8, 128], BF16)
    g_blk = dram_in(nc, "g_blk", [NBLK, 128, 8, 128], BF16)
    qi_blk = dram_in(nc, "qi_blk", [NBLK, 128, 4, 128], BF16)
    wi_blk = dram_in(nc, "wi_blk", [NBLK, 128, 8], F32)
    h1_blk = dram_in(nc, "h1_blk", [NBLK, 128, D], F32)
    kT_all = dram_in(nc, "kT_all", [128, 2, SEQ], BF16)
    v_all = dram_in(nc, "v_all", [SEQ, 256], BF16)
    ki_all = dram_in(nc, "ki_all", [128, SEQ], BF16)
    cmask = dram_in(nc, "cmask", [128, 512], BF16)
    oh = dram_in(nc, "oh", [32, GLEN], F32)
    rel_bias = dram_in(nc, "rel_bias", [32, 8], F32)
    w_out = dram_in(nc, "w_out", [D, D], F32)
    y = dram_out(nc, "y", [NBLK, 128, D], F32)
    gtab = nc.dram_tensor("gtab", [8, GLEN], F32, kind="Internal")
    gtab_buf = Buf("gtab")
    P = Prog(nc)
    with P.es:
        c = make_consts(P)
        ident4 = P.sb("ident4", [128, 512], BF16)
        for h in range(4):
            P.copy(ident4[:, h * 128:(h + 1) * 128], c["ident"].all())
        kT = P.sb("kT", [128, 2, SEQ], BF16)
        Vs = P.sb("Vs", [128, 64, 256], BF16)
        ki = P.sb("ki", [128, SEQ], BF16)
        Wo = P.sb("Wo", [128, 8, D], BF16)
        cm = P.sb("cm", [128, 512], BF16)
        biasT = P.sb("biasT", [128, NPOS, 1024], BF16)
        for g in range(2):
            P.dma("sp", kT[:, g, :], DV(kT_all, kT_all.ap()[:, g, :]))
        for part in range(4):
            P.dma("sp", Vs[:, part * 16:(part + 1) * 16, :],
                  DV(v_all, dap(v_all, part * 16 * 128 * 256, [[256, 128], [128 * 256, 16], [1, 256]])))
        P.dma("sp", ki.all(), DV(ki_all))
        P.dma("sp", cm.all(), DV(cmask))
        for k in range(8):
            P.dma("pool", Wo[:, k, :], DV(w_out, w_out.ap()[k * 128:(k + 1) * 128, :]))
        NR3 = 6
        ring = [P.ps("ring%d" % i, [128, 512]) for i in range(NR3)]
        num = [P.ps("num%d" % g, [128, 512]) for g in range(2)]
        rr = {"ring": 0}

        def nring():
            for _ in range(NR3):
                t = ring[rr["ring"] % NR3]
                rr["ring"] += 1
                if t.buf.lw is None or t.buf.rd:
                    return t
            raise RuntimeError("PSUM ring exhausted: every bank holds unread results")

        with contextlib.ExitStack() as es2:
            old_es, P.es = P.es, es2
            Jf = P.sb("Jf", [128, 128], F32)
            tmpi = P.sb("tmpi", [128, 128], F32)
            P.iota(tmpi.all(), [[1, 128]], -127, 1)
            P.ts(Jf.all(), tmpi.all(), 0.0, None, op0=ALU.is_equal)
            rb = P.sb("rb", [32, 8], F32)
            rb31 = P.sb("rb31", [32, 8], F32)
            ohs = P.sb("ohs", [32, GLEN], F32)
            gsb = P.sb("gsb", [8, GLEN], F32)
            hk = [P.sb("hk%d" % i, [128, 8, 128], F32) for i in range(2)]
            P.dma("sp", rb.all(), DV(rel_bias))
            P.dma("sp", rb31.all(), DV(rel_bias, dap(rel_bias, 31 * 8, [[0, 32], [1, 8]])))
            P.dma("sp", ohs.all(), DV(oh))
            P.tt(rb.all(), rb.all(), rb31.all(), ALU.subtract)
            for ch in range((GLEN + 511) // 512):
                n = min(512, GLEN - ch * 512)
                pr_ = nring()
                P.mm(pr_[0:8, 0:n], rb.all(), ohs[:, ch * 512:ch * 512 + n])
                P.copy(gsb[:, ch * 512:ch * 512 + n], pr_[0:8, 0:n])
            P.dma("sp", DV(gtab, buf=gtab_buf), gsb.all(), primary=gtab_buf)
            for p in range(NPOS):
                hkt = hk[p % 2]
                P.dma("sp", hkt.all(), DV(gtab, dap(gtab, p * 128, [[1, 128], [GLEN, 8], [1, 128]]), buf=gtab_buf))
                for half in range(2):
                    pr_ = nring()
                    P.mm(pr_.all(), Jf.all(), hkt[:, half * 4:(half + 1) * 4, :])
                    P.copy(biasT[:, p, half * 512:(half + 1) * 512], pr_.all(), e=("act" if half else "dve"))
            barrier(P)
            P.es = old_es

        score = P.sb("score", [128, SEQ], F32)
        sc_bufs = [Buf("sc%d" % i) for i in range(SEQ // 512)]

        def scv(a, b):
            return V(score.t[:, a:b], tuple(sc_bufs[a // 512:(b + 511) // 512]))

        JW = SEQ
        nmall = [P.sb("nmall%d" % i, [128, SEQ], BF16) for i in range(2)]
        qT = [P.sb("qT%d" % i, [128, 8, 128], BF16) for i in range(2)]
        gT = [P.sb("gT%d" % i, [128, 8, 128], BF16) for i in range(1)]
        PmSum = [P.sb("PmSum%d" % g, [128, 512], F32) for g in range(2)]
        ones_f = P.sb("ones_f", [128, 128], F32)
        P.memset(ones_f.all(), 1.0)
        qiT = [P.sb("qiT%d" % i, [128, 4, 128], BF16) for i in range(2)]
        wi = [P.sb("wi%d" % i, [128, 8], F32) for i in range(2)]
        h1b = [P.sb("h1b%d" % i, [128, 512], F32) for i in range(1)]
        NPM = 2
        Pm = [P.sb("Pm%d" % i, [128, 512], BF16) for i in range(NPM)]
        rd = [P.sb("rd%d" % i, [128, 512], F32) for i in range(1)]
        catT = P.sb("catT", [128, 8, 128], BF16)
        small = {n: P.sb("b_" + n, [128, 1], F32) for n in ("lo", "hi", "w", "mid", "nmid", "cnt", "cnt2", "sel")}
        nm_j = [(Buf("jD%d" % i), Buf("jA%d" % i)) for i in range(2)]
        pow2 = P.sb("pow2", [128, BIS_ITERS], F32)
        wall = P.sb("wall", [128, BIS_ITERS], F32)
        for k in range(BIS_ITERS):
            P.memset(pow2[:, k:k + 1], 2.0 ** -(k + 1))
        cn = {"Pm": 0}
        outs = []

        def stage1(i):
            steps = []
            nkt = 4 * i + 4
            nk = nkt * 128
            q_, qi_, wi_ = qT[i % 2], qiT[i % 2], wi[i % 2]
            S = small
            nm = nmall[i % 2]
            jD, jA = nm_j[i % 2]
            nm8 = nm.t[:].bitcast(I8)
            nD = ((nk // 2 + 127) // 128) * 128
            nA = nk - nD

            def loads():
                P.dma("sp", qi_.all(), DV(qi_blk, qi_blk.ap()[i]))
                P.dma("sp", wi_.all(), DV(wi_blk, wi_blk.ap()[i]))
                P.dma("sp", q_.all(), DV(q_blk, q_blk.ap()[i]))
            steps.append(loads)

            def idx(chunks, h0):
                for h in range(h0, h0 + 4):
                    for c5 in chunks:
                        sc = scv(c5 * 512, (c5 + 1) * 512)
                        pI = nring()
                        lo_p = (h % 2) * 64
                        P.mm(pI.all(), qi_[lo_p:lo_p + 64, h // 2, :], ki[lo_p:lo_p + 64, c5 * 512:(c5 + 1) * 512])
                        P.act(pI.all(), pI.all(), AF.Relu)
                        if h == 0:
                            P.ts(sc, pI.all(), wi_[:, 0:1], None, op0=ALU.mult)
                        else:
                            P.stt(sc, pI.all(), wi_[:, h:h + 1], sc, ALU.mult, ALU.add)
            for c5 in range(0, (i + 1) if not DBG.get('no_idx') else 0, 2):
                chunks = [c5] + ([c5 + 1] if c5 + 1 <= i else [])
                for h0 in (0, 4):
                    steps.append(lambda chunks=chunks, h0=h0: idx(chunks, h0))

            def bis_init():
                P.reduce(S["lo"].all(), scv(0, nk), ALU.min)
                P.tt(scv(nk - 512, nk), scv(nk - 512, nk), cm.all(), ALU.add)
                P.reduce(S["hi"].all(), scv(0, nk), ALU.max)
                P.ts(S["w"].all(), S["hi"].all(), 1.0, S["lo"].all(), op0=ALU.add, op1=ALU.subtract)
                P.ts(wall.all(), pow2.all(), S["w"].all(), None, op0=ALU.mult)
                P.tt(S["mid"].all(), S["lo"].all(), wall[:, 0:1], ALU.add)
            steps.append(bis_init)

            def bis_iter(k):
                P.ts(V(nm8[:, 0:nk], (jD,)), scv(0, nk), S["mid"].all(), 0.0, op0=ALU.is_ge, op1=ALU.add,
                     accum=S["cnt"].all())
                P.stt(S["sel"].all(), S["cnt"].all(), 255.5, wall[:, k:k + 1], ALU.is_ge, ALU.mult)
                if k + 1 < BIS_ITERS:
                    P.ts(S["mid"].all(), S["sel"].all(), S["lo"].all(), wall[:, k + 1:k + 2], op0=ALU.add, op1=ALU.add)
                P.tt(S["lo"].all(), S["lo"].all(), S["sel"].all(), ALU.add)
            for it in range(BIS_ITERS):
                steps.append(lambda it=it: bis_iter(it))

            def negmask(a0, a1):
                P.ts(V(nm.t[:, a0:a1], (nm.buf, jD, jA)), scv(a0, a1), S["lo"].all(), NEG, op0=ALU.is_lt, op1=ALU.mult)
            for a0 in range(0, nk, 2048):
                steps.append(lambda a0=a0: negmask(a0, min(nk, a0 + 2048)))
            return steps

        def stage2(i):
            steps = []
            nkt = 4 * i + 4
            q_, g_, hb, nm = qT[i % 2], gT[0], h1b[0], nmall[i % 2]

            def tile_(m):
                pos = nkt - 1 - m
                for g in range(2):
                    pL = nring()
                    P.mm(pL.all(), kT[:, g, m * 128:(m + 1) * 128], q_[:, 4 * g:4 * g + 4, :], start=True, stop=False)
                    P.mm(pL.all(), V(nm.t[:, m * 128:(m + 1) * 128], (nm.buf,) + nm_j[i % 2]), ident4.all(), start=False,
                         stop=(pos >= NPOS))
                    if pos < NPOS:
                        P.mm(pL.all(), c["ident"].all(), biasT[:, pos, g * 512:(g + 1) * 512], start=False, stop=True)
                    pm = Pm[cn["Pm"] % NPM]
                    cn["Pm"] += 1
                    P.act(pm.all(), pL.all(), AF.Exp)
                    P.mm(num[g].all(), Vs[:, m, g * 128:(g + 1) * 128], pm.all(), start=(m == 0), stop=(m == nkt - 1))
                    if m == 0:
                        P.copy(PmSum[g].all(), pm.all(), e="pool")
                    else:
                        P.tt(PmSum[g].all(), PmSum[g].all(), pm.all(), ALU.add, e="pool")
            for m in range(nkt if not DBG.get('no_att') else 1):
                steps.append(lambda m=m: tile_(m))

            def epi():
                P.dma("sp", g_.all(), DV(g_blk, g_blk.ap()[i]))
                den = [nring(), nring()]
                for g in range(2):
                    P.mm(den[g].all(), ones_f.all(), PmSum[g].all())
                for g in range(2):
                    r_ = rd[0]
                    P.act(r_.all(), den[g].all(), AF.Ln)
                    P.act(r_.all(), r_.all(), AF.Exp, scale=-1.0)
                    tmp = nring()
                    P.tt(tmp.all(), num[g].all(), r_.all(), ALU.mult)
                    P.tt(catT[:, 4 * g:4 * g + 4, :], tmp.all().re("p (h t) -> p h t", h=4), g_[:, 4 * g:4 * g + 4, :], ALU.mult)
                for half in range(2):
                    cs_ = slice(half * 512, (half + 1) * 512)
                    P.dma("sp", hb.all(), DV(h1_blk, h1_blk.ap()[i][:, cs_]))
                    po = nring()
                    for h in range(8):
                        P.mm(po.all(), catT[:, h, :], Wo[:, h, cs_], start=(h == 0), stop=(h == 7))
                    P.tt(hb.all(), po.all(), hb.all(), ALU.add)
                    P.dma("pool", DV(y, y.ap()[i][:, cs_]), hb.all(), primary=hb.buf)
                outs.append(hb.buf)
            steps.append(epi)
            return steps

        def interleave(sa, sb):
            na, nb = len(sa), len(sb)
            ia = ib = 0
            while ia < na or ib < nb:
                if ib >= nb or (ia < na and ia * nb <= ib * na):
                    sa[ia]()
                    ia += 1
                else:
                    sb[ib]()
                    ib += 1

        order = list(range(nblk - 1, -1, -1))
        for st_ in stage1(order[0]):
            st_()
        for oi, i in enumerate(order):
            s2 = stage2(i)
            s1 = stage1(order[oi + 1]) if oi + 1 < nblk else []
            fl = DBG.get('frontload', 0.25)
            if fl and s1:
                nidx = len(s1) - BIS_ITERS - 1 - ((4 * order[oi + 1] + 4) * 128 + 2047) // 2048
                cut2 = max(1, int(len(s2) * fl))
                interleave(s1[:nidx], s2[:cut2])
                interleave(s1[nidx:], s2[cut2:])
            else:
                interleave(s1, s2)
        P.finish(outs)
    return nc, P


def t5_bucket_np(d):
    d = np.maximum(d, 0)
    nf = np.maximum(d, 1).astype(np.float32)
    large = 16 + (np.log(nf / np.float32(16)) / np.float32(math.log(1024 / 16)) * np.float32(16)).astype(np.int32)
    large = np.minimum(large, 31)
    return np.where(d < 16, d, large)


def p3_consts(j):
    e = np.arange(GLEN)
    dist = (j - 3) * 128 + e - 127
    b = t5_bucket_np(dist)
    ohm = np.zeros((32, GLEN), np.float32)
    ohm[b, e] = 1.0
    t = np.arange(128)[:, None]
    r = np.arange(512)[None, :]
    s_rel = r - j * 128
    cmask = np.where(s_rel <= t, 0.0, -1e30).astype(np.float32).astype(ml_dtypes.bfloat16)
    return ohm, cmask


EV_Q, EV_K, EV_V, EV_GA, EV_U, EV_GB = 0, 512, 640, 768, 1280, 1792
WQ, WKD, WVD, WGA, WU, WGB = 0, 512, 768, 1024, 1536, 2048
TWO_PI = 2.0 * math.pi
CW1 = 6.28125
CW2 = TWO_PI - CW1


def sincos(P, wk, ph, s_out, c_out):
    ki, kf, r, m = wk["ki"], wk["kf"], wk["r"], wk["m"]
    P.ts(ki, ph, 1.0 / TWO_PI, None, op0=ALU.mult)
    P.copy(kf, ki)
    P.stt(r, kf, -CW1, ph, ALU.mult, ALU.add)
    P.stt(r, kf, -CW2, r, ALU.mult, ALU.add)
    for (outv, shift) in ((s_out, 0.0), (c_out, math.pi / 2)):
        if shift:
            P.ts(r, r, shift, None, op0=ALU.add)
        P.ts(m, r, math.pi, -TWO_PI, op0=ALU.is_gt, op1=ALU.mult)
        P.tt(r, r, m, ALU.add)
        P.ts(m, r, -math.pi, TWO_PI, op0=ALU.is_lt, op1=ALU.mult)
        P.tt(r, r, m, ALU.add)
        P.ts(r, r, math.pi, -math.pi, op0=ALU.min, op1=ALU.max)
        P.act(outv, r, AF.Sin)


def cmul(P, o_re, o_im, a_re, a_im, b_re, b_im, t1, t2):
    P.tt(t1, a_re, b_re, ALU.mult)
    P.tt(t2, a_im, b_im, ALU.mult)
    P.tt(o_re, t1, t2, ALU.subtract)
    P.tt(t1, a_re, b_im, ALU.mult)
    P.tt(t2, a_im, b_re, ALU.mult)
    P.tt(o_im, t1, t2, ALU.add)


def interleave_steps(sa, sb):
    na, nb = len(sa), len(sb)
    ia = ib = 0
    while ia < na or ib < nb:
        if ib >= nb or (ia < na and ia * nb <= ib * na):
            sa[ia]()
            ia += 1
        else:
            sb[ib]()
            ib += 1


def build_p2(mode="full", nsb=TPC // 512):
    full = mode == "full"
    nc = bass.Bass("TRN2", target_bir_lowering=False)
    x_own = dram_in(nc, "x_own", [TPC, D], F32)
    w_in = dram_in(nc, "w_in", [D, 2304], F32)
    ng_fm_d = dram_in(nc, "ng_fm", [128, 8], F32)
    a_re_f = dram_in(nc, "a_re_f", [2048], F32)
    a_im_f = dram_in(nc, "a_im_f", [2048], F32)
    ldt_f = dram_in(nc, "ldt_f", [2048], F32)
    a_re_s = dram_in(nc, "a_re_s", [128, 16], F32)
    a_im_s = dram_in(nc, "a_im_s", [128, 16], F32)
    ldt_s = dram_in(nc, "ldt_s", [128, 16], F32)
    b_blk_re = dram_in(nc, "b_blk_re", [128, 4, 128], F32)
    b_blk_im = dram_in(nc, "b_blk_im", [128, 4, 128], F32)
    if full:
        x_halo = dram_in(nc, "x_halo", [128, D], F32)
        firstneg = dram_in(nc, "firstneg", [128, 1], F32)
        w_out = dram_in(nc, "w_out", [D, D], F32)
        glu_w = dram_in(nc, "glu_w", [512, 1024], F32)
        qg2 = dram_in(nc, "qg2", [128, 1], F32)
        kg2 = dram_in(nc, "kg2", [128, 1], F32)
        sinks_row = dram_in(nc, "sinks_row", [128, 8], F32)
        rel_bias = dram_in(nc, "rel_bias", [32, 8], F32)
        oh_swa = dram_in(nc, "oh_swa", [32, 384], F32)
        neg_swa = dram_in(nc, "neg_swa", [8, 384], F32)
        c_blk_re = dram_in(nc, "c_blk_re", [128, 16, 128], F32)
        c_blk_im = dram_in(nc, "c_blk_im", [128, 16, 128], F32)
        d_fm_d = dram_in(nc, "d_fm", [128, 4], F32)
        glu_b_fm_d = dram_in(nc, "glu_b_fm", [128, 8], F32)
        eprev = dram_in(nc, "eprev", [3, 128, 32], F32)
        h1_out = dram_out(nc, "h1", [TPC, D], F32)
        gtab = nc.dram_tensor("gtab0", [8, 384], F32, kind="Internal")
        gtab_buf = Buf("gtab0")
    else:
        eloc = dram_out(nc, "eloc", [128, 32], F32)
    P = Prog(nc)
    with P.es:
        c = make_consts(P)
        wu0 = WU if full else 0
        g_fm = P.sb("g_fm", [128, 8], F32)
        P.dma("sp", g_fm.all(), DV(ng_fm_d))
        Wr = P.sb("Wr", [128, 16, 128], BF16)
        Ws = P.sb("Ws", [128, 16, 128], BF16)
        BBp = P.sb("BBp", [128, 4, 512], BF16)
        P.memset(BBp.all(), 0.0)
        A128r = P.sb("A128r", [128, 16], F32)
        A128i = P.sb("A128i", [128, 16], F32)
        A127r = P.sb("A127r", [128, 16], F32)
        A127i = P.sb("A127i", [128, 16], F32)
        A1r = P.sb("A1r", [128, 16], F32)
        A1i = P.sb("A1i", [128, 16], F32)
        Zr = P.sb("Zr", [128, 16], F32)
        Zi = P.sb("Zi", [128, 16], F32)
        if full:
            Vr = P.sb("Vr", [128, 16, 128], F32)
            Vi = P.sb("Vi", [128, 16, 128], F32)
            d_fm = P.sb("d_fm_s", [128, 4], F32)
            glu_b = P.sb("glu_b", [128, 8], F32)
            BT = [P.sb("BT%d" % i, [128, 1024], F32) for i in range(3)]
            esink = P.sb("esink", [128, 8], F32)
            gq = P.sb("gq", [128, 1], F32)
            gk = P.sb("gk", [128, 1], F32)
            ones2 = P.sb("ones2", [128, 128], BF16)
            P.dma("sp", d_fm.all(), DV(d_fm_d))
            P.dma("sp", glu_b.all(), DV(glu_b_fm_d))
            P.dma("sp", gq.all(), DV(qg2))
            P.dma("sp", gk.all(), DV(kg2))
            P.ts(gq.all(), gq.all(), 64.0 ** -0.5, None, op0=ALU.mult)
            P.dma("sp", esink.all(), DV(sinks_row))
            P.act(esink.all(), esink.all(), AF.Exp)
            P.memset(ones2.all(), 0.0)
            P.memset(ones2[0:64, 0:64], 1.0)
            P.memset(ones2[64:128, 64:128], 1.0)

        NR = 6
        ring = [P.ps("ring%d" % i, [128, 512]) for i in range(NR)]
        py_bank = P.ps("py_bank", [128, 512]) if full else None
        pst = P.ps("pst", [128, 1024], BF16)
        rr = {"ring": 0}

        def nring():
            for _ in range(NR):
                t = ring[rr["ring"] % NR]
                rr["ring"] += 1
                if t.buf.lw is None or t.buf.rd:
                    return t
            raise RuntimeError("PSUM ring exhausted: every bank holds unread results")

        jf = P.sb("jf", [128, 1], F32)
        P.iota(jf.all(), [[1, 1]], 0, 1)
        io_i = P.sb("io_i", [128, 128], F32)
        P.iota(io_i.all(), [[1, 128]], 0, 0)
        tri_f = P.sb("tri_f", [128, 128], F32)
        P.iota(tri_f.all(), [[1, 128]], 0, -1)
        tmpi_e = P.sb("tmpi_e", [128, 128], F32)
        P.iota(tmpi_e.all(), [[1, 128]], -127, 1)
        ncolW = 2560 if full else 512
        W = P.sb("W", [128, 8, ncolW], BF16)
        wbufs = [Buf("Wk%d" % k) for k in range(8)]

        def Wk(k, c0, n):
            return V(W.t[:, k, c0:c0 + n], (wbufs[k],))

        if full:
            Cre = P.sb("Cre", [128, 16, 128], BF16)
            Cim = P.sb("Cim", [128, 16, 128], BF16)
            Wo = P.sb("Wo", [128, 8, D], BF16)
            Wg = P.sb("Wg", [128, 4, 1024], BF16)

        def issue_weight_loads():
            for k in range(8):
                rows = slice(k * 128, (k + 1) * 128)

                def ld(dst0, n, src0):
                    P.dma("pool", V(W.t[:, k, dst0:dst0 + n], (wbufs[k],)), DV(w_in, w_in.ap()[rows, src0:src0 + n]),
                          primary=wbufs[k])
                if full:
                    ld(WU, 512, EV_U)
                    ld(WQ, 512, EV_Q)
                    for g in range(2):
                        for dup in range(2):
                            ld(WKD + g * 128 + dup * 64, 64, EV_K + g * 64)
                            ld(WVD + g * 128 + dup * 64, 64, EV_V + g * 64)
                    ld(WGA, 512, EV_GA)
                    ld(WGB, 512, EV_GB)
                else:
                    ld(0, 512, EV_U)
            if full:
                P.dma("pool", Cre.all(), DV(c_blk_re))
                P.dma("pool", Cim.all(), DV(c_blk_im))
                for k in range(4):
                    P.dma("pool", Wg[:, k, :], DV(glu_w, glu_w.ap()[k * 128:(k + 1) * 128, :]))
                for k in range(8):
                    P.dma("pool", Wo[:, k, :], DV(w_out, w_out.ap()[k * 128:(k + 1) * 128, :]))
                P.ts(Cim.all(), Cim.all(), -1.0, None, op0=ALU.mult, e="pool")
        tri = P.sb("tri", [128, 128], BF16)
        P.ts(tri.all(), tri_f.all(), 0.0, None, op0=ALU.is_ge)
        with contextlib.ExitStack() as es2:
            old_es, P.es = P.es, es2
            N = 2048
            T = {n: P.sb("t_" + n, [128, N], F32) for n in ("ard", "ang", "a", "b", "mag", "kf", "r", "m", "s", "c")}
            Tki = P.sb("t_ki", [128, N], I32)

            def scw(n):
                return {"ki": Tki[:, 0:n], "kf": T["kf"][:, 0:n], "r": T["r"][:, 0:n], "m": T["m"][:, 0:n]}

            P.dma("sp", T["a"].all(), DV(ldt_f, dap(ldt_f, 0, [[0, 128], [1, N]])))
            P.act(T["a"].all(), T["a"].all(), AF.Exp)
            P.dma("sp", T["ard"].all(), DV(a_re_f, dap(a_re_f, 0, [[0, 128], [1, N]])))
            P.dma("sp", T["ang"].all(), DV(a_im_f, dap(a_im_f, 0, [[0, 128], [1, N]])))
            issue_weight_loads()
            P.tt(T["ard"].all(), T["ard"].all(), T["a"].all(), ALU.mult)
            P.tt(T["ang"].all(), T["ang"].all(), T["a"].all(), ALU.mult)
            P.ts(T["b"].all(), T["ard"].all(), jf.all(), -1.0, op0=ALU.mult, op1=ALU.mult)
            P.act(T["mag"].all(), T["b"].all(), AF.Exp)
            P.ts(T["b"].all(), T["ang"].all(), jf.all(), None, op0=ALU.mult)
            sincos(P, scw(N), T["b"].all(), T["s"].all(), T["c"].all())
            P.tt(Wr.all().re("p a b -> p (a b)"), T["mag"].all(), T["c"].all(), ALU.mult)
            P.tt(Ws.all().re("p a b -> p (a b)"), T["mag"].all(), T["s"].all(), ALU.mult)
            Fr, Fi = T["ard"], T["ang"]
            P.act(T["mag"].all(), T["ard"].all(), AF.Exp)
            sincos(P, scw(N), T["ang"].all(), T["s"].all(), T["c"].all())
            are_row, aim_row = T["kf"], T["r"]
            P.dma("sp", are_row.all(), DV(a_re_f, dap(a_re_f, 0, [[0, 128], [1, N]])))
            P.dma("sp", aim_row.all(), DV(a_im_f, dap(a_im_f, 0, [[0, 128], [1, N]])))
            P.tt(T["c"].all(), T["mag"].all(), T["c"].all(), ALU.mult)
            P.tt(T["s"].all(), T["mag"].all(), T["s"].all(), ALU.mult)
            P.ts(T["c"].all(), T["c"].all(), -1.0, None, op0=ALU.add)
            P.tt(T["a"].all(), are_row.all(), are_row.all(), ALU.mult)
            P.tt(T["b"].all(), aim_row.all(), aim_row.all(), ALU.mult)
            P.tt(T["a"].all(), T["a"].all(), T["b"].all(), ALU.add)
            P.recip(T["a"].all(), T["a"].all())
            P.tt(T["b"].all(), T["c"].all(), are_row.all(), ALU.mult)
            P.tt(T["m"].all(), T["s"].all(), aim_row.all(), ALU.mult)
            P.tt(T["b"].all(), T["b"].all(), T["m"].all(), ALU.add)
            P.tt(Fr.all(), T["b"].all(), T["a"].all(), ALU.mult)
            P.tt(T["b"].all(), T["s"].all(), are_row.all(), ALU.mult)
            P.tt(T["m"].all(), T["c"].all(), aim_row.all(), ALU.mult)
            P.tt(T["b"].all(), T["b"].all(), T["m"].all(), ALU.subtract)
            P.tt(Fi.all(), T["b"].all(), T["a"].all(), ALU.mult)
            bre = T["s"].all()[:, 0:512].re("p (q s) -> p q s", q=4)
            bim = T["c"].all()[:, 0:512].re("p (q s) -> p q s", q=4)
            P.dma("sp", bre, DV(b_blk_re))
            P.dma("sp", bim, DV(b_blk_im))
            t1 = T["m"].all()[:, 0:128]
            t2 = T["m"].all()[:, 128:256]
            for st4 in range(4):
                ps_ = slice(32 * st4, 32 * st4 + 32)
                for q in range(4):
                    st = 4 * q + st4
                    fr = Fr[ps_, st * 128:(st + 1) * 128]
                    fi = Fi[ps_, st * 128:(st + 1) * 128]
                    P.tt(t1[ps_, :], bre[ps_, q, :], fr, ALU.mult)
                    P.tt(t2[ps_, :], bim[ps_, q, :], fi, ALU.mult)
                    co = (st4 % 2) * 256
                    P.tt(BBp[ps_, q, co:co + 128], t1[ps_, :], t2[ps_, :], ALU.subtract)
                    P.tt(t1[ps_, :], bre[ps_, q, :], fi, ALU.mult)
                    P.tt(t2[ps_, :], bim[ps_, q, :], fr, ALU.mult)
                    P.tt(BBp[ps_, q, co + 128:co + 256], t1[ps_, :], t2[ps_, :], ALU.add)
            sp_ = {n: P.sb("sp_" + n, [128, 16], F32) for n in ("ard", "ang", "dt", "a", "b", "mag", "kf", "r", "m", "s", "c")}
            spki = P.sb("sp_ki", [128, 16], I32)
            P.dma("sp", sp_["dt"].all(), DV(ldt_s))
            P.act(sp_["dt"].all(), sp_["dt"].all(), AF.Exp)
            P.dma("sp", sp_["ard"].all(), DV(a_re_s))
            P.dma("sp", sp_["ang"].all(), DV(a_im_s))
            P.tt(sp_["ard"].all(), sp_["ard"].all(), sp_["dt"].all(), ALU.mult)
            P.tt(sp_["ang"].all(), sp_["ang"].all(), sp_["dt"].all(), ALU.mult)
            spw = {"ki": spki.all(), "kf": sp_["kf"].all(), "r": sp_["r"].all(), "m": sp_["m"].all()}
            for (mult_, orr, oii) in ((128.0, A128r, A128i), (127.0, A127r, A127i), (1.0, A1r, A1i)):
                P.ts(sp_["a"].all(), sp_["ard"].all(), mult_, None, op0=ALU.mult)
                P.act(sp_["mag"].all(), sp_["a"].all(), AF.Exp)
                P.ts(sp_["b"].all(), sp_["ang"].all(), mult_, None, op0=ALU.mult)
                sincos(P, spw, sp_["b"].all(), sp_["s"].all(), sp_["c"].all())
                P.tt(orr.all(), sp_["mag"].all(), sp_["c"].all(), ALU.mult)
                P.tt(oii.all(), sp_["mag"].all(), sp_["s"].all(), ALU.mult)
            if full:
                for st in range(16):
                    P.act(T["mag"][:, st * 128:(st + 1) * 128], io_i.all(), AF.Exp, scale=sp_["ard"][:, st:st + 1])
                    P.ts(T["b"][:, st * 128:(st + 1) * 128], io_i.all(), sp_["ang"][:, st:st + 1], None, op0=ALU.mult)
                sincos(P, scw(N), T["b"].all(), T["s"].all(), T["c"].all())
                P.tt(Vr.all().re("p a b -> p (a b)"), T["mag"].all(), T["c"].all(), ALU.mult)
                P.tt(Vi.all().re("p a b -> p (a b)"), T["mag"].all(), T["s"].all(), ALU.mult)
                ep = [P.sb("ep%d" % i, [128, 32], F32) for i in range(3)]
                for i in range(3):
                    P.dma("sp", ep[i].all(), DV(eprev, eprev.ap()[i]))
                pa = [P.sb("pa%d" % i, [128, 16], F32) for i in range(8)]
                cur_r, cur_i = A128r, A128i
                for sqi in range(4):
                    nr, ni = pa[2 * (sqi % 2)], pa[2 * (sqi % 2) + 1]
                    cmul(P, nr.all(), ni.all(), cur_r.all(), cur_i.all(), cur_r.all(), cur_i.all(), pa[4].all(), pa[5].all())
                    cur_r, cur_i = nr, ni
                ar, ai = pa[6], pa[7]
                xr = P.sb("xsr", [128, 16], F32)
                xi = P.sb("xsi", [128, 16], F32)
                cmul(P, ar.all(), ai.all(), cur_r.all(), cur_i.all(), ep[2][:, 0:16], ep[2][:, 16:32], pa[4].all(), pa[5].all())
                P.tt(ar.all(), ar.all(), ep[1][:, 0:16], ALU.add)
                P.tt(ai.all(), ai.all(), ep[1][:, 16:32], ALU.add)
                cmul(P, xr.all(), xi.all(), cur_r.all(), cur_i.all(), ar.all(), ai.all(), pa[4].all(), pa[5].all())
                P.tt(xr.all(), xr.all(), ep[0][:, 0:16], ALU.add)
                P.tt(xi.all(), xi.all(), ep[0][:, 16:32], ALU.add)
                cmul(P, Zr.all(), Zi.all(), A1r.all(), A1i.all(), xr.all(), xi.all(), pa[4].all(), pa[5].all())
                def swa_bias_setup():
                    Jf = P.sb("Jf", [128, 128], F32)
                    P.ts(Jf.all(), tmpi_e.all(), 0.0, None, op0=ALU.is_equal)
                    rb = P.sb("rb", [32, 8], F32)
                    ohs = P.sb("ohs", [32, 384], F32)
                    ngs = P.sb("ngs", [8, 384], F32)
                    P.dma("sp", ngs.all(), DV(neg_swa))
                    gsb = P.sb("gsb", [8, 384], F32)
                    fneg = P.sb("fneg", [128, 1], F32)
                    hk = [P.sb("hk%d" % i, [128, 8, 128], F32) for i in range(2)]
                    P.dma("sp", rb.all(), DV(rel_bias))
                    P.dma("sp", ohs.all(), DV(oh_swa))
                    P.dma("sp", fneg.all(), DV(firstneg))
                    pr_ = nring()
                    P.mm(pr_[0:8, 0:384], rb.all(), ohs.all())
                    P.tt(gsb.all(), pr_[0:8, 0:384], ngs.all(), ALU.add)
                    P.dma("sp", DV(gtab, buf=gtab_buf), gsb.all(), primary=gtab_buf)
                    for kt in range(2):
                        hkt = hk[kt]
                        off = 128 if kt == 0 else 0
                        P.dma("sp", hkt.all(), DV(gtab, dap(gtab, off, [[1, 128], [384, 8], [1, 128]]), buf=gtab_buf))
                        for half in range(2):
                            pr_ = nring()
                            P.mm(pr_.all(), Jf.all(), hkt[:, half * 4:(half + 1) * 4, :])
                            P.copy(BT[kt][:, half * 512:(half + 1) * 512], pr_.all())
                    P.ts(BT[2].all(), BT[0].all(), fneg.all(), None, op0=ALU.add)
            else:
                P.memset(Zr.all(), 0.0)
                P.memset(Zi.all(), 0.0)
            barrier(P)
            P.es = old_es
        if full and not DBG.get('no_bias'):
            with contextlib.ExitStack() as es3:
                old_es, P.es = P.es, es3
                swa_bias_setup()
                barrier(P)
                P.es = old_es

        wk = make_norm_work(P, "n_", pst)
        nxb = 4 if full else 2
        xb = [P.sb("xb%d" % i, [128, D], F32) for i in range(nxb)]
        hnT = P.sb("hnT", [128, 8, 512], BF16)
        uT = P.sb("uT", [128, 4, 512], BF16)
        tm = [P.sb("tm%d" % i, [128, 512], F32) for i in range(4)]
        tmB = [P.sb("tmB%d" % i, [128, 512], F32) for i in range(4)]
        tmsets = [tm, tm] if full else [tm, tmB]
        vre = [P.sb("vre%d" % i, [128, 4, 128], BF16) for i in range(2)]
        vim = [P.sb("vim%d" % i, [128, 4, 128], BF16) for i in range(2)]
        cnt = {"x": 0, "v": 0, "o": 0}
        outs = []
        if full:
            qT = P.sb("qTm", [128, 8, 512], BF16)
            P.memset(qT.all(), 0.0)
            kTd = P.sb("kTd", [128, 2, 640], BF16)
            Vd = P.sb("Vd", [128, 5, 256], BF16)
            gaT = P.sb("gaT", [128, 4, 512], BF16)
            gbT = P.sb("gbT", [128, 4, 512], BF16)
            caT = gaT
            cbT = gbT
            gyT = P.sb("gyT", [128, 4, 512], BF16)
            sqb = [P.sb("sqb%d" % i, [128, 512], BF16) for i in range(2)]
            sdb = [tm[2], tm[3]]
            Sb = [P.sb("Sb%d" % i, [128, 512], F32) for i in range(2)]
            PT = [P.sb("PT%d" % i, [128, 2, 512], BF16) for i in range(2)]
            dtot = Sb[1]
            xre = [P.sb("xre%d" % i, [128, 4, 128], BF16) for i in range(1)]
            xim = [P.sb("xim%d" % i, [128, 4, 128], BF16) for i in range(1)]
            ta = [P.sb("ta%d" % i, [128, 16], F32) for i in range(6)]
            gl = {"y": tm[1], "t": tm[0]}
            sg = [Sb[0]]
        else:
            esum = P.ps("esum", [128, 32])
            Sr = P.sb("Sr", [128, 16], F32)
            Si = P.sb("Si", [128, 16], F32)
            ta = [P.sb("ta%d" % i, [128, 16], F32) for i in range(6)]
            P.memset(Sr.all(), 0.0)
            P.memset(Si.all(), 0.0)

        def kv_project(blk_cols, kslot, vslot):
            for g in range(2):
                pr_ = nring()
                for k in range(8):
                    P.mm(pr_[:, 0:128], Wk(k, WKD + g * 128, 128), hnT[:, k, blk_cols], start=(k == 0), stop=(k == 7))
                s = sqb[g]
                P.act(s[:, 0:128], pr_[:, 0:128], AF.Square)
                p2 = nring()
                P.mm(p2[:, 0:128], ones2.all(), s[:, 0:128])
                d_ = sdb[g]
                P.act(d_[:, 0:128], p2[:, 0:128], AF.Ln, bias=wk["epsb"].all(), scale=1.0 / 64)
                P.act(d_[:, 0:128], d_[:, 0:128], AF.Exp, scale=-0.5)
                P.stt(kTd[:, g, kslot * 128:(kslot + 1) * 128], pr_[:, 0:128], gk.all(), d_[:, 0:128], ALU.mult, ALU.mult)
            pr_ = nring()
            for k in range(8):
                P.mm(pr_[:, 0:256], hnT[:, k, blk_cols], Wk(k, WVD, 256), start=(k == 0), stop=(k == 7))
            P.copy(Vd[:, vslot, :], pr_[:, 0:256], e="act")

        if full:
            xh = xb[nxb - 1]
            P.dma("sp", xh.all(), DV(x_halo))
            rmsnorm_to_fm(P, c, xh.all(), hnT[:, :, 0:128], g_fm, wk)
            kv_project(slice(0, 128), 0, 0)

        for sb_ in range(nsb):
            xs = []
            for bl in range(4):
                t0 = sb_ * 512 + bl * 128
                x = xb[cnt["x"] % nxb]
                cnt["x"] += 1
                xs.append(x)
                P.dma("sp", x.all(), DV(x_own, x_own.ap()[t0:t0 + 128, :]))
                rmsnorm_to_fm(P, c, x.all(), hnT[:, :, bl * 128:(bl + 1) * 128], g_fm, wk)
            for q in range(4):
                pr_ = nring()
                for k in range(8):
                    P.mm(pr_.all(), Wk(k, wu0 + q * 128, 128), hnT[:, k, :], start=(k == 0), stop=(k == 7))
                P.copy(uT[:, q, :], pr_.all(), e="act")
            if full:
                for t in range(4):
                    pr_ = nring()
                    for k in range(8):
                        P.mm(pr_.all(), Wk(k, WQ + t * 128, 128), hnT[:, k, :], start=(k == 0), stop=(k == 7))
                    s = sqb[t % 2]
                    P.act(s.all(), pr_.all(), AF.Square)
                    p2 = nring()
                    P.mm(p2.all(), ones2.all(), s.all())
                    d_ = sdb[t % 2]
                    P.act(d_.all(), p2.all(), AF.Ln, bias=wk["epsb"].all(), scale=1.0 / 64)
                    P.act(d_.all(), d_.all(), AF.Exp, scale=-0.5)
                    P.stt(qT[0:64, 2 * t, :], pr_[0:64, :], gq[0:64, :], d_[0:64, :], ALU.mult, ALU.mult)
                    P.stt(qT[64:128, 2 * t + 1, :], pr_[64:128, :], gq[64:128, :], d_[64:128, :], ALU.mult, ALU.mult)
                for (dst, c0) in ((gaT, WGA), (gbT, WGB)):
                    for t in range(4):
                        pr_ = nring()
                        for k in range(8):
                            P.mm(pr_.all(), Wk(k, c0 + t * 128, 128), hnT[:, k, :], start=(k == 0), stop=(k == 7))
                        P.act(dst[:, t, :], pr_.all(), AF.Silu)
                for bl in range(4):
                    kv_project(slice(bl * 128, (bl + 1) * 128), bl + 1, bl + 1)
                swa_steps = []

                def swa_step(bl, g, sb_=sb_):
                    qcols = slice(bl * 128, (bl + 1) * 128)
                    first = (sb_ == 0 and bl == 0)
                    if True:
                        banks = [nring(), nring()]
                        for kt in range(2):
                            kslot = bl + kt
                            for hh in range(4):
                                h = 4 * g + hh
                                lp = slice((h % 2) * 64, (h % 2) * 64 + 64)
                                P.mm(banks[kt][:, hh * 128:(hh + 1) * 128], kTd[:, g, kslot * 128:(kslot + 1) * 128],
                                     qT[:, h, qcols])
                        pt = PT[g]
                        for kt in range(2):
                            bt = BT[2] if (first and kt == 0) else BT[kt]
                            P.tt(Sb[kt].all(), banks[kt].all(), bt[:, g * 512:(g + 1) * 512], ALU.add)
                            P.act(pt[:, kt, :], Sb[kt].all(), AF.Exp)
                        pn, pd = nring(), nring()
                        for kt in range(2):
                            P.mm(pn.all(), Vd[:, bl + kt, g * 128:(g + 1) * 128], pt[:, kt, :], start=(kt == 0), stop=(kt == 1))
                        for kt in range(2):
                            P.mm(pd.all(), c["ones"].all(), pt[:, kt, :], start=(kt == 0), stop=(kt == 1))
                        for hh in range(4):
                            h = 4 * g + hh
                            P.ts(dtot[:, hh * 128:(hh + 1) * 128], pd[:, hh * 128:(hh + 1) * 128], esink[:, h:h + 1], None,
                                 op0=ALU.add)
                        P.act(dtot.all(), dtot.all(), AF.Ln)
                        P.act(dtot.all(), dtot.all(), AF.Exp, scale=-1.0)
                        for hh in range(4):
                            h = 4 * g + hh
                            lp = slice((h % 2) * 64, (h % 2) * 64 + 64)
                            o = caT[lp, h // 2, qcols]
                            tmo = Sb[0][lp, hh * 128:(hh + 1) * 128]
                            P.tt(tmo, pn[lp, hh * 128:(hh + 1) * 128], dtot[lp, hh * 128:(hh + 1) * 128], ALU.mult)
                            P.tt(o, tmo, gaT[lp, h // 2, qcols], ALU.mult)
                def halo_step():
                    P.copy(kTd[:, :, 0:128], kTd[:, :, 512:640], e="pool")
                    P.copy(Vd[:, 0, :], Vd[:, 4, :], e="pool")
                for bl in range(0 if DBG.get('no_swa') else 4):
                    for g in range(2):
                        swa_steps.append(lambda bl=bl, g=g: swa_step(bl, g))
                swa_steps.append(halo_step)
            ssm_steps = []
            pyd = {}

            def ssm_A_step(bl, q):
                tcols = slice(bl * 128, (bl + 1) * 128)
                if True:
                    bu = [nring(), nring()]
                    for hb_ in range(2):
                        ps_ = slice(64 * hb_, 64 * hb_ + 64)
                        P.mm(bu[hb_].all(), uT[ps_, q, tcols], BBp[ps_, q, :])
                    vr_, vi_ = vre[(4 * bl + q) % 2], vim[(4 * bl + q) % 2]
                    tmA = tmsets[(4 * bl + q) % 2]
                    for hb_ in range(2):
                        bre_ = bu[hb_].all().re("p (a c s) -> p a c s", a=2, c=2)[:, :, 0, :]
                        bim_ = bu[hb_].all().re("p (a c s) -> p a c s", a=2, c=2)[:, :, 1, :]
                        sts = slice(4 * q + 2 * hb_, 4 * q + 2 * hb_ + 2)
                        wr_, ws_ = Wr[:, sts, :], Ws[:, sts, :]
                        o = slice(hb_ * 256, hb_ * 256 + 256)
                        P.tt(tmA[0][:, o].re("p (a s) -> p a s", a=2), bre_, wr_, ALU.mult)
                        P.tt(tmA[1][:, o].re("p (a s) -> p a s", a=2), bim_, ws_, ALU.mult)
                        P.tt(tmA[2][:, o].re("p (a s) -> p a s", a=2), bim_, wr_, ALU.mult)
                        P.tt(tmA[3][:, o].re("p (a s) -> p a s", a=2), bre_, ws_, ALU.mult)
                    P.tt(vr_.all().re("p a s -> p (a s)"), tmA[0].all(), tmA[1].all(), ALU.add, e="pool")
                    P.tt(vi_.all().re("p a s -> p (a s)"), tmA[2].all(), tmA[3].all(), ALU.subtract, e="pool")
                    if full and DBG.get('no_ssm2'):
                        return
                    if not full:
                        return
            def esum_step(bl, q):
                vr_, vi_ = vre[(4 * bl + q) % 2], vim[(4 * bl + q) % 2]
                for st4 in range(4):
                    st = 4 * q + st4
                    P.mm(esum[:, st:st + 1], vr_[:, st4, :], c["ones"][:, 0:1])
                    P.mm(esum[:, 16 + st:17 + st], vi_[:, st4, :], c["ones"][:, 0:1])

            def ssm_B_step(bl, q):
                tcols = slice(bl * 128, (bl + 1) * 128)
                vr_, vi_ = vre[(4 * bl + q) % 2], vim[(4 * bl + q) % 2]
                if True:
                    csr, csi = nring(), nring()
                    for st4 in range(4):
                        P.mm(csr[:, st4 * 128:(st4 + 1) * 128], vr_[:, st4, :], tri.all())
                        P.mm(csi[:, st4 * 128:(st4 + 1) * 128], vi_[:, st4, :], tri.all())
                    xr_, xi_ = xre[0], xim[0]
                    for st4 in range(4):
                        st = 4 * q + st4
                        cr = csr[:, st4 * 128:(st4 + 1) * 128]
                        ci = csi[:, st4 * 128:(st4 + 1) * 128]
                        o = slice(st4 * 128, (st4 + 1) * 128)
                        P.stt(tmB[0][:, o], cr, Zr[:, st:st + 1], Vr[:, st, :], ALU.add, ALU.mult)
                        P.stt(tmB[1][:, o], ci, Zi[:, st:st + 1], Vi[:, st, :], ALU.add, ALU.mult)
                        P.stt(tmB[2][:, o], cr, Zr[:, st:st + 1], Vi[:, st, :], ALU.add, ALU.mult)
                        P.stt(tmB[3][:, o], ci, Zi[:, st:st + 1], Vr[:, st, :], ALU.add, ALU.mult)
                    sq_ = slice(4 * q, 4 * q + 4)
                    cr127 = csr.all().re("p (a s) -> p a s", a=4)[:, :, 127]
                    ci127 = csi.all().re("p (a s) -> p a s", a=4)[:, :, 127]
                    P.tt(ta[0][:, 0:4], Zr[:, sq_], cr127, ALU.add)
                    P.tt(ta[1][:, 0:4], Zi[:, sq_], ci127, ALU.add)
                    cmul(P, Zr[:, sq_], Zi[:, sq_], A128r[:, sq_], A128i[:, sq_], ta[0][:, 0:4], ta[1][:, 0:4],
                         ta[2][:, 0:4], ta[3][:, 0:4])
                    P.tt(xr_.all().re("p a s -> p (a s)"), tmB[0].all(), tmB[1].all(), ALU.subtract, e="pool")
                    P.tt(xi_.all().re("p a s -> p (a s)"), tmB[2].all(), tmB[3].all(), ALU.add, e="pool")
                    pyd['py'] = py_bank
                    py = py_bank
                    for st4 in range(4):
                        st = 4 * q + st4
                        P.mm(py[:, q * 128:(q + 1) * 128], Cre[:, st, :], xr_[:, st4, :], start=(st4 == 0), stop=False)
                        P.mm(py[:, q * 128:(q + 1) * 128], Cim[:, st, :], xi_[:, st4, :], start=False, stop=(st4 == 3))
            def ssm_tail_step(bl):
                tcols = slice(bl * 128, (bl + 1) * 128)
                if not full:
                    P.copy(ta[4].all(), esum[:, 0:16])
                    P.copy(ta[5].all(), esum[:, 16:32])
                    cmul(P, ta[0].all(), ta[1].all(), A127r.all(), A127i.all(), ta[4].all(), ta[5].all(), ta[2].all(), ta[3].all())
                    cmul(P, ta[4].all(), ta[5].all(), A128r.all(), A128i.all(), Sr.all(), Si.all(), ta[2].all(), ta[3].all())
                    P.tt(Sr.all(), ta[0].all(), ta[4].all(), ALU.add)
                    P.tt(Si.all(), ta[1].all(), ta[5].all(), ALU.add)
                    return
                if DBG.get('no_ssm2'):
                    return
                for q in range(4):
                    P.stt(gl["y"][:, q * 128:(q + 1) * 128], uT[:, q, tcols], d_fm[:, q:q + 1], pyd['py'][:, q * 128:(q + 1) * 128],
                          ALU.mult, ALU.add)
                P.act(gyT[:, :, tcols], gl["y"].all().re("p (q t) -> p q t", q=4), AF.Gelu_apprx_tanh)
            items = [(bl, q) for bl in range(4) for q in range(4)]
            if full and not DBG.get('no_ssm2'):
                for k in range(2):
                    ssm_steps.append(lambda k=k: ssm_A_step(*items[k]))
                for k in range(16):
                    ssm_steps.append(lambda k=k: ssm_B_step(*items[k]))
                    if k + 2 < 16:
                        ssm_steps.append(lambda k=k: ssm_A_step(*items[k + 2]))
                    if items[k][1] == 3:
                        ssm_steps.append(lambda k=k: ssm_tail_step(items[k][0]))
            elif not full:
                ssm_steps.append(lambda: ssm_A_step(*items[0]))
                for k in range(16):
                    if k + 1 < 16:
                        ssm_steps.append(lambda k=k: ssm_A_step(*items[k + 1]))
                    ssm_steps.append(lambda k=k: esum_step(*items[k]))
                    if items[k][1] == 3:
                        ssm_steps.append(lambda k=k: ssm_tail_step(items[k][0]))
            else:
                for bl in range(4):
                    for q in range(4):
                        ssm_steps.append(lambda bl=bl, q=q: ssm_A_step(bl, q))
            interleave_steps(swa_steps if full else [], ssm_steps)
            if not full:
                continue
            for f in range(0 if DBG.get('no_glu') else 4):
                pa_, pb_ = nring(), nring()
                for q in range(4):
                    P.mm(pa_.all(), Wg[:, q, f * 128:(f + 1) * 128], gyT[:, q, :], start=(q == 0), stop=(q == 3))
                for q in range(4):
                    P.mm(pb_.all(), Wg[:, q, 512 + f * 128:512 + (f + 1) * 128], gyT[:, q, :], start=(q == 0), stop=(q == 3))
                s_ = sg[0]
                P.act(s_.all(), pb_.all(), AF.Sigmoid, bias=glu_b[:, 4 + f:5 + f])
                P.stt(gl["t"].all(), pa_.all(), glu_b[:, f:f + 1], s_.all(), ALU.add, ALU.mult)
                P.tt(cbT[:, f, :], gl["t"].all(), gbT[:, f, :], ALU.mult)
            for bl in range(4):
                tcols = slice(bl * 128, (bl + 1) * 128)
                t0 = sb_ * 512 + bl * 128
                o_ = xs[bl]
                for half in range(0 if DBG.get('no_out') else 2):
                    po = nring()
                    for k in range(8):
                        lhs = caT[:, k, tcols] if k < 4 else cbT[:, k - 4, tcols]
                        P.mm(po.all(), lhs, Wo[:, k, half * 512:(half + 1) * 512], start=(k == 0), stop=(k == 7))
                    P.tt(o_[:, half * 512:(half + 1) * 512], po.all(), xs[bl][:, half * 512:(half + 1) * 512], ALU.add)
                P.dma("pool", DV(h1_out, h1_out.ap()[t0:t0 + 128, :]), o_.all(), primary=o_.buf)
                outs.append(o_.buf)
        if not full:
            eo = P.sb("eo", [128, 32], F32)
            P.copy(eo[:, 0:16], Sr.all())
            P.copy(eo[:, 16:32], Si.all())
            P.dma("pool", DV(eloc), eo.all(), primary=eo.buf)
            outs.append(eo.buf)
        P.finish(outs)
    return nc, P


def swa_onehot():
    e = np.arange(384)
    d = e - 127
    valid = (d >= 0) & (d < 128)
    oh = np.zeros((32, 384), np.float32)
    oh[t5_bucket_np(d)[valid], e[valid]] = 1.0
    neg = np.tile(np.where(valid, 0.0, NEG).astype(np.float32)[None, :], (8, 1))
    return oh, neg


def l0_inputs(inp, b, r, eprev=None, full=True):
    f32 = np.float32
    x = inp["x"][b]
    d = {}
    d["x_own"] = np.ascontiguousarray(x[r * TPC:(r + 1) * TPC])
    d["w_in"] = np.ascontiguousarray(inp["ev_w_in"][0])
    d["ng_fm"] = np.ascontiguousarray(inp["norm_g"][0].reshape(8, 128).T)
    a_re = inp["ev_ssm_a_re"][0]
    a_im = inp["ev_ssm_a_im"][0]
    ldt = np.repeat(inp["ev_ssm_log_dt"][0], 64)
    d["a_re_f"] = np.ascontiguousarray(a_re.reshape(2048))
    d["a_im_f"] = np.ascontiguousarray(a_im.reshape(2048))
    d["ldt_f"] = np.ascontiguousarray(ldt)
    d["a_re_s"] = np.ascontiguousarray(a_re.reshape(16, 128).T)
    d["a_im_s"] = np.ascontiguousarray(a_im.reshape(16, 128).T)
    d["ldt_s"] = np.ascontiguousarray(ldt.reshape(16, 128).T)
    for nm, src in (("b_blk_re", inp["ev_ssm_b_re"][0]), ("b_blk_im", inp["ev_ssm_b_im"][0])):
        blk = np.zeros((4, 2, 16, 4, 2, 64), f32)
        s6 = src.reshape(4, 4, 2, 64, 16)
        for g2 in range(2):
            blk[:, g2, :, :, g2, :] = s6[:, :, g2].transpose(1, 3, 0, 2)
        d[nm] = np.ascontiguousarray(blk.reshape(128, 4, 128))
    if not full:
        return d
    d["x_halo"] = np.ascontiguousarray(x[r * TPC - 128:r * TPC]) if r > 0 else np.zeros((128, D), f32)
    d["firstneg"] = np.full((128, 1), NEG if r == 0 else 0.0, f32)
    d["w_out"] = np.ascontiguousarray(inp["ev_w_out"][0])
    d["glu_w"] = np.ascontiguousarray(inp["ev_glu_w"][0])
    d["qg2"] = np.ascontiguousarray(np.tile(inp["ev_q_norm_g"][0], 2)[:, None])
    d["kg2"] = np.ascontiguousarray(np.tile(inp["ev_k_norm_g"][0], 2)[:, None])
    d["sinks_row"] = np.ascontiguousarray(np.tile(inp["ev_sinks"][0][None, :], (128, 1)))
    d["rel_bias"] = np.ascontiguousarray(inp["rel_bias"])
    d["oh_swa"], d["neg_swa"] = swa_onehot()
    for nm, src in (("c_blk_re", inp["ev_ssm_c_re"][0]), ("c_blk_im", inp["ev_ssm_c_im"][0])):
        blk = np.zeros((2, 64, 16, 8, 16), f32)
        s5 = src.reshape(16, 2, 16, 64)
        for st in range(16):
            for g2 in range(2):
                blk[g2, :, st, 2 * (st % 4) + g2, :] = s5[st, g2].T
        d[nm] = np.ascontiguousarray(blk.reshape(128, 16, 128))
    d["d_fm"] = np.ascontiguousarray(inp["ev_ssm_d"][0].reshape(4, 128).T)
    d["glu_b_fm"] = np.ascontiguousarray(inp["ev_glu_b"][0].reshape(8, 128).T)
    d["eprev"] = np.zeros((3, 128, 32), f32) if eprev is None else np.ascontiguousarray(eprev)
    return d


_PROGS = {}


def _prog(name):
    if name not in _PROGS:
        if name == "p1":
            _PROGS[name] = build_p2("p1")[0]
        elif name == "p2":
            _PROGS[name] = build_p2("full")[0]
        elif name == "p2b":
            _PROGS[name] = build_p2b()[0]
        elif name == "p3":
            _PROGS[name] = build_p3()[0]
    return _PROGS[name]


def _run(name, maps):
    return run_bass_kernel_spmd(_prog(name), maps, core_ids=list(range(NCORES))).results


def kernel(**inputs):
    inp = {k: np.asarray(v) for k, v in inputs.items()}
    f32 = np.float32
    r1 = _run("p1", [l0_inputs(inp, c // 4, c % 4, full=False) for c in range(NCORES)])
    eloc = [np.asarray(r1[c]["eloc"], f32) for c in range(NCORES)]
    maps = []
    for c in range(NCORES):
        b, r = c // 4, c % 4
        ep = np.zeros((3, 128, 32), f32)
        for kk in range(min(r, 3)):
            ep[kk] = eloc[4 * b + r - 1 - kk]
        maps.append(l0_inputs(inp, b, r, eprev=ep))
    r2 = _run("p2", maps)
    h1 = [np.asarray(r2[c]["h1"], f32) for c in range(NCORES)]
    maps = [{"h1": h1[c], "w_in": np.ascontiguousarray(inp["od_w_in"][0]), "ng": np.ascontiguousarray(inp["norm_g"][1]),
             "qg": np.ascontiguousarray(inp["od_q_norm_g"][0]), "kg": np.ascontiguousarray(inp["od_k_norm_g"][0])}
            for c in range(NCORES)]
    r3 = _run("p2b", maps)
    maps = []
    for c in range(NCORES):
        b, j = c // 4, c % 4
        cat = lambda nm, ax: np.concatenate([np.asarray(r3[4 * b + r][nm]) for r in range(4)], axis=ax)
        ohm, cmask = p3_consts(j)
        h1b = np.concatenate([h1[4 * b + r] for r in range(4)], axis=0).reshape(64, 128, D)
        maps.append({
            "q_blk": np.ascontiguousarray(cat("q_out", 0)[j::4]),
            "g_blk": np.ascontiguousarray(cat("g_out", 0)[j::4]),
            "qi_blk": np.ascontiguousarray(cat("qi_out", 0)[j::4]),
            "wi_blk": np.ascontiguousarray(cat("wi_out", 0).reshape(64, 128, 8)[j::4]),
            "h1_blk": np.ascontiguousarray(h1b[j::4]),
            "kT_all": np.ascontiguousarray(cat("kT_out", 2)),
            "v_all": np.ascontiguousarray(cat("v_out", 0)),
            "ki_all": np.ascontiguousarray(cat("ki_out", 1)),
            "cmask": cmask, "oh": ohm,
            "rel_bias": np.ascontiguousarray(inp["rel_bias"]),
            "w_out": np.ascontiguousarray(inp["od_w_out"][0]),
        })
    r4 = _run("p3", maps)
    out = np.zeros((BATCH, SEQ // 128, 128, D), f32)
    for c in range(NCORES):
        b, j = c // 4, c % 4
        out[b, j::4] = np.asarray(r4[c]["y"], f32)
    return out.reshape(BATCH, SEQ, D)
```

```python
import contextlib
import math
import numpy as np
import ml_dtypes
import concourse.bass as bass
import concourse.mybir as mybir
from concourse.bass_utils import run_bass_kernel_spmd

F32 = mybir.dt.float32
BF16 = mybir.dt.bfloat16
I32 = mybir.dt.int32
I8 = mybir.dt.int8
ALU = mybir.AluOpType
AF = mybir.ActivationFunctionType
AX = mybir.AxisListType

NCORES = 8
D = 1024
SEQ = 8192
BATCH = 2
TPC = 2048
NBLK = TPC // 128
EPS = 1e-6
NEG = -30000.0
DBG = {}
SEM_LIMIT = 30000


class Buf:
    __slots__ = ("name", "lw", "rd", "dsem", "dcnt")

    def __init__(self, name):
        self.name = name
        self.lw = None
        self.rd = {}
        self.dsem = {}
        self.dcnt = {}


class V:
    __slots__ = ("ap", "bufs")

    def __init__(self, ap, bufs):
        self.ap = ap
        self.bufs = bufs

    def __getitem__(self, idx):
        return V(self.ap[idx], self.bufs)

    def bc(self, shape):
        return V(self.ap.broadcast_to(list(shape)), self.bufs)

    def re(self, pat, **kw):
        return V(self.ap.rearrange(pat, **kw), self.bufs)

    def bitcast(self, dt):
        return V(self.ap.bitcast(dt), self.bufs)


class Tile:
    def __init__(self, P, name, shape, dtype, space="sbuf"):
        nc = P.nc
        if space == "sbuf":
            self.t = P.es.enter_context(nc.sbuf_tensor(name, list(shape), dtype))
        elif space == "psum":
            self.t = P.es.enter_context(nc.psum_tensor(name, list(shape), dtype))
        else:
            raise ValueError(space)
        self.buf = Buf(name)
        self.name = name
        self.shape = shape

    def __getitem__(self, idx):
        return V(self.t[idx], (self.buf,))

    def v(self, idx, buf):
        return V(self.t[idx], (buf,))

    def all(self):
        return V(self.t[:], (self.buf,))


class Prog:
    def __init__(self, nc):
        self.nc = nc
        self.es = contextlib.ExitStack()
        self.eng = {"pe": nc.tensor, "dve": nc.vector, "act": nc.scalar, "pool": nc.gpsimd, "sp": nc.sync}
        self.semh = {}
        self.esem = {}
        self.cnt = {}
        self.epoch = {}
        self.waited = {e: {} for e in self.eng}
        self.nsem = 0
        for e in ("pe", "dve", "act", "pool"):
            self.epoch[e] = 0
            self._new_eng_sem(e)
        self.out_waits = []
        self.n_instr = 0

    def _sem(self, name):
        h = self.es.enter_context(self.nc.semaphore(name))
        self.semh[name] = h
        self.nsem += 1
        return name

    def _new_eng_sem(self, e):
        name = "c_%s_%d" % (e, self.epoch[e])
        self._sem(name)
        self.esem[e] = name
        self.cnt[e] = 0
        self.epoch[e] += 1

    def sb(self, name, shape, dtype):
        return Tile(self, name, shape, dtype, "sbuf")

    def ps(self, name, shape, dtype=F32):
        return Tile(self, name, shape, dtype, "psum")

    def _deps(self, e, reads, writes):
        deps = {}

        def add(sn, val, src, kind):
            if src == e and e == "pe":
                return
            if deps.get(sn, 0) < val:
                deps[sn] = val

        for b in reads:
            if b.lw is not None:
                add(b.lw[0], b.lw[1], b.lw[2], "raw")
        for b in writes:
            if b.lw is not None:
                add(b.lw[0], b.lw[1], b.lw[2], "waw")
            for sn, (v, se) in b.rd.items():
                add(sn, v, se, "war")
        h = self.eng[e]
        w = self.waited[e]
        for sn, v in deps.items():
            if w.get(sn, 0) >= v:
                continue
            h.wait_ge(self.semh[sn], v)
            w[sn] = v
            self.n_instr += 1

    def op(self, e, fn, ins=(), outs=()):
        reads = []
        for x in ins:
            if isinstance(x, V):
                reads.extend(x.bufs)
        writes = []
        for x in outs:
            if isinstance(x, V):
                writes.extend(x.bufs)
        self._deps(e, reads, writes)
        i = fn(self.eng[e])
        self.cnt[e] += 1
        self.n_instr += 1
        sn = self.esem[e]
        v = self.cnt[e]
        i.then_inc(self.semh[sn], 1)
        for b in writes:
            b.lw = (sn, v, e)
            b.rd = {}
        for b in reads:
            if b not in writes:
                b.rd[sn] = (v, e)
        if v >= SEM_LIMIT:
            self._new_eng_sem(e)
        return i

    def dma(self, q, out, in_, primary=None, nc_kwargs=None):
        reads = list(in_.bufs)
        writes = list(out.bufs)
        self._deps(q, reads, writes)
        if primary is None:
            primary = writes[0] if writes else reads[0]
        qc = "sw" if q == "pool" else "hw"
        if qc not in primary.dsem:
            primary.dsem[qc] = self._sem("d%s_%s" % (qc, primary.name))
            primary.dcnt[qc] = 0
        kw = nc_kwargs or {}
        i = self.eng[q].dma_start(out=out.ap, in_=in_.ap, **kw)
        primary.dcnt[qc] += 16
        sn, val = primary.dsem[qc], primary.dcnt[qc]
        i.then_inc(self.semh[sn], 16)
        self.n_instr += 1
        for b in writes:
            b.lw = (sn, val, "dma")
            b.rd = {}
        for b in reads:
            b.rd[sn] = (val, "dma")
        return (sn, val)

    def finish(self, bufs):
        h = self.eng["sp"]
        done = {}
        for b in bufs:
            if b.lw is not None:
                done[b.lw[0]] = max(done.get(b.lw[0], 0), b.lw[1])
            for sn, (v, se) in b.rd.items():
                done[sn] = max(done.get(sn, 0), v)
        for sn, v in done.items():
            h.wait_ge(self.semh[sn], v)

    def mm(self, out, lhsT, rhs, start=True, stop=True):
        return self.op("pe", lambda h: h.matmul(out.ap, lhsT=lhsT.ap, rhs=rhs.ap, start=start, stop=stop),
                       ins=(lhsT, rhs), outs=(out,))

    def tr(self, out, in_, ident):
        return self.op("pe", lambda h: h.transpose(out.ap, in_.ap, ident.ap), ins=(in_, ident), outs=(out,))

    def act(self, out, in_, func, bias=None, scale=None, accum=None, e="act"):
        kw = {}
        ins = [in_]
        outs = [out]
        if bias is not None:
            kw["bias"] = bias.ap if isinstance(bias, V) else bias
            ins.append(bias)
        if scale is not None:
            kw["scale"] = scale.ap if isinstance(scale, V) else scale
            ins.append(scale)
        if accum is not None:
            kw["accum_out"] = accum.ap
            outs.append(accum)
        return self.op(e, lambda h: h.activation(out=out.ap, in_=in_.ap, func=func, **kw), ins=ins, outs=outs)

    def ts(self, out, in0, s1, s2=None, op0=ALU.mult, op1=None, accum=None, e="dve"):
        kw = {}
        ins = [in0, s1, s2]
        outs = [out]
        if op1 is not None:
            kw["op1"] = op1
        if accum is not None:
            kw["accum_out"] = accum.ap
            outs.append(accum)
        a1 = s1.ap if isinstance(s1, V) else s1
        a2 = s2.ap if isinstance(s2, V) else s2
        return self.op(e, lambda h: h.tensor_scalar(out=out.ap, in0=in0.ap, scalar1=a1, scalar2=a2, op0=op0, **kw),
                       ins=ins, outs=outs)

    def tt(self, out, in0, in1, op, e="dve"):
        return self.op(e, lambda h: h.tensor_tensor(out=out.ap, in0=in0.ap, in1=in1.ap, op=op),
                       ins=(in0, in1), outs=(out,))

    def stt(self, out, in0, s, in1, op0, op1):
        a = s.ap if isinstance(s, V) else s
        return self.op("dve", lambda h: h.scalar_tensor_tensor(out=out.ap, in0=in0.ap, scalar=a, in1=in1.ap,
                                                                op0=op0, op1=op1),
                       ins=(in0, s, in1), outs=(out,))

    def copy(self, out, in_, e="dve"):
        if e == "act":
            return self.op("act", lambda h: h.copy(out=out.ap, in_=in_.ap), ins=(in_,), outs=(out,))
        return self.op(e, lambda h: h.tensor_copy(out=out.ap, in_=in_.ap), ins=(in_,), outs=(out,))

    def recip(self, out, in_):
        return self.op("dve", lambda h: h.reciprocal(out=out.ap, in_=in_.ap), ins=(in_,), outs=(out,))

    def reduce(self, out, in_, op, axis=AX.X):
        return self.op("dve", lambda h: h.tensor_reduce(out=out.ap, in_=in_.ap, axis=axis, op=op),
                       ins=(in_,), outs=(out,))

    def memset(self, out, val, e="dve"):
        return self.op(e, lambda h: h.memset(out.ap, val), ins=(), outs=(out,))

    def iota(self, out, pattern, base, cm):
        return self.op("pool", lambda h: h.iota(out.ap, pattern=pattern, base=base, channel_multiplier=cm,
                                                allow_small_or_imprecise_dtypes=True), ins=(), outs=(out,))


def dram_in(nc, name, shape, dtype):
    return nc.dram_tensor(name, list(shape), dtype, kind="ExternalInput")


def dram_out(nc, name, shape, dtype):
    return nc.dram_tensor(name, list(shape), dtype, kind="ExternalOutput")


def DV(t, ap=None, buf=None):
    return V(t.ap() if ap is None else ap, (buf,) if buf is not None else ())


def dap(t, offset, pattern):
    return bass.AP(t, offset, [list(p) for p in pattern])


def barrier(P):
    tgt = {P.esem[e]: P.cnt[e] for e in ("pe", "dve", "act", "pool") if P.cnt[e] > 0}
    for e in ("pe", "dve", "act", "pool", "sp"):
        for sn, v in tgt.items():
            if P.waited[e].get(sn, 0) < v:
                P.eng[e].wait_ge(P.semh[sn], v)
                P.waited[e][sn] = v


def make_consts(P):
    c = {}
    c["ident"] = P.sb("c_ident", [128, 128], BF16)
    with contextlib.ExitStack() as es2:
        old_es, P.es = P.es, es2
        io = P.sb("c_iota", [128, 128], F32)
        P.iota(io.all(), [[1, 128]], 0, -1)
        P.ts(c["ident"].all(), io.all(), 0.0, None, op0=ALU.is_equal)
        barrier(P)
        P.es = old_es
    c["ones"] = P.sb("c_ones", [128, 128], BF16)
    P.memset(c["ones"].all(), 1.0)
    return c


def load_fm_vec(P, name, dram_t, n):
    t = P.sb(name, [128, n], F32)
    P.dma("sp", t.all(), DV(dram_t, dap(dram_t, 0, [[1, 128], [128, n]])),
          nc_kwargs={"allow_slow_non_contiguous": True})
    return t


def rmsnorm_to_fm(P, c, x_v, hnT_v, g_fm, wk, nfeat=1024):
    nk = nfeat // 128
    P.act(wk["junk"].all(), x_v, AF.Square, accum=wk["ss"].all())
    P.act(wk["sd"].all(), wk["ss"].all(), AF.Sqrt, bias=wk["epsb"].all(), scale=1.0 / nfeat)
    P.recip(wk["rstd"].all(), wk["sd"].all())
    P.ts(wk["xn"].all(), x_v, wk["rstd"].all(), None, op0=ALU.mult)
    if DBG.get('no_tr'):
        return
    pst = wk["pst"]
    for k in range(nk):
        P.tr(pst[:, k * 128:(k + 1) * 128], wk["xn"][:, k * 128:(k + 1) * 128], c["ident"].all())
    if DBG.get('no_tt'):
        return
    if DBG.get('tt_copy'):
        P.copy(hnT_v, pst.all().re("p (k t) -> p k t", k=nk))
        return
    for k in range(nk):
        if k % 2 == 0:
            P.ts(hnT_v[:, k, :], pst[:, k * 128:(k + 1) * 128], g_fm[:, k:k + 1], None, op0=ALU.mult)
        else:
            P.act(hnT_v[:, k, :], pst[:, k * 128:(k + 1) * 128], AF.Copy, scale=g_fm[:, k:k + 1])


def make_norm_work(P, pfx, pst):
    wk = {}
    wk["junk"] = P.sb(pfx + "junk", [128, 1024], BF16)
    wk["ss"] = P.sb(pfx + "ss", [128, 1], F32)
    wk["sd"] = P.sb(pfx + "sd", [128, 1], F32)
    wk["rstd"] = P.sb(pfx + "rstd", [128, 1], F32)
    wk["xn"] = P.sb(pfx + "xn", [128, 1024], BF16)
    wk["epsb"] = P.sb(pfx + "epsb", [128, 1], F32)
    P.memset(wk["epsb"].all(), EPS)
    wk["pst"] = pst
    return wk


OD_Q, OD_K, OD_V, OD_G, OD_QI, OD_KI, OD_WI = 0, 1024, 1280, 1536, 2560, 3072, 3136


def build_p2b(nsb=TPC // 512, do_tiles=True, do_blocks=True):
    nc = bass.Bass("TRN2", target_bir_lowering=False)
    h1 = dram_in(nc, "h1", [TPC, D], F32)
    w_in = dram_in(nc, "w_in", [D, 3144], F32)
    ng = dram_in(nc, "ng", [D], F32)
    qg = dram_in(nc, "qg", [128], F32)
    kg = dram_in(nc, "kg", [128], F32)
    q_out = dram_out(nc, "q_out", [NBLK, 128, 8, 128], BF16)
    g_out = dram_out(nc, "g_out", [NBLK, 128, 8, 128], BF16)
    qi_out = dram_out(nc, "qi_out", [NBLK, 128, 4, 128], BF16)
    wi_out = dram_out(nc, "wi_out", [TPC, 8], F32)
    kT_out = dram_out(nc, "kT_out", [128, 2, TPC], BF16)
    v_out = dram_out(nc, "v_out", [TPC, 256], BF16)
    ki_out = dram_out(nc, "ki_out", [128, TPC], BF16)
    P = Prog(nc)
    with P.es:
        c = make_consts(P)
        W = P.sb("W", [128, 8, 3200], BF16)
        Wwi = P.sb("Wwi", [128, 8, 8], BF16)
        wbufs = [Buf("Wk%d" % k) for k in range(8)]
        stg = [P.sb("stg%d" % i, [128, 3136], F32) for i in range(2)]
        for k in range(8):
            wv = V(W.t[:, k, :], (wbufs[k],))
            if k % 2 == 0:
                P.dma("pool", wv[:, 0:3136], DV(w_in, w_in.ap()[k * 128:(k + 1) * 128, 0:3136]), primary=wbufs[k])
                P.dma("pool", wv[:, 3136:3200], DV(w_in, w_in.ap()[k * 128:(k + 1) * 128, OD_KI:OD_KI + 64]),
                      primary=wbufs[k])
            else:
                st_ = stg[(k // 2) % 2]
                P.dma("sp", st_.all(), DV(w_in, w_in.ap()[k * 128:(k + 1) * 128, 0:3136]))
                P.copy(wv[:, 0:3136], st_.all(), e="act")
                P.copy(wv[:, 3136:3200], st_[:, OD_KI:OD_KI + 64], e="act")
        P.dma("pool", Wwi.all(), DV(w_in, dap(w_in, OD_WI, [[3144, 128], [128 * 3144, 8], [1, 8]])))

        def Wk(k, c0, n):
            return V(W.t[:, k, c0:c0 + n], (wbufs[k],))

        g_fm = load_fm_vec(P, "g_fm", ng, 8)
        gq = P.sb("gq", [128, 1], F32)
        gk = P.sb("gk", [128, 1], F32)
        P.dma("sp", gq.all(), DV(qg, dap(qg, 0, [[1, 128], [1, 1]])))
        P.dma("sp", gk.all(), DV(kg, dap(kg, 0, [[1, 128], [1, 1]])))
        P.ts(gq.all(), gq.all(), 128.0 ** -0.5, None, op0=ALU.mult)
        pst = P.ps("pst", [128, 1024], BF16)
        wk = make_norm_work(P, "n_", pst)
        xb = [P.sb("xb%d" % i, [128, 1024], F32) for i in range(2)]
        hnT = P.sb("hnT", [128, 8, 512], BF16)
        ring = [P.ps("pr%d" % i, [128, 512]) for i in range(3)]
        ring2 = [P.ps("ps2_%d" % i, [128, 512]) for i in range(2)]
        ptm = P.ps("ptm", [128, 512])
        ptm2 = P.ps("ptm2", [128, 512])
        sq = [P.sb("sq%d" % i, [128, 512], BF16) for i in range(2)]
        sd = [P.sb("sdq%d" % i, [128, 512], F32) for i in range(2)]
        ob = [P.sb("ob%d" % i, [128, 512], BF16) for i in range(4)]
        vb = [P.sb("vb%d" % i, [128, 256], BF16) for i in range(2)]
        wib = [P.sb("wib%d" % i, [128, 8], F32) for i in range(2)]
        rr = [0, 0, 0, 0]
        outs = []

        def nxt(lst, idx):
            t = lst[rr[idx] % len(lst)]
            rr[idx] += 1
            return t

        blk_sz = 128 * 8 * 128
        for sb_ in range(nsb):
            for bl in range(4 if do_blocks else 0):
                t0 = sb_ * 512 + bl * 128
                x = xb[bl % 2]
                P.dma("sp", x.all(), DV(h1, h1.ap()[t0:t0 + 128, :]))
                rmsnorm_to_fm(P, c, x.all(), hnT[:, :, bl * 128:(bl + 1) * 128], g_fm, wk)
                if DBG.get('no_tm'):
                    continue
                for k in range(8):
                    P.mm(ptm[:, 0:256], hnT[:, k, bl * 128:(bl + 1) * 128], Wk(k, OD_V, 256), start=(k == 0), stop=(k == 7))
                v_sb = vb[bl % 2]
                P.copy(v_sb.all(), ptm[:, 0:256], e="act")
                if not DBG.get('no_vst'):
                    P.dma("pool", DV(v_out, v_out.ap()[t0:t0 + 128, :]), v_sb.all(), primary=v_sb.buf)
                    outs += [v_sb.buf]
                if DBG.get('no_wi'):
                    continue
                for k in range(8):
                    P.mm(ptm2[:, 0:8], hnT[:, k, bl * 128:(bl + 1) * 128], Wwi[:, k, :], start=(k == 0), stop=(k == 7))
                w_sb = wib[bl % 2]
                P.ts(w_sb.all(), ptm2[:, 0:8], (8.0 ** -0.5) * (64.0 ** -0.5), None, op0=ALU.mult)
                P.dma("pool", DV(wi_out, wi_out.ap()[t0:t0 + 128, :]), w_sb.all(), primary=w_sb.buf)
                outs += [w_sb.buf]
            tiles = [("q", h, OD_Q + h * 128) for h in range(8)] + [("k", g, OD_K + g * 128) for g in range(2)] + \
                    [("g", h, OD_G + h * 128) for h in range(8)] + [("qi", pr, OD_QI + pr * 128) for pr in range(4)] + \
                    [("ki", 0, OD_KI)]
            for (kind, idx, c0) in (tiles if do_tiles else []):
                pr_ = nxt(ring, 0)
                for k in range(8):
                    P.mm(pr_.all(), Wk(k, c0, 128), hnT[:, k, :], start=(k == 0), stop=(k == 7))
                o = nxt(ob, 1)
                if kind in ("q", "k"):
                    s = nxt(sq, 2)
                    P.act(s.all(), pr_.all(), AF.Square)
                    p2 = nxt(ring2, 3)
                    P.mm(p2.all(), c["ones"].all(), s.all())
                    d_ = sd[(rr[3] - 1) % 2]
                    P.act(d_.all(), p2.all(), AF.Ln, bias=wk["epsb"].all(), scale=1.0 / 128)
                    P.act(d_.all(), d_.all(), AF.Exp, scale=-0.5)
                    P.stt(o.all(), pr_.all(), (gq if kind == "q" else gk).all(), d_.all(), ALU.mult, ALU.mult)
                elif kind == "g":
                    P.act(o.all(), pr_.all(), AF.Silu)
                else:
                    P.copy(o.all(), pr_.all(), e="act")
                o3 = o.all().re("p (b t) -> p b t", b=4)
                if kind == "q":
                    dst = dap(q_out, sb_ * 4 * blk_sz + idx * 128, [[8 * 128, 128], [blk_sz, 4], [1, 128]])
                    P.dma("sp", DV(q_out, dst), o3, primary=o.buf)
                elif kind == "g":
                    dst = dap(g_out, sb_ * 4 * blk_sz + idx * 128, [[8 * 128, 128], [blk_sz, 4], [1, 128]])
                    P.dma("sp", DV(g_out, dst), o3, primary=o.buf)
                elif kind == "qi":
                    bs = 128 * 4 * 128
                    dst = dap(qi_out, sb_ * 4 * bs + idx * 128, [[4 * 128, 128], [bs, 4], [1, 128]])
                    P.dma("sp", DV(qi_out, dst), o3, primary=o.buf)
                elif kind == "k":
                    P.dma("sp", DV(kT_out, kT_out.ap()[:, idx, sb_ * 512:(sb_ + 1) * 512]), o.all(), primary=o.buf)
                else:
                    P.dma("sp", DV(ki_out, ki_out.ap()[:, sb_ * 512:(sb_ + 1) * 512]), o.all(), primary=o.buf)
                outs.append(o.buf)
        P.finish(outs)
    return nc, P


NPOS = 12
GLEN = NPOS * 128 + 128
BIS_ITERS = 21


def barrier(P):
    tgt = {P.esem[e]: P.cnt[e] for e in ("pe", "dve", "act", "pool") if P.cnt[e] > 0}
    for e in ("pe", "dve", "act", "pool", "sp"):
        for sn, v in tgt.items():
            if P.waited[e].get(sn, 0) < v:
                P.eng[e].wait_ge(P.semh[sn], v)
                P.waited[e][sn] = v


def build_p3(nblk=NBLK):
    nc = bass.Bass("TRN2", target_bir_lowering=False)
    q_blk = dram_in(nc, "q_blk", [NBLK, 128, 8, 128], BF16)
    g_blk = dram_in(nc, "g_blk", [NBLK, 128, 8, 128], BF16)
    qi_blk = dram_in(nc, "qi_blk", [NBLK, 128, 4, 128], BF16)
    wi_blk = dram_in(nc, "wi_blk", [NBLK, 128, 8], F32)
    h1_blk = dram_in(nc, "h1_blk", [NBLK, 128, D], F32)
    kT_all = dram_in(nc, "kT_all", [128, 2, SEQ], BF16)
    v_all = dram_in(nc, "v_all", [SEQ, 256], BF16)
    ki_all = dram_in(nc, "ki_all", [128, SEQ], BF16)
    cmask = dram_in(nc, "cmask", [128, 512], BF16)
    oh = dram_in(nc, "oh", [32, GLEN], F32)
    rel_bias = dram_in(nc, "rel_bias", [32, 8], F32)
    w_out = dram_in(nc, "w_out", [D, D], F32)
    y = dram_out(nc, "y", [NBLK, 128, D], F32)
    gtab = nc.dram_tensor("gtab", [8, GLEN], F32, kind="Internal")
    gtab_buf = Buf("gtab")
    P = Prog(nc)
    with P.es:
        c = make_consts(P)
        ident4 = P.sb("ident4", [128, 512], BF16)
        for h in range(4):
            P.copy(ident4[:, h * 128:(h + 1) * 128], c["ident"].all())
        kT = P.sb("kT", [128, 2, SEQ], BF16)
        Vs = P.sb("Vs", [128, 64, 256], BF16)
        ki = P.sb("ki", [128, SEQ], BF16)
        Wo = P.sb("Wo", [128, 8, D], BF16)
        cm = P.sb("cm", [128, 512], BF16)
        biasT = P.sb("biasT", [128, NPOS, 1024], BF16)
        for g in range(2):
            P.dma("sp", kT[:, g, :], DV(kT_all, kT_all.ap()[:, g, :]))
        for part in range(4):
            P.dma("sp", Vs[:, part * 16:(part + 1) * 16, :],
                  DV(v_all, dap(v_all, part * 16 * 128 * 256, [[256, 128], [128 * 256, 16], [1, 256]])))
        P.dma("sp", ki.all(), DV(ki_all))
        P.dma("sp", cm.all(), DV(cmask))
        for k in range(8):
            P.dma("pool", Wo[:, k, :], DV(w_out, w_out.ap()[k * 128:(k + 1) * 128, :]))
        NR3 = 6
        ring = [P.ps("ring%d" % i, [128, 512]) for i in range(NR3)]
        num = [P.ps("num%d" % g, [128, 512]) for g in range(2)]
        rr = {"ring": 0}

        def nring():
            for _ in range(NR3):
                t = ring[rr["ring"] % NR3]
                rr["ring"] += 1
                if t.buf.lw is None or t.buf.rd:
                    return t
            raise RuntimeError("PSUM ring exhausted: every bank holds unread results")

        with contextlib.ExitStack() as es2:
            old_es, P.es = P.es, es2
            Jf = P.sb("Jf", [128, 128], F32)
            tmpi = P.sb("tmpi", [128, 128], F32)
            P.iota(tmpi.all(), [[1, 128]], -127, 1)
            P.ts(Jf.all(), tmpi.all(), 0.0, None, op0=ALU.is_equal)
            rb = P.sb("rb", [32, 8], F32)
            rb31 = P.sb("rb31", [32, 8], F32)
            ohs = P.sb("ohs", [32, GLEN], F32)
            gsb = P.sb("gsb", [8, GLEN], F32)
            hk = [P.sb("hk%d" % i, [128, 8, 128], F32) for i in range(2)]
            P.dma("sp", rb.all(), DV(rel_bias))
            P.dma("sp", rb31.all(), DV(rel_bias, dap(rel_bias, 31 * 8, [[0, 32], [1, 8]])))
            P.dma("sp", ohs.all(), DV(oh))
            P.tt(rb.all(), rb.all(), rb31.all(), ALU.subtract)
            for ch in range((GLEN + 511) // 512):
                n = min(512, GLEN - ch * 512)
                pr_ = nring()
                P.mm(pr_[0:8, 0:n], rb.all(), ohs[:, ch * 512:ch * 512 + n])
                P.copy(gsb[:, ch * 512:ch * 512 + n], pr_[0:8, 0:n])
            P.dma("sp", DV(gtab, buf=gtab_buf), gsb.all(), primary=gtab_buf)
            for p in range(NPOS):
                hkt = hk[p % 2]
                P.dma("sp", hkt.all(), DV(gtab, dap(gtab, p * 128, [[1, 128], [GLEN, 8], [1, 128]]), buf=gtab_buf))
                for half in range(2):
                    pr_ = nring()
                    P.mm(pr_.all(), Jf.all(), hkt[:, half * 4:(half + 1) * 4, :])
                    P.copy(biasT[:, p, half * 512:(half + 1) * 512], pr_.all(), e=("act" if half else "dve"))
            barrier(P)
            P.es = old_es

        score = P.sb("score", [128, SEQ], F32)
        sc_bufs = [Buf("sc%d" % i) for i in range(SEQ // 512)]

        def scv(a, b):
            return V(score.t[:, a:b], tuple(sc_bufs[a // 512:(b + 511) // 512]))

        JW = SEQ
        nmall = [P.sb("nmall%d" % i, [128, SEQ], BF16) for i in range(2)]
        qT = [P.sb("qT%d" % i, [128, 8, 128], BF16) for i in range(2)]
        gT = [P.sb("gT%d" % i, [128, 8, 128], BF16) for i in range(1)]
        PmSum = [P.sb("PmSum%d" % g, [128, 512], F32) for g in range(2)]
        ones_f = P.sb("ones_f", [128, 128], F32)
        P.memset(ones_f.all(), 1.0)
        qiT = [P.sb("qiT%d" % i, [128, 4, 128], BF16) for i in range(2)]
        wi = [P.sb("wi%d" % i, [128, 8], F32) for i in range(2)]
        h1b = [P.sb("h1b%d" % i, [128, 512], F32) for i in range(1)]
        NPM = 2
        Pm = [P.sb("Pm%d" % i, [128, 512], BF16) for i in range(NPM)]
        rd = [P.sb("rd%d" % i, [128, 512], F32) for i in range(1)]
        catT = P.sb("catT", [128, 8, 128], BF16)
        small = {n: P.sb("b_" + n, [128, 1], F32) for n in ("lo", "hi", "w", "mid", "nmid", "cnt", "cnt2", "sel")}
        nm_j = [(Buf("jD%d" % i), Buf("jA%d" % i)) for i in range(2)]
        pow2 = P.sb("pow2", [128, BIS_ITERS], F32)
        wall = P.sb("wall", [128, BIS_ITERS], F32)
        for k in range(BIS_ITERS):
            P.memset(pow2[:, k:k + 1], 2.0 ** -(k + 1))
        cn = {"Pm": 0}
        outs = []

        def stage1(i):
            steps = []
            nkt = 4 * i + 4
            nk = nkt * 128
            q_, qi_, wi_ = qT[i % 2], qiT[i % 2], wi[i % 2]
            S = small
            nm = nmall[i % 2]
            jD, jA = nm_j[i % 2]
            nm8 = nm.t[:].bitcast(I8)
            nD = ((nk // 2 + 127) // 128) * 128
            nA = nk - nD

            def loads():
                P.dma("sp", qi_.all(), DV(qi_blk, qi_blk.ap()[i]))
                P.dma("sp", wi_.all(), DV(wi_blk, wi_blk.ap()[i]))
                P.dma("sp", q_.all(), DV(q_blk, q_blk.ap()[i]))
            steps.append(loads)

            def idx(chunks, h0):
                for h in range(h0, h0 + 4):
                    for c5 in chunks:
                        sc = scv(c5 * 512, (c5 + 1) * 512)
                        pI = nring()
                        lo_p = (h % 2) * 64
                        P.mm(pI.all(), qi_[lo_p:lo_p + 64, h // 2, :], ki[lo_p:lo_p + 64, c5 * 512:(c5 + 1) * 512])
                        P.act(pI.all(), pI.all(), AF.Relu)
                        if h == 0:
                            P.ts(sc, pI.all(), wi_[:, 0:1], None, op0=ALU.mult)
                        else:
                            P.stt(sc, pI.all(), wi_[:, h:h + 1], sc, ALU.mult, ALU.add)
            for c5 in range(0, (i + 1) if not DBG.get('no_idx') else 0, 2):
                chunks = [c5] + ([c5 + 1] if c5 + 1 <= i else [])
                for h0 in (0, 4):
                    steps.append(lambda chunks=chunks, h0=h0: idx(chunks, h0))

            def bis_init():
                P.reduce(S["lo"].all(), scv(0, nk), ALU.min)
                P.tt(scv(nk - 512, nk), scv(nk - 512, nk), cm.all(), ALU.add)
                P.reduce(S["hi"].all(), scv(0, nk), ALU.max)
                P.ts(S["w"].all(), S["hi"].all(), 1.0, S["lo"].all(), op0=ALU.add, op1=ALU.subtract)
                P.ts(wall.all(), pow2.all(), S["w"].all(), None, op0=ALU.mult)
                P.tt(S["mid"].all(), S["lo"].all(), wall[:, 0:1], ALU.add)
            steps.append(bis_init)

            def bis_iter(k):
                P.ts(V(nm8[:, 0:nk], (jD,)), scv(0, nk), S["mid"].all(), 0.0, op0=ALU.is_ge, op1=ALU.add,
                     accum=S["cnt"].all())
                P.stt(S["sel"].all(), S["cnt"].all(), 255.5, wall[:, k:k + 1], ALU.is_ge, ALU.mult)
                if k + 1 < BIS_ITERS:
                    P.ts(S["mid"].all(), S["sel"].all(), S["lo"].all(), wall[:, k + 1:k + 2], op0=ALU.add, op1=ALU.add)
                P.tt(S["lo"].all(), S["lo"].all(), S["sel"].all(), ALU.add)
            for it in range(BIS_ITERS):
                steps.append(lambda it=it: bis_iter(it))

            def negmask(a0, a1):
                P.ts(V(nm.t[:, a0:a1], (nm.buf, jD, jA)), scv(a0, a1), S["lo"].all(), NEG, op0=ALU.is_lt, op1=ALU.mult)
            for a0 in range(0, nk, 2048):
                steps.append(lambda a0=a0: negmask(a0, min(nk, a0 + 2048)))
            return steps

        def stage2(i):
            steps = []
            nkt = 4 * i + 4
            q_, g_, hb, nm = qT[i % 2], gT[0], h1b[0], nmall[i % 2]

            def tile_(m):
                pos = nkt - 1 - m
                for g in range(2):
                    pL = nring()
                    P.mm(pL.all(), kT[:, g, m * 128:(m + 1) * 128], q_[:, 4 * g:4 * g + 4, :], start=True, stop=False)
                    P.mm(pL.all(), V(nm.t[:, m * 128:(m + 1) * 128], (nm.buf,) + nm_j[i % 2]), ident4.all(), start=False,
                         stop=(pos >= NPOS))
                    if pos < NPOS:
                        P.mm(pL.all(), c["ident"].all(), biasT[:, pos, g * 512:(g + 1) * 512], start=False, stop=True)
                    pm = Pm[cn["Pm"] % NPM]
                    cn["Pm"] += 1
                    P.act(pm.all(), pL.all(), AF.Exp)
                    P.mm(num[g].all(), Vs[:, m, g * 128:(g + 1) * 128], pm.all(), start=(m == 0), stop=(m == nkt - 1))
                    if m == 0:
                        P.copy(PmSum[g].all(), pm.all(), e="pool")
                    else:
                        P.tt(PmSum[g].all(), PmSum[g].all(), pm.all(), ALU.add, e="pool")
            for m in range(nkt if not DBG.get('no_att') else 1):
                steps.append(lambda m=m: tile_(m))

            def epi():
                P.dma("sp", g_.all(), DV(g_blk, g_blk.ap()[i]))
                den = [nring(), nring()]
                for g in range(2):
                    P.mm(den[g].all(), ones_f.all(), PmSum[g].all())
                for g in range(2):
                    r_ = rd[0]
                    P.act(r_.all(), den[g].all(), AF.Ln)
                    P.act(r_.all(), r_.all(), AF.Exp, scale=-1.0)
                    tmp = nring()
                    P.tt(tmp.all(), num[g].all(), r_.all(), ALU.mult)
                    P.tt(catT[:, 4 * g:4 * g + 4, :], tmp.all().re("p (h t) -> p h t", h=4), g_[:, 4 * g:4 * g + 4, :], ALU.mult)
                for half in range(2):
                    cs_ = slice(half * 512, (half + 1) * 512)
                    P.dma("sp", hb.all(), DV(h1_blk, h1_blk.ap()[i][:, cs_]))
                    po = nring()
                    for h in range(8):
                        P.mm(po.all(), catT[:, h, :], Wo[:, h, cs_], start=(h == 0), stop=(h == 7))
                    P.tt(hb.all(), po.all(), hb.all(), ALU.add)
                    P.dma("pool", DV(y, y.ap()[i][:, cs_]), hb.all(), primary=hb.buf)
                outs.append(hb.buf)
            steps.append(epi)
            return steps

        def interleave(sa, sb):
            na, nb = len(sa), len(sb)
            ia = ib = 0
            while ia < na or ib < nb:
                if ib >= nb or (ia < na and ia * nb <= ib * na):
                    sa[ia]()
                    ia += 1
                else:
                    sb[ib]()
                    ib += 1

        order = list(range(nblk - 1, -1, -1))
        for st_ in stage1(order[0]):
            st_()
        for oi, i in enumerate(order):
            s2 = stage2(i)
            s1 = stage1(order[oi + 1]) if oi + 1 < nblk else []
            fl = DBG.get('frontload', 0.25)
            if fl and s1:
                nidx = len(s1) - BIS_ITERS - 1 - ((4 * order[oi + 1] + 4) * 128 + 2047) // 2048
                cut2 = max(1, int(len(s2) * fl))
                interleave(s1[:nidx], s2[:cut2])
                interleave(s1[nidx:], s2[cut2:])
            else:
                interleave(s1, s2)
        P.finish(outs)
    return nc, P


def t5_bucket_np(d):
    d = np.maximum(d, 0)
    nf = np.maximum(d, 1).astype(np.float32)
    large = 16 + (np.log(nf / np.float32(16)) / np.float32(math.log(1024 / 16)) * np.float32(16)).astype(np.int32)
    large = np.minimum(large, 31)
    return np.where(d < 16, d, large)


def p3_consts(j):
    e = np.arange(GLEN)
    dist = (j - 3) * 128 + e - 127
    b = t5_bucket_np(dist)
    ohm = np.zeros((32, GLEN), np.float32)
    ohm[b, e] = 1.0
    t = np.arange(128)[:, None]
    r = np.arange(512)[None, :]
    s_rel = r - j * 128
    cmask = np.where(s_rel <= t, 0.0, -1e30).astype(np.float32).astype(ml_dtypes.bfloat16)
    return ohm, cmask


EV_Q, EV_K, EV_V, EV_GA, EV_U, EV_GB = 0, 512, 640, 768, 1280, 1792
WQ, WKD, WVD, WGA, WU, WGB = 0, 512, 768, 1024, 1536, 2048
TWO_PI = 2.0 * math.pi
CW1 = 6.28125
CW2 = TWO_PI - CW1


def sincos(P, wk, ph, s_out, c_out):
    ki, kf, r, m = wk["ki"], wk["kf"], wk["r"], wk["m"]
    P.ts(ki, ph, 1.0 / TWO_PI, None, op0=ALU.mult)
    P.copy(kf, ki)
    P.stt(r, kf, -CW1, ph, ALU.mult, ALU.add)
    P.stt(r, kf, -CW2, r, ALU.mult, ALU.add)
    for (outv, shift) in ((s_out, 0.0), (c_out, math.pi / 2)):
        if shift:
            P.ts(r, r, shift, None, op0=ALU.add)
        P.ts(m, r, math.pi, -TWO_PI, op0=ALU.is_gt, op1=ALU.mult)
        P.tt(r, r, m, ALU.add)
        P.ts(m, r, -math.pi, TWO_PI, op0=ALU.is_lt, op1=ALU.mult)
        P.tt(r, r, m, ALU.add)
        P.ts(r, r, math.pi, -math.pi, op0=ALU.min, op1=ALU.max)
        P.act(outv, r, AF.Sin)


def cmul(P, o_re, o_im, a_re, a_im, b_re, b_im, t1, t2):
    P.tt(t1, a_re, b_re, ALU.mult)
    P.tt(t2, a_im, b_im, ALU.mult)
    P.tt(o_re, t1, t2, ALU.subtract)
    P.tt(t1, a_re, b_im, ALU.mult)
    P.tt(t2, a_im, b_re, ALU.mult)
    P.tt(o_im, t1, t2, ALU.add)


def interleave_steps(sa, sb):
    na, nb = len(sa), len(sb)
    ia = ib = 0
    while ia < na or ib < nb:
        if ib >= nb or (ia < na and ia * nb <= ib * na):
            sa[ia]()
            ia += 1
        else:
            sb[ib]()
            ib += 1


def build_p2(mode="full", nsb=TPC // 512):
    full = mode == "full"
    nc = bass.Bass("TRN2", target_bir_lowering=False)
    x_own = dram_in(nc, "x_own", [TPC, D], F32)
    w_in = dram_in(nc, "w_in", [D, 2304], F32)
    ng_fm_d = dram_in(nc, "ng_fm", [128, 8], F32)
    a_re_f = dram_in(nc, "a_re_f", [2048], F32)
    a_im_f = dram_in(nc, "a_im_f", [2048], F32)
    ldt_f = dram_in(nc, "ldt_f", [2048], F32)
    a_re_s = dram_in(nc, "a_re_s", [128, 16], F32)
    a_im_s = dram_in(nc, "a_im_s", [128, 16], F32)
    ldt_s = dram_in(nc, "ldt_s", [128, 16], F32)
    b_blk_re = dram_in(nc, "b_blk_re", [128, 4, 128], F32)
    b_blk_im = dram_in(nc, "b_blk_im", [128, 4, 128], F32)
    if full:
        x_halo = dram_in(nc, "x_halo", [128, D], F32)
        firstneg = dram_in(nc, "firstneg", [128, 1], F32)
        w_out = dram_in(nc, "w_out", [D, D], F32)
        glu_w = dram_in(nc, "glu_w", [512, 1024], F32)
        qg2 = dram_in(nc, "qg2", [128, 1], F32)
        kg2 = dram_in(nc, "kg2", [128, 1], F32)
        sinks_row = dram_in(nc, "sinks_row", [128, 8], F32)
        rel_bias = dram_in(nc, "rel_bias", [32, 8], F32)
        oh_swa = dram_in(nc, "oh_swa", [32, 384], F32)
        neg_swa = dram_in(nc, "neg_swa", [8, 384], F32)
        c_blk_re = dram_in(nc, "c_blk_re", [128, 16, 128], F32)
        c_blk_im = dram_in(nc, "c_blk_im", [128, 16, 128], F32)
        d_fm_d = dram_in(nc, "d_fm", [128, 4], F32)
        glu_b_fm_d = dram_in(nc, "glu_b_fm", [128, 8], F32)
        eprev = dram_in(nc, "eprev", [3, 128, 32], F32)
        h1_out = dram_out(nc, "h1", [TPC, D], F32)
        gtab = nc.dram_tensor("gtab0", [8, 384], F32, kind="Internal")
        gtab_buf = Buf("gtab0")
        tab_w = dram_in(nc, "tab_w", [128, 2, 2048], BF16)
        tab_bb = dram_in(nc, "tab_bb", [128, 4, 512], BF16)
    else:
        eloc = dram_out(nc, "eloc", [128, 32], F32)
        tab_w_o = dram_out(nc, "tab_w", [128, 2, 2048], BF16)
        tab_bb_o = dram_out(nc, "tab_bb", [128, 4, 512], BF16)
    P = Prog(nc)
    with P.es:
        c = make_consts(P)
        wu0 = WU if full else 0
        g_fm = P.sb("g_fm", [128, 8], F32)
        P.dma("sp", g_fm.all(), DV(ng_fm_d))
        Wr = P.sb("Wr", [128, 16, 128], BF16)
        Ws = P.sb("Ws", [128, 16, 128], BF16)
        BBp = P.sb("BBp", [128, 4, 512], BF16)
        if not full:
            P.memset(BBp.all(), 0.0)
        A128r = P.sb("A128r", [128, 16], F32)
        A128i = P.sb("A128i", [128, 16], F32)
        A127r = P.sb("A127r", [128, 16], F32)
        A127i = P.sb("A127i", [128, 16], F32)
        A1r = P.sb("A1r", [128, 16], F32)
        A1i = P.sb("A1i", [128, 16], F32)
        Zr = P.sb("Zr", [128, 16], F32)
        Zi = P.sb("Zi", [128, 16], F32)
        if full:
            Vr = P.sb("Vr", [128, 16, 128], F32)
            Vi = P.sb("Vi", [128, 16, 128], F32)
            d_fm = P.sb("d_fm_s", [128, 4], F32)
            glu_b = P.sb("glu_b", [128, 8], F32)
            BT = [P.sb("BT%d" % i, [128, 1024], F32) for i in range(3)]
            esink = P.sb("esink", [128, 8], F32)
            gq = P.sb("gq", [128, 1], F32)
            gk = P.sb("gk", [128, 1], F32)
            ones2 = P.sb("ones2", [128, 128], BF16)
            P.dma("sp", d_fm.all(), DV(d_fm_d))
            P.dma("sp", glu_b.all(), DV(glu_b_fm_d))
            P.dma("sp", gq.all(), DV(qg2))
            P.dma("sp", gk.all(), DV(kg2))
            P.ts(gq.all(), gq.all(), 64.0 ** -0.5, None, op0=ALU.mult)
            P.dma("sp", esink.all(), DV(sinks_row))
            P.act(esink.all(), esink.all(), AF.Exp)
            P.memset(ones2.all(), 0.0)
            P.memset(ones2[0:64, 0:64], 1.0)
            P.memset(ones2[64:128, 64:128], 1.0)

        NR = 6
        ring = [P.ps("ring%d" % i, [128, 512]) for i in range(NR)]
        py_bank = P.ps("py_bank", [128, 512]) if full else None
        pst = P.ps("pst", [128, 1024], BF16)
        rr = {"ring": 0}

        def nring():
            for _ in range(NR):
                t = ring[rr["ring"] % NR]
                rr["ring"] += 1
                if t.buf.lw is None or t.buf.rd:
                    return t
            raise RuntimeError("PSUM ring exhausted: every bank holds unread results")

        jf = P.sb("jf", [128, 1], F32)
        P.iota(jf.all(), [[1, 1]], 0, 1)
        io_i = P.sb("io_i", [128, 128], F32)
        P.iota(io_i.all(), [[1, 128]], 0, 0)
        tri_f = P.sb("tri_f", [128, 128], F32)
        P.iota(tri_f.all(), [[1, 128]], 0, -1)
        tmpi_e = P.sb("tmpi_e", [128, 128], F32)
        P.iota(tmpi_e.all(), [[1, 128]], -127, 1)
        ncolW = 2560 if full else 512
        W = P.sb("W", [128, 8, ncolW], BF16)
        wbufs = [Buf("Wk%d" % k) for k in range(8)]

        def Wk(k, c0, n):
            return V(W.t[:, k, c0:c0 + n], (wbufs[k],))

        if full:
            Cre = P.sb("Cre", [128, 16, 128], BF16)
            Cim = P.sb("Cim", [128, 16, 128], BF16)
            Wo = P.sb("Wo", [128, 8, D], BF16)
            Wg = P.sb("Wg", [128, 4, 1024], BF16)

        def issue_weight_loads(stg=None):
            for k in range(8):
                rows = slice(k * 128, (k + 1) * 128)
                if stg is not None and k % 2 == 1:
                    st_ = stg[(k // 2) % 2]
                    P.dma("sp", st_.all(), DV(w_in, w_in.ap()[rows, :]))

                    def cp(dst0, n, src0):
                        P.copy(V(W.t[:, k, dst0:dst0 + n], (wbufs[k],)), st_[:, src0:src0 + n], e="act")
                    cp(WU, 512, EV_U)
                    cp(WQ, 512, EV_Q)
                    for g in range(2):
                        for dup in range(2):
                            cp(WKD + g * 128 + dup * 64, 64, EV_K + g * 64)
                            cp(WVD + g * 128 + dup * 64, 64, EV_V + g * 64)
                    cp(WGA, 512, EV_GA)
                    cp(WGB, 512, EV_GB)
                    continue

                def ld(dst0, n, src0):
                    P.dma("pool", V(W.t[:, k, dst0:dst0 + n], (wbufs[k],)), DV(w_in, w_in.ap()[rows, src0:src0 + n]),
                          primary=wbufs[k])
                if full:
                    ld(WU, 512, EV_U)
                    ld(WQ, 512, EV_Q)
                    for g in range(2):
                        for dup in range(2):
                            ld(WKD + g * 128 + dup * 64, 64, EV_K + g * 64)
                            ld(WVD + g * 128 + dup * 64, 64, EV_V + g * 64)
                    ld(WGA, 512, EV_GA)
                    ld(WGB, 512, EV_GB)
                else:
                    ld(0, 512, EV_U)
            if full:
                P.dma("pool", Cre.all(), DV(c_blk_re))
                P.dma("pool", Cim.all(), DV(c_blk_im))
                for k in range(4):
                    P.dma("pool", Wg[:, k, :], DV(glu_w, glu_w.ap()[k * 128:(k + 1) * 128, :]))
                for k in range(8):
                    P.dma("pool", Wo[:, k, :], DV(w_out, w_out.ap()[k * 128:(k + 1) * 128, :]))
                P.ts(Cim.all(), Cim.all(), -1.0, None, op0=ALU.mult, e="pool")
        tri = P.sb("tri", [128, 128], BF16)
        P.ts(tri.all(), tri_f.all(), 0.0, None, op0=ALU.is_ge)
        with contextlib.ExitStack() as es2:
            old_es, P.es = P.es, es2
            N = 2048
            tnames = ("b", "mag", "kf", "r", "m", "s", "c") if full else ("ard", "ang", "a", "b", "mag", "kf", "r", "m", "s", "c")
            T = {n: P.sb("t_" + n, [128, N], F32) for n in tnames}
            stg = [P.sb("stg%d" % i, [128, 2304], F32) for i in range(2)] if full else None
            Tki = P.sb("t_ki", [128, N], I32)

            def scw(n):
                return {"ki": Tki[:, 0:n], "kf": T["kf"][:, 0:n], "r": T["r"][:, 0:n], "m": T["m"][:, 0:n]}

            if full:
                P.dma("sp", Wr.all().re("p a b -> p (a b)"), DV(tab_w, tab_w.ap()[:, 0, :]))
                P.dma("sp", Ws.all().re("p a b -> p (a b)"), DV(tab_w, tab_w.ap()[:, 1, :]))
                P.dma("sp", BBp.all(), DV(tab_bb))
                issue_weight_loads(stg)
            else:
                P.dma("sp", T["a"].all(), DV(ldt_f, dap(ldt_f, 0, [[0, 128], [1, N]])))
                P.act(T["a"].all(), T["a"].all(), AF.Exp)
                P.dma("sp", T["ard"].all(), DV(a_re_f, dap(a_re_f, 0, [[0, 128], [1, N]])))
                P.dma("sp", T["ang"].all(), DV(a_im_f, dap(a_im_f, 0, [[0, 128], [1, N]])))
                issue_weight_loads()
                P.tt(T["ard"].all(), T["ard"].all(), T["a"].all(), ALU.mult)
                P.tt(T["ang"].all(), T["ang"].all(), T["a"].all(), ALU.mult)
                P.ts(T["b"].all(), T["ard"].all(), jf.all(), -1.0, op0=ALU.mult, op1=ALU.mult)
                P.act(T["mag"].all(), T["b"].all(), AF.Exp)
                P.ts(T["b"].all(), T["ang"].all(), jf.all(), None, op0=ALU.mult)
                sincos(P, scw(N), T["b"].all(), T["s"].all(), T["c"].all())
                P.tt(Wr.all().re("p a b -> p (a b)"), T["mag"].all(), T["c"].all(), ALU.mult)
                P.tt(Ws.all().re("p a b -> p (a b)"), T["mag"].all(), T["s"].all(), ALU.mult)
                Fr, Fi = T["ard"], T["ang"]
                P.act(T["mag"].all(), T["ard"].all(), AF.Exp)
                sincos(P, scw(N), T["ang"].all(), T["s"].all(), T["c"].all())
                are_row, aim_row = T["kf"], T["r"]
                P.dma("sp", are_row.all(), DV(a_re_f, dap(a_re_f, 0, [[0, 128], [1, N]])))
                P.dma("sp", aim_row.all(), DV(a_im_f, dap(a_im_f, 0, [[0, 128], [1, N]])))
                P.tt(T["c"].all(), T["mag"].all(), T["c"].all(), ALU.mult)
                P.tt(T["s"].all(), T["mag"].all(), T["s"].all(), ALU.mult)
                P.ts(T["c"].all(), T["c"].all(), -1.0, None, op0=ALU.add)
                P.tt(T["a"].all(), are_row.all(), are_row.all(), ALU.mult)
                P.tt(T["b"].all(), aim_row.all(), aim_row.all(), ALU.mult)
                P.tt(T["a"].all(), T["a"].all(), T["b"].all(), ALU.add)
                P.recip(T["a"].all(), T["a"].all())
                P.tt(T["b"].all(), T["c"].all(), are_row.all(), ALU.mult)
                P.tt(T["m"].all(), T["s"].all(), aim_row.all(), ALU.mult)
                P.tt(T["b"].all(), T["b"].all(), T["m"].all(), ALU.add)
                P.tt(Fr.all(), T["b"].all(), T["a"].all(), ALU.mult)
                P.tt(T["b"].all(), T["s"].all(), are_row.all(), ALU.mult)
                P.tt(T["m"].all(), T["c"].all(), aim_row.all(), ALU.mult)
                P.tt(T["b"].all(), T["b"].all(), T["m"].all(), ALU.subtract)
                P.tt(Fi.all(), T["b"].all(), T["a"].all(), ALU.mult)
                bre = T["s"].all()[:, 0:512].re("p (q s) -> p q s", q=4)
                bim = T["c"].all()[:, 0:512].re("p (q s) -> p q s", q=4)
                P.dma("sp", bre, DV(b_blk_re))
                P.dma("sp", bim, DV(b_blk_im))
                t1 = T["m"].all()[:, 0:128]
                t2 = T["m"].all()[:, 128:256]
                for st4 in range(4):
                    ps_ = slice(32 * st4, 32 * st4 + 32)
                    for q in range(4):
                        st = 4 * q + st4
                        fr = Fr[ps_, st * 128:(st + 1) * 128]
                        fi = Fi[ps_, st * 128:(st + 1) * 128]
                        P.tt(t1[ps_, :], bre[ps_, q, :], fr, ALU.mult)
                        P.tt(t2[ps_, :], bim[ps_, q, :], fi, ALU.mult)
                        co = (st4 % 2) * 256
                        P.tt(BBp[ps_, q, co:co + 128], t1[ps_, :], t2[ps_, :], ALU.subtract)
                        P.tt(t1[ps_, :], bre[ps_, q, :], fi, ALU.mult)
                        P.tt(t2[ps_, :], bim[ps_, q, :], fr, ALU.mult)
                        P.tt(BBp[ps_, q, co + 128:co + 256], t1[ps_, :], t2[ps_, :], ALU.add)
                P.dma("sp", DV(tab_w_o, tab_w_o.ap()[:, 0, :]), Wr.all().re("p a b -> p (a b)"), primary=Wr.buf)
                P.dma("sp", DV(tab_w_o, tab_w_o.ap()[:, 1, :]), Ws.all().re("p a b -> p (a b)"), primary=Ws.buf)
                P.dma("sp", DV(tab_bb_o), BBp.all(), primary=BBp.buf)
                tab_outs = [Wr.buf, Ws.buf, BBp.buf]
            sp_ = {n: P.sb("sp_" + n, [128, 16], F32) for n in ("ard", "ang", "dt", "a", "b", "mag", "kf", "r", "m", "s", "c")}
            spki = P.sb("sp_ki", [128, 16], I32)
            P.dma("sp", sp_["dt"].all(), DV(ldt_s))
            P.act(sp_["dt"].all(), sp_["dt"].all(), AF.Exp)
            P.dma("sp", sp_["ard"].all(), DV(a_re_s))
            P.dma("sp", sp_["ang"].all(), DV(a_im_s))
            P.tt(sp_["ard"].all(), sp_["ard"].all(), sp_["dt"].all(), ALU.mult)
            P.tt(sp_["ang"].all(), sp_["ang"].all(), sp_["dt"].all(), ALU.mult)
            spw = {"ki": spki.all(), "kf": sp_["kf"].all(), "r": sp_["r"].all(), "m": sp_["m"].all()}
            for (mult_, orr, oii) in ((128.0, A128r, A128i), (127.0, A127r, A127i), (1.0, A1r, A1i)):
                P.ts(sp_["a"].all(), sp_["ard"].all(), mult_, None, op0=ALU.mult)
                P.act(sp_["mag"].all(), sp_["a"].all(), AF.Exp)
                P.ts(sp_["b"].all(), sp_["ang"].all(), mult_, None, op0=ALU.mult)
                sincos(P, spw, sp_["b"].all(), sp_["s"].all(), sp_["c"].all())
                P.tt(orr.all(), sp_["mag"].all(), sp_["c"].all(), ALU.mult)
                P.tt(oii.all(), sp_["mag"].all(), sp_["s"].all(), ALU.mult)
            if full:
                for st in range(16):
                    P.act(T["mag"][:, st * 128:(st + 1) * 128], io_i.all(), AF.Exp, scale=sp_["ard"][:, st:st + 1])
                    P.ts(T["b"][:, st * 128:(st + 1) * 128], io_i.all(), sp_["ang"][:, st:st + 1], None, op0=ALU.mult)
                sincos(P, scw(N), T["b"].all(), T["s"].all(), T["c"].all())
                P.tt(Vr.all().re("p a b -> p (a b)"), T["mag"].all(), T["c"].all(), ALU.mult)
                P.tt(Vi.all().re("p a b -> p (a b)"), T["mag"].all(), T["s"].all(), ALU.mult)
                ep = [P.sb("ep%d" % i, [128, 32], F32) for i in range(3)]
                for i in range(3):
                    P.dma("sp", ep[i].all(), DV(eprev, eprev.ap()[i]))
                pa = [P.sb("pa%d" % i, [128, 16], F32) for i in range(8)]
                cur_r, cur_i = A128r, A128i
                for sqi in range(4):
                    nr, ni = pa[2 * (sqi % 2)], pa[2 * (sqi % 2) + 1]
                    cmul(P, nr.all(), ni.all(), cur_r.all(), cur_i.all(), cur_r.all(), cur_i.all(), pa[4].all(), pa[5].all())
                    cur_r, cur_i = nr, ni
                ar, ai = pa[6], pa[7]
                xr = P.sb("xsr", [128, 16], F32)
                xi = P.sb("xsi", [128, 16], F32)
                cmul(P, ar.all(), ai.all(), cur_r.all(), cur_i.all(), ep[2][:, 0:16], ep[2][:, 16:32], pa[4].all(), pa[5].all())
                P.tt(ar.all(), ar.all(), ep[1][:, 0:16], ALU.add)
                P.tt(ai.all(), ai.all(), ep[1][:, 16:32], ALU.add)
                cmul(P, xr.all(), xi.all(), cur_r.all(), cur_i.all(), ar.all(), ai.all(), pa[4].all(), pa[5].all())
                P.tt(xr.all(), xr.all(), ep[0][:, 0:16], ALU.add)
                P.tt(xi.all(), xi.all(), ep[0][:, 16:32], ALU.add)
                cmul(P, Zr.all(), Zi.all(), A1r.all(), A1i.all(), xr.all(), xi.all(), pa[4].all(), pa[5].all())
                def swa_bias_setup():
                    Jf = P.sb("Jf", [128, 128], F32)
                    P.ts(Jf.all(), tmpi_e.all(), 0.0, None, op0=ALU.is_equal)
                    rb = P.sb("rb", [32, 8], F32)
                    ohs = P.sb("ohs", [32, 384], F32)
                    ngs = P.sb("ngs", [8, 384], F32)
                    P.dma("sp", ngs.all(), DV(neg_swa))
                    gsb = P.sb("gsb", [8, 384], F32)
                    fneg = P.sb("fneg", [128, 1], F32)
                    hk = [P.sb("hk%d" % i, [128, 8, 128], F32) for i in range(2)]
                    P.dma("sp", rb.all(), DV(rel_bias))
                    P.dma("sp", ohs.all(), DV(oh_swa))
                    P.dma("sp", fneg.all(), DV(firstneg))
                    pr_ = nring()
                    P.mm(pr_[0:8, 0:384], rb.all(), ohs.all())
                    P.tt(gsb.all(), pr_[0:8, 0:384], ngs.all(), ALU.add)
                    P.dma("sp", DV(gtab, buf=gtab_buf), gsb.all(), primary=gtab_buf)
                    for kt in range(2):
                        hkt = hk[kt]
                        off = 128 if kt == 0 else 0
                        P.dma("sp", hkt.all(), DV(gtab, dap(gtab, off, [[1, 128], [384, 8], [1, 128]]), buf=gtab_buf))
                        for half in range(2):
                            pr_ = nring()
                            P.mm(pr_.all(), Jf.all(), hkt[:, half * 4:(half + 1) * 4, :])
                            P.copy(BT[kt][:, half * 512:(half + 1) * 512], pr_.all())
                    P.ts(BT[2].all(), BT[0].all(), fneg.all(), None, op0=ALU.add)
            else:
                P.memset(Zr.all(), 0.0)
                P.memset(Zi.all(), 0.0)
            barrier(P)
            P.es = old_es
        if full and not DBG.get('no_bias'):
            with contextlib.ExitStack() as es3:
                old_es, P.es = P.es, es3
                swa_bias_setup()
                barrier(P)
                P.es = old_es

        wk = make_norm_work(P, "n_", pst)
        nxb = 4 if full else 2
        xb = [P.sb("xb%d" % i, [128, D], F32) for i in range(nxb)]
        hnT = P.sb("hnT", [128, 8, 512], BF16)
        uT = P.sb("uT", [128, 4, 512], BF16)
        tm = [P.sb("tm%d" % i, [128, 512], F32) for i in range(4)]
        tmB = [P.sb("tmB%d" % i, [128, 512], F32) for i in range(4)]
        tmsets = [tm, tm] if full else [tm, tmB]
        vre = [P.sb("vre%d" % i, [128, 4, 128], BF16) for i in range(2)]
        vim = [P.sb("vim%d" % i, [128, 4, 128], BF16) for i in range(2)]
        cnt = {"x": 0, "v": 0, "o": 0}
        outs = []
        if full:
            qT = P.sb("qTm", [128, 8, 512], BF16)
            P.memset(qT.all(), 0.0)
            kTd = P.sb("kTd", [128, 2, 640], BF16)
            Vd = P.sb("Vd", [128, 5, 256], BF16)
            gaT = P.sb("gaT", [128, 4, 512], BF16)
            gbT = P.sb("gbT", [128, 4, 512], BF16)
            caT = gaT
            cbT = gbT
            gyT = P.sb("gyT", [128, 4, 512], BF16)
            sqb = [P.sb("sqb%d" % i, [128, 512], BF16) for i in range(2)]
            sdb = [tm[2], tm[3]]
            Sb = [P.sb("Sb%d" % i, [128, 512], F32) for i in range(2)]
            PT = [P.sb("PT%d" % i, [128, 2, 512], BF16) for i in range(2)]
            dtot = Sb[1]
            xre = [P.sb("xre%d" % i, [128, 4, 128], BF16) for i in range(1)]
            xim = [P.sb("xim%d" % i, [128, 4, 128], BF16) for i in range(1)]
            ta = [P.sb("ta%d" % i, [128, 16], F32) for i in range(6)]
            gl = {"y": tm[1], "t": tm[0]}
            sg = [Sb[0]]
        else:
            esum = P.ps("esum", [128, 32])
            Sr = P.sb("Sr", [128, 16], F32)
            Si = P.sb("Si", [128, 16], F32)
            ta = [P.sb("ta%d" % i, [128, 16], F32) for i in range(6)]
            P.memset(Sr.all(), 0.0)
            P.memset(Si.all(), 0.0)

        def kv_project(blk_cols, kslot, vslot):
            for g in range(2):
                pr_ = nring()
                for k in range(8):
                    P.mm(pr_[:, 0:128], Wk(k, WKD + g * 128, 128), hnT[:, k, blk_cols], start=(k == 0), stop=(k == 7))
                s = sqb[g]
                P.act(s[:, 0:128], pr_[:, 0:128], AF.Square)
                p2 = nring()
                P.mm(p2[:, 0:128], ones2.all(), s[:, 0:128])
                d_ = sdb[g]
                P.act(d_[:, 0:128], p2[:, 0:128], AF.Ln, bias=wk["epsb"].all(), scale=1.0 / 64)
                P.act(d_[:, 0:128], d_[:, 0:128], AF.Exp, scale=-0.5)
                P.stt(kTd[:, g, kslot * 128:(kslot + 1) * 128], pr_[:, 0:128], gk.all(), d_[:, 0:128], ALU.mult, ALU.mult)
            pr_ = nring()
            for k in range(8):
                P.mm(pr_[:, 0:256], hnT[:, k, blk_cols], Wk(k, WVD, 256), start=(k == 0), stop=(k == 7))
            P.copy(Vd[:, vslot, :], pr_[:, 0:256], e="act")

        if full:
            xh = xb[nxb - 1]
            P.dma("sp", xh.all(), DV(x_halo))
            rmsnorm_to_fm(P, c, xh.all(), hnT[:, :, 0:128], g_fm, wk)
            kv_project(slice(0, 128), 0, 0)

        for sb_ in range(nsb):
            xs = []
            for bl in range(4):
                t0 = sb_ * 512 + bl * 128
                x = xb[cnt["x"] % nxb]
                cnt["x"] += 1
                xs.append(x)
                P.dma("sp", x.all(), DV(x_own, x_own.ap()[t0:t0 + 128, :]))
                rmsnorm_to_fm(P, c, x.all(), hnT[:, :, bl * 128:(bl + 1) * 128], g_fm, wk)
            for q in range(4):
                pr_ = nring()
                for k in range(8):
                    P.mm(pr_.all(), Wk(k, wu0 + q * 128, 128), hnT[:, k, :], start=(k == 0), stop=(k == 7))
                P.copy(uT[:, q, :], pr_.all(), e="act")
            if full:
                for t in range(4):
                    pr_ = nring()
                    for k in range(8):
                        P.mm(pr_.all(), Wk(k, WQ + t * 128, 128), hnT[:, k, :], start=(k == 0), stop=(k == 7))
                    s = sqb[t % 2]
                    P.act(s.all(), pr_.all(), AF.Square)
                    p2 = nring()
                    P.mm(p2.all(), ones2.all(), s.all())
                    d_ = sdb[t % 2]
                    P.act(d_.all(), p2.all(), AF.Ln, bias=wk["epsb"].all(), scale=1.0 / 64)
                    P.act(d_.all(), d_.all(), AF.Exp, scale=-0.5)
                    P.stt(qT[0:64, 2 * t, :], pr_[0:64, :], gq[0:64, :], d_[0:64, :], ALU.mult, ALU.mult)
                    P.stt(qT[64:128, 2 * t + 1, :], pr_[64:128, :], gq[64:128, :], d_[64:128, :], ALU.mult, ALU.mult)
                for (dst, c0) in ((gaT, WGA), (gbT, WGB)):
                    for t in range(4):
                        pr_ = nring()
                        for k in range(8):
                            P.mm(pr_.all(), Wk(k, c0 + t * 128, 128), hnT[:, k, :], start=(k == 0), stop=(k == 7))
                        P.act(dst[:, t, :], pr_.all(), AF.Silu)
                for bl in range(4):
                    kv_project(slice(bl * 128, (bl + 1) * 128), bl + 1, bl + 1)
                swa_steps = []

                def swa_step(bl, g, sb_=sb_):
                    qcols = slice(bl * 128, (bl + 1) * 128)
                    first = (sb_ == 0 and bl == 0)
                    if True:
                        banks = [nring(), nring()]
                        for kt in range(2):
                            kslot = bl + kt
                            for hh in range(4):
                                h = 4 * g + hh
                                lp = slice((h % 2) * 64, (h % 2) * 64 + 64)
                                P.mm(banks[kt][:, hh * 128:(hh + 1) * 128], kTd[:, g, kslot * 128:(kslot + 1) * 128],
                                     qT[:, h, qcols])
                        pt = PT[g]
                        for kt in range(2):
                            bt = BT[2] if (first and kt == 0) else BT[kt]
                            P.tt(Sb[kt].all(), banks[kt].all(), bt[:, g * 512:(g + 1) * 512], ALU.add)
                            P.act(pt[:, kt, :], Sb[kt].all(), AF.Exp)
                        pn, pd = nring(), nring()
                        for kt in range(2):
                            P.mm(pn.all(), Vd[:, bl + kt, g * 128:(g + 1) * 128], pt[:, kt, :], start=(kt == 0), stop=(kt == 1))
                        for kt in range(2):
                            P.mm(pd.all(), c["ones"].all(), pt[:, kt, :], start=(kt == 0), stop=(kt == 1))
                        for hh in range(4):
                            h = 4 * g + hh
                            P.ts(dtot[:, hh * 128:(hh + 1) * 128], pd[:, hh * 128:(hh + 1) * 128], esink[:, h:h + 1], None,
                                 op0=ALU.add)
                        P.act(dtot.all(), dtot.all(), AF.Ln)
                        P.act(dtot.all(), dtot.all(), AF.Exp, scale=-1.0)
                        for hh in range(4):
                            h = 4 * g + hh
                            lp = slice((h % 2) * 64, (h % 2) * 64 + 64)
                            o = caT[lp, h // 2, qcols]
                            tmo = Sb[0][lp, hh * 128:(hh + 1) * 128]
                            P.tt(tmo, pn[lp, hh * 128:(hh + 1) * 128], dtot[lp, hh * 128:(hh + 1) * 128], ALU.mult)
                            P.tt(o, tmo, gaT[lp, h // 2, qcols], ALU.mult)
                def halo_step():
                    P.copy(kTd[:, :, 0:128], kTd[:, :, 512:640], e="pool")
                    P.copy(Vd[:, 0, :], Vd[:, 4, :], e="pool")
                for bl in range(0 if DBG.get('no_swa') else 4):
                    for g in range(2):
                        swa_steps.append(lambda bl=bl, g=g: swa_step(bl, g))
                swa_steps.append(halo_step)
            ssm_steps = []
            pyd = {}

            def ssm_A_step(bl, q):
                tcols = slice(bl * 128, (bl + 1) * 128)
                if True:
                    bu = [nring(), nring()]
                    for hb_ in range(2):
                        ps_ = slice(64 * hb_, 64 * hb_ + 64)
                        P.mm(bu[hb_].all(), uT[ps_, q, tcols], BBp[ps_, q, :])
                    vr_, vi_ = vre[(4 * bl + q) % 2], vim[(4 * bl + q) % 2]
                    tmA = tmsets[(4 * bl + q) % 2]
                    for hb_ in range(2):
                        bre_ = bu[hb_].all().re("p (a c s) -> p a c s", a=2, c=2)[:, :, 0, :]
                        bim_ = bu[hb_].all().re("p (a c s) -> p a c s", a=2, c=2)[:, :, 1, :]
                        sts = slice(4 * q + 2 * hb_, 4 * q + 2 * hb_ + 2)
                        wr_, ws_ = Wr[:, sts, :], Ws[:, sts, :]
                        o = slice(hb_ * 256, hb_ * 256 + 256)
                        P.tt(tmA[0][:, o].re("p (a s) -> p a s", a=2), bre_, wr_, ALU.mult)
                        P.tt(tmA[1][:, o].re("p (a s) -> p a s", a=2), bim_, ws_, ALU.mult)
                        P.tt(tmA[2][:, o].re("p (a s) -> p a s", a=2), bim_, wr_, ALU.mult)
                        P.tt(tmA[3][:, o].re("p (a s) -> p a s", a=2), bre_, ws_, ALU.mult)
                    P.tt(vr_.all().re("p a s -> p (a s)"), tmA[0].all(), tmA[1].all(), ALU.add, e="pool")
                    P.tt(vi_.all().re("p a s -> p (a s)"), tmA[2].all(), tmA[3].all(), ALU.subtract, e="pool")
                    if full and DBG.get('no_ssm2'):
                        return
                    if not full:
                        return
            def esum_step(bl, q):
                vr_, vi_ = vre[(4 * bl + q) % 2], vim[(4 * bl + q) % 2]
                for st4 in range(4):
                    st = 4 * q + st4
                    P.mm(esum[:, st:st + 1], vr_[:, st4, :], c["ones"][:, 0:1])
                    P.mm(esum[:, 16 + st:17 + st], vi_[:, st4, :], c["ones"][:, 0:1])

            def ssm_B_step(bl, q):
                tcols = slice(bl * 128, (bl + 1) * 128)
                vr_, vi_ = vre[(4 * bl + q) % 2], vim[(4 * bl + q) % 2]
                if True:
                    csr, csi = nring(), nring()
                    for st4 in range(4):
                        P.mm(csr[:, st4 * 128:(st4 + 1) * 128], vr_[:, st4, :], tri.all())
                        P.mm(csi[:, st4 * 128:(st4 + 1) * 128], vi_[:, st4, :], tri.all())
                    xr_, xi_ = xre[0], xim[0]
                    for st4 in range(4):
                        st = 4 * q + st4
                        cr = csr[:, st4 * 128:(st4 + 1) * 128]
                        ci = csi[:, st4 * 128:(st4 + 1) * 128]
                        o = slice(st4 * 128, (st4 + 1) * 128)
                        P.stt(tmB[0][:, o], cr, Zr[:, st:st + 1], Vr[:, st, :], ALU.add, ALU.mult)
                        P.stt(tmB[1][:, o], ci, Zi[:, st:st + 1], Vi[:, st, :], ALU.add, ALU.mult)
                        P.stt(tmB[2][:, o], cr, Zr[:, st:st + 1], Vi[:, st, :], ALU.add, ALU.mult)
                        P.stt(tmB[3][:, o], ci, Zi[:, st:st + 1], Vr[:, st, :], ALU.add, ALU.mult)
                    sq_ = slice(4 * q, 4 * q + 4)
                    cr127 = csr.all().re("p (a s) -> p a s", a=4)[:, :, 127]
                    ci127 = csi.all().re("p (a s) -> p a s", a=4)[:, :, 127]
                    P.tt(ta[0][:, 0:4], Zr[:, sq_], cr127, ALU.add)
                    P.tt(ta[1][:, 0:4], Zi[:, sq_], ci127, ALU.add)
                    cmul(P, Zr[:, sq_], Zi[:, sq_], A128r[:, sq_], A128i[:, sq_], ta[0][:, 0:4], ta[1][:, 0:4],
                         ta[2][:, 0:4], ta[3][:, 0:4])
                    P.tt(xr_.all().re("p a s -> p (a s)"), tmB[0].all(), tmB[1].all(), ALU.subtract, e="pool")
                    P.tt(xi_.all().re("p a s -> p (a s)"), tmB[2].all(), tmB[3].all(), ALU.add, e="pool")
                    pyd['py'] = py_bank
                    py = py_bank
                    for st4 in range(4):
                        st = 4 * q + st4
                        P.mm(py[:, q * 128:(q + 1) * 128], Cre[:, st, :], xr_[:, st4, :], start=(st4 == 0), stop=False)
                        P.mm(py[:, q * 128:(q + 1) * 128], Cim[:, st, :], xi_[:, st4, :], start=False, stop=(st4 == 3))
            def ssm_tail_step(bl):
                tcols = slice(bl * 128, (bl + 1) * 128)
                if not full:
                    P.copy(ta[4].all(), esum[:, 0:16])
                    P.copy(ta[5].all(), esum[:, 16:32])
                    cmul(P, ta[0].all(), ta[1].all(), A127r.all(), A127i.all(), ta[4].all(), ta[5].all(), ta[2].all(), ta[3].all())
                    cmul(P, ta[4].all(), ta[5].all(), A128r.all(), A128i.all(), Sr.all(), Si.all(), ta[2].all(), ta[3].all())
                    P.tt(Sr.all(), ta[0].all(), ta[4].all(), ALU.add)
                    P.tt(Si.all(), ta[1].all(), ta[5].all(), ALU.add)
                    return
                if DBG.get('no_ssm2'):
                    return
                for q in range(4):
                    P.stt(gl["y"][:, q * 128:(q + 1) * 128], uT[:, q, tcols], d_fm[:, q:q + 1], pyd['py'][:, q * 128:(q + 1) * 128],
                          ALU.mult, ALU.add)
                P.act(gyT[:, :, tcols], gl["y"].all().re("p (q t) -> p q t", q=4), AF.Gelu_apprx_tanh)
            items = [(bl, q) for bl in range(4) for q in range(4)]
            if full and not DBG.get('no_ssm2'):
                for k in range(2):
                    ssm_steps.append(lambda k=k: ssm_A_step(*items[k]))
                for k in range(16):
                    ssm_steps.append(lambda k=k: ssm_B_step(*items[k]))
                    if k + 2 < 16:
                        ssm_steps.append(lambda k=k: ssm_A_step(*items[k + 2]))
                    if items[k][1] == 3:
                        ssm_steps.append(lambda k=k: ssm_tail_step(items[k][0]))
            elif not full:
                ssm_steps.append(lambda: ssm_A_step(*items[0]))
                for k in range(16):
                    if k + 1 < 16:
                        ssm_steps.append(lambda k=k: ssm_A_step(*items[k + 1]))
                    ssm_steps.append(lambda k=k: esum_step(*items[k]))
                    if items[k][1] == 3:
                        ssm_steps.append(lambda k=k: ssm_tail_step(items[k][0]))
            else:
                for bl in range(4):
                    for q in range(4):
                        ssm_steps.append(lambda bl=bl, q=q: ssm_A_step(bl, q))
            interleave_steps(swa_steps if full else [], ssm_steps)
            if not full:
                continue
            for f in range(0 if DBG.get('no_glu') else 4):
                pa_, pb_ = nring(), nring()
                for q in range(4):
                    P.mm(pa_.all(), Wg[:, q, f * 128:(f + 1) * 128], gyT[:, q, :], start=(q == 0), stop=(q == 3))
                for q in range(4):
                    P.mm(pb_.all(), Wg[:, q, 512 + f * 128:512 + (f + 1) * 128], gyT[:, q, :], start=(q == 0), stop=(q == 3))
                s_ = sg[0]
                P.act(s_.all(), pb_.all(), AF.Sigmoid, bias=glu_b[:, 4 + f:5 + f])
                P.stt(gl["t"].all(), pa_.all(), glu_b[:, f:f + 1], s_.all(), ALU.add, ALU.mult)
                P.tt(cbT[:, f, :], gl["t"].all(), gbT[:, f, :], ALU.mult)
            for bl in range(4):
                tcols = slice(bl * 128, (bl + 1) * 128)
                t0 = sb_ * 512 + bl * 128
                o_ = xs[bl]
                for half in range(0 if DBG.get('no_out') else 2):
                    po = nring()
                    for k in range(8):
                        lhs = caT[:, k, tcols] if k < 4 else cbT[:, k - 4, tcols]
                        P.mm(po.all(), lhs, Wo[:, k, half * 512:(half + 1) * 512], start=(k == 0), stop=(k == 7))
                    P.tt(o_[:, half * 512:(half + 1) * 512], po.all(), xs[bl][:, half * 512:(half + 1) * 512], ALU.add)
                P.dma("pool", DV(h1_out, h1_out.ap()[t0:t0 + 128, :]), o_.all(), primary=o_.buf)
                outs.append(o_.buf)
        if not full:
            eo = P.sb("eo", [128, 32], F32)
            P.copy(eo[:, 0:16], Sr.all())
            P.copy(eo[:, 16:32], Si.all())
            P.dma("pool", DV(eloc), eo.all(), primary=eo.buf)
            outs.append(eo.buf)
            outs += tab_outs
        P.finish(outs)
    return nc, P


def swa_onehot():
    e = np.arange(384)
    d = e - 127
    valid = (d >= 0) & (d < 128)
    oh = np.zeros((32, 384), np.float32)
    oh[t5_bucket_np(d)[valid], e[valid]] = 1.0
    neg = np.tile(np.where(valid, 0.0, NEG).astype(np.float32)[None, :], (8, 1))
    return oh, neg


def l0_inputs(inp, b, r, eprev=None, full=True):
    f32 = np.float32
    x = inp["x"][b]
    d = {}
    d["x_own"] = np.ascontiguousarray(x[r * TPC:(r + 1) * TPC])
    d["w_in"] = np.ascontiguousarray(inp["ev_w_in"][0])
    d["ng_fm"] = np.ascontiguousarray(inp["norm_g"][0].reshape(8, 128).T)
    a_re = inp["ev_ssm_a_re"][0]
    a_im = inp["ev_ssm_a_im"][0]
    ldt = np.repeat(inp["ev_ssm_log_dt"][0], 64)
    d["a_re_f"] = np.ascontiguousarray(a_re.reshape(2048))
    d["a_im_f"] = np.ascontiguousarray(a_im.reshape(2048))
    d["ldt_f"] = np.ascontiguousarray(ldt)
    d["a_re_s"] = np.ascontiguousarray(a_re.reshape(16, 128).T)
    d["a_im_s"] = np.ascontiguousarray(a_im.reshape(16, 128).T)
    d["ldt_s"] = np.ascontiguousarray(ldt.reshape(16, 128).T)
    for nm, src in (("b_blk_re", inp["ev_ssm_b_re"][0]), ("b_blk_im", inp["ev_ssm_b_im"][0])):
        blk = np.zeros((4, 2, 16, 4, 2, 64), f32)
        s6 = src.reshape(4, 4, 2, 64, 16)
        for g2 in range(2):
            blk[:, g2, :, :, g2, :] = s6[:, :, g2].transpose(1, 3, 0, 2)
        d[nm] = np.ascontiguousarray(blk.reshape(128, 4, 128))
    if not full:
        return d
    d["x_halo"] = np.ascontiguousarray(x[r * TPC - 128:r * TPC]) if r > 0 else np.zeros((128, D), f32)
    d["firstneg"] = np.full((128, 1), NEG if r == 0 else 0.0, f32)
    d["w_out"] = np.ascontiguousarray(inp["ev_w_out"][0])
    d["glu_w"] = np.ascontiguousarray(inp["ev_glu_w"][0])
    d["qg2"] = np.ascontiguousarray(np.tile(inp["ev_q_norm_g"][0], 2)[:, None])
    d["kg2"] = np.ascontiguousarray(np.tile(inp["ev_k_norm_g"][0], 2)[:, None])
    d["sinks_row"] = np.ascontiguousarray(np.tile(inp["ev_sinks"][0][None, :], (128, 1)))
    d["rel_bias"] = np.ascontiguousarray(inp["rel_bias"])
    d["oh_swa"], d["neg_swa"] = swa_onehot()
    for nm, src in (("c_blk_re", inp["ev_ssm_c_re"][0]), ("c_blk_im", inp["ev_ssm_c_im"][0])):
        blk = np.zeros((2, 64, 16, 8, 16), f32)
        s5 = src.reshape(16, 2, 16, 64)
        for st in range(16):
            for g2 in range(2):
                blk[g2, :, st, 2 * (st % 4) + g2, :] = s5[st, g2].T
        d[nm] = np.ascontiguousarray(blk.reshape(128, 16, 128))
    d["d_fm"] = np.ascontiguousarray(inp["ev_ssm_d"][0].reshape(4, 128).T)
    d["glu_b_fm"] = np.ascontiguousarray(inp["ev_glu_b"][0].reshape(8, 128).T)
    d["eprev"] = np.zeros((3, 128, 32), f32) if eprev is None else np.ascontiguousarray(eprev)
    return d


_PROGS = {}


def _prog(name):
    if name not in _PROGS:
        if name == "p1":
            _PROGS[name] = build_p2("p1")[0]
        elif name == "p2":
            _PROGS[name] = build_p2("full")[0]
        elif name == "p2b":
            _PROGS[name] = build_p2b()[0]
        elif name == "p3":
            _PROGS[name] = build_p3()[0]
    return _PROGS[name]


def _run(name, maps):
    return run_bass_kernel_spmd(_prog(name), maps, core_ids=list(range(NCORES))).results


def kernel(**inputs):
    inp = {k: np.asarray(v) for k, v in inputs.items()}
    f32 = np.float32
    r1 = _run("p1", [l0_inputs(inp, c // 4, c % 4, full=False) for c in range(NCORES)])
    eloc = [np.asarray(r1[c]["eloc"], f32) for c in range(NCORES)]
    tabs = [(np.asarray(r1[c]["tab_w"]), np.asarray(r1[c]["tab_bb"])) for c in range(NCORES)]
    maps = []
    for c in range(NCORES):
        b, r = c // 4, c % 4
        ep = np.zeros((3, 128, 32), f32)
        for kk in range(min(r, 3)):
            ep[kk] = eloc[4 * b + r - 1 - kk]
        m_ = l0_inputs(inp, b, r, eprev=ep)
        m_["tab_w"], m_["tab_bb"] = tabs[c]
        maps.append(m_)
    r2 = _run("p2", maps)
    h1 = [np.asarray(r2[c]["h1"], f32) for c in range(NCORES)]
    maps = [{"h1": h1[c], "w_in": np.ascontiguousarray(inp["od_w_in"][0]), "ng": np.ascontiguousarray(inp["norm_g"][1]),
             "qg": np.ascontiguousarray(inp["od_q_norm_g"][0]), "kg": np.ascontiguousarray(inp["od_k_norm_g"][0])}
            for c in range(NCORES)]
    r3 = _run("p2b", maps)
    maps = []
    for c in range(NCORES):
        b, j = c // 4, c % 4
        cat = lambda nm, ax: np.concatenate([np.asarray(r3[4 * b + r][nm]) for r in range(4)], axis=ax)
        ohm, cmask = p3_consts(j)
        h1b = np.concatenate([h1[4 * b + r] for r in range(4)], axis=0).reshape(64, 128, D)
        maps.append({
            "q_blk": np.ascontiguousarray(cat("q_out", 0)[j::4]),
            "g_blk": np.ascontiguousarray(cat("g_out", 0)[j::4]),
            "qi_blk": np.ascontiguousarray(cat("qi_out", 0)[j::4]),
            "wi_blk": np.ascontiguousarray(cat("wi_out", 0).reshape(64, 128, 8)[j::4]),
            "h1_blk": np.ascontiguousarray(h1b[j::4]),
            "kT_all": np.ascontiguousarray(cat("kT_out", 2)),
            "v_all": np.ascontiguousarray(cat("v_out", 0)),
            "ki_all": np.ascontiguousarray(cat("ki_out", 1)),
            "cmask": cmask, "oh": ohm,
            "rel_bias": np.ascontiguousarray(inp["rel_bias"]),
            "w_out": np.ascontiguousarray(inp["od_w_out"][0]),
        })
    r4 = _run("p3", maps)
    out = np.zeros((BATCH, SEQ // 128, 128, D), f32)
    for c in range(NCORES):
        b, j = c // 4, c % 4
        out[b, j::4] = np.asarray(r4[c]["y"], f32)
    return out.reshape(BATCH, SEQ, D)
```
